# Optimizing a Trainium2 kernel written in Bass

```python
import math
import jax, jax.numpy as jnp
from jax import lax
import numpy as np

D_MODEL = 1024
BATCH = 32
SEQ = 256
DEPTH = 4
DEC_BATCH = 8
DEC_SEQ = 1024
PAST_LEN = 256

GRID_W = 64
N_EVEN = (DEPTH + 1) // 2
N_ODD = DEPTH // 2
EPS = 1e-6
N_MOD = 9
GLA_HEADS = 4
GLA_DK = 64
GLA_DV = 128
GLA_QK = GLA_HEADS * GLA_DK
GLA_VW = GLA_HEADS * GLA_DV
GLA_GATE_RANK = 16
GLA_GATE_NORM = 16.0
GLA_CHUNK = 64
MLA_HEADS = 8
MLA_NOPE = 64
MLA_ROPE = 32
MLA_V = 64
MLA_QK_DIM = MLA_NOPE + MLA_ROPE
MLA_Q_RANK = 384
MLA_KV_RANK = 256
ROPE_BASE = 10000.0
ROPE_AXIS_PAIRS = MLA_ROPE // 4
ATTN_BLOCK = 128
EVEN_SPLIT = (GLA_QK, GLA_QK, GLA_VW, GLA_VW, GLA_GATE_RANK, GLA_GATE_RANK, MLA_Q_RANK, MLA_KV_RANK, MLA_ROPE)
EVEN_IN_WIDTH = 2 * GLA_QK + 2 * GLA_VW + 2 * GLA_GATE_RANK + MLA_Q_RANK + MLA_KV_RANK + MLA_ROPE
EVEN_OUT_WIDTH = GLA_VW + MLA_HEADS * MLA_V
CMLP_CHUNK = 128
CMLP_GROUPS = 4
CMLP_WIDTH = 2 * D_MODEL
FFN_HIDDEN = 2816

kernel_name = 'hybrid_gla_mla_gmlp_diffusion_step'


def rms_norm(x, g):
    xf = x.astype(jnp.float32)
    y = xf * lax.rsqrt(jnp.mean(xf * xf, axis=-1, keepdims=True) + EPS)
    return (y * g.astype(jnp.float32)).astype(x.dtype)


def split_cols(h, sizes):
    out, o = [], 0
    for s in sizes:
        out.append(h[..., o:o + s])
        o += s
    return out


def to_heads(x, n_heads):
    b, t, w = x.shape
    return x.reshape(b, t, n_heads, w // n_heads).transpose(0, 2, 1, 3)


def from_heads(x):
    b, h, t, d = x.shape
    return x.transpose(0, 2, 1, 3).reshape(b, t, h * d)


def modulation(cond, w_ada, b_ada):
    m = jax.nn.silu(cond) @ w_ada + b_ada
    return jnp.split(m[:, None, :], N_MOD, axis=-1)


def modulate(x, g, shift, scale):
    return rms_norm(x, g) * (1 + scale) + shift


def swiglu(h, w_gate, w_up, w_down):
    return (jax.nn.silu(h @ w_gate) * (h @ w_up)) @ w_down


def axial_rope_tables(n):
    rows = n // GRID_W
    row = jnp.repeat(jnp.arange(rows, dtype=jnp.float32), GRID_W)
    col = jnp.tile(jnp.arange(GRID_W, dtype=jnp.float32), rows)
    inv = ROPE_BASE ** (-jnp.arange(ROPE_AXIS_PAIRS, dtype=jnp.float32) / ROPE_AXIS_PAIRS)
    ang = jnp.concatenate([row[:, None] * inv, col[:, None] * inv], axis=-1)
    return jnp.cos(ang), jnp.sin(ang)


def apply_axial_rope(x, cos, sin):
    cos = cos.astype(x.dtype)
    sin = sin.astype(x.dtype)
    x1, x2 = x[..., 0::2], x[..., 1::2]
    return jnp.stack([x1 * cos - x2 * sin, x1 * sin + x2 * cos], axis=-1).reshape(x.shape)


def rope_tail(x, rope):
    if rope is None:
        return x
    cos, sin = rope
    return jnp.concatenate([x[..., :MLA_NOPE], apply_axial_rope(x[..., MLA_NOPE:], cos, sin)], axis=-1)


def block_attention(q, k, v):
    b, h, n, dh = q.shape
    scale = dh ** -0.5
    qb = q.reshape(b, h, n // ATTN_BLOCK, ATTN_BLOCK, dh).transpose(2, 0, 1, 3, 4)

    def one_block(qi):
        s = jnp.einsum('bhqd,bhkd->bhqk', qi, k).astype(jnp.float32) * scale
        p = jax.nn.softmax(s, axis=-1).astype(v.dtype)
        return jnp.einsum('bhqk,bhkd->bhqd', p, v)

    o = lax.map(one_block, qb)
    return o.transpose(1, 2, 0, 3, 4).reshape(b, h, n, v.shape[-1])


def gla_log_decay(g_low, w2, bias):
    z = (g_low @ w2 + bias).astype(jnp.float32)
    return jax.nn.log_sigmoid(z) / GLA_GATE_NORM


def gla_chunk_scan(q, k, v, log_a, s0):
    b, h, t, dk = q.shape
    dv = v.shape[-1]
    nc = t // GLA_CHUNK

    def to_chunks(x):
        return x.reshape(b, h, nc, GLA_CHUNK, x.shape[-1]).transpose(2, 0, 1, 3, 4)

    lower = jnp.tril(jnp.ones((GLA_CHUNK, GLA_CHUNK), dtype=bool))[:, :, None]

    def step(s, xs):
        qc, kc, vc, lc = xs
        cum = jnp.cumsum(lc, axis=2)
        o_inter = jnp.einsum('bhcd,bhde->bhce', qc * jnp.exp(cum), s)
        diff = jnp.where(lower, cum[:, :, :, None, :] - cum[:, :, None, :, :], -jnp.inf)
        att = jnp.einsum('bhtd,bhsd,bhtsd->bhts', qc, kc, jnp.exp(diff))
        o = o_inter + jnp.einsum('bhts,bhse->bhte', att, vc)
        last = cum[:, :, -1:, :]
        s_new = jnp.exp(last[:, :, 0, :, None]) * s + jnp.einsum('bhcd,bhce->bhde', kc * jnp.exp(last - cum), vc)
        return s_new, o

    s_fin, o = lax.scan(step, s0, (to_chunks(q), to_chunks(k), to_chunks(v), to_chunks(log_a)))
    return o.transpose(1, 2, 0, 3, 4).reshape(b, h, t, dv), s_fin


def gla_bidirectional(q, k, v, la_f, la_b, s0_f, s0_b):
    o_f, s_f = gla_chunk_scan(q, k, v, la_f, s0_f)
    flip = lambda x: jnp.flip(x, axis=2)
    o_b, s_b = gla_chunk_scan(flip(q), flip(k), flip(v), flip(la_b), s0_b)
    return o_f + flip(o_b), s_f, s_b


def mla_queries(cq_raw, qa_g, qb_w, qn_g, rope):
    cq = rms_norm(cq_raw, qa_g)
    q = jnp.einsum('bnr,rhe->bhne', cq, qb_w)
    return rope_tail(rms_norm(q, qn_g), rope)


def mla_keys_values(ckv, krope, kvb_w, kn_g, rope):
    b, m, _ = ckv.shape
    kv = jnp.einsum('bmr,rhe->bhme', ckv, kvb_w)
    k_nope, v = kv[..., :MLA_NOPE], kv[..., MLA_NOPE:]
    k_r = jnp.broadcast_to(krope[:, None], (b, MLA_HEADS, m, MLA_ROPE))
    k = rms_norm(jnp.concatenate([k_nope, k_r], axis=-1), kn_g)
    return rope_tail(k, rope), v


def even_mixer(h, w_in, w_out, gate_w2, gate_b, gla_g, qa_g, qb_w, kva_g, kvb_w, qn_g, kn_g, ctx_cache):
    b, n, _ = h.shape
    f32 = jnp.float32
    q, k, v, og, gl_f, gl_b, cq_raw, ckv_raw, krope = split_cols(h @ w_in, EVEN_SPLIT)
    qh = to_heads(q, GLA_HEADS).astype(f32) * GLA_DK ** -0.5
    kh = to_heads(k, GLA_HEADS).astype(f32)
    vh = to_heads(v, GLA_HEADS).astype(f32)
    la_f = to_heads(gla_log_decay(gl_f, gate_w2[0], gate_b[0]), GLA_HEADS)
    la_b = to_heads(gla_log_decay(gl_b, gate_w2[1], gate_b[1]), GLA_HEADS)
    if ctx_cache is None:
        s0_f = jnp.zeros((b, GLA_HEADS, GLA_DK, GLA_DV), f32)
        s0_b = s0_f
        rope = None
    else:
        ckv_c, krope_c, s_c = ctx_cache
        s0_f = s_c[:, 0].astype(f32)
        s0_b = s_c[:, 1].astype(f32)
        rope = axial_rope_tables(n)
    o, s_f, s_b = gla_bidirectional(qh, kh, vh, la_f, la_b, s0_f, s0_b)
    gla_out = from_heads(rms_norm(o, gla_g)).astype(h.dtype) * jax.nn.silu(og)
    ckv = rms_norm(ckv_raw, kva_g)
    qm = mla_queries(cq_raw, qa_g, qb_w, qn_g, rope)
    km, vm = mla_keys_values(ckv, krope, kvb_w, kn_g, rope)
    if ctx_cache is not None:
        kc, vc = mla_keys_values(ckv_c, krope_c, kvb_w, kn_g, None)
        km = jnp.concatenate([kc, km], axis=2)
        vm = jnp.concatenate([vc, vm], axis=2)
    mla_out = from_heads(block_attention(qm, km, vm))
    out = jnp.concatenate([gla_out, mla_out], axis=-1) @ w_out
    if ctx_cache is None:
        return out, (ckv, krope, jnp.stack([s_f, s_b], axis=1))
    return out, None


def odd_mixer(h, w_in, v_g, w_s, b_s, w_out):
    b, n, _ = h.shape
    uv = jax.nn.gelu(h @ w_in)
    u, v = uv[..., :CMLP_WIDTH], uv[..., CMLP_WIDTH:]
    v = rms_norm(v, v_g)
    vg = v.reshape(b, n // CMLP_CHUNK, CMLP_CHUNK, CMLP_GROUPS, CMLP_WIDTH // CMLP_GROUPS)
    mixed = jnp.einsum('gpq,bcqge->bcpge', w_s, vg) + b_s.T[:, :, None]
    return (u * mixed.reshape(b, n, CMLP_WIDTH)) @ w_out


def setup_inputs(seed: int = 0) -> dict:
    key = jax.random.key(seed)
    ks = iter(jax.random.split(key, 40))

    def nrm(shape, scale=1.0):
        return jax.random.normal(next(ks), shape, jnp.float32) * scale

    D, F = D_MODEL, FFN_HIDDEN
    return {
        'x_prompt': nrm((BATCH, SEQ, D)),
        'x_sample': nrm((DEC_BATCH, DEC_SEQ, D)),
        'c': nrm((DEC_BATCH, D)),
        'c_ctx': nrm((D,)),
        'cache_ckv': nrm((DEC_BATCH, N_EVEN, PAST_LEN, MLA_KV_RANK)),
        'cache_krope': nrm((DEC_BATCH, N_EVEN, PAST_LEN, MLA_ROPE)),
        'state_gla': nrm((DEC_BATCH, N_EVEN, 2, GLA_HEADS, GLA_DK, GLA_DV)),
        'ada_w': nrm((DEPTH, D, N_MOD * D), D ** -0.5),
        'ada_b': nrm((DEPTH, N_MOD * D), 0.01),
        'norm_g': 1.0 + nrm((DEPTH, 3, D), 0.02),
        'ffn1_wg': nrm((DEPTH, D, F), D ** -0.5),
        'ffn1_wu': nrm((DEPTH, D, F), D ** -0.5),
        'ffn1_wd': nrm((DEPTH, F, D), F ** -0.5),
        'ffn2_wg': nrm((DEPTH, D, F), D ** -0.5),
        'ffn2_wu': nrm((DEPTH, D, F), D ** -0.5),
        'ffn2_wd': nrm((DEPTH, F, D), F ** -0.5),
        'even_w_in': nrm((N_EVEN, D, EVEN_IN_WIDTH), D ** -0.5),
        'even_w_out': nrm((N_EVEN, EVEN_OUT_WIDTH, D), EVEN_OUT_WIDTH ** -0.5),
        'gla_gate_w2': nrm((N_EVEN, 2, GLA_GATE_RANK, GLA_QK), GLA_GATE_RANK ** -0.5),
        'gla_gate_b': nrm((N_EVEN, 2, GLA_QK), 0.1),
        'gla_norm_g': 1.0 + nrm((N_EVEN, GLA_DV), 0.02),
        'mla_qa_g': 1.0 + nrm((N_EVEN, MLA_Q_RANK), 0.02),
        'mla_qb_w': nrm((N_EVEN, MLA_Q_RANK, MLA_HEADS, MLA_QK_DIM), MLA_Q_RANK ** -0.5),
        'mla_kva_g': 1.0 + nrm((N_EVEN, MLA_KV_RANK), 0.02),
        'mla_kvb_w': nrm((N_EVEN, MLA_KV_RANK, MLA_HEADS, MLA_NOPE + MLA_V), MLA_KV_RANK ** -0.5),
        'mla_qn_g': 1.0 + nrm((N_EVEN, MLA_QK_DIM), 0.02),
        'mla_kn_g': 1.0 + nrm((N_EVEN, MLA_QK_DIM), 0.02),
        'odd_w_in': nrm((N_ODD, D, 2 * CMLP_WIDTH), D ** -0.5),
        'odd_v_g': 1.0 + nrm((N_ODD, CMLP_WIDTH), 0.02),
        'odd_ws': nrm((N_ODD, CMLP_GROUPS, CMLP_CHUNK, CMLP_CHUNK), CMLP_CHUNK ** -0.5),
        'odd_bs': 1.0 + nrm((N_ODD, CMLP_GROUPS, CMLP_CHUNK), 0.1),
        'odd_w_out': nrm((N_ODD, CMLP_WIDTH, D), CMLP_WIDTH ** -0.5),
    }


def reference(x_prompt, x_sample, c, c_ctx, cache_ckv, cache_krope, state_gla, ada_w, ada_b, norm_g,
              ffn1_wg, ffn1_wu, ffn1_wd, ffn2_wg, ffn2_wu, ffn2_wd, even_w_in, even_w_out,
              gla_gate_w2, gla_gate_b, gla_norm_g, mla_qa_g, mla_qb_w, mla_kva_g, mla_kvb_w,
              mla_qn_g, mla_kn_g, odd_w_in, odd_v_g, odd_ws, odd_bs, odd_w_out):

    def run_trunk(x, cond, ctx_caches):
        new_ckv, new_krope, new_gla = [], [], []
        for i in range(DEPTH):
            sh1, sc1, g1, sh2, sc2, g2, sh3, sc3, g3 = modulation(cond, ada_w[i], ada_b[i])
            x = x + 0.5 * g1 * swiglu(modulate(x, norm_g[i, 0], sh1, sc1), ffn1_wg[i], ffn1_wu[i], ffn1_wd[i])
            hm = modulate(x, norm_g[i, 1], sh2, sc2)
            j = i // 2
            if i % 2 == 0:
                cache_j = None if ctx_caches is None else (ctx_caches[0][:, j], ctx_caches[1][:, j], ctx_caches[2][:, j])
                mix, st = even_mixer(hm, even_w_in[j], even_w_out[j], gla_gate_w2[j], gla_gate_b[j], gla_norm_g[j],
                                     mla_qa_g[j], mla_qb_w[j], mla_kva_g[j], mla_kvb_w[j], mla_qn_g[j], mla_kn_g[j],
                                     cache_j)
                if st is not None:
                    new_ckv.append(st[0])
                    new_krope.append(st[1])
                    new_gla.append(st[2].astype(x.dtype))
            else:
                mix = odd_mixer(hm, odd_w_in[j], odd_v_g[j], odd_ws[j], odd_bs[j], odd_w_out[j])
            x = x + g2 * mix
            x = x + 0.5 * g3 * swiglu(modulate(x, norm_g[i, 2], sh3, sc3), ffn2_wg[i], ffn2_wu[i], ffn2_wd[i])
        return x, new_ckv, new_krope, new_gla

    y_prompt, ckv_list, krope_list, gla_list = run_trunk(x_prompt, c_ctx[None, :], None)
    new_ckv = jnp.stack(ckv_list, axis=1)
    new_krope = jnp.stack(krope_list, axis=1)
    new_gla = jnp.stack(gla_list, axis=1)
    y_sample, _, _, _ = run_trunk(x_sample, c, (cache_ckv, cache_krope, state_gla))
    return (y_prompt, y_sample, new_ckv, new_krope, new_gla)
```

```python
import contextlib
import numpy as np
import concourse.bass as bass
import concourse.mybir as mybir
from concourse.bass_utils import run_bass_kernel_spmd

F32 = mybir.dt.float32
BF16 = mybir.dt.bfloat16
AF = mybir.ActivationFunctionType
ALU = mybir.AluOpType

D = 1024
KC = 8
NTOK = 2048
TN = 512
NT = NTOK // TN
DEPTH = 4
FH = 2816
FC = FH // 128
GCH = 2
NG = FC // GCH
EPS = 1e-6
SLOT = 6144
NSLOT = 3
SEM_LIMIT = 30000


class Buf:
    __slots__ = ("name", "w", "r")

    def __init__(self, name):
        self.name = name
        self.w = None
        self.r = {}


class Eng:
    def __init__(self, nc, h, name, nsem):
        self.nc = nc
        self.h = h
        self.name = name
        self.sems = [nc.alloc_semaphore(name=f"e_{name}_{i}") for i in range(nsem)]
        self.si = 0
        self.cnt = 0
        self.seen = {}
        self.own = set(id(s) for s in self.sems)

    def wait(self, tok, same=False):
        if tok is None:
            return
        sem, val = tok
        if id(sem) in self.own and not same:
            return
        k = id(sem)
        if self.seen.get(k, 0) >= val:
            return
        self.h.wait_ge(sem, val)
        self.seen[k] = val

    def signal(self, ins):
        sem = self.sems[self.si]
        ins.then_inc(sem, 1)
        self.cnt += 1
        tok = (sem, self.cnt)
        if self.cnt >= SEM_LIMIT:
            self.si += 1
            self.cnt = 0
        return tok


class Builder:
    def __init__(self, cfg):
        self.cfg = cfg
        nc = bass.Bass("TRN2", target_bir_lowering=False)
        self.nc = nc
        self.pe = Eng(nc, nc.tensor, "pe", 1)
        self.act = Eng(nc, nc.scalar, "act", 2)
        self.dve = Eng(nc, nc.vector, "dve", 2)
        self.pool = Eng(nc, nc.gpsimd, "pool", 0)
        self.sp = Eng(nc, nc.sync, "sp", 0)
        self.es = contextlib.ExitStack()
        self.dram = {}
        self.nbuf = 0

    def din(self, name, shape, dt=F32):
        t = self.nc.dram_tensor(name, list(shape), dt, kind="ExternalInput").ap()
        self.dram[name] = t
        return t

    def dout(self, name, shape, dt=F32):
        t = self.nc.dram_tensor(name, list(shape), dt, kind="ExternalOutput").ap()
        self.dram[name] = t
        return t

    def sb(self, name, shape, dt, stack=None):
        self.nbuf += 1
        return (stack or self.es).enter_context(self.nc.sbuf_tensor(f"{name}_{self.nbuf}", list(shape), dt))

    def buf(self, name="b"):
        self.nbuf += 1
        return Buf(f"{name}{self.nbuf}")

    def op(self, eng, fns, reads=(), writes=(), same=False):
        for b in reads:
            eng.wait(b.w, same)
        for b in writes:
            eng.wait(b.w, same)
            for t in b.r.values():
                eng.wait(t, same)
        if not isinstance(fns, (list, tuple)):
            fns = [fns]
        ins = None
        for f in fns:
            ins = f()
        tok = eng.signal(ins)
        for b in reads:
            b.r[eng.name] = tok
        for b in writes:
            b.w = tok
            b.r = {}
        return tok

    def dma(self, q, sem_state, out, in_, reads=(), writes=(), **kw):
        for t in getattr(self, "last_bar", []):
            q.wait(t)
        for b in reads:
            q.wait(b.w)
        for b in writes:
            q.wait(b.w)
            for t in b.r.values():
                q.wait(t)
        q.h.dma_start(out=out, in_=in_, **kw).then_inc(sem_state[0], 16)
        sem_state[1] += 16
        tok = (sem_state[0], sem_state[1])
        for b in reads:
            b.r["dma_" + q.name] = tok
        for b in writes:
            b.w = tok
            b.r = {}
        return tok

    def newsem(self, name):
        return [self.nc.alloc_semaphore(name=name), 0]

    def barrier(self, engs=None):
        engs = engs or [self.pe, self.act, self.dve]
        toks = []
        for e in engs:
            if e.cnt > 0:
                toks.append((e.sems[e.si], e.cnt))
        for e in engs:
            for t in toks:
                e.wait(t)
        self.last_bar = toks

    def ring_init(self):
        self.ring = self.sb("ring", [128, NSLOT, SLOT], BF16)
        self.ring_bufs = [self.buf("slot") for _ in range(NSLOT)]
        self.ring_sems = [self.newsem(f"ring{i}") for i in range(NSLOT)]
        self.ring_i = 0

    def ring_load(self, pieces):
        s = self.ring_i % NSLOT
        self.ring_i += 1
        b = self.ring_bufs[s]
        for piece in pieces:
            off, dims, src = piece[0], piece[1], piece[2]
            npart = piece[3] if len(piece) > 3 else 128
            n = int(np.prod(dims))
            dst = self.ring[0:npart, s, off:off + n]
            if len(dims) == 2:
                dst = dst.rearrange("p (a b) -> p a b", a=dims[0])
            for t in b.r.values():
                self.pool.wait(t)
            self.pool.h.dma_start(out=dst, in_=src).then_inc(self.ring_sems[s][0], 16)
            self.ring_sems[s][1] += 16
        b.w = (self.ring_sems[s][0], self.ring_sems[s][1])
        b.r = {}
        return s, b

    def build(self):
        cfg = self.cfg
        nc = self.nc
        depth = cfg.get("depth", DEPTH)
        xT_d = self.din("xT", [D, NTOK])
        condT_d = self.din("condT", [128, KC, 2])
        adab_d = self.din("adab", [128, DEPTH, 72])
        normg_d = self.din("normg", [128, DEPTH, 3, KC])
        ada_w = self.din("ada_w", [DEPTH, D, 9 * D])
        wg = [self.din("ffn1_wg", [DEPTH, D, FH]), self.din("ffn2_wg", [DEPTH, D, FH])]
        wu = [self.din("ffn1_wu", [DEPTH, D, FH]), self.din("ffn2_wu", [DEPTH, D, FH])]
        wd = [self.din("ffn1_wd", [DEPTH, FH, D]), self.din("ffn2_wd", [DEPTH, FH, D])]
        yT_d = self.dout("yT", [D, NTOK])

        self.xT = self.sb("xT_sb", [128, KC, NTOK], F32)
        self.x_bufs = [self.buf("x") for _ in range(NT)]
        self.ones_bf = self.sb("ones_bf", [128, 128], BF16)
        self.condT = self.sb("condT_sb", [128, KC, 2], F32)
        self.scond = self.sb("scond", [128, KC, 2], BF16)
        self.adab = self.sb("adab_sb", [128, DEPTH, 72], F32)
        self.normg = self.sb("normg_sb", [128, DEPTH, 3, KC], F32)
        self.mod2 = [self.sb("mod_sb", [128, 72, 2], F32)] * 2
        self.modA2 = [self.sb("modA", [128, 3, KC, 2], F32)] * 2
        self.modB2 = [self.sb("modB", [128, 3, KC, 2], F32)] * 2
        self.modG2 = [self.sb("modG", [128, 3, KC, 2], F32)] * 2
        self.mod_bufs = [self.buf("mod")] * 2
        self.mod_first = [True, True]
        self.ring_init()
        self.psum = [self.es.enter_context(nc.psum_tensor(f"ps{i}", [128, TN], F32)) for i in range(8)]
        self.pbuf = [self.buf("ps") for _ in range(8)]
        self.setup_sem = self.newsem("setup")
        self.setup2_sem = self.newsem("setup2")
        self.const_buf = self.buf("const")

        for kc in range(KC):
            nc.sync.dma_start(out=self.xT[:, kc, :], in_=xT_d[kc * 128:(kc + 1) * 128, :]).then_inc(self.setup_sem[0], 16)
            self.setup_sem[1] += 16
        for dst, src in ((self.condT, condT_d), (self.adab, adab_d), (self.normg, normg_d)):
            nc.sync.dma_start(out=dst[:], in_=src).then_inc(self.setup_sem[0], 16)
            self.setup_sem[1] += 16
        self.extra_setup()
        setup_tok = (self.setup_sem[0], self.setup_sem[1])
        setup2_tok = (self.setup2_sem[0], self.setup2_sem[1])
        for e in (self.pe, self.act, self.dve):
            e.wait(setup_tok)
            e.wait(setup2_tok)
        self.op(self.dve, lambda: nc.vector.memset(self.ones_bf[:], 1.0), writes=[self.const_buf])
        self.scond_buf = self.buf("scond")
        self.op(self.act, lambda: nc.scalar.activation(out=self.scond[:], in_=self.condT[:], func=AF.Silu),
                writes=[self.scond_buf])
        self.op(self.dve, lambda: nc.vector.tensor_copy(out=self.w2aug_bf[:], in_=self.cs["w2aug"][:]), writes=[self.const_buf])

        layers = list(cfg.get("layers", range(depth)))
        self.ada_w = ada_w
        pre_done = False
        for i, l in enumerate(layers):
            par = i % 2
            if not pre_done:
                self.mod_blocks(l, par, range(18))
                self.mod_finish(l, par)
            self.mod, self.modA, self.modB, self.modG = self.mod2[par], self.modA2[par], self.modB2[par], self.modG2[par]
            self.mod_buf = self.mod_bufs[par]
            pre_done = False
            overlap = (i + 1 < len(layers)) and cfg.get("mod_overlap", True) and cfg.get("ffn", True) and cfg.get("ffn2", True)
            if overlap:
                nl, npar = layers[i + 1], (i + 1) % 2
            if cfg.get("ffn", True):
                hook = None
                if overlap:
                    hook = lambda g, nl=nl, npar=npar: self.mod_blocks(nl, npar, [g] if g < 7 else [])
                self.ffn(l, 0, wg[0], wu[0], wd[0], hook=hook)
                if overlap:
                    self.mod_finish(nl, npar, 0, 28, derive=False)
            if cfg.get("mixer", True):
                self.mixer(l)
            if cfg.get("ffn", True) and cfg.get("ffn2", True):
                hook = None
                if overlap:
                    hook = lambda g, nl=nl, npar=npar: self.mod_blocks(nl, npar, [7 + g])
                self.ffn(l, 2, wg[1], wu[1], wd[1], hook=hook)
                if overlap:
                    self.mod_finish(nl, npar, 28, 72, derive=True)
                    pre_done = True

        self.finish_outputs()
        osem = self.newsem("out")
        for b in self.x_bufs:
            self.sp.wait(b.w)
        for kc in range(KC):
            nc.sync.dma_start(out=yT_d[kc * 128:(kc + 1) * 128, :], in_=self.xT[:, kc, :]).then_inc(osem[0], 16)
            osem[1] += 16
        for s in self.out_sems:
            nc.sync.wait_ge(s[0], s[1])
        nc.sync.wait_ge(osem[0], osem[1])
        self.es.close()
        return nc

    def extra_setup(self):
        nc = self.nc
        self.out_sems = []
        self.odd_w_in = self.din("odd_w_in", [2, D, 4096])
        self.odd_w_out = self.din("odd_w_out", [2, 2048, D])
        oddvg_d = self.din("oddvg", [128, 2, 16])
        wsT_d = self.din("wsT", [128, 2, 4, 128])
        bsb_d = self.din("bsb", [128, 2, 4, 128])
        self.oddvg = self.sb("oddvg_sb", [128, 2, 16], F32)
        self.wsT = self.sb("wsT_sb", [128, 2, 4, 128], F32)
        self.bsb = self.sb("bsb_sb", [128, 2, 4, 128], F32)
        for dst, src in ((self.oddvg, oddvg_d), (self.wsT, wsT_d), (self.bsb, bsb_d)):
            nc.sync.dma_start(out=dst[:], in_=src).then_inc(self.setup_sem[0], 16)
            self.setup_sem[1] += 16
        self.wv_sem = self.newsem("wv")
        self.even_w_in = self.din("even_w_in", [2, D, 2240])
        self.win_sw = self.din("win_sw", [2, D, 96])
        self.even_w_out = self.din("even_w_out", [2, D, D])
        self.qbw = self.din("qbw", [2, 384, 768])
        self.qbw_sw = self.din("qbw_sw", [2, 384, 768])
        self.kvbK = self.din("kvbK", [2, 256, 512])
        self.kvbV = self.din("kvbV", [2, 256, 512])
        self.ckvcT_d = self.din("ckvcT", [2, 256, 256])
        self.kropecT_d = self.din("kropecT", [2, 32, 256])
        self.sgla_d = self.din("sgla", [2, 2, 4, 64, 128])
        self.ckvT_o = self.dout("ckvT_o", [2, 256, 1024])
        self.kropeT_o = self.dout("kropeT_o", [2, 32, 1024])
        self.gla_o = self.dout("gla_o", [4, 2, 2, 4, 64, 128])
        cs = {}
        self.cs = cs
        for nm, shp in (("gmask", [128, 4, 128]), ("mask4", [128, 2, 4, 128]), ("rope", [96, 2, 1024])):
            dd = self.din(nm, shp)
            t = self.sb(nm + "_sb", shp, BF16)
            nc.gpsimd.dma_start(out=t[:], in_=dd).then_inc(self.setup2_sem[0], 16)
            self.setup2_sem[1] += 16
            cs[nm] = t
        for nm, shp in (("qkng", [96, 2, 4]), ("glag", [128, 2]), ("qag", [128, 2, 3]), ("kvag", [128, 2, 2]),
                        ("w2aug", [33, 2, 2, 256])):
            dd = self.din(nm, shp)
            t = self.sb(nm + "_sb", shp, F32)
            nc.sync.dma_start(out=t[:], in_=dd).then_inc(self.setup_sem[0], 16)
            self.setup_sem[1] += 16
            cs[nm] = t
        self.cs = cs
        self.gmask_bf = cs["gmask"]
        self.w2aug_bf = self.sb("w2aug_bf", [33, 2, 2, 256], BF16)
        self.ld1 = self.newsem("ld1")
        self.ld2 = self.newsem("ld2")
        self.st1 = self.newsem("st1")
        self.st2 = self.newsem("st2")
        self.st3 = self.newsem("st3")
        self.out_sems += [self.st1, self.st2, self.st3]

    def finish_outputs(self):
        pass

    def mixer(self, l):
        if l % 2 == 1:
            self.odd_mixer(l)
        else:
            self.even_mixer(l)
        self.barrier()

    def even_mixer(self, l):
        for hf in range(2):
            self.even_half(l, hf)

    def _tok(self, eng):
        return (eng.sems[eng.si], eng.cnt)

    def even_half(self, l, hf):
        nc = self.nc
        j = l // 2
        k = 1
        L = 1024
        T0 = hf * L
        tiles = [2 * hf, 2 * hf + 1]
        win = self.even_w_in[j].rearrange("(kc p) f -> p kc f", p=128)
        winsw = self.win_sw[j].rearrange("(kc p) f -> p kc f", p=128)
        wout = self.even_w_out[j].rearrange("(c p) d -> p c d", p=128)
        cs = self.cs
        X = mybir.AxisListType.X
        pe, act, dve = self.pe, self.act, self.dve
        PS = self.psum
        PB = self.pbuf
        with contextlib.ExitStack() as hs:
            ogT = self.sb("e_ogT", [128, 4, L], BF16, hs); og_buf = self.buf("og")
            cqn = self.sb("e_cqn", [128, 3, L], BF16, hs); cqn_buf = self.buf("cqn")
            ckvn = self.sb("e_ckvn", [128, 2, L], BF16, hs); ckvn_buf = self.buf("ckvn")
            krT = self.sb("e_krT", [96, L], F32, hs); kr_buf = self.buf("kr")
            krsw = self.sb("e_krsw", [96, L], F32, hs); krsw_buf = self.buf("krsw")
            with contextlib.ExitStack() as gs:
                qT = self.sb("e_qT", [128, 2, L], BF16, gs); q_buf = self.buf("q")
                kT = self.sb("e_kT", [128, 2, L], BF16, gs); k_buf = self.buf("k")
                gl = self.sb("e_gl", [33, L], BF16, gs); gl_buf = self.buf("gl")
                ktok = self.sb("e_ktok", [128, 8, 256], BF16, gs); ktok_buf = self.buf("ktok")
                vtok = self.sb("e_vtok", [128, 8, 512], BF16, gs); vtok_buf = self.buf("vtok")
                with contextlib.ExitStack() as ps_:
                    hT = self.sb("e_hT", [128, KC, L], BF16, ps_)
                    h_bufs = [self.buf("h"), self.buf("h")]
                    tmp, rstd = self.norm_tmp(ps_)
                    self.norm_modulate(k, tiles, hT, h_bufs, tmp, rstd)
                    self.barrier()
                    stg = tmp["t2"]; stg_buf = self.buf("stg")
                    sq3 = tmp["sq"]; sq3_buf = self.buf("sq3")
                    rs2 = rstd["t"]; rs2_buf = self.buf("rs2")
                    self.op(dve, lambda: nc.vector.memset(gl[:], 1.0), writes=[gl_buf])
                    bi = [0]

                    def fm_group(s, sbuf_, ncols, c0, m, li, bank):
                        fns = [lambda kc=kc: nc.tensor.matmul(
                            PS[bank][0:m, :], lhsT=self.ring[:, s, kc * ncols + c0: kc * ncols + c0 + m],
                            rhs=hT[:, kc, li * TN:(li + 1) * TN], start=(kc == 0), stop=(kc == KC - 1)) for kc in range(KC)]
                        self.op(pe, fns, reads=[sbuf_, h_bufs[li]], writes=[PB[bank]])

                    def nb():
                        b = bi[0] % 4
                        bi[0] += 1
                        return b

                    s, sb_ = self.ring_load([(0, [KC, 512], win[:, :, 0:512])])
                    for li in range(2):
                        for c in range(2):
                            b = nb()
                            fm_group(s, sb_, 512, c * 128, 128, li, b)
                            self.op(act, lambda c=c, li=li, b=b: nc.scalar.activation(
                                out=qT[:, c, li * TN:(li + 1) * TN], in_=PS[b][:], func=AF.Copy, scale=0.125),
                                reads=[PB[b]], writes=[q_buf])
                        for c in range(2):
                            b = nb()
                            fm_group(s, sb_, 512, 256 + c * 128, 128, li, b)
                            self.op(dve, lambda c=c, li=li, b=b: nc.vector.tensor_copy(
                                out=kT[:, c, li * TN:(li + 1) * TN], in_=PS[b][:]), reads=[PB[b]], writes=[k_buf])
                    for blk in range(8):
                        b = nb()
                        fns = [lambda kc=kc, blk=blk, b=b: nc.tensor.matmul(
                            PS[b][:, 0:256], lhsT=hT[:, kc, blk * 128:(blk + 1) * 128],
                            rhs=self.ring[:, s, kc * 512 + 256: kc * 512 + 512], start=(kc == 0), stop=(kc == KC - 1)) for kc in range(KC)]
                        self.op(pe, fns, reads=[sb_, h_bufs[blk // 4]], writes=[PB[b]])
                        self.op(act, lambda blk=blk, b=b: nc.scalar.activation(out=ktok[:, blk, :], in_=PS[b][:, 0:256], func=AF.Copy),
                                reads=[PB[b]], writes=[ktok_buf])
                    s, sb_ = self.ring_load([(0, [KC, 512], win[:, :, 512:1024])])
                    for blk in range(8):
                        b = nb()
                        fns = [lambda kc=kc, blk=blk, b=b: nc.tensor.matmul(
                            PS[b][:], lhsT=hT[:, kc, blk * 128:(blk + 1) * 128],
                            rhs=self.ring[:, s, kc * 512: kc * 512 + 512], start=(kc == 0), stop=(kc == KC - 1)) for kc in range(KC)]
                        self.op(pe, fns, reads=[sb_, h_bufs[blk // 4]], writes=[PB[b]])
                        self.op(dve, lambda blk=blk, b=b: nc.vector.tensor_copy(out=vtok[:, blk, :], in_=PS[b][:]),
                                reads=[PB[b]], writes=[vtok_buf])
                    s, sb_ = self.ring_load([(0, [KC, 512], win[:, :, 1024:1536])])
                    for li in range(2):
                        for c in range(4):
                            b = nb()
                            fm_group(s, sb_, 512, c * 128, 128, li, b)
                            self.op(act, lambda c=c, li=li, b=b: nc.scalar.activation(
                                out=ogT[:, c, li * TN:(li + 1) * TN], in_=PS[b][:], func=AF.Silu),
                                reads=[PB[b]], writes=[og_buf])
                    s, sb_ = self.ring_load([(0, [KC, 416], win[:, :, 1536:1952])])
                    for li in range(2):
                        b = nb()
                        fm_group(s, sb_, 416, 0, 32, li, b)
                        self.op(act, lambda li=li, b=b: nc.scalar.activation(
                            out=gl[0:32, li * TN:(li + 1) * TN], in_=PS[b][0:32, :], func=AF.Copy),
                            reads=[PB[b]], writes=[gl_buf])
                        bs = [nb() for _ in range(3)]
                        for c in range(3):
                            fm_group(s, sb_, 416, 32 + c * 128, 128, li, bs[c])
                            self.op(act, lambda c=c, b=bs[c]: nc.scalar.activation(out=sq3[:, c, :], in_=PS[b][:], func=AF.Square),
                                    reads=[PB[bs[c]]], writes=[sq3_buf])
                        self.rms_rstd(sq3, sq3_buf, 3, 128, 384, rs2, rs2_buf)
                        for c in range(3):
                            self.op(dve, lambda c=c, li=li, b=bs[c]: nc.vector.scalar_tensor_tensor(
                                out=cqn[:, c, li * TN:(li + 1) * TN], in0=PS[b][:], scalar=cs["qag"][:, j, c:c + 1], in1=rs2[:],
                                op0=ALU.mult, op1=ALU.mult), reads=[PB[bs[c]], rs2_buf], writes=[cqn_buf])
                    s, sb_ = self.ring_load([(0, [KC, 288], win[:, :, 1952:2240]), (2304, [KC, 96], winsw[:, :, :])])
                    for li in range(2):
                        bs = [nb() for _ in range(2)]
                        for c in range(2):
                            fm_group(s, sb_, 288, c * 128, 128, li, bs[c])
                            self.op(act, lambda c=c, b=bs[c]: nc.scalar.activation(out=sq3[:, c, :], in_=PS[b][:], func=AF.Square),
                                    reads=[PB[bs[c]]], writes=[sq3_buf])
                        self.rms_rstd(sq3, sq3_buf, 2, 128, 256, rs2, rs2_buf)
                        for c in range(2):
                            self.op(dve, lambda c=c, li=li, b=bs[c]: nc.vector.scalar_tensor_tensor(
                                out=stg[:, c, :], in0=PS[b][:], scalar=cs["kvag"][:, j, c:c + 1], in1=rs2[:],
                                op0=ALU.mult, op1=ALU.mult), reads=[PB[bs[c]], rs2_buf], writes=[stg_buf])
                        self.op(act, lambda li=li: nc.scalar.activation(out=ckvn[:, :, li * TN:(li + 1) * TN], in_=stg[:], func=AF.Copy),
                                reads=[stg_buf], writes=[ckvn_buf])
                        if hf == 0:
                            self.dma(self.sp, self.st1, self.ckvT_o[j].rearrange("(c p) t -> p c t", p=128)[:, :, li * TN:(li + 1) * TN],
                                     stg[:], reads=[stg_buf])
                        b = nb()
                        fm_group(s, sb_, 288, 192, 96, li, b)
                        self.op(act, lambda li=li, b=b: nc.scalar.activation(
                            out=krT[64:96, li * TN:(li + 1) * TN], in_=PS[b][64:96, :], func=AF.Copy),
                            reads=[PB[b]], writes=[kr_buf])
                        if hf == 1:
                            b = nb()
                            fns = [lambda kc=kc, li=li, b=b: nc.tensor.matmul(
                                PS[b][0:96, :], lhsT=self.ring[:, s, 2304 + kc * 96: 2304 + (kc + 1) * 96],
                                rhs=hT[:, kc, li * TN:(li + 1) * TN], start=(kc == 0), stop=(kc == KC - 1)) for kc in range(KC)]
                            self.op(pe, fns, reads=[sb_, h_bufs[li]], writes=[PB[b]])
                            self.op(act, lambda li=li, b=b: nc.scalar.activation(
                                out=krsw[64:96, li * TN:(li + 1) * TN], in_=PS[b][64:96, :], func=AF.Copy),
                                reads=[PB[b]], writes=[krsw_buf])
                    if hf == 0:
                        self.dma(self.sp, self.st3, self.kropeT_o[j], krT[64:96, :], reads=[kr_buf])
                    self.barrier()
                    for e_ in (pe, act, dve):
                        e_.wait((self.st1[0], self.st1[1]))
                if self.cfg.get("even_stop", 9) <= 1:
                    return
                with contextlib.ExitStack() as ws:
                    oT = self.sb("g_oT", [128, 4, L], F32, ws); o_buf = self.buf("o")
                    ez = self.sb("g_ez", [128, 256], F32, ws)
                    lg = self.sb("g_l", [128, 256], BF16, ws); l_buf = self.buf("l")
                    Eq = self.sb("g_Eq", [128, 2, 2, 128], F32, ws); Eq_bufs = [self.buf("Eq"), self.buf("Eq")]
                    Ek = self.sb("g_Ek", [128, 2, 128], F32, ws)
                    Er = self.sb("g_Er", [128, 256], F32, ws); E_buf = self.buf("E")
                    qe = self.sb("g_qe", [128, 2, 2, 128], BF16, ws); qe_bufs = [self.buf("qe"), self.buf("qe")]
                    ke = self.sb("g_ke", [128, 2, 128], BF16, ws)
                    kl = self.sb("g_kl", [128, 256], BF16, ws); qk_buf = self.buf("qk")
                    attm = self.sb("g_attm", [128, 2, 4, 128], BF16, ws); attm_bufs = [self.buf("attm"), self.buf("attm")]
                    S = self.sb("g_S", [128, 2, 2, 128], F32, ws); S_bufs = [self.buf("S"), self.buf("S")]
                    Sbf = self.sb("g_Sbf", [128, 3, 2, 2, 128], BF16, ws); Sbf_bufs = [self.buf("Sbf") for _ in range(3)]
                    self.op(dve, lambda: nc.vector.memset(Sbf[:], 0.0), writes=Sbf_bufs)
                    nseq = 4 if hf == 0 else 1
                    bps = 8 // nseq
                    sgl = self.sgla_d[j]
                    items = []
                    for dr in range(2):
                        for sq in range(nseq):
                            blks = list(range(sq * bps, (sq + 1) * bps))
                            if dr == 1:
                                blks = blks[::-1]
                            for bi_, blk in enumerate(blks):
                                items.append(dict(dr=dr, sq=sq, blk=blk, first=(bi_ == 0), last=(bi_ == len(blks) - 1), par=len(items) % 2))
                    st_ = {"sv": 0, "p": 0}

                    def P1(it):
                        dr, tsl = it["dr"], slice(it["blk"] * 128, (it["blk"] + 1) * 128)
                        self.op(pe, lambda: nc.tensor.matmul(PS[0][:, 0:256], lhsT=gl[0:33, tsl], rhs=self.w2aug_bf[0:33, j, dr, :],
                                                             start=True, stop=True), reads=[gl_buf, self.const_buf], writes=[PB[0]])
                        self.op(act, lambda: nc.scalar.activation(out=ez[:], in_=PS[0][:, 0:256], func=AF.Exp, scale=-1.0),
                                reads=[PB[0]], writes=[l_buf])
                        self.op(act, lambda: nc.scalar.activation(out=lg[:], in_=ez[:], func=AF.Ln, bias=1.0),
                                reads=[], writes=[l_buf])

                    def P2(it):
                        dr, par, blk = it["dr"], it["par"], it["blk"]
                        tsl = slice(blk * 128, (blk + 1) * 128)
                        fns = [lambda hp=hp: nc.tensor.matmul(PS[1][:, hp * 128:(hp + 1) * 128], lhsT=lg[:, hp * 128:(hp + 1) * 128],
                                                              rhs=self.gmask_bf[:, dr, :], start=True, stop=True) for hp in range(2)]
                        self.op(pe, fns, reads=[l_buf], writes=[PB[1]])
                        self.op(pe, lambda: nc.tensor.matmul(PS[2][:, 0:256], lhsT=self.gmask_bf[:, 2 + dr, :], rhs=lg[:],
                                                             start=True, stop=True), reads=[l_buf], writes=[PB[2]])
                        self.op(act, lambda: nc.scalar.activation(out=Eq[:, par].rearrange("p a b -> p (a b)"), in_=PS[1][:, 0:256], func=AF.Exp),
                                reads=[PB[1]], writes=[Eq_bufs[par]])
                        self.op(act, lambda: nc.scalar.activation(out=Ek[:].rearrange("p a b -> p (a b)"), in_=PS[1][:, 0:256], func=AF.Exp, scale=-1.0),
                                reads=[PB[1]], writes=[E_buf])
                        self.op(act, lambda: nc.scalar.activation(out=Er[:], in_=PS[2][:, 0:256], func=AF.Exp),
                                reads=[PB[2]], writes=[])
                        E_buf.w = self._tok(act)
                        self.op(dve, lambda: nc.vector.tensor_tensor(out=qe[:, par], in0=qT[:, :, tsl], in1=Eq[:, par], op=ALU.mult),
                                reads=[Eq_bufs[par], q_buf], writes=[qe_bufs[par]])
                        self.op(dve, lambda: nc.vector.tensor_tensor(out=ke[:], in0=kT[:, :, tsl], in1=Ek[:], op=ALU.mult),
                                reads=[k_buf, E_buf], writes=[qk_buf])
                        self.op(dve, lambda: nc.vector.tensor_tensor(out=kl[:], in0=ktok[:, blk, :], in1=Er[:], op=ALU.mult),
                                reads=[ktok_buf], writes=[])
                        qk_buf.w = self._tok(dve)

                    def P3a(it):
                        dr, par = it["dr"], it["par"]
                        for hh in range(2):
                            bank = 3 if hh == 0 else 0
                            fns = [lambda hp=hp, hh=hh, bank=bank: nc.tensor.matmul(
                                PS[bank][:, hp * 128:(hp + 1) * 128], lhsT=ke[hh * 64:hh * 64 + 64, hp, :],
                                rhs=qe[hh * 64:hh * 64 + 64, par, hp, :], start=True, stop=True) for hp in range(2)]
                            self.op(pe, fns, reads=[qk_buf, qe_bufs[par]], writes=[PB[bank]])
                            self.op(dve, lambda hh=hh, bank=bank: nc.vector.tensor_tensor(
                                out=attm[:, par, hh::2, :], in0=PS[bank][:, 0:256].rearrange("p (a b) -> p a b", a=2),
                                in1=cs["mask4"][:, dr, 0:2, :], op=ALU.mult), reads=[PB[bank]], writes=[attm_bufs[par]] if hh == 0 else [])
                        attm_bufs[par].w = self._tok(dve)

                    def P3b(it):
                        blk = it["blk"]
                        for c2 in range(2):
                            fns = [lambda c2=c2, hp=hp: nc.tensor.matmul(
                                PS[4 + c2][:, hp * 256:(hp + 1) * 256], lhsT=kl[c2 * 64:(c2 + 1) * 64, hp * 128:(hp + 1) * 128],
                                rhs=vtok[c2 * 64:(c2 + 1) * 64, blk, hp * 256:(hp + 1) * 256], start=True, stop=True) for hp in range(2)]
                            self.op(pe, fns, reads=[qk_buf, vtok_buf], writes=[PB[4 + c2]])

                    def copy_S(ver):
                        p = st_["p"]
                        self.op(act, [lambda hh=hh, p=p: nc.scalar.activation(out=Sbf[hh * 64:hh * 64 + 64, ver, :, hh, :],
                                                                              in_=S[hh * 64:hh * 64 + 64, p, :, :], func=AF.Copy) for hh in range(2)],
                                reads=[S_bufs[p]], writes=[Sbf_bufs[ver]])

                    def P4(it):
                        dr, par, sq = it["dr"], it["par"], it["sq"]
                        if it["first"]:
                            p = st_["p"]
                            if hf == 0:
                                self.op(dve, lambda p=p: nc.vector.memset(S[:, p], 0.0), writes=[S_bufs[p]])
                            else:
                                self.dma(self.sp, self.ld2, S[:, p], sgl[dr].rearrange("(hp hh) d e -> (hh d) hp e", hh=2), writes=[S_bufs[p]])
                            copy_S(st_["sv"])
                        corder = [0, 1] if dr == 0 else [1, 0]
                        svs = [st_["sv"]]
                        for c2 in corder:
                            last = (c2 * 64 + 63) if dr == 0 else (c2 * 64)
                            p = st_["p"]
                            q_ = 1 - p
                            first_w = True
                            for hp in range(2):
                                for hh in range(2):
                                    r0 = hh * 64
                                    self.op(dve, lambda hp=hp, hh=hh, r0=r0, c2=c2, last=last, p=p, q_=q_: nc.vector.scalar_tensor_tensor(
                                        out=S[r0:r0 + 64, q_, hp, :], in0=S[r0:r0 + 64, p, hp, :], scalar=Eq[r0:r0 + 64, par, hp, last:last + 1],
                                        in1=PS[4 + c2][r0:r0 + 64, hp * 256 + hh * 128: hp * 256 + (hh + 1) * 128],
                                        op0=ALU.mult, op1=ALU.add), reads=[PB[4 + c2], Eq_bufs[par], S_bufs[p]],
                                        writes=[S_bufs[q_]] if first_w else [])
                                    first_w = False
                            S_bufs[q_].w = self._tok(dve)
                            st_["p"] = q_
                            nsv = (svs[-1] + 1) % 3
                            copy_S(nsv)
                            svs.append(nsv)
                        it["svs"] = svs
                        it["corder"] = corder
                        st_["sv"] = svs[2]
                        if it["last"] and hf == 0:
                            p = st_["p"]
                            self.dma(self.sp, self.st2, self.gla_o[sq, j, dr].rearrange("(hp hh) d e -> (hh d) hp e", hh=2), S[:, p],
                                     reads=[S_bufs[p]])

                    def P5(it):
                        dr, par, blk, svs, corder = it["dr"], it["par"], it["blk"], it["svs"], it["corder"]
                        tsl = slice(blk * 128, (blk + 1) * 128)
                        for h in range(4):
                            hp, r0 = h // 2, (h % 2) * 64
                            bank = 6 + h % 2
                            fns = [lambda h=h, bank=bank: nc.tensor.matmul(PS[bank][:, 0:128], lhsT=vtok[:, blk, h * 128:(h + 1) * 128],
                                                                           rhs=attm[:, par, h, :], start=True, stop=False)]
                            for ci_, c2 in enumerate(corder):
                                fns.append(lambda c2=c2, ci_=ci_, hp=hp, r0=r0, bank=bank: nc.tensor.matmul(
                                    PS[bank][:, c2 * 64:(c2 + 1) * 64], lhsT=Sbf[:, svs[ci_], hp, r0 // 64, :],
                                    rhs=qe[:, par, hp, c2 * 64:(c2 + 1) * 64], start=False, stop=(ci_ == 1)))
                            self.op(pe, fns, reads=[vtok_buf, attm_bufs[par], qe_bufs[par], Sbf_bufs[svs[0]], Sbf_bufs[svs[1]]], writes=[PB[bank]])
                            if dr == 0:
                                self.op(act, lambda h=h, bank=bank: nc.scalar.activation(out=oT[:, h, tsl], in_=PS[bank][:, 0:128], func=AF.Copy),
                                        reads=[PB[bank]], writes=[o_buf])
                            else:
                                self.op(dve, lambda h=h, bank=bank: nc.vector.tensor_tensor(
                                    out=oT[:, h, tsl], in0=PS[bank][:, 0:128], in1=oT[:, h, tsl], op=ALU.add),
                                    reads=[PB[bank]], writes=[o_buf])

                    P1(items[0]); P2(items[0]); P3a(items[0])
                    for i_, it in enumerate(items):
                        nxt = items[i_ + 1] if i_ + 1 < len(items) else None
                        if nxt is not None:
                            P1(nxt)
                        P3b(it)
                        if nxt is not None:
                            P2(nxt)
                        P4(it)
                        if nxt is not None:
                            P3a(nxt)
                        P5(it)
                    with contextlib.ExitStack() as ns:
                        sqh = self.sb("g_sq", [128, 1, TN], BF16, ns); sqh_buf = self.buf("sqh")
                        rs = self.sb("g_rs", [128, TN], F32, ns); rs_buf = self.buf("rs")
                        t3 = self.sb("g_t3", [128, TN], F32, ns); t3_buf = self.buf("t3")
                        for li in range(2):
                            for h in range(4):
                                sl = slice(li * TN, (li + 1) * TN)
                                self.op(act, lambda h=h, sl=sl: nc.scalar.activation(out=sqh[:, 0, :], in_=oT[:, h, sl], func=AF.Square),
                                        reads=[o_buf], writes=[sqh_buf])
                                self.rms_rstd(sqh, sqh_buf, 1, 128, 128, rs, rs_buf)
                                self.op(dve, lambda h=h, sl=sl: nc.vector.scalar_tensor_tensor(
                                    out=t3[:], in0=oT[:, h, sl], scalar=cs["glag"][:, j:j + 1], in1=rs[:], op0=ALU.mult, op1=ALU.mult),
                                    reads=[o_buf, rs_buf], writes=[t3_buf])
                                self.op(dve, lambda h=h, sl=sl: nc.vector.tensor_tensor(out=ogT[:, h, sl], in0=t3[:], in1=ogT[:, h, sl], op=ALU.mult),
                                        reads=[t3_buf], writes=[og_buf])
                    self.barrier()
                    for e_ in (pe, act, dve):
                        e_.wait((self.st2[0], self.st2[1]))
                    if self.cfg.get("dump_o3") and hf == 1:
                        dsem = self.newsem("dbg")
                        self.out_sems.append(dsem)
                        for nm_, t_, shp_, dt_ in (("og3", ogT, [128, 4, L], BF16), ("oT3", oT, [128, 4, L], F32)):
                            self.sp.wait(self._tok(pe)); self.sp.wait(self._tok(act)); self.sp.wait(self._tok(dve))
                            nc.sync.dma_start(out=self.dout("dbg_" + nm_, shp_, dt_), in_=t_[:]).then_inc(dsem[0], 16)
                            dsem[1] += 16
                        for e_ in (pe, act, dve):
                            e_.wait((dsem[0], dsem[1]))
            if self.cfg.get("even_stop", 9) <= 2:
                return
            self.mla_half(l, hf, ogT, og_buf, cqn, cqn_buf, ckvn, ckvn_buf, krT, kr_buf, krsw, krsw_buf, wout)

    def mla_half(self, l, hf, ogT, og_buf, cqn, cqn_buf, ckvn, ckvn_buf, krT, kr_buf, krsw, krsw_buf, wout):
        nc = self.nc
        j = l // 2
        k = 1
        L = 1024
        cs = self.cs
        pe, act, dve = self.pe, self.act, self.dve
        PS, PB = self.psum, self.pbuf
        X = mybir.AxisListType.X
        koff = 256 if hf == 1 else 0
        nk = L + koff
        nkt = nk // 128
        SC = 96 ** -0.5
        g_q, g_qs, g_k, g_ks = (cs["qkng"][:, j, i:i + 1] for i in range(4))
        with contextlib.ExitStack() as ms:
            mlaT = self.sb("m_mlaT", [64, 8, L], BF16, ms); mla_buf = self.buf("mla")
            Kb = self.sb("m_Kb", [96, 2, nk], BF16, ms); kb_bufs = [self.buf("kb"), self.buf("kb")]
            Vt = self.sb("m_Vt", [128, nkt, 512], BF16, ms); vt_buf = self.buf("vt")
            ckvc = self.sb("m_ckvc", [128, 2, 256], BF16, ms); ckvc_buf = self.buf("ckvc")
            ssn = self.sb("m_ssn", [128, nkt, 8], F32, ms); ssn_buf = self.buf("ssn")
            ssr = self.sb("m_ssr", [128, nkt], F32, ms); ssr_buf = self.buf("ssr")
            rk = self.sb("m_rk", [128, nkt, 8], F32, ms); rk_buf = self.buf("rk")
            gq2 = self.sb("m_gq2", [96, 1], F32, ms); gq2_buf = self.buf("gq2")
            sq_, sqb = self.ring_load([(0, [3, 768], self.qbw[j].rearrange("(c p) f -> p c f", p=128)),
                                       (2304, [3, 768], self.qbw_sw[j].rearrange("(c p) f -> p c f", p=128))])
            sk_, skb = self.ring_load([(0, [2, 512], self.kvbK[j].rearrange("(c p) f -> p c f", p=128)),
                                       (1024, [2, 512], self.kvbV[j].rearrange("(c p) f -> p c f", p=128))])
            self.op(dve, lambda: nc.vector.tensor_tensor(out=gq2[:], in0=g_q, in1=g_k, op=ALU.mult), writes=[gq2_buf])
            with contextlib.ExitStack() as sa:
                krg = self.sb("m_krg", [96, nk], F32, sa); krg_buf = self.buf("krg")
                krr = self.sb("m_krr", [96, nk], BF16, sa); krr_buf = self.buf("krr")
                krc = self.sb("m_krc", [96, 256], F32, sa); krc_buf = self.buf("krc")
                ckvf = self.sb("m_ckvf", [128, 2, 256], F32, sa); ckvf_buf = self.buf("ckvf")
                sqt = self.sb("m_sqt", [128, 2, 512], F32, sa); sqt_bufs = [self.buf("sqt"), self.buf("sqt")]
                t1 = self.sb("m_t1a", [96, TN], F32, sa)
                t2 = self.sb("m_t2a", [96, TN], F32, sa); t_buf = self.buf("t12")
                if hf == 1:
                    for c in range(2):
                        self.dma(self.sp, self.ld1, ckvf[:, c, :], self.ckvcT_d[j, c * 128:(c + 1) * 128, :], writes=[ckvf_buf] if c == 0 else [])
                    self.dma(self.sp, self.ld1, krc[64:96, :], self.kropecT_d[j], writes=[krc_buf])
                    tok = (self.ld1[0], self.ld1[1])
                    ckvf_buf.w = tok
                    krc_buf.w = tok
                    self.op(dve, lambda: nc.vector.tensor_copy(out=ckvc[:], in_=ckvf[:]), reads=[ckvf_buf], writes=[ckvc_buf])
                    self.op(dve, lambda: nc.vector.tensor_scalar(out=krg[64:96, 0:256], in0=krc[64:96, :], scalar1=g_k[64:96, :],
                                                                 scalar2=None, op0=ALU.mult), reads=[krc_buf], writes=[krg_buf])
                    self.op(act, lambda: nc.scalar.activation(out=krr[64:96, 0:256], in_=krc[64:96, :], func=AF.Square),
                            reads=[krc_buf], writes=[krr_buf])
                    for li in range(2):
                        sl = slice(li * TN, (li + 1) * TN)
                        self.op(dve, lambda sl=sl: nc.vector.scalar_tensor_tensor(
                            out=t1[64:96, :], in0=krT[64:96, sl], scalar=g_k[64:96, :], in1=cs["rope"][64:96, 0, sl],
                            op0=ALU.mult, op1=ALU.mult), reads=[kr_buf], writes=[t_buf])
                        self.op(dve, lambda sl=sl: nc.vector.scalar_tensor_tensor(
                            out=t2[64:96, :], in0=krsw[64:96, sl], scalar=g_ks[64:96, :], in1=cs["rope"][64:96, 1, sl],
                            op0=ALU.mult, op1=ALU.mult), reads=[krsw_buf], writes=[])
                        self.op(dve, lambda li=li: nc.vector.tensor_tensor(
                            out=krg[64:96, koff + li * TN: koff + (li + 1) * TN], in0=t1[64:96, :], in1=t2[64:96, :], op=ALU.add),
                            reads=[], writes=[krg_buf])
                else:
                    self.op(dve, lambda: nc.vector.tensor_scalar(out=krg[64:96, :], in0=krT[64:96, :], scalar1=g_k[64:96, :],
                                                                 scalar2=None, op0=ALU.mult), reads=[kr_buf], writes=[krg_buf])
                self.op(act, lambda: nc.scalar.activation(out=krr[64:96, koff:nk], in_=krT[64:96, :], func=AF.Square),
                        reads=[kr_buf], writes=[krr_buf])
                for bb in range(2):
                    self.op(act if bb == 0 else dve,
                            (lambda bb=bb: nc.scalar.activation(out=Kb[64:96, bb, :], in_=krg[64:96, :], func=AF.Copy)) if bb == 0 else
                            (lambda bb=bb: nc.vector.tensor_copy(out=Kb[64:96, bb, :], in_=krg[64:96, :])),
                            reads=[krg_buf], writes=[kb_bufs[bb]])
                fns = [lambda kt=kt: nc.tensor.matmul(PS[6][:, kt:kt + 1], lhsT=krr[64:96, kt * 128:(kt + 1) * 128],
                                                      rhs=self.ones_bf[64:96, 0:1], start=True, stop=True) for kt in range(nkt)]
                self.op(pe, fns, reads=[krr_buf, self.const_buf], writes=[PB[6]])
                self.op(act, lambda: nc.scalar.activation(out=ssr[:], in_=PS[6][:, 0:nkt], func=AF.Copy), reads=[PB[6]], writes=[ssr_buf])
                for kt in range(nkt):
                    if kt * 128 < koff:
                        src, srcb, s0 = ckvc, ckvc_buf, kt * 128
                    else:
                        src, srcb, s0 = ckvn, ckvn_buf, kt * 128 - koff
                    b0 = kt % 2
                    fns = [lambda rc=rc, src=src, s0=s0, b0=b0: nc.tensor.matmul(
                        PS[b0][:], lhsT=src[:, rc, s0:s0 + 128], rhs=self.ring[:, sk_, rc * 512:(rc + 1) * 512],
                        start=(rc == 0), stop=(rc == 1)) for rc in range(2)]
                    self.op(pe, fns, reads=[skb, srcb], writes=[PB[b0]])
                    self.op(act, lambda b0=b0: nc.scalar.activation(out=sqt[:, b0, :], in_=PS[b0][:], func=AF.Square),
                            reads=[PB[b0]], writes=[sqt_bufs[b0]])
                    self.op(dve, lambda kt=kt, b0=b0: nc.vector.tensor_reduce(
                        out=ssn[:, kt, :], in_=sqt[:, b0, :].rearrange("p (h e) -> p h e", e=64), axis=X, op=ALU.add),
                        reads=[sqt_bufs[b0]], writes=[ssn_buf])
                    b1 = 2 + kt % 2
                    fns = [lambda rc=rc, src=src, s0=s0, b1=b1: nc.tensor.matmul(
                        PS[b1][:], lhsT=src[:, rc, s0:s0 + 128], rhs=self.ring[:, sk_, 1024 + rc * 512: 1024 + (rc + 1) * 512],
                        start=(rc == 0), stop=(rc == 1)) for rc in range(2)]
                    self.op(pe, fns, reads=[skb, srcb], writes=[PB[b1]])
                    self.op(dve, lambda kt=kt, b1=b1: nc.vector.tensor_copy(out=Vt[:, kt, :], in_=PS[b1][:]), reads=[PB[b1]], writes=[vt_buf])
                for kt in range(nkt):
                    self.op(dve, lambda kt=kt: nc.vector.tensor_scalar(out=rk[:, kt, :], in0=ssn[:, kt, :], scalar1=ssr[:, kt:kt + 1],
                                                                       scalar2=None, op0=ALU.add), reads=[ssn_buf, ssr_buf], writes=[rk_buf], same=True)
                self.op(act, lambda: nc.scalar.activation(out=rk[:].rearrange("p a b -> p (a b)"), in_=rk[:].rearrange("p a b -> p (a b)"),
                                                          func=AF.Ln, scale=1.0 / 96, bias=EPS), reads=[], writes=[rk_buf], same=True)
                self.op(act, lambda: nc.scalar.activation(out=rk[:].rearrange("p a b -> p (a b)"), in_=rk[:].rearrange("p a b -> p (a b)"),
                                                          func=AF.Exp, scale=-0.5, bias=float(np.log(SC))), reads=[], writes=[rk_buf], same=True)
                self.barrier()
            with contextlib.ExitStack() as sbk_:
                Qn = self.sb("m_Qn", [96, 8, TN], BF16, sbk_); qn_bufs = [self.buf("qn") for _ in range(8)]
                sq96 = self.sb("m_sq", [96, 2, TN], BF16, sbk_); sq_bufs = [self.buf("sq96"), self.buf("sq96")]
                rs96 = self.sb("m_rs", [96, 2, TN], F32, sbk_); rs_bufs = [self.buf("rs96"), self.buf("rs96")]
                t1 = self.sb("m_t1", [96, TN], F32, sbk_)
                t2 = self.sb("m_t2", [96, TN], F32, sbk_); t_buf = self.buf("t12")
                PT = self.sb("m_PT", [128, 2, TN], BF16, sbk_); pt_bufs = [self.buf("pt"), self.buf("pt")]
                rden = self.sb("m_rden", [64, 1, TN], F32, sbk_); rden_bufs = [self.buf("rden")]
                si = 0
                ei = 0
                pairs = [(li, h) for li in range(2) for h in range(8)]

                def q_steps(pi):
                    li, h = pairs[pi]
                    sl = slice(li * TN, (li + 1) * TN)
                    qb_ = 4 + pi % 2
                    db_ = pi % 2
                    steps = []

                    def s1():
                        fns = [lambda c3=c3: nc.tensor.matmul(
                            PS[qb_][0:96, :], lhsT=self.ring[:, sq_, c3 * 768 + h * 96: c3 * 768 + (h + 1) * 96],
                            rhs=cqn[:, c3, sl], start=(c3 == 0), stop=(c3 == 2)) for c3 in range(3)]
                        self.op(pe, fns, reads=[sqb, cqn_buf], writes=[PB[qb_]])
                        if hf == 1:
                            fns = [lambda c3=c3: nc.tensor.matmul(
                                PS[6][0:96, :], lhsT=self.ring[:, sq_, 2304 + c3 * 768 + h * 96: 2304 + c3 * 768 + (h + 1) * 96],
                                rhs=cqn[:, c3, sl], start=(c3 == 0), stop=(c3 == 2)) for c3 in range(3)]
                            self.op(pe, fns, reads=[sqb, cqn_buf], writes=[PB[6]])
                    steps.append(s1)
                    steps.append(lambda: self.op(act, lambda: nc.scalar.activation(out=sq96[:, db_, :], in_=PS[qb_][0:96, :], func=AF.Square),
                                                 reads=[PB[qb_]], writes=[sq_bufs[db_]]))
                    steps.append(lambda: self.op(pe, lambda: nc.tensor.matmul(PS[7][0:96, :], lhsT=self.ones_bf[0:96, 0:96], rhs=sq96[0:96, db_, :],
                                                                              start=True, stop=True), reads=[sq_bufs[db_], self.const_buf], writes=[PB[7]]))
                    steps.append(lambda: self.op(act, lambda: nc.scalar.activation(out=rs96[:, db_, :], in_=PS[7][0:96, :], func=AF.Ln, scale=1.0 / 96, bias=EPS),
                                                 reads=[PB[7]], writes=[rs_bufs[db_]]))
                    steps.append(lambda: self.op(act, lambda: nc.scalar.activation(out=rs96[:, db_, :], in_=rs96[:, db_, :], func=AF.Exp, scale=-0.5),
                                                 reads=[], writes=[rs_bufs[db_]]))
                    steps.append(lambda: self.op(dve, lambda: nc.vector.scalar_tensor_tensor(
                        out=Qn[0:64, h, :], in0=PS[qb_][0:64, :], scalar=gq2[0:64, :], in1=rs96[0:64, db_, :], op0=ALU.mult, op1=ALU.mult),
                        reads=[PB[qb_], rs_bufs[db_], gq2_buf], writes=[qn_bufs[h]]))

                    def s7():
                        if hf == 0:
                            self.op(dve, lambda: nc.vector.scalar_tensor_tensor(
                                out=Qn[64:96, h, :], in0=PS[qb_][64:96, :], scalar=g_q[64:96, :], in1=rs96[64:96, db_, :], op0=ALU.mult, op1=ALU.mult),
                                reads=[PB[qb_], rs_bufs[db_]], writes=[])
                        else:
                            self.op(dve, lambda: nc.vector.scalar_tensor_tensor(
                                out=t1[64:96, :], in0=PS[qb_][64:96, :], scalar=g_q[64:96, :], in1=cs["rope"][64:96, 0, sl],
                                op0=ALU.mult, op1=ALU.mult), reads=[PB[qb_]], writes=[t_buf])
                            self.op(dve, lambda: nc.vector.scalar_tensor_tensor(
                                out=t2[64:96, :], in0=PS[6][64:96, :], scalar=g_qs[64:96, :], in1=cs["rope"][64:96, 1, sl],
                                op0=ALU.mult, op1=ALU.mult), reads=[PB[6]], writes=[])
                            self.op(dve, lambda: nc.vector.tensor_tensor(out=t1[64:96, :], in0=t1[64:96, :], in1=t2[64:96, :], op=ALU.add),
                                    reads=[], writes=[])
                            self.op(dve, lambda: nc.vector.tensor_tensor(out=Qn[64:96, h, :], in0=t1[64:96, :], in1=rs96[64:96, db_, :], op=ALU.mult),
                                    reads=[rs_bufs[db_]], writes=[])
                        qn_bufs[h].w = self._tok(dve)
                    steps.append(s7)
                    return steps

                iters = []
                for pi, (li, h) in enumerate(pairs):
                    if hf == 1:
                        groups = [(0, TN, list(range(nkt)))]
                    else:
                        groups = [(s2 * 256, 256, [li * 4 + s2 * 2, li * 4 + s2 * 2 + 1]) for s2 in range(2)]
                    for gi, (q0, nq, kts) in enumerate(groups):
                        for ki, kt in enumerate(kts):
                            iters.append((pi, q0, nq, kt, ki == 0, ki == len(kts) - 1, gi == 0 and ki == 0))
                kb_done = {}
                qpend = {}

                def run_q(pi, n):
                    st = qpend.get(pi)
                    while st and n > 0:
                        st.pop(0)()
                        n -= 1

                def emit_k(pi):
                    nonlocal ei
                    li, h = pairs[pi]
                    kbi = pi % 2
                    if hf == 1:
                        kcols = [(0, 256, ckvc, ckvc_buf, 0), (256, 512, ckvn, ckvn_buf, 0), (768, 512, ckvn, ckvn_buf, 512)]
                    else:
                        kcols = [(li * TN, TN, ckvn, ckvn_buf, li * TN)]
                    for ci_, (c0, w, src, srcb, s0) in enumerate(kcols):
                        fns = [lambda rc=rc, src=src, s0=s0, w=w, h=h: nc.tensor.matmul(
                            PS[6][0:64, 0:w], lhsT=self.ring[:, sk_, rc * 512 + h * 64: rc * 512 + (h + 1) * 64],
                            rhs=src[:, rc, s0:s0 + w], start=(rc == 0), stop=(rc == 1)) for rc in range(2)]
                        self.op(pe, fns, reads=[skb, srcb], writes=[PB[6]])
                        if ei % 2 == 0:
                            self.op(act, lambda c0=c0, w=w, kbi=kbi: nc.scalar.activation(out=Kb[0:64, kbi, c0:c0 + w], in_=PS[6][0:64, 0:w], func=AF.Copy),
                                    reads=[PB[6]], writes=[kb_bufs[kbi]] if ci_ == 0 else [])
                        else:
                            self.op(dve, lambda c0=c0, w=w, kbi=kbi: nc.vector.tensor_copy(out=Kb[0:64, kbi, c0:c0 + w], in_=PS[6][0:64, 0:w]),
                                    reads=[PB[6]], writes=[kb_bufs[kbi]] if ci_ == 0 else [])
                        ei += 1
                    kb_done[pi] = [self._tok(act), self._tok(dve)]

                def emit_s(it):
                    nonlocal si
                    pi, q0, nq, kt, first, lastk, newhead = it
                    li, h = pairs[pi]
                    if newhead:
                        run_q(pi, 99)
                        emit_k(pi)
                        if pi + 1 < len(pairs):
                            qpend[pi + 1] = q_steps(pi + 1)
                    sbk = si % 2
                    si += 1
                    for tk_ in kb_done[pi]:
                        pe.wait(tk_)
                    self.op(pe, lambda: nc.tensor.matmul(
                        PS[sbk][:, 0:nq], lhsT=Kb[0:96, pi % 2, kt * 128:(kt + 1) * 128], rhs=Qn[0:96, h, q0:q0 + nq], start=True, stop=True),
                        reads=[kb_bufs[pi % 2], qn_bufs[h]], writes=[PB[sbk]])
                    return sbk

                qpend[0] = q_steps(0)
                nstep = 1 if hf == 1 else 2
                sb_next = emit_s(iters[0])
                for idx, it in enumerate(iters):
                    pi, q0, nq, kt, first, lastk, newhead = it
                    li, h = pairs[pi]
                    sbk = sb_next
                    self.op(act, lambda: nc.scalar.activation(
                        out=PT[:, sbk, 0:nq], in_=PS[sbk][:, 0:nq], func=AF.Exp, scale=rk[:, kt, h:h + 1]),
                        reads=[PB[sbk], rk_buf], writes=[pt_bufs[sbk]])
                    if idx + 1 < len(iters):
                        sb_next = emit_s(iters[idx + 1])
                    ob, dbk = 2, 3
                    self.op(pe, [lambda: nc.tensor.matmul(
                        PS[ob][0:64, 0:nq], lhsT=Vt[:, kt, h * 64:(h + 1) * 64], rhs=PT[:, sbk, 0:nq], start=first, stop=lastk),
                        lambda: nc.tensor.matmul(
                        PS[dbk][0:64, 0:nq], lhsT=self.ones_bf[:, 0:64], rhs=PT[:, sbk, 0:nq], start=first, stop=lastk)],
                        reads=[vt_buf, pt_bufs[sbk], self.const_buf], writes=[PB[ob], PB[dbk]] if first else [])
                    run_q(pi + 1, nstep)
                    if lastk:
                        tk = self._tok(pe)
                        PB[ob].w = tk
                        PB[dbk].w = tk
                        self.op(act, lambda: nc.scalar.activation(out=rden[:, 0, 0:nq], in_=PS[dbk][0:64, 0:nq], func=AF.Ln),
                                reads=[PB[dbk]], writes=[rden_bufs[0]])
                        self.op(act, lambda: nc.scalar.activation(out=rden[:, 0, 0:nq], in_=rden[:, 0, 0:nq], func=AF.Exp, scale=-1.0),
                                reads=[], writes=[rden_bufs[0]])
                        self.op(dve, lambda: nc.vector.tensor_tensor(
                            out=mlaT[0:64, h, li * TN + q0: li * TN + q0 + nq], in0=PS[ob][0:64, 0:nq], in1=rden[:, 0, 0:nq], op=ALU.mult),
                            reads=[PB[ob], rden_bufs[0]], writes=[mla_buf])
                self.barrier()
            wo = self.even_w_out[j]
            sa_, sab = self.ring_load([(0, [4, D], wout[:, 0:4, :])])
            sb1, sbb1 = self.ring_load([(0, [4, D], wo[512:768, :].rearrange("(h p) d -> p h d", p=64), 64)])
            sb2, sbb2 = self.ring_load([(0, [4, D], wo[768:1024, :].rearrange("(h p) d -> p h d", p=64), 64)])
            di = 0
            for li in range(2):
                t = 2 * hf + li
                jc = hf
                sl = slice(li * TN, (li + 1) * TN)
                for dc in range(KC):
                    bank = 4 + di % 3
                    di += 1
                    fns = [lambda c=c, dc=dc, bank=bank: nc.tensor.matmul(
                        PS[bank][:], lhsT=self.ring[:, sa_, c * 1024 + dc * 128: c * 1024 + (dc + 1) * 128], rhs=ogT[:, c, sl],
                        start=(c == 0), stop=False) for c in range(4)]
                    for hh_ in range(8):
                        sx = sb1 if hh_ < 4 else sb2
                        fns.append(lambda hh_=hh_, sx=sx, dc=dc, bank=bank: nc.tensor.matmul(
                            PS[bank][:], lhsT=self.ring[0:64, sx, (hh_ % 4) * 1024 + dc * 128: (hh_ % 4) * 1024 + (dc + 1) * 128],
                            rhs=mlaT[0:64, hh_, sl], start=False, stop=(hh_ == 7)))
                    self.op(pe, fns, reads=[sab, sbb1, sbb2, og_buf, mla_buf], writes=[PB[bank]])
                    xs = self.xT[:, dc, t * TN:(t + 1) * TN]
                    self.op(dve, lambda xs=xs, dc=dc, jc=jc, bank=bank: nc.vector.scalar_tensor_tensor(
                        out=xs, in0=PS[bank][:], scalar=self.modG[:, k, dc, jc:jc + 1], in1=xs,
                        op0=ALU.mult, op1=ALU.add), reads=[PB[bank], self.mod_buf], writes=[self.x_bufs[t]])
            self.barrier()
            for e_ in (pe, act, dve):
                e_.wait((self.st3[0], self.st3[1]))

    def rms_rstd_w(self, sq, sq_buf, rows, nfeat, out, out_buf, w):
        nc = self.nc
        self.op(self.pe, lambda: nc.tensor.matmul(self.psum[7][0:rows, 0:w], lhsT=self.ones_bf[0:rows, 0:rows], rhs=sq[0:rows, 0, 0:w],
                                                  start=True, stop=True), reads=[sq_buf, self.const_buf], writes=[self.pbuf[7]])
        self.op(self.act, lambda: nc.scalar.activation(out=out[0:rows, 0:w], in_=self.psum[7][0:rows, 0:w], func=AF.Ln, scale=1.0 / nfeat, bias=EPS),
                reads=[self.pbuf[7]], writes=[out_buf])
        self.op(self.act, lambda: nc.scalar.activation(out=out[0:rows, 0:w], in_=out[0:rows, 0:w], func=AF.Exp, scale=-0.5),
                reads=[], writes=[out_buf])

    def rms_rstd(self, sq, sq_buf, nch, rows, nfeat, out, out_buf):
        nc = self.nc
        fns = [lambda c=c: nc.tensor.matmul(self.psum[7][0:rows, :], lhsT=self.ones_bf[0:rows, 0:rows], rhs=sq[0:rows, c, :],
                                            start=(c == 0), stop=(c == nch - 1)) for c in range(nch)]
        self.op(self.pe, fns, reads=[sq_buf, self.const_buf], writes=[self.pbuf[7]])
        self.op(self.act, lambda: nc.scalar.activation(out=out[0:rows, :], in_=self.psum[7][0:rows, :], func=AF.Ln, scale=1.0 / nfeat, bias=EPS),
                reads=[self.pbuf[7]], writes=[out_buf])
        self.op(self.act, lambda: nc.scalar.activation(out=out[0:rows, :], in_=out[0:rows, :], func=AF.Exp, scale=-0.5),
                reads=[], writes=[out_buf])

    def odd_mixer(self, l):
        nc = self.nc
        j = l // 2
        k = 1
        win = self.odd_w_in[j].rearrange("(kc p) f -> p kc f", p=128)
        wout = self.odd_w_out[j].rearrange("(c p) d -> p c d", p=128)
        with contextlib.ExitStack() as st:
            wv = self.sb("odd_wv", [128, KC, 2048], BF16, st)
            wv_buf = self.buf("wv")
            hT = self.sb("odd_hT", [128, KC, TN], BF16, st)
            h_bufs = [self.buf("h")]
            mix = self.sb("odd_mix", [128, 16, TN], BF16, st)
            mix_buf = self.buf("mix")
            gv = self.sb("odd_gv", [128, 2048], BF16, st)
            gv_buf = self.buf("gv")
            sqf = self.sb("odd_sqf", [128, 2, 512], F32, st)
            sq_bufs = [self.buf("sqf"), self.buf("sqf")]
            ss = self.sb("odd_ss", [128, 16], F32, st)
            ss_bufs = [self.buf("ss"), self.buf("ss")]
            wp = self.sb("odd_wp", [128, 4, 128], BF16, st)
            wp_buf = self.buf("wp")
            ut = self.sb("odd_u", [128, 2, TN], F32, st)
            u_bufs = [self.buf("u"), self.buf("u")]
            tmp, rstd = self.norm_tmp(st)
            for kc in range(KC):
                self.dma(self.pool, self.wv_sem, wv[:, kc, :], win[:, kc, 2048:4096], writes=[wv_buf] if kc == 0 else [])
            wv_buf.w = (self.wv_sem[0], self.wv_sem[1])
            bi = 0
            mi = 0
            for t in range(NT):
                jc = 0 if t < NT // 2 else 1
                self.norm_modulate(k, [t], hT, h_bufs, tmp, rstd)
                gvs = [gv, tmp["sq"][:].rearrange("p a b -> p (a b)")]
                gvb = [gv_buf, tmp["buf"]]

                def vfront(q4):
                    nonlocal bi
                    gq, gqb = gvs[q4 % 2], gvb[q4 % 2]
                    so = (q4 % 2) * 8
                    for g in range(4):
                        bank = bi % 4
                        bi += 1
                        fns = [lambda kc=kc, g=g, bank=bank: nc.tensor.matmul(
                            self.psum[bank][:], lhsT=hT[:, kc, q4 * 128:(q4 + 1) * 128],
                            rhs=wv[:, kc, g * 512:(g + 1) * 512], start=(kc == 0), stop=(kc == KC - 1)) for kc in range(KC)]
                        self.op(self.pe, fns, reads=[h_bufs[0], wv_buf], writes=[self.pbuf[bank]])
                        self.op(self.act, [
                            lambda g=g, bank=bank: nc.scalar.activation(out=gq[:, g * 512:(g + 1) * 512], in_=self.psum[bank][:],
                                                                        func=AF.Gelu_apprx_tanh),
                            lambda g=g, bank=bank: nc.scalar.activation(out=sqf[:, g % 2, :], in_=gq[:, g * 512:(g + 1) * 512],
                                                                        func=AF.Square)],
                            reads=[self.pbuf[bank]], writes=([gqb] if g == 0 else []) + [sq_bufs[g % 2]])
                        self.op(self.dve, lambda g=g: nc.vector.tensor_reduce(
                            out=ss[:, so + g:so + g + 1], in_=sqf[:, g % 2, :], axis=mybir.AxisListType.X, op=ALU.add),
                            reads=[sq_bufs[g % 2]], writes=[ss_bufs[q4 % 2]])
                    gqb.w = (self.act.sems[self.act.si], self.act.cnt)

                def vback(q4):
                    nonlocal mi
                    gq, gqb = gvs[q4 % 2], gvb[q4 % 2]
                    so = (q4 % 2) * 8
                    sb_ = ss_bufs[q4 % 2]
                    self.op(self.dve, lambda: nc.vector.tensor_reduce(out=ss[:, so + 4:so + 5], in_=ss[:, so:so + 4], axis=mybir.AxisListType.X, op=ALU.add),
                            reads=[], writes=[sb_], same=True)
                    self.op(self.act, lambda: nc.scalar.activation(out=ss[:, so + 5:so + 6], in_=ss[:, so + 4:so + 5], func=AF.Ln, scale=1.0 / 2048, bias=EPS),
                            reads=[], writes=[sb_], same=True)
                    self.op(self.act, lambda: nc.scalar.activation(out=ss[:, so + 6:so + 7], in_=ss[:, so + 5:so + 6], func=AF.Exp, scale=-0.5),
                            reads=[], writes=[sb_], same=True)
                    self.op(self.dve, lambda: nc.vector.tensor_scalar(
                        out=wp[:].rearrange("p a b -> p (a b)"), in0=self.wsT[:, j].rearrange("p a b -> p (a b)"),
                        scalar1=ss[:, so + 6:so + 7], scalar2=None, op0=ALU.mult), reads=[sb_], writes=[wp_buf])
                    for g in range(4):
                        bank = 4 + mi % 2
                        mi += 1
                        fns = [lambda g=g, cc=cc, bank=bank: nc.tensor.matmul(
                            self.psum[bank][:, cc * 128:(cc + 1) * 128], lhsT=gq[:, (g * 4 + cc) * 128:(g * 4 + cc + 1) * 128],
                            rhs=wp[:, g, :], start=True, stop=True) for cc in range(4)]
                        self.op(self.pe, fns, reads=[gqb, wp_buf], writes=[self.pbuf[bank]])
                        for cc in range(4):
                            c16 = g * 4 + cc
                            self.op(self.dve, lambda g=g, cc=cc, c16=c16, bank=bank: nc.vector.scalar_tensor_tensor(
                                out=mix[:, c16, q4 * 128:(q4 + 1) * 128], in0=self.psum[bank][:, cc * 128:(cc + 1) * 128],
                                scalar=self.oddvg[:, j, c16:c16 + 1], in1=self.bsb[:, j, g, :], op0=ALU.mult, op1=ALU.add),
                                reads=[self.pbuf[bank]], writes=[mix_buf] if (q4 == 0 and c16 == 0) else [])

                vfront(0)
                for q4 in range(4):
                    if q4 + 1 < 4:
                        vfront(q4 + 1)
                    vback(q4)
                mix_buf.w = (self.dve.sems[self.dve.si], self.dve.cnt)
                for sl in range(4):
                    s, sbuf_ = self.ring_load([(0, [KC, 512], win[:, :, sl * 512:(sl + 1) * 512])])
                    for cc in range(4):
                        c16 = sl * 4 + cc
                        bank = bi % 4
                        bi += 1
                        fns = [lambda kc=kc, cc=cc, s=s, bank=bank: nc.tensor.matmul(
                            self.psum[bank][:], lhsT=self.ring[:, s, kc * 512 + cc * 128: kc * 512 + (cc + 1) * 128],
                            rhs=hT[:, kc, :], start=(kc == 0), stop=(kc == KC - 1)) for kc in range(KC)]
                        self.op(self.pe, fns, reads=[sbuf_, h_bufs[0]], writes=[self.pbuf[bank]])
                        ub = c16 % 2
                        self.op(self.act, lambda ub=ub, bank=bank: nc.scalar.activation(
                            out=ut[:, ub, :], in_=self.psum[bank][:], func=AF.Gelu_apprx_tanh),
                            reads=[self.pbuf[bank]], writes=[u_bufs[ub]])
                        self.op(self.dve, lambda ub=ub, c16=c16: nc.vector.tensor_tensor(
                            out=mix[:, c16, :], in0=ut[:, ub, :], in1=mix[:, c16, :], op=ALU.mult),
                            reads=[u_bufs[ub]], writes=[mix_buf] if c16 == 0 else [])
                mix_buf.w = (self.dve.sems[self.dve.si], self.dve.cnt)
                mix_buf.r = {}
                for dc in range(KC):
                    if dc % 2 == 0:
                        so, sob = self.ring_load([(0, [16, 256], wout[:, :, dc * 128:(dc + 2) * 128])])
                    bank = 4 + mi % 3
                    mi += 1
                    fns = [lambda c16=c16, dc=dc, bank=bank, so=so: nc.tensor.matmul(
                        self.psum[bank][:], lhsT=self.ring[:, so, c16 * 256 + (dc % 2) * 128: c16 * 256 + (dc % 2 + 1) * 128],
                        rhs=mix[:, c16, :], start=(c16 == 0), stop=(c16 == 15)) for c16 in range(16)]
                    self.op(self.pe, fns, reads=[sob, mix_buf], writes=[self.pbuf[bank]])
                    xs = self.xT[:, dc, t * TN:(t + 1) * TN]
                    self.op(self.dve, lambda xs=xs, dc=dc, jc=jc, bank=bank: nc.vector.scalar_tensor_tensor(
                        out=xs, in0=self.psum[bank][:], scalar=self.modG[:, k, dc, jc:jc + 1], in1=xs,
                        op0=ALU.mult, op1=ALU.add), reads=[self.pbuf[bank], self.mod_buf], writes=[self.x_bufs[t]])

    def mod_blocks(self, l, par, cbs):
        nc = self.nc
        aw = self.ada_w[l].rearrange("(kc p) f -> p kc f", p=128)
        pb = self.pbuf[7]
        psv = self.psum[7][:, 0:144].rearrange("p (c j) -> p c j", j=2)
        for cb in cbs:
            s, sbuf_ = self.ring_load([(0, [KC, 512], aw[:, :, cb * 512:(cb + 1) * 512])])
            fns = []
            for cc in range(4):
                c = cb * 4 + cc
                for kc in range(KC):
                    fns.append(lambda c=c, cc=cc, kc=kc, s=s: nc.tensor.matmul(
                        psv[:, c, :], lhsT=self.ring[:, s, kc * 512 + cc * 128: kc * 512 + (cc + 1) * 128],
                        rhs=self.scond[:, kc, :], start=(kc == 0), stop=(kc == KC - 1)))
            first = (cb == 0 or cb == 7)
            self.op(self.pe, fns, reads=[sbuf_, self.scond_buf], writes=[pb] if first else [])
            pb.w = (self.pe.sems[self.pe.si], self.pe.cnt)

    def mod_finish(self, l, par, c0=0, c1=72, derive=True):
        nc = self.nc
        pb = self.pbuf[7]
        psv = self.psum[7][:, 0:144].rearrange("p (c j) -> p c j", j=2)
        mod, modA, modB, modG, mod_buf = self.mod2[par], self.modA2[par], self.modB2[par], self.modG2[par], self.mod_bufs[par]
        for j in range(2):
            self.op(self.dve, lambda j=j: nc.vector.tensor_tensor(
                out=mod[:, c0:c1, j], in0=psv[:, c0:c1, j], in1=self.adab[:, l, c0:c1], op=ALU.add),
                reads=[pb], writes=[mod_buf])
        if not derive:
            return
        for k in range(3):
            sh = mod[:, (3 * k) * 8:(3 * k) * 8 + 8, :]
            sc = mod[:, (3 * k + 1) * 8:(3 * k + 1) * 8 + 8, :]
            gt = mod[:, (3 * k + 2) * 8:(3 * k + 2) * 8 + 8, :]
            for j in range(2):
                self.op(self.dve, lambda k=k, j=j, sc=sc: nc.vector.scalar_tensor_tensor(
                    out=modA[:, k, :, j], in0=sc[:, :, j], scalar=1.0, in1=self.normg[:, l, k, :],
                    op0=ALU.add, op1=ALU.mult), reads=[mod_buf], same=True)
            self.op(self.dve, lambda k=k, sh=sh: nc.vector.tensor_copy(out=modB[:, k, :, :], in_=sh), writes=[])
            self.op(self.dve, lambda k=k, gt=gt: nc.vector.tensor_scalar(
                out=modG[:, k, :, :], in0=gt, scalar1=(1.0 if k == 1 else 0.5), scalar2=None, op0=ALU.mult),
                writes=[])
        mod_buf.w = (self.dve.sems[self.dve.si], self.dve.cnt)
        mod_buf.r = {}
        pb.r["dve"] = mod_buf.w

    def norm_modulate(self, k, tiles, hT, h_bufs, tmp, rstd):
        for li, t in enumerate(tiles):
            self.norm_modulate_tile(k, t, li, hT, h_bufs, tmp, rstd)

    def norm_modulate_tile(self, k, t, li, hT, h_bufs, tmp, rstd):
        nc = self.nc
        if True:
            j = 0 if t < NT // 2 else 1
            xs = self.xT[:, :, t * TN:(t + 1) * TN]
            xb = self.x_bufs[t]
            tb = tmp["buf"]
            self.op(self.act, lambda xs=xs: nc.scalar.activation(out=tmp["sq"][:], in_=xs, func=AF.Square),
                    reads=[xb], writes=[tb])
            fns = [lambda kc=kc: nc.tensor.matmul(self.psum[7][:], lhsT=self.ones_bf[:], rhs=tmp["sq"][:, kc, :],
                                                  start=(kc == 0), stop=(kc == KC - 1)) for kc in range(KC)]
            self.op(self.pe, fns, reads=[tb, self.const_buf], writes=[self.pbuf[7]])
            rb = rstd["buf"]
            self.op(self.act, [lambda: nc.scalar.activation(out=rstd["t"][:], in_=self.psum[7][:], func=AF.Ln,
                                                            scale=1.0 / D, bias=EPS),
                               lambda: nc.scalar.activation(out=rstd["t"][:], in_=rstd["t"][:], func=AF.Exp, scale=-0.5)],
                    reads=[self.pbuf[7]], writes=[rb])
            for kc in range(KC):
                t2 = tmp["t2buf"][kc % 2]
                self.op(self.dve, lambda kc=kc, t=t, j=j: nc.vector.scalar_tensor_tensor(
                    out=tmp["t2"][:, kc % 2, :], in0=self.xT[:, kc, t * TN:(t + 1) * TN],
                    scalar=self.modA[:, k, kc, j:j + 1], in1=rstd["t"][:], op0=ALU.mult, op1=ALU.mult),
                    reads=[xb, rb, self.mod_buf], writes=[t2])
                self.op(self.act, lambda kc=kc, li=li, j=j: nc.scalar.activation(
                    out=hT[:, kc, li * TN:(li + 1) * TN], in_=tmp["t2"][:, kc % 2, :], func=AF.Identity,
                    bias=self.modB[:, k, kc, j:j + 1], scale=1.0),
                    reads=[t2, self.mod_buf], writes=[h_bufs[li]] if kc == 0 else [])
            h_bufs[li].w = (self.act.sems[self.act.si], self.act.cnt)

    def norm_tmp(self, stack):
        tmp = {"sq": self.sb("n_sq", [128, KC, TN], BF16, stack), "buf": self.buf("nsq"),
               "t2": self.sb("n_t2", [128, 2, TN], F32, stack), "t2buf": [self.buf("nt2"), self.buf("nt2")]}
        rstd = {"t": self.sb("n_rstd", [128, TN], F32, stack), "buf": self.buf("rstd")}
        return tmp, rstd

    def ffn(self, l, k, wg, wu, wd, hook=None):
        nc = self.nc
        with contextlib.ExitStack() as st:
            hT = self.sb("ffn_hT", [128, KC, NTOK], BF16, st)
            h_bufs = [self.buf("h") for _ in range(NT)]
            aT = self.sb("ffn_aT", [128, 2, GCH, TN], BF16, st)
            a_bufs = [self.buf("a"), self.buf("a")]
            sg = self.sb("ffn_sg", [128, 2, TN], F32, st)
            sg_bufs = [self.buf("sg"), self.buf("sg")]
            tmp, rstd = self.norm_tmp(st)
            self.norm_modulate(k, list(range(NT)), hT, h_bufs, tmp, rstd)
            wgl = wg[l].rearrange("(kc p) f -> p kc f", p=128)
            wul = wu[l].rearrange("(kc p) f -> p kc f", p=128)
            wdl = wd[l].rearrange("(c p) d -> p c d", p=128)
            pend = None
            u = 0
            ci = 0
            dn = 0

            def down_step(pd, dc):
                nonlocal dn
                s, sbuf_, t, ab, au = pd
                j = 0 if t < NT // 2 else 1
                if True:
                    bank = 4 + dn % 3
                    dn += 1
                    fns = [lambda c=c, dc=dc, s=s, au=au, bank=bank: nc.tensor.matmul(
                        self.psum[bank][:], lhsT=self.ring[:, s, 4096 + c * 1024 + dc * 128: 4096 + c * 1024 + (dc + 1) * 128],
                        rhs=aT[:, au, c, :], start=(c == 0), stop=(c == GCH - 1)) for c in range(GCH)]
                    self.op(self.pe, fns, reads=[sbuf_, ab], writes=[self.pbuf[bank]])
                    xs = self.xT[:, dc, t * TN:(t + 1) * TN]
                    self.op(self.dve, lambda xs=xs, dc=dc, j=j, bank=bank: nc.vector.scalar_tensor_tensor(
                        out=xs, in0=self.psum[bank][:], scalar=self.modG[:, k, dc, j:j + 1], in1=xs,
                        op0=ALU.mult, op1=ALU.add), reads=[self.pbuf[bank], self.mod_buf], writes=[self.x_bufs[t]])

            psteps = []

            def run_down(n):
                for _ in range(n):
                    if psteps:
                        pd_, dc_ = psteps.pop(0)
                        down_step(pd_, dc_)

            for g in range(NG):
                s, sbuf_ = self.ring_load([
                    (0, [KC, GCH * 128], wgl[:, :, g * GCH * 128:(g + 1) * GCH * 128]),
                    (2048, [KC, GCH * 128], wul[:, :, g * GCH * 128:(g + 1) * GCH * 128]),
                    (4096, [GCH, D], wdl[:, g * GCH:(g + 1) * GCH, :]),
                ])
                for t in range(NT):
                    au = u % 2
                    ab = a_bufs[au]
                    for c in range(GCH):
                        gb = ci % 2
                        ub = 2 + ci % 2
                        ci += 1
                        fns = [lambda kc=kc, c=c, s=s, t=t, gb=gb: nc.tensor.matmul(
                            self.psum[gb][:], lhsT=self.ring[:, s, kc * 256 + c * 128: kc * 256 + (c + 1) * 128],
                            rhs=hT[:, kc, t * TN:(t + 1) * TN], start=(kc == 0), stop=(kc == KC - 1)) for kc in range(KC)]
                        self.op(self.pe, fns, reads=[sbuf_, h_bufs[t]], writes=[self.pbuf[gb]])
                        run_down(KC // (2 * GCH))
                        fns = [lambda kc=kc, c=c, s=s, t=t, ub=ub: nc.tensor.matmul(
                            self.psum[ub][:], lhsT=self.ring[:, s, 2048 + kc * 256 + c * 128: 2048 + kc * 256 + (c + 1) * 128],
                            rhs=hT[:, kc, t * TN:(t + 1) * TN], start=(kc == 0), stop=(kc == KC - 1)) for kc in range(KC)]
                        self.op(self.pe, fns, reads=[sbuf_, h_bufs[t]], writes=[self.pbuf[ub]])
                        run_down(KC // (2 * GCH))
                        self.op(self.act, lambda gb=gb: nc.scalar.activation(out=sg[:, gb, :], in_=self.psum[gb][:], func=AF.Silu),
                                reads=[self.pbuf[gb]], writes=[sg_bufs[gb]])
                        self.op(self.dve, lambda gb=gb, ub=ub, au=au, c=c: nc.vector.tensor_tensor(
                            out=aT[:, au, c, :], in0=self.psum[ub][:], in1=sg[:, gb, :], op=ALU.mult),
                            reads=[self.pbuf[ub], sg_bufs[gb]], writes=[ab] if c == 0 else [])
                    ab.w = (self.dve.sems[self.dve.si], self.dve.cnt)
                    run_down(KC)
                    pend = (s, sbuf_, t, ab, au)
                    psteps.extend((pend, dc) for dc in range(KC))
                    u += 1
                if hook is not None:
                    hook(g)
            run_down(KC)
            self.barrier()


def _host_layout(inputs, core):
    i = core
    xp = np.asarray(inputs["x_prompt"])[4 * i:4 * i + 4].reshape(1024, D)
    xs = np.asarray(inputs["x_sample"])[i]
    xT = np.ascontiguousarray(np.concatenate([xp, xs], axis=0).T)
    cond = np.stack([np.asarray(inputs["c_ctx"]), np.asarray(inputs["c"])[i]], axis=-1)
    condT = np.ascontiguousarray(cond.reshape(KC, 128, 2).transpose(1, 0, 2))
    return {"xT": xT, "condT": condT,
            "ckvcT": np.ascontiguousarray(np.asarray(inputs["cache_ckv"])[i].transpose(0, 2, 1)),
            "kropecT": np.ascontiguousarray(np.asarray(inputs["cache_krope"])[i].transpose(0, 2, 1)),
            "sgla": np.ascontiguousarray(np.asarray(inputs["state_gla"])[i])}


def _const_tables():
    idx = np.arange(128)
    same = (idx[:, None] // 64) == (idx[None, :] // 64)
    le = idx[:, None] <= idx[None, :]
    ge = idx[:, None] >= idx[None, :]
    gt = idx[:, None] > idx[None, :]
    lt = idx[:, None] < idx[None, :]
    c = np.float32(-1.0 / 16.0)
    gm = np.zeros((128, 6, 128), np.float32)
    gm[:, 0] = np.where(same & le, c, 0)
    gm[:, 1] = np.where(same & ge, c, 0)
    gm[:, 2] = np.where(same & gt, c, 0)
    gm[:, 3] = np.where(same & lt, c, 0)
    gm[:, 4] = np.where(same & le, 1, 0)
    gm[:, 5] = np.where(same & ge, 1, 0)
    m4 = np.zeros((128, 2, 4, 128), np.float32)
    m4[:, 0] = gm[:, 4][:, None, :]
    m4[:, 1] = gm[:, 5][:, None, :]
    pos = np.arange(1024)
    row = (pos // 64).astype(np.float32)
    col = (pos % 64).astype(np.float32)
    inv = (np.float32(10000.0) ** (-np.arange(8, dtype=np.float32) / np.float32(8))).astype(np.float32)
    ang = np.concatenate([row[:, None] * inv, col[:, None] * inv], axis=-1).astype(np.float32)
    cosv, sinv = np.cos(ang).astype(np.float32), np.sin(ang).astype(np.float32)
    rope = np.zeros((96, 2, 1024), np.float32)
    for f in range(32):
        rope[64 + f, 0] = cosv[:, f // 2]
        rope[64 + f, 1] = sinv[:, f // 2] * (-1.0 if f % 2 == 0 else 1.0)
    return gm, m4, rope


def _shared_layout(inputs):
    sh = {}
    A = lambda n: np.asarray(inputs[n], dtype=np.float32)
    ada_b = A("ada_b")
    sh["adab"] = np.ascontiguousarray(ada_b.reshape(DEPTH, 72, 128).transpose(2, 0, 1))
    ng = A("norm_g")
    sh["normg"] = np.ascontiguousarray(ng.reshape(DEPTH, 3, KC, 128).transpose(3, 0, 1, 2))
    vg = A("odd_v_g")
    sh["oddvg"] = np.ascontiguousarray(vg.reshape(2, 16, 128).transpose(2, 0, 1))
    sh["wsT"] = np.ascontiguousarray(A("odd_ws").transpose(3, 0, 1, 2))
    sh["bsb"] = np.ascontiguousarray(np.broadcast_to(A("odd_bs")[None], (128, 2, 4, 128)))
    for n in ("odd_w_in", "odd_w_out", "ada_w", "ffn1_wg", "ffn1_wu", "ffn1_wd", "ffn2_wg", "ffn2_wu", "ffn2_wd",
              "even_w_in", "even_w_out"):
        sh[n] = np.ascontiguousarray(A(n))
    swap = np.arange(32) ^ 1
    win = A("even_w_in")
    wsw = np.zeros((2, D, 96), np.float32)
    wsw[:, :, 64:96] = win[:, :, 2208:2240][:, :, swap]
    sh["win_sw"] = wsw
    qb = A("mla_qb_w")
    sh["qbw"] = np.ascontiguousarray(qb.reshape(2, 384, 768))
    qs = qb.copy()
    qs[..., 64:96] = qb[..., 64:96][..., swap]
    sh["qbw_sw"] = np.ascontiguousarray(qs.reshape(2, 384, 768))
    kvb = A("mla_kvb_w")
    sh["kvbK"] = np.ascontiguousarray(kvb[..., :64].reshape(2, 256, 512))
    sh["kvbV"] = np.ascontiguousarray(kvb[..., 64:].reshape(2, 256, 512))
    gm, m4, rope = _const_tables()
    sh["gmask"], sh["mask4"], sh["rope"] = np.ascontiguousarray(gm[:, 0:4]), m4, rope
    qn, kn = A("mla_qn_g"), A("mla_kn_g")
    sw96 = np.arange(96)
    sw96[64:96] = 64 + swap
    sh["qkng"] = np.ascontiguousarray(np.stack([qn, qn[:, sw96], kn, kn[:, sw96]], axis=-1).transpose(1, 0, 2))
    sh["glag"] = np.ascontiguousarray(A("gla_norm_g").T)
    sh["qag"] = np.ascontiguousarray(A("mla_qa_g").reshape(2, 3, 128).transpose(2, 0, 1))
    sh["kvag"] = np.ascontiguousarray(A("mla_kva_g").reshape(2, 2, 128).transpose(2, 0, 1))
    w2 = A("gla_gate_w2")
    gb = A("gla_gate_b")
    w2aug = np.zeros((33, 2, 2, 256), np.float32)
    w2aug[0:16, :, 0, :] = w2[:, 0].transpose(1, 0, 2)
    w2aug[16:32, :, 1, :] = w2[:, 1].transpose(1, 0, 2)
    w2aug[32] = gb
    sh["w2aug"] = w2aug
    return sh


def run(inputs, cfg, cores=8, trace=False):
    b = Builder(cfg)
    nc = b.build()
    sh = _shared_layout(inputs)
    in_maps = []
    for i in range(cores):
        m = dict(sh)
        m.update(_host_layout(inputs, i))
        in_maps.append(m)
    res = run_bass_kernel_spmd(nc, in_maps, core_ids=list(range(cores)), trace=trace)
    return res


def kernel(**inputs):
    res = run(inputs, {})
    B, S, DB, DS = 32, 256, 8, 1024
    y_prompt = np.zeros((B, S, D), np.float32)
    y_sample = np.zeros((DB, DS, D), np.float32)
    new_ckv = np.zeros((B, 2, S, 256), np.float32)
    new_krope = np.zeros((B, 2, S, 32), np.float32)
    new_gla = np.zeros((B, 2, 2, 4, 64, 128), np.float32)
    for i, r in enumerate(res.results):
        yT = np.asarray(r["yT"])
        y_prompt[4 * i:4 * i + 4] = yT[:, :1024].T.reshape(4, S, D)
        y_sample[i] = yT[:, 1024:].T
        ck = np.asarray(r["ckvT_o"])
        new_ckv[4 * i:4 * i + 4] = ck.reshape(2, 256, 4, S).transpose(2, 0, 3, 1)
        kr = np.asarray(r["kropeT_o"])
        new_krope[4 * i:4 * i + 4] = kr.reshape(2, 32, 4, S).transpose(2, 0, 3, 1)
        new_gla[4 * i:4 * i + 4] = np.asarray(r["gla_o"])
    return (y_prompt, y_sample, new_ckv, new_krope, new_gla)


def check_states(r, states, inp):
    for (l, st) in states:
        j = l // 2
        ck = np.asarray(r["ckvT_o"])[j].reshape(256, 4, 256).transpose(1, 2, 0)
        kr = np.asarray(r["kropeT_o"])[j].reshape(32, 4, 256).transpose(1, 2, 0)
        gl = np.asarray(r["gla_o"])[:, j]
        for nm, a, b in (("ckv", ck, np.asarray(st[0])), ("krope", kr, np.asarray(st[1])), ("gla", gl, np.asarray(st[2]))):
            print("state", nm, "layer", l, "relvar", ((a - b) ** 2).mean() / (b ** 2).mean())
```

```python
import contextlib
import numpy as np
import concourse.bass as bass
import concourse.mybir as mybir
from concourse.bass_utils import run_bass_kernel_spmd

F32 = mybir.dt.float32
BF16 = mybir.dt.bfloat16
AF = mybir.ActivationFunctionType
ALU = mybir.AluOpType

D = 1024
KC = 8
NTOK = 2048
TN = 512
NT = NTOK // TN
DEPTH = 4
FH = 2816
FC = FH // 128
GCH = 2
NG = FC // GCH
EPS = 1e-6
SLOT = 6144
NSLOT = 3
SEM_LIMIT = 30000


class Buf:
    __slots__ = ("name", "w", "r")

    def __init__(self, name):
        self.name = name
        self.w = None
        self.r = {}


class Eng:
    def __init__(self, nc, h, name, nsem):
        self.nc = nc
        self.h = h
        self.name = name
        self.sems = [nc.alloc_semaphore(name=f"e_{name}_{i}") for i in range(nsem)]
        self.si = 0
        self.cnt = 0
        self.seen = {}
        self.own = set(id(s) for s in self.sems)

    def wait(self, tok, same=False):
        if tok is None:
            return
        sem, val = tok
        if id(sem) in self.own and not same:
            return
        k = id(sem)
        if self.seen.get(k, 0) >= val:
            return
        self.h.wait_ge(sem, val)
        self.seen[k] = val

    def signal(self, ins):
        sem = self.sems[self.si]
        ins.then_inc(sem, 1)
        self.cnt += 1
        tok = (sem, self.cnt)
        if self.cnt >= SEM_LIMIT:
            self.si += 1
            self.cnt = 0
        return tok


class Builder:
    def __init__(self, cfg):
        self.cfg = cfg
        nc = bass.Bass("TRN2", target_bir_lowering=False)
        self.nc = nc
        self.pe = Eng(nc, nc.tensor, "pe", 1)
        self.act = Eng(nc, nc.scalar, "act", 2)
        self.dve = Eng(nc, nc.vector, "dve", 2)
        self.pool = Eng(nc, nc.gpsimd, "pool", 0)
        self.sp = Eng(nc, nc.sync, "sp", 0)
        self.es = contextlib.ExitStack()
        self.dram = {}
        self.nbuf = 0

    def din(self, name, shape, dt=F32):
        t = self.nc.dram_tensor(name, list(shape), dt, kind="ExternalInput").ap()
        self.dram[name] = t
        return t

    def dout(self, name, shape, dt=F32):
        t = self.nc.dram_tensor(name, list(shape), dt, kind="ExternalOutput").ap()
        self.dram[name] = t
        return t

    def sb(self, name, shape, dt, stack=None):
        self.nbuf += 1
        return (stack or self.es).enter_context(self.nc.sbuf_tensor(f"{name}_{self.nbuf}", list(shape), dt))

    def buf(self, name="b"):
        self.nbuf += 1
        return Buf(f"{name}{self.nbuf}")

    def op(self, eng, fns, reads=(), writes=(), same=False):
        for b in reads:
            eng.wait(b.w, same)
        for b in writes:
            eng.wait(b.w, same)
            for t in b.r.values():
                eng.wait(t, same)
        if not isinstance(fns, (list, tuple)):
            fns = [fns]
        ins = None
        for f in fns:
            ins = f()
        tok = eng.signal(ins)
        for b in reads:
            b.r[eng.name] = tok
        for b in writes:
            b.w = tok
            b.r = {}
        return tok

    def dma(self, q, sem_state, out, in_, reads=(), writes=(), **kw):
        for t in getattr(self, "last_bar", []):
            q.wait(t)
        for b in reads:
            q.wait(b.w)
        for b in writes:
            q.wait(b.w)
            for t in b.r.values():
                q.wait(t)
        q.h.dma_start(out=out, in_=in_, **kw).then_inc(sem_state[0], 16)
        sem_state[1] += 16
        tok = (sem_state[0], sem_state[1])
        for b in reads:
            b.r["dma_" + q.name] = tok
        for b in writes:
            b.w = tok
            b.r = {}
        return tok

    def newsem(self, name):
        return [self.nc.alloc_semaphore(name=name), 0]

    def barrier(self, engs=None):
        engs = engs or [self.pe, self.act, self.dve]
        toks = []
        for e in engs:
            if e.cnt > 0:
                toks.append((e.sems[e.si], e.cnt))
        for e in engs:
            for t in toks:
                e.wait(t)
        self.last_bar = toks

    def ring_init(self):
        self.ring = self.sb("ring", [128, NSLOT, SLOT], BF16)
        self.ring_bufs = [self.buf("slot") for _ in range(NSLOT)]
        self.ring_sems = [self.newsem(f"ring{i}") for i in range(NSLOT)]
        self.ring_i = 0

    def ring_load(self, pieces):
        s = self.ring_i % NSLOT
        self.ring_i += 1
        b = self.ring_bufs[s]
        for piece in pieces:
            off, dims, src = piece[0], piece[1], piece[2]
            npart = piece[3] if len(piece) > 3 else 128
            n = int(np.prod(dims))
            dst = self.ring[0:npart, s, off:off + n]
            if len(dims) == 2:
                dst = dst.rearrange("p (a b) -> p a b", a=dims[0])
            for t in b.r.values():
                self.pool.wait(t)
            self.pool.h.dma_start(out=dst, in_=src).then_inc(self.ring_sems[s][0], 16)
            self.ring_sems[s][1] += 16
        b.w = (self.ring_sems[s][0], self.ring_sems[s][1])
        b.r = {}
        return s, b

    def build(self):
        cfg = self.cfg
        nc = self.nc
        depth = cfg.get("depth", DEPTH)
        xT_d = self.din("xT", [D, NTOK])
        condT_d = self.din("condT", [128, KC, 2])
        adab_d = self.din("adab", [128, DEPTH, 72])
        normg_d = self.din("normg", [128, DEPTH, 3, KC])
        ada_w = self.din("ada_w", [DEPTH, D, 9 * D])
        wg = [self.din("ffn1_wg", [DEPTH, D, FH]), self.din("ffn2_wg", [DEPTH, D, FH])]
        wu = [self.din("ffn1_wu", [DEPTH, D, FH]), self.din("ffn2_wu", [DEPTH, D, FH])]
        wd = [self.din("ffn1_wd", [DEPTH, FH, D]), self.din("ffn2_wd", [DEPTH, FH, D])]
        yT_d = self.dout("yT", [D, NTOK])

        self.xT = self.sb("xT_sb", [128, KC, NTOK], F32)
        self.x_bufs = [self.buf("x") for _ in range(NT)]
        self.ones_bf = self.sb("ones_bf", [128, 128], BF16)
        self.condT = self.sb("condT_sb", [128, KC, 2], F32)
        self.scond = self.sb("scond", [128, KC, 2], BF16)
        self.adab = self.sb("adab_sb", [128, DEPTH, 72], F32)
        self.normg = self.sb("normg_sb", [128, DEPTH, 3, KC], F32)
        self.mod2 = [self.sb("mod_sb", [128, 72, 2], F32)] * 2
        self.modA2 = [self.sb("modA", [128, 3, KC, 2], F32)] * 2
        self.modB2 = [self.sb("modB", [128, 3, KC, 2], F32)] * 2
        self.modG2 = [self.sb("modG", [128, 3, KC, 2], F32)] * 2
        self.mod_bufs = [self.buf("mod")] * 2
        self.mod_first = [True, True]
        self.ring_init()
        self.psum = [self.es.enter_context(nc.psum_tensor(f"ps{i}", [128, TN], F32)) for i in range(8)]
        self.pbuf = [self.buf("ps") for _ in range(8)]
        self.setup_sem = self.newsem("setup")
        self.setup2_sem = self.newsem("setup2")
        self.const_buf = self.buf("const")

        for kc in range(KC):
            nc.sync.dma_start(out=self.xT[:, kc, :], in_=xT_d[kc * 128:(kc + 1) * 128, :]).then_inc(self.setup_sem[0], 16)
            self.setup_sem[1] += 16
        for dst, src in ((self.condT, condT_d), (self.adab, adab_d), (self.normg, normg_d)):
            nc.sync.dma_start(out=dst[:], in_=src).then_inc(self.setup_sem[0], 16)
            self.setup_sem[1] += 16
        self.extra_setup()
        setup_tok = (self.setup_sem[0], self.setup_sem[1])
        setup2_tok = (self.setup2_sem[0], self.setup2_sem[1])
        for e in (self.pe, self.act, self.dve):
            e.wait(setup_tok)
            e.wait(setup2_tok)
        self.op(self.dve, lambda: nc.vector.memset(self.ones_bf[:], 1.0), writes=[self.const_buf])
        self.scond_buf = self.buf("scond")
        self.op(self.act, lambda: nc.scalar.activation(out=self.scond[:], in_=self.condT[:], func=AF.Silu),
                writes=[self.scond_buf])
        self.op(self.dve, lambda: nc.vector.tensor_copy(out=self.w2aug_bf[:], in_=self.cs["w2aug"][:]), writes=[self.const_buf])

        layers = list(cfg.get("layers", range(depth)))
        self.ada_w = ada_w
        pre_done = False
        for i, l in enumerate(layers):
            par = i % 2
            if not pre_done:
                self.mod_blocks(l, par, range(18))
                self.mod_finish(l, par)
            self.mod, self.modA, self.modB, self.modG = self.mod2[par], self.modA2[par], self.modB2[par], self.modG2[par]
            self.mod_buf = self.mod_bufs[par]
            pre_done = False
            overlap = (i + 1 < len(layers)) and cfg.get("mod_overlap", True) and cfg.get("ffn", True) and cfg.get("ffn2", True)
            if overlap:
                nl, npar = layers[i + 1], (i + 1) % 2
            if cfg.get("ffn", True):
                hook = None
                if overlap:
                    hook = lambda g, nl=nl, npar=npar: self.mod_blocks(nl, npar, [g] if g < 7 else [])
                self.ffn(l, 0, wg[0], wu[0], wd[0], hook=hook)
                if overlap:
                    self.mod_finish(nl, npar, 0, 28, derive=False)
            if cfg.get("mixer", True):
                self.mixer(l)
            if cfg.get("ffn", True) and cfg.get("ffn2", True):
                hook = None
                if overlap:
                    hook = lambda g, nl=nl, npar=npar: self.mod_blocks(nl, npar, [7 + g])
                self.ffn(l, 2, wg[1], wu[1], wd[1], hook=hook)
                if overlap:
                    self.mod_finish(nl, npar, 28, 72, derive=True)
                    pre_done = True

        self.finish_outputs()
        osem = self.newsem("out")
        for b in self.x_bufs:
            self.sp.wait(b.w)
        for kc in range(KC):
            nc.sync.dma_start(out=yT_d[kc * 128:(kc + 1) * 128, :], in_=self.xT[:, kc, :]).then_inc(osem[0], 16)
            osem[1] += 16
        for s in self.out_sems:
            nc.sync.wait_ge(s[0], s[1])
        nc.sync.wait_ge(osem[0], osem[1])
        self.es.close()
        return nc

    def extra_setup(self):
        nc = self.nc
        self.out_sems = []
        self.odd_w_in = self.din("odd_w_in", [2, D, 4096])
        self.odd_w_out = self.din("odd_w_out", [2, 2048, D])
        oddvg_d = self.din("oddvg", [128, 2, 16])
        wsT_d = self.din("wsT", [128, 2, 4, 128])
        bsb_d = self.din("bsb", [128, 2, 4, 128])
        self.oddvg = self.sb("oddvg_sb", [128, 2, 16], F32)
        self.wsT = self.sb("wsT_sb", [128, 2, 4, 128], F32)
        self.bsb = self.sb("bsb_sb", [128, 2, 4, 128], F32)
        for dst, src in ((self.oddvg, oddvg_d), (self.wsT, wsT_d), (self.bsb, bsb_d)):
            nc.sync.dma_start(out=dst[:], in_=src).then_inc(self.setup_sem[0], 16)
            self.setup_sem[1] += 16
        self.wv_sem = self.newsem("wv")
        self.even_w_in = self.din("even_w_in", [2, D, 2240])
        self.win_sw = self.din("win_sw", [2, D, 96])
        self.even_w_out = self.din("even_w_out", [2, D, D])
        self.qbw = self.din("qbw", [2, 384, 768])
        self.qbw_sw = self.din("qbw_sw", [2, 384, 768])
        self.kvbK = self.din("kvbK", [2, 256, 512])
        self.kvbV = self.din("kvbV", [2, 256, 512])
        self.ckvcT_d = self.din("ckvcT", [2, 256, 256])
        self.kropecT_d = self.din("kropecT", [2, 32, 256])
        self.sgla_d = self.din("sgla", [2, 2, 4, 64, 128])
        self.ckvT_o = self.dout("ckvT_o", [2, 256, 1024])
        self.kropeT_o = self.dout("kropeT_o", [2, 32, 1024])
        self.gla_o = self.dout("gla_o", [4, 2, 2, 4, 64, 128])
        cs = {}
        self.cs = cs
        for nm, shp in (("gmask", [128, 4, 128]), ("mask4", [128, 2, 4, 128]), ("rope", [96, 2, 1024])):
            dd = self.din(nm, shp)
            t = self.sb(nm + "_sb", shp, BF16)
            nc.gpsimd.dma_start(out=t[:], in_=dd).then_inc(self.setup2_sem[0], 16)
            self.setup2_sem[1] += 16
            cs[nm] = t
        for nm, shp in (("qkng", [96, 2, 4]), ("glag", [128, 2]), ("qag", [128, 2, 3]), ("kvag", [128, 2, 2]),
                        ("w2aug", [33, 2, 2, 256])):
            dd = self.din(nm, shp)
            t = self.sb(nm + "_sb", shp, F32)
            nc.sync.dma_start(out=t[:], in_=dd).then_inc(self.setup_sem[0], 16)
            self.setup_sem[1] += 16
            cs[nm] = t
        self.cs = cs
        self.gmask_bf = cs["gmask"]
        self.w2aug_bf = self.sb("w2aug_bf", [33, 2, 2, 256], BF16)
        self.ld1 = self.newsem("ld1")
        self.ld2 = self.newsem("ld2")
        self.st1 = self.newsem("st1")
        self.st2 = self.newsem("st2")
        self.st3 = self.newsem("st3")
        self.out_sems += [self.st1, self.st2, self.st3]

    def finish_outputs(self):
        pass

    def mixer(self, l):
        if l % 2 == 1:
            self.odd_mixer(l)
        else:
            self.even_mixer(l)
        self.barrier()

    def even_mixer(self, l):
        for hf in range(2):
            self.even_half(l, hf)

    def _tok(self, eng):
        return (eng.sems[eng.si], eng.cnt)

    def even_half(self, l, hf):
        nc = self.nc
        j = l // 2
        k = 1
        L = 1024
        T0 = hf * L
        tiles = [2 * hf, 2 * hf + 1]
        win = self.even_w_in[j].rearrange("(kc p) f -> p kc f", p=128)
        winsw = self.win_sw[j].rearrange("(kc p) f -> p kc f", p=128)
        wout = self.even_w_out[j].rearrange("(c p) d -> p c d", p=128)
        cs = self.cs
        X = mybir.AxisListType.X
        pe, act, dve = self.pe, self.act, self.dve
        PS = self.psum
        PB = self.pbuf
        with contextlib.ExitStack() as hs:
            ogT = self.sb("e_ogT", [128, 4, L], BF16, hs); og_buf = self.buf("og")
            cqn = self.sb("e_cqn", [128, 3, L], BF16, hs); cqn_buf = self.buf("cqn")
            ckvn = self.sb("e_ckvn", [128, 2, L], BF16, hs); ckvn_buf = self.buf("ckvn")
            krT = self.sb("e_krT", [96, L], F32, hs); kr_buf = self.buf("kr")
            krsw = self.sb("e_krsw", [96, L], F32, hs); krsw_buf = self.buf("krsw")
            with contextlib.ExitStack() as gs:
                qT = self.sb("e_qT", [128, 2, L], BF16, gs); q_buf = self.buf("q")
                kT = self.sb("e_kT", [128, 2, L], BF16, gs); k_buf = self.buf("k")
                gl = self.sb("e_gl", [33, L], BF16, gs); gl_buf = self.buf("gl")
                ktok = self.sb("e_ktok", [128, 8, 256], BF16, gs); ktok_buf = self.buf("ktok")
                vtok = self.sb("e_vtok", [128, 8, 512], BF16, gs); vtok_buf = self.buf("vtok")
                with contextlib.ExitStack() as ps_:
                    hT = self.sb("e_hT", [128, KC, L], BF16, ps_)
                    h_bufs = [self.buf("h"), self.buf("h")]
                    tmp, rstd = self.norm_tmp(ps_)
                    self.norm_modulate(k, tiles, hT, h_bufs, tmp, rstd)
                    self.barrier()
                    stg = tmp["t2"]; stg_buf = self.buf("stg")
                    sq3 = tmp["sq"]; sq3_buf = self.buf("sq3")
                    rs2 = rstd["t"]; rs2_buf = self.buf("rs2")
                    self.op(dve, lambda: nc.vector.memset(gl[:], 1.0), writes=[gl_buf])
                    bi = [0]

                    def fm_group(s, sbuf_, ncols, c0, m, li, bank):
                        fns = [lambda kc=kc: nc.tensor.matmul(
                            PS[bank][0:m, :], lhsT=self.ring[:, s, kc * ncols + c0: kc * ncols + c0 + m],
                            rhs=hT[:, kc, li * TN:(li + 1) * TN], start=(kc == 0), stop=(kc == KC - 1)) for kc in range(KC)]
                        self.op(pe, fns, reads=[sbuf_, h_bufs[li]], writes=[PB[bank]])

                    def nb():
                        b = bi[0] % 4
                        bi[0] += 1
                        return b

                    s, sb_ = self.ring_load([(0, [KC, 512], win[:, :, 0:512])])
                    for li in range(2):
                        for c in range(2):
                            b = nb()
                            fm_group(s, sb_, 512, c * 128, 128, li, b)
                            self.op(act, lambda c=c, li=li, b=b: nc.scalar.activation(
                                out=qT[:, c, li * TN:(li + 1) * TN], in_=PS[b][:], func=AF.Copy, scale=0.125),
                                reads=[PB[b]], writes=[q_buf])
                        for c in range(2):
                            b = nb()
                            fm_group(s, sb_, 512, 256 + c * 128, 128, li, b)
                            self.op(dve, lambda c=c, li=li, b=b: nc.vector.tensor_copy(
                                out=kT[:, c, li * TN:(li + 1) * TN], in_=PS[b][:]), reads=[PB[b]], writes=[k_buf])
                    for blk in range(8):
                        b = nb()
                        fns = [lambda kc=kc, blk=blk, b=b: nc.tensor.matmul(
                            PS[b][:, 0:256], lhsT=hT[:, kc, blk * 128:(blk + 1) * 128],
                            rhs=self.ring[:, s, kc * 512 + 256: kc * 512 + 512], start=(kc == 0), stop=(kc == KC - 1)) for kc in range(KC)]
                        self.op(pe, fns, reads=[sb_, h_bufs[blk // 4]], writes=[PB[b]])
                        self.op(act, lambda blk=blk, b=b: nc.scalar.activation(out=ktok[:, blk, :], in_=PS[b][:, 0:256], func=AF.Copy),
                                reads=[PB[b]], writes=[ktok_buf])
                    s, sb_ = self.ring_load([(0, [KC, 512], win[:, :, 512:1024])])
                    for blk in range(8):
                        b = nb()
                        fns = [lambda kc=kc, blk=blk, b=b: nc.tensor.matmul(
                            PS[b][:], lhsT=hT[:, kc, blk * 128:(blk + 1) * 128],
                            rhs=self.ring[:, s, kc * 512: kc * 512 + 512], start=(kc == 0), stop=(kc == KC - 1)) for kc in range(KC)]
                        self.op(pe, fns, reads=[sb_, h_bufs[blk // 4]], writes=[PB[b]])
                        self.op(dve, lambda blk=blk, b=b: nc.vector.tensor_copy(out=vtok[:, blk, :], in_=PS[b][:]),
                                reads=[PB[b]], writes=[vtok_buf])
                    s, sb_ = self.ring_load([(0, [KC, 512], win[:, :, 1024:1536])])
                    for li in range(2):
                        for c in range(4):
                            b = nb()
                            fm_group(s, sb_, 512, c * 128, 128, li, b)
                            self.op(act, lambda c=c, li=li, b=b: nc.scalar.activation(
                                out=ogT[:, c, li * TN:(li + 1) * TN], in_=PS[b][:], func=AF.Silu),
                                reads=[PB[b]], writes=[og_buf])
                    s, sb_ = self.ring_load([(0, [KC, 416], win[:, :, 1536:1952])])
                    for li in range(2):
                        b = nb()
                        fm_group(s, sb_, 416, 0, 32, li, b)
                        self.op(act, lambda li=li, b=b: nc.scalar.activation(
                            out=gl[0:32, li * TN:(li + 1) * TN], in_=PS[b][0:32, :], func=AF.Copy),
                            reads=[PB[b]], writes=[gl_buf])
                        bs = [nb() for _ in range(3)]
                        for c in range(3):
                            fm_group(s, sb_, 416, 32 + c * 128, 128, li, bs[c])
                            self.op(act, lambda c=c, b=bs[c]: nc.scalar.activation(out=sq3[:, c, :], in_=PS[b][:], func=AF.Square),
                                    reads=[PB[bs[c]]], writes=[sq3_buf])
                        self.rms_rstd(sq3, sq3_buf, 3, 128, 384, rs2, rs2_buf)
                        for c in range(3):
                            self.op(dve, lambda c=c, li=li, b=bs[c]: nc.vector.scalar_tensor_tensor(
                                out=cqn[:, c, li * TN:(li + 1) * TN], in0=PS[b][:], scalar=cs["qag"][:, j, c:c + 1], in1=rs2[:],
                                op0=ALU.mult, op1=ALU.mult), reads=[PB[bs[c]], rs2_buf], writes=[cqn_buf])
                    s, sb_ = self.ring_load([(0, [KC, 288], win[:, :, 1952:2240]), (2304, [KC, 96], winsw[:, :, :])])
                    for li in range(2):
                        bs = [nb() for _ in range(2)]
                        for c in range(2):
                            fm_group(s, sb_, 288, c * 128, 128, li, bs[c])
                            self.op(act, lambda c=c, b=bs[c]: nc.scalar.activation(out=sq3[:, c, :], in_=PS[b][:], func=AF.Square),
                                    reads=[PB[bs[c]]], writes=[sq3_buf])
                        self.rms_rstd(sq3, sq3_buf, 2, 128, 256, rs2, rs2_buf)
                        for c in range(2):
                            self.op(dve, lambda c=c, li=li, b=bs[c]: nc.vector.scalar_tensor_tensor(
                                out=stg[:, c, :], in0=PS[b][:], scalar=cs["kvag"][:, j, c:c + 1], in1=rs2[:],
                                op0=ALU.mult, op1=ALU.mult), reads=[PB[bs[c]], rs2_buf], writes=[stg_buf])
                        self.op(act, lambda li=li: nc.scalar.activation(out=ckvn[:, :, li * TN:(li + 1) * TN], in_=stg[:], func=AF.Copy),
                                reads=[stg_buf], writes=[ckvn_buf])
                        if hf == 0:
                            self.dma(self.sp, self.st1, self.ckvT_o[j].rearrange("(c p) t -> p c t", p=128)[:, :, li * TN:(li + 1) * TN],
                                     stg[:], reads=[stg_buf])
                        b = nb()
                        fm_group(s, sb_, 288, 192, 96, li, b)
                        self.op(act, lambda li=li, b=b: nc.scalar.activation(
                            out=krT[64:96, li * TN:(li + 1) * TN], in_=PS[b][64:96, :], func=AF.Copy),
                            reads=[PB[b]], writes=[kr_buf])
                        if hf == 1:
                            b = nb()
                            fns = [lambda kc=kc, li=li, b=b: nc.tensor.matmul(
                                PS[b][0:96, :], lhsT=self.ring[:, s, 2304 + kc * 96: 2304 + (kc + 1) * 96],
                                rhs=hT[:, kc, li * TN:(li + 1) * TN], start=(kc == 0), stop=(kc == KC - 1)) for kc in range(KC)]
                            self.op(pe, fns, reads=[sb_, h_bufs[li]], writes=[PB[b]])
                            self.op(act, lambda li=li, b=b: nc.scalar.activation(
                                out=krsw[64:96, li * TN:(li + 1) * TN], in_=PS[b][64:96, :], func=AF.Copy),
                                reads=[PB[b]], writes=[krsw_buf])
                    if hf == 0:
                        self.dma(self.sp, self.st3, self.kropeT_o[j], krT[64:96, :], reads=[kr_buf])
                    self.barrier()
                    for e_ in (pe, act, dve):
                        e_.wait((self.st1[0], self.st1[1]))
                if self.cfg.get("even_stop", 9) <= 1:
                    return
                with contextlib.ExitStack() as ws:
                    oT = self.sb("g_oT", [128, 4, L], F32, ws); o_buf = self.buf("o")
                    ez = self.sb("g_ez", [128, 256], F32, ws)
                    lg = self.sb("g_l", [128, 256], BF16, ws); l_buf = self.buf("l")
                    Eq = self.sb("g_Eq", [128, 2, 2, 128], F32, ws); Eq_bufs = [self.buf("Eq"), self.buf("Eq")]
                    Ek = self.sb("g_Ek", [128, 2, 128], F32, ws)
                    Er = self.sb("g_Er", [128, 256], F32, ws); E_buf = self.buf("E")
                    qe = self.sb("g_qe", [128, 2, 2, 128], BF16, ws); qe_bufs = [self.buf("qe"), self.buf("qe")]
                    ke = self.sb("g_ke", [128, 2, 128], BF16, ws)
                    kl = self.sb("g_kl", [128, 256], BF16, ws); qk_buf = self.buf("qk")
                    attm = self.sb("g_attm", [128, 2, 4, 128], BF16, ws); attm_bufs = [self.buf("attm"), self.buf("attm")]
                    S = self.sb("g_S", [128, 2, 2, 128], F32, ws); S_bufs = [self.buf("S"), self.buf("S")]
                    Sbf = self.sb("g_Sbf", [128, 3, 2, 2, 128], BF16, ws); Sbf_bufs = [self.buf("Sbf") for _ in range(3)]
                    self.op(dve, lambda: nc.vector.memset(Sbf[:], 0.0), writes=Sbf_bufs)
                    nseq = 4 if hf == 0 else 1
                    bps = 8 // nseq
                    sgl = self.sgla_d[j]
                    items = []
                    for dr in range(2):
                        for sq in range(nseq):
                            blks = list(range(sq * bps, (sq + 1) * bps))
                            if dr == 1:
                                blks = blks[::-1]
                            for bi_, blk in enumerate(blks):
                                items.append(dict(dr=dr, sq=sq, blk=blk, first=(bi_ == 0), last=(bi_ == len(blks) - 1), par=len(items) % 2))
                    st_ = {"sv": 0, "p": 0}

                    def P1(it):
                        dr, tsl = it["dr"], slice(it["blk"] * 128, (it["blk"] + 1) * 128)
                        self.op(pe, lambda: nc.tensor.matmul(PS[0][:, 0:256], lhsT=gl[0:33, tsl], rhs=self.w2aug_bf[0:33, j, dr, :],
                                                             start=True, stop=True), reads=[gl_buf, self.const_buf], writes=[PB[0]])
                        self.op(act, lambda: nc.scalar.activation(out=ez[:], in_=PS[0][:, 0:256], func=AF.Exp, scale=-1.0),
                                reads=[PB[0]], writes=[l_buf])
                        self.op(act, lambda: nc.scalar.activation(out=lg[:], in_=ez[:], func=AF.Ln, bias=1.0),
                                reads=[], writes=[l_buf])

                    def P2(it):
                        dr, par, blk = it["dr"], it["par"], it["blk"]
                        tsl = slice(blk * 128, (blk + 1) * 128)
                        fns = [lambda hp=hp: nc.tensor.matmul(PS[1][:, hp * 128:(hp + 1) * 128], lhsT=lg[:, hp * 128:(hp + 1) * 128],
                                                              rhs=self.gmask_bf[:, dr, :], start=True, stop=True) for hp in range(2)]
                        self.op(pe, fns, reads=[l_buf], writes=[PB[1]])
                        self.op(pe, lambda: nc.tensor.matmul(PS[2][:, 0:256], lhsT=self.gmask_bf[:, 2 + dr, :], rhs=lg[:],
                                                             start=True, stop=True), reads=[l_buf], writes=[PB[2]])
                        self.op(act, lambda: nc.scalar.activation(out=Eq[:, par].rearrange("p a b -> p (a b)"), in_=PS[1][:, 0:256], func=AF.Exp),
                                reads=[PB[1]], writes=[Eq_bufs[par]])
                        self.op(act, lambda: nc.scalar.activation(out=Ek[:].rearrange("p a b -> p (a b)"), in_=PS[1][:, 0:256], func=AF.Exp, scale=-1.0),
                                reads=[PB[1]], writes=[E_buf])
                        self.op(act, lambda: nc.scalar.activation(out=Er[:], in_=PS[2][:, 0:256], func=AF.Exp),
                                reads=[PB[2]], writes=[])
                        E_buf.w = self._tok(act)
                        self.op(dve, lambda: nc.vector.tensor_tensor(out=qe[:, par], in0=qT[:, :, tsl], in1=Eq[:, par], op=ALU.mult),
                                reads=[Eq_bufs[par], q_buf], writes=[qe_bufs[par]])
                        self.op(dve, lambda: nc.vector.tensor_tensor(out=ke[:], in0=kT[:, :, tsl], in1=Ek[:], op=ALU.mult),
                                reads=[k_buf, E_buf], writes=[qk_buf])
                        self.op(dve, lambda: nc.vector.tensor_tensor(out=kl[:], in0=ktok[:, blk, :], in1=Er[:], op=ALU.mult),
                                reads=[ktok_buf], writes=[])
                        qk_buf.w = self._tok(dve)

                    def P3a(it):
                        dr, par = it["dr"], it["par"]
                        for hh in range(2):
                            bank = 3 if hh == 0 else 0
                            fns = [lambda hp=hp, hh=hh, bank=bank: nc.tensor.matmul(
                                PS[bank][:, hp * 128:(hp + 1) * 128], lhsT=ke[hh * 64:hh * 64 + 64, hp, :],
                                rhs=qe[hh * 64:hh * 64 + 64, par, hp, :], start=True, stop=True) for hp in range(2)]
                            self.op(pe, fns, reads=[qk_buf, qe_bufs[par]], writes=[PB[bank]])
                            self.op(dve, lambda hh=hh, bank=bank: nc.vector.tensor_tensor(
                                out=attm[:, par, hh::2, :], in0=PS[bank][:, 0:256].rearrange("p (a b) -> p a b", a=2),
                                in1=cs["mask4"][:, dr, 0:2, :], op=ALU.mult), reads=[PB[bank]], writes=[attm_bufs[par]] if hh == 0 else [])
                        attm_bufs[par].w = self._tok(dve)

                    def P3b(it):
                        blk = it["blk"]
                        for c2 in range(2):
                            fns = [lambda c2=c2, hp=hp: nc.tensor.matmul(
                                PS[4 + c2][:, hp * 256:(hp + 1) * 256], lhsT=kl[c2 * 64:(c2 + 1) * 64, hp * 128:(hp + 1) * 128],
                                rhs=vtok[c2 * 64:(c2 + 1) * 64, blk, hp * 256:(hp + 1) * 256], start=True, stop=True) for hp in range(2)]
                            self.op(pe, fns, reads=[qk_buf, vtok_buf], writes=[PB[4 + c2]])

                    def copy_S(ver):
                        p = st_["p"]
                        self.op(act, [lambda hh=hh, p=p: nc.scalar.activation(out=Sbf[hh * 64:hh * 64 + 64, ver, :, hh, :],
                                                                              in_=S[hh * 64:hh * 64 + 64, p, :, :], func=AF.Copy) for hh in range(2)],
                                reads=[S_bufs[p]], writes=[Sbf_bufs[ver]])

                    def P4(it):
                        dr, par, sq = it["dr"], it["par"], it["sq"]
                        if it["first"]:
                            p = st_["p"]
                            if hf == 0:
                                self.op(dve, lambda p=p: nc.vector.memset(S[:, p], 0.0), writes=[S_bufs[p]])
                            else:
                                self.dma(self.sp, self.ld2, S[:, p], sgl[dr].rearrange("(hp hh) d e -> (hh d) hp e", hh=2), writes=[S_bufs[p]])
                            copy_S(st_["sv"])
                        corder = [0, 1] if dr == 0 else [1, 0]
                        svs = [st_["sv"]]
                        for c2 in corder:
                            last = (c2 * 64 + 63) if dr == 0 else (c2 * 64)
                            p = st_["p"]
                            q_ = 1 - p
                            first_w = True
                            for hp in range(2):
                                for hh in range(2):
                                    r0 = hh * 64
                                    self.op(dve, lambda hp=hp, hh=hh, r0=r0, c2=c2, last=last, p=p, q_=q_: nc.vector.scalar_tensor_tensor(
                                        out=S[r0:r0 + 64, q_, hp, :], in0=S[r0:r0 + 64, p, hp, :], scalar=Eq[r0:r0 + 64, par, hp, last:last + 1],
                                        in1=PS[4 + c2][r0:r0 + 64, hp * 256 + hh * 128: hp * 256 + (hh + 1) * 128],
                                        op0=ALU.mult, op1=ALU.add), reads=[PB[4 + c2], Eq_bufs[par], S_bufs[p]],
                                        writes=[S_bufs[q_]] if first_w else [])
                                    first_w = False
                            S_bufs[q_].w = self._tok(dve)
                            st_["p"] = q_
                            nsv = (svs[-1] + 1) % 3
                            copy_S(nsv)
                            svs.append(nsv)
                        it["svs"] = svs
                        it["corder"] = corder
                        st_["sv"] = svs[2]
                        if it["last"] and hf == 0:
                            p = st_["p"]
                            self.dma(self.sp, self.st2, self.gla_o[sq, j, dr].rearrange("(hp hh) d e -> (hh d) hp e", hh=2), S[:, p],
                                     reads=[S_bufs[p]])

                    def P5(it):
                        dr, par, blk, svs, corder = it["dr"], it["par"], it["blk"], it["svs"], it["corder"]
                        tsl = slice(blk * 128, (blk + 1) * 128)
                        bank = 6 + par
                        fns = []
                        for h in range(4):
                            hp, r0 = h // 2, (h % 2) * 64
                            fns.append(lambda h=h: nc.tensor.matmul(PS[bank][:, h * 128:(h + 1) * 128], lhsT=vtok[:, blk, h * 128:(h + 1) * 128],
                                                                    rhs=attm[:, par, h, :], start=True, stop=False))
                            for ci_, c2 in enumerate(corder):
                                fns.append(lambda h=h, c2=c2, ci_=ci_, hp=hp, r0=r0: nc.tensor.matmul(
                                    PS[bank][:, h * 128 + c2 * 64: h * 128 + (c2 + 1) * 64], lhsT=Sbf[:, svs[ci_], hp, r0 // 64, :],
                                    rhs=qe[:, par, hp, c2 * 64:(c2 + 1) * 64], start=False, stop=(ci_ == 1)))
                        self.op(pe, fns, reads=[vtok_buf, attm_bufs[par], qe_bufs[par], Sbf_bufs[svs[0]], Sbf_bufs[svs[1]]], writes=[PB[bank]])
                        pv = PS[bank][:].rearrange("p (h t) -> p h t", h=4)
                        if dr == 0:
                            self.op(act, lambda: nc.scalar.activation(out=oT[:, :, tsl], in_=pv, func=AF.Copy),
                                    reads=[PB[bank]], writes=[o_buf])
                        else:
                            self.op(dve, lambda: nc.vector.tensor_tensor(out=oT[:, :, tsl], in0=pv, in1=oT[:, :, tsl], op=ALU.add),
                                    reads=[PB[bank]], writes=[o_buf])

                    P1(items[0]); P2(items[0]); P3a(items[0])
                    for i_, it in enumerate(items):
                        nxt = items[i_ + 1] if i_ + 1 < len(items) else None
                        if nxt is not None:
                            P1(nxt)
                        P3b(it)
                        if nxt is not None:
                            P2(nxt)
                        P4(it)
                        if nxt is not None:
                            P3a(nxt)
                        P5(it)
                    with contextlib.ExitStack() as ns:
                        sqh = self.sb("g_sq", [128, 1, TN], BF16, ns); sqh_buf = self.buf("sqh")
                        rs = self.sb("g_rs", [128, TN], F32, ns); rs_buf = self.buf("rs")
                        t3 = self.sb("g_t3", [128, TN], F32, ns); t3_buf = self.buf("t3")
                        for li in range(2):
                            for h in range(4):
                                sl = slice(li * TN, (li + 1) * TN)
                                self.op(act, lambda h=h, sl=sl: nc.scalar.activation(out=sqh[:, 0, :], in_=oT[:, h, sl], func=AF.Square),
                                        reads=[o_buf], writes=[sqh_buf])
                                self.rms_rstd(sqh, sqh_buf, 1, 128, 128, rs, rs_buf)
                                self.op(dve, lambda h=h, sl=sl: nc.vector.scalar_tensor_tensor(
                                    out=t3[:], in0=oT[:, h, sl], scalar=cs["glag"][:, j:j + 1], in1=rs[:], op0=ALU.mult, op1=ALU.mult),
                                    reads=[o_buf, rs_buf], writes=[t3_buf])
                                self.op(dve, lambda h=h, sl=sl: nc.vector.tensor_tensor(out=ogT[:, h, sl], in0=t3[:], in1=ogT[:, h, sl], op=ALU.mult),
                                        reads=[t3_buf], writes=[og_buf])
                    self.barrier()
                    for e_ in (pe, act, dve):
                        e_.wait((self.st2[0], self.st2[1]))
                    if self.cfg.get("dump_o3") and hf == 1:
                        dsem = self.newsem("dbg")
                        self.out_sems.append(dsem)
                        for nm_, t_, shp_, dt_ in (("og3", ogT, [128, 4, L], BF16), ("oT3", oT, [128, 4, L], F32)):
                            self.sp.wait(self._tok(pe)); self.sp.wait(self._tok(act)); self.sp.wait(self._tok(dve))
                            nc.sync.dma_start(out=self.dout("dbg_" + nm_, shp_, dt_), in_=t_[:]).then_inc(dsem[0], 16)
                            dsem[1] += 16
                        for e_ in (pe, act, dve):
                            e_.wait((dsem[0], dsem[1]))
            if self.cfg.get("even_stop", 9) <= 2:
                return
            self.mla_half(l, hf, ogT, og_buf, cqn, cqn_buf, ckvn, ckvn_buf, krT, kr_buf, krsw, krsw_buf, wout)

    def mla_half(self, l, hf, ogT, og_buf, cqn, cqn_buf, ckvn, ckvn_buf, krT, kr_buf, krsw, krsw_buf, wout):
        nc = self.nc
        j = l // 2
        k = 1
        L = 1024
        cs = self.cs
        pe, act, dve = self.pe, self.act, self.dve
        PS, PB = self.psum, self.pbuf
        X = mybir.AxisListType.X
        koff = 256 if hf == 1 else 0
        nk = L + koff
        nkt = nk // 128
        SC = 96 ** -0.5
        g_q, g_qs, g_k, g_ks = (cs["qkng"][:, j, i:i + 1] for i in range(4))
        with contextlib.ExitStack() as ms:
            mlaT = self.sb("m_mlaT", [64, 8, L], BF16, ms); mla_buf = self.buf("mla")
            Kb = self.sb("m_Kb", [96, 2, nk], BF16, ms); kb_bufs = [self.buf("kb"), self.buf("kb")]
            Vt = self.sb("m_Vt", [128, nkt, 512], BF16, ms); vt_buf = self.buf("vt")
            ckvc = self.sb("m_ckvc", [128, 2, 256], BF16, ms); ckvc_buf = self.buf("ckvc")
            ssn = self.sb("m_ssn", [128, nkt, 8], F32, ms); ssn_buf = self.buf("ssn")
            ssr = self.sb("m_ssr", [128, nkt], F32, ms); ssr_buf = self.buf("ssr")
            rk = self.sb("m_rk", [128, nkt, 8], F32, ms); rk_buf = self.buf("rk")
            gq2 = self.sb("m_gq2", [96, 1], F32, ms); gq2_buf = self.buf("gq2")
            sq_, sqb = self.ring_load([(0, [3, 768], self.qbw[j].rearrange("(c p) f -> p c f", p=128)),
                                       (2304, [3, 768], self.qbw_sw[j].rearrange("(c p) f -> p c f", p=128))])
            sk_, skb = self.ring_load([(0, [2, 512], self.kvbK[j].rearrange("(c p) f -> p c f", p=128)),
                                       (1024, [2, 512], self.kvbV[j].rearrange("(c p) f -> p c f", p=128))])
            self.op(dve, lambda: nc.vector.tensor_tensor(out=gq2[:], in0=g_q, in1=g_k, op=ALU.mult), writes=[gq2_buf])
            with contextlib.ExitStack() as sa:
                krg = self.sb("m_krg", [96, nk], F32, sa); krg_buf = self.buf("krg")
                krr = self.sb("m_krr", [96, nk], BF16, sa); krr_buf = self.buf("krr")
                krc = self.sb("m_krc", [96, 256], F32, sa); krc_buf = self.buf("krc")
                ckvf = self.sb("m_ckvf", [128, 2, 256], F32, sa); ckvf_buf = self.buf("ckvf")
                sqt = self.sb("m_sqt", [128, 2, 512], F32, sa); sqt_bufs = [self.buf("sqt"), self.buf("sqt")]
                t1 = self.sb("m_t1a", [96, TN], F32, sa)
                t2 = self.sb("m_t2a", [96, TN], F32, sa); t_buf = self.buf("t12")
                if hf == 1:
                    for c in range(2):
                        self.dma(self.sp, self.ld1, ckvf[:, c, :], self.ckvcT_d[j, c * 128:(c + 1) * 128, :], writes=[ckvf_buf] if c == 0 else [])
                    self.dma(self.sp, self.ld1, krc[64:96, :], self.kropecT_d[j], writes=[krc_buf])
                    tok = (self.ld1[0], self.ld1[1])
                    ckvf_buf.w = tok
                    krc_buf.w = tok
                    self.op(dve, lambda: nc.vector.tensor_copy(out=ckvc[:], in_=ckvf[:]), reads=[ckvf_buf], writes=[ckvc_buf])
                    self.op(dve, lambda: nc.vector.tensor_scalar(out=krg[64:96, 0:256], in0=krc[64:96, :], scalar1=g_k[64:96, :],
                                                                 scalar2=None, op0=ALU.mult), reads=[krc_buf], writes=[krg_buf])
                    self.op(act, lambda: nc.scalar.activation(out=krr[64:96, 0:256], in_=krc[64:96, :], func=AF.Square),
                            reads=[krc_buf], writes=[krr_buf])
                    for li in range(2):
                        sl = slice(li * TN, (li + 1) * TN)
                        self.op(dve, lambda sl=sl: nc.vector.scalar_tensor_tensor(
                            out=t1[64:96, :], in0=krT[64:96, sl], scalar=g_k[64:96, :], in1=cs["rope"][64:96, 0, sl],
                            op0=ALU.mult, op1=ALU.mult), reads=[kr_buf], writes=[t_buf])
                        self.op(dve, lambda sl=sl: nc.vector.scalar_tensor_tensor(
                            out=t2[64:96, :], in0=krsw[64:96, sl], scalar=g_ks[64:96, :], in1=cs["rope"][64:96, 1, sl],
                            op0=ALU.mult, op1=ALU.mult), reads=[krsw_buf], writes=[])
                        self.op(dve, lambda li=li: nc.vector.tensor_tensor(
                            out=krg[64:96, koff + li * TN: koff + (li + 1) * TN], in0=t1[64:96, :], in1=t2[64:96, :], op=ALU.add),
                            reads=[], writes=[krg_buf])
                else:
                    self.op(dve, lambda: nc.vector.tensor_scalar(out=krg[64:96, :], in0=krT[64:96, :], scalar1=g_k[64:96, :],
                                                                 scalar2=None, op0=ALU.mult), reads=[kr_buf], writes=[krg_buf])
                self.op(act, lambda: nc.scalar.activation(out=krr[64:96, koff:nk], in_=krT[64:96, :], func=AF.Square),
                        reads=[kr_buf], writes=[krr_buf])
                for bb in range(2):
                    self.op(act if bb == 0 else dve,
                            (lambda bb=bb: nc.scalar.activation(out=Kb[64:96, bb, :], in_=krg[64:96, :], func=AF.Copy)) if bb == 0 else
                            (lambda bb=bb: nc.vector.tensor_copy(out=Kb[64:96, bb, :], in_=krg[64:96, :])),
                            reads=[krg_buf], writes=[kb_bufs[bb]])
                fns = [lambda kt=kt: nc.tensor.matmul(PS[6][:, kt:kt + 1], lhsT=krr[64:96, kt * 128:(kt + 1) * 128],
                                                      rhs=self.ones_bf[64:96, 0:1], start=True, stop=True) for kt in range(nkt)]
                self.op(pe, fns, reads=[krr_buf, self.const_buf], writes=[PB[6]])
                self.op(act, lambda: nc.scalar.activation(out=ssr[:], in_=PS[6][:, 0:nkt], func=AF.Copy), reads=[PB[6]], writes=[ssr_buf])
                for kt in range(nkt):
                    if kt * 128 < koff:
                        src, srcb, s0 = ckvc, ckvc_buf, kt * 128
                    else:
                        src, srcb, s0 = ckvn, ckvn_buf, kt * 128 - koff
                    b0 = kt % 2
                    fns = [lambda rc=rc, src=src, s0=s0, b0=b0: nc.tensor.matmul(
                        PS[b0][:], lhsT=src[:, rc, s0:s0 + 128], rhs=self.ring[:, sk_, rc * 512:(rc + 1) * 512],
                        start=(rc == 0), stop=(rc == 1)) for rc in range(2)]
                    self.op(pe, fns, reads=[skb, srcb], writes=[PB[b0]])
                    self.op(act, lambda b0=b0: nc.scalar.activation(out=sqt[:, b0, :], in_=PS[b0][:], func=AF.Square),
                            reads=[PB[b0]], writes=[sqt_bufs[b0]])
                    self.op(dve, lambda kt=kt, b0=b0: nc.vector.tensor_reduce(
                        out=ssn[:, kt, :], in_=sqt[:, b0, :].rearrange("p (h e) -> p h e", e=64), axis=X, op=ALU.add),
                        reads=[sqt_bufs[b0]], writes=[ssn_buf])
                    b1 = 2 + kt % 2
                    fns = [lambda rc=rc, src=src, s0=s0, b1=b1: nc.tensor.matmul(
                        PS[b1][:], lhsT=src[:, rc, s0:s0 + 128], rhs=self.ring[:, sk_, 1024 + rc * 512: 1024 + (rc + 1) * 512],
                        start=(rc == 0), stop=(rc == 1)) for rc in range(2)]
                    self.op(pe, fns, reads=[skb, srcb], writes=[PB[b1]])
                    self.op(dve, lambda kt=kt, b1=b1: nc.vector.tensor_copy(out=Vt[:, kt, :], in_=PS[b1][:]), reads=[PB[b1]], writes=[vt_buf])
                for kt in range(nkt):
                    self.op(dve, lambda kt=kt: nc.vector.tensor_scalar(out=rk[:, kt, :], in0=ssn[:, kt, :], scalar1=ssr[:, kt:kt + 1],
                                                                       scalar2=None, op0=ALU.add), reads=[ssn_buf, ssr_buf], writes=[rk_buf], same=True)
                self.op(act, lambda: nc.scalar.activation(out=rk[:].rearrange("p a b -> p (a b)"), in_=rk[:].rearrange("p a b -> p (a b)"),
                                                          func=AF.Ln, scale=1.0 / 96, bias=EPS), reads=[], writes=[rk_buf], same=True)
                self.op(act, lambda: nc.scalar.activation(out=rk[:].rearrange("p a b -> p (a b)"), in_=rk[:].rearrange("p a b -> p (a b)"),
                                                          func=AF.Exp, scale=-0.5, bias=float(np.log(SC))), reads=[], writes=[rk_buf], same=True)
                self.barrier()
            with contextlib.ExitStack() as sbk_:
                Qn = self.sb("m_Qn", [96, 8, TN], BF16, sbk_); qn_bufs = [self.buf("qn") for _ in range(8)]
                sq96 = self.sb("m_sq", [96, 2, TN], BF16, sbk_); sq_bufs = [self.buf("sq96"), self.buf("sq96")]
                rs96 = self.sb("m_rs", [96, 2, TN], F32, sbk_); rs_bufs = [self.buf("rs96"), self.buf("rs96")]
                t1 = self.sb("m_t1", [96, TN], F32, sbk_)
                t2 = self.sb("m_t2", [96, TN], F32, sbk_); t_buf = self.buf("t12")
                PT = self.sb("m_PT", [128, 2, TN], BF16, sbk_); pt_bufs = [self.buf("pt"), self.buf("pt")]
                rden = self.sb("m_rden", [64, 1, TN], F32, sbk_); rden_bufs = [self.buf("rden")]
                si = 0
                ei = 0
                pairs = [(li, h) for li in range(2) for h in range(8)]

                def q_steps(pi):
                    li, h = pairs[pi]
                    sl = slice(li * TN, (li + 1) * TN)
                    qb_ = 4 + pi % 2
                    db_ = pi % 2
                    steps = []

                    def s1():
                        fns = [lambda c3=c3: nc.tensor.matmul(
                            PS[qb_][0:96, :], lhsT=self.ring[:, sq_, c3 * 768 + h * 96: c3 * 768 + (h + 1) * 96],
                            rhs=cqn[:, c3, sl], start=(c3 == 0), stop=(c3 == 2)) for c3 in range(3)]
                        self.op(pe, fns, reads=[sqb, cqn_buf], writes=[PB[qb_]])
                        if hf == 1:
                            fns = [lambda c3=c3: nc.tensor.matmul(
                                PS[6][0:96, :], lhsT=self.ring[:, sq_, 2304 + c3 * 768 + h * 96: 2304 + c3 * 768 + (h + 1) * 96],
                                rhs=cqn[:, c3, sl], start=(c3 == 0), stop=(c3 == 2)) for c3 in range(3)]
                            self.op(pe, fns, reads=[sqb, cqn_buf], writes=[PB[6]])
                    steps.append(s1)
                    steps.append(lambda: self.op(act, lambda: nc.scalar.activation(out=sq96[:, db_, :], in_=PS[qb_][0:96, :], func=AF.Square),
                                                 reads=[PB[qb_]], writes=[sq_bufs[db_]]))
                    steps.append(lambda: self.op(pe, lambda: nc.tensor.matmul(PS[7][0:96, :], lhsT=self.ones_bf[0:96, 0:96], rhs=sq96[0:96, db_, :],
                                                                              start=True, stop=True), reads=[sq_bufs[db_], self.const_buf], writes=[PB[7]]))
                    steps.append(lambda: self.op(act, lambda: nc.scalar.activation(out=rs96[:, db_, :], in_=PS[7][0:96, :], func=AF.Ln, scale=1.0 / 96, bias=EPS),
                                                 reads=[PB[7]], writes=[rs_bufs[db_]]))
                    steps.append(lambda: self.op(act, lambda: nc.scalar.activation(out=rs96[:, db_, :], in_=rs96[:, db_, :], func=AF.Exp, scale=-0.5),
                                                 reads=[], writes=[rs_bufs[db_]]))
                    steps.append(lambda: self.op(dve, lambda: nc.vector.scalar_tensor_tensor(
                        out=Qn[0:64, h, :], in0=PS[qb_][0:64, :], scalar=gq2[0:64, :], in1=rs96[0:64, db_, :], op0=ALU.mult, op1=ALU.mult),
                        reads=[PB[qb_], rs_bufs[db_], gq2_buf], writes=[qn_bufs[h]]))

                    def s7():
                        if hf == 0:
                            self.op(dve, lambda: nc.vector.scalar_tensor_tensor(
                                out=Qn[64:96, h, :], in0=PS[qb_][64:96, :], scalar=g_q[64:96, :], in1=rs96[64:96, db_, :], op0=ALU.mult, op1=ALU.mult),
                                reads=[PB[qb_], rs_bufs[db_]], writes=[])
                        else:
                            self.op(dve, lambda: nc.vector.scalar_tensor_tensor(
                                out=t1[64:96, :], in0=PS[qb_][64:96, :], scalar=g_q[64:96, :], in1=cs["rope"][64:96, 0, sl],
                                op0=ALU.mult, op1=ALU.mult), reads=[PB[qb_]], writes=[t_buf])
                            self.op(dve, lambda: nc.vector.scalar_tensor_tensor(
                                out=t2[64:96, :], in0=PS[6][64:96, :], scalar=g_qs[64:96, :], in1=cs["rope"][64:96, 1, sl],
                                op0=ALU.mult, op1=ALU.mult), reads=[PB[6]], writes=[])
                            self.op(dve, lambda: nc.vector.tensor_tensor(out=t1[64:96, :], in0=t1[64:96, :], in1=t2[64:96, :], op=ALU.add),
                                    reads=[], writes=[])
                            self.op(dve, lambda: nc.vector.tensor_tensor(out=Qn[64:96, h, :], in0=t1[64:96, :], in1=rs96[64:96, db_, :], op=ALU.mult),
                                    reads=[rs_bufs[db_]], writes=[])
                        qn_bufs[h].w = self._tok(dve)
                    steps.append(s7)
                    return steps

                iters = []
                for pi, (li, h) in enumerate(pairs):
                    if hf == 1:
                        groups = [(0, TN, list(range(nkt)))]
                    else:
                        groups = [(s2 * 256, 256, [li * 4 + s2 * 2, li * 4 + s2 * 2 + 1]) for s2 in range(2)]
                    for gi, (q0, nq, kts) in enumerate(groups):
                        for ki, kt in enumerate(kts):
                            iters.append((pi, q0, nq, kt, ki == 0, ki == len(kts) - 1, gi == 0 and ki == 0))
                kb_done = {}
                qpend = {}

                def run_q(pi, n):
                    st = qpend.get(pi)
                    while st and n > 0:
                        st.pop(0)()
                        n -= 1

                def emit_k(pi):
                    nonlocal ei
                    li, h = pairs[pi]
                    kbi = pi % 2
                    if hf == 1:
                        kcols = [(0, 256, ckvc, ckvc_buf, 0), (256, 512, ckvn, ckvn_buf, 0), (768, 512, ckvn, ckvn_buf, 512)]
                    else:
                        kcols = [(li * TN, TN, ckvn, ckvn_buf, li * TN)]
                    for ci_, (c0, w, src, srcb, s0) in enumerate(kcols):
                        fns = [lambda rc=rc, src=src, s0=s0, w=w, h=h: nc.tensor.matmul(
                            PS[6][0:64, 0:w], lhsT=self.ring[:, sk_, rc * 512 + h * 64: rc * 512 + (h + 1) * 64],
                            rhs=src[:, rc, s0:s0 + w], start=(rc == 0), stop=(rc == 1)) for rc in range(2)]
                        self.op(pe, fns, reads=[skb, srcb], writes=[PB[6]])
                        if ei % 2 == 0:
                            self.op(act, lambda c0=c0, w=w, kbi=kbi: nc.scalar.activation(out=Kb[0:64, kbi, c0:c0 + w], in_=PS[6][0:64, 0:w], func=AF.Copy),
                                    reads=[PB[6]], writes=[kb_bufs[kbi]] if ci_ == 0 else [])
                        else:
                            self.op(dve, lambda c0=c0, w=w, kbi=kbi: nc.vector.tensor_copy(out=Kb[0:64, kbi, c0:c0 + w], in_=PS[6][0:64, 0:w]),
                                    reads=[PB[6]], writes=[kb_bufs[kbi]] if ci_ == 0 else [])
                        ei += 1
                    kb_done[pi] = [self._tok(act), self._tok(dve)]

                def emit_s(it):
                    nonlocal si
                    pi, q0, nq, kt, first, lastk, newhead = it
                    li, h = pairs[pi]
                    if newhead:
                        run_q(pi, 99)
                        emit_k(pi)
                        if pi + 1 < len(pairs):
                            qpend[pi + 1] = q_steps(pi + 1)
                    sbk = si % 2
                    si += 1
                    for tk_ in kb_done[pi]:
                        pe.wait(tk_)
                    self.op(pe, lambda: nc.tensor.matmul(
                        PS[sbk][:, 0:nq], lhsT=Kb[0:96, pi % 2, kt * 128:(kt + 1) * 128], rhs=Qn[0:96, h, q0:q0 + nq], start=True, stop=True),
                        reads=[kb_bufs[pi % 2], qn_bufs[h]], writes=[PB[sbk]])
                    return sbk

                qpend[0] = q_steps(0)
                nstep = 1 if hf == 1 else 2
                sb_next = emit_s(iters[0])
                for idx, it in enumerate(iters):
                    pi, q0, nq, kt, first, lastk, newhead = it
                    li, h = pairs[pi]
                    sbk = sb_next
                    self.op(act, lambda: nc.scalar.activation(
                        out=PT[:, sbk, 0:nq], in_=PS[sbk][:, 0:nq], func=AF.Exp, scale=rk[:, kt, h:h + 1]),
                        reads=[PB[sbk], rk_buf], writes=[pt_bufs[sbk]])
                    if idx + 1 < len(iters):
                        sb_next = emit_s(iters[idx + 1])
                    ob, dbk = 2, 3
                    self.op(pe, [lambda: nc.tensor.matmul(
                        PS[ob][0:64, 0:nq], lhsT=Vt[:, kt, h * 64:(h + 1) * 64], rhs=PT[:, sbk, 0:nq], start=first, stop=lastk),
                        lambda: nc.tensor.matmul(
                        PS[dbk][0:64, 0:nq], lhsT=self.ones_bf[:, 0:64], rhs=PT[:, sbk, 0:nq], start=first, stop=lastk)],
                        reads=[vt_buf, pt_bufs[sbk], self.const_buf], writes=[PB[ob], PB[dbk]] if first else [])
                    run_q(pi + 1, nstep)
                    if lastk:
                        tk = self._tok(pe)
                        PB[ob].w = tk
                        PB[dbk].w = tk
                        self.op(act, lambda: nc.scalar.activation(out=rden[:, 0, 0:nq], in_=PS[dbk][0:64, 0:nq], func=AF.Ln),
                                reads=[PB[dbk]], writes=[rden_bufs[0]])
                        self.op(act, lambda: nc.scalar.activation(out=rden[:, 0, 0:nq], in_=rden[:, 0, 0:nq], func=AF.Exp, scale=-1.0),
                                reads=[], writes=[rden_bufs[0]])
                        self.op(dve, lambda: nc.vector.tensor_tensor(
                            out=mlaT[0:64, h, li * TN + q0: li * TN + q0 + nq], in0=PS[ob][0:64, 0:nq], in1=rden[:, 0, 0:nq], op=ALU.mult),
                            reads=[PB[ob], rden_bufs[0]], writes=[mla_buf])
                self.barrier()
            wo = self.even_w_out[j]
            sa_, sab = self.ring_load([(0, [4, D], wout[:, 0:4, :])])
            sb1, sbb1 = self.ring_load([(0, [4, D], wo[512:768, :].rearrange("(h p) d -> p h d", p=64), 64)])
            sb2, sbb2 = self.ring_load([(0, [4, D], wo[768:1024, :].rearrange("(h p) d -> p h d", p=64), 64)])
            di = 0
            for li in range(2):
                t = 2 * hf + li
                jc = hf
                sl = slice(li * TN, (li + 1) * TN)
                for dc in range(KC):
                    bank = 4 + di % 3
                    di += 1
                    fns = [lambda c=c, dc=dc, bank=bank: nc.tensor.matmul(
                        PS[bank][:], lhsT=self.ring[:, sa_, c * 1024 + dc * 128: c * 1024 + (dc + 1) * 128], rhs=ogT[:, c, sl],
                        start=(c == 0), stop=False) for c in range(4)]
                    for hh_ in range(8):
                        sx = sb1 if hh_ < 4 else sb2
                        fns.append(lambda hh_=hh_, sx=sx, dc=dc, bank=bank: nc.tensor.matmul(
                            PS[bank][:], lhsT=self.ring[0:64, sx, (hh_ % 4) * 1024 + dc * 128: (hh_ % 4) * 1024 + (dc + 1) * 128],
                            rhs=mlaT[0:64, hh_, sl], start=False, stop=(hh_ == 7)))
                    self.op(pe, fns, reads=[sab, sbb1, sbb2, og_buf, mla_buf], writes=[PB[bank]])
                    xs = self.xT[:, dc, t * TN:(t + 1) * TN]
                    self.op(dve, lambda xs=xs, dc=dc, jc=jc, bank=bank: nc.vector.scalar_tensor_tensor(
                        out=xs, in0=PS[bank][:], scalar=self.modG[:, k, dc, jc:jc + 1], in1=xs,
                        op0=ALU.mult, op1=ALU.add), reads=[PB[bank], self.mod_buf], writes=[self.x_bufs[t]])
            self.barrier()
            for e_ in (pe, act, dve):
                e_.wait((self.st3[0], self.st3[1]))

    def rms_rstd_w(self, sq, sq_buf, rows, nfeat, out, out_buf, w):
        nc = self.nc
        self.op(self.pe, lambda: nc.tensor.matmul(self.psum[7][0:rows, 0:w], lhsT=self.ones_bf[0:rows, 0:rows], rhs=sq[0:rows, 0, 0:w],
                                                  start=True, stop=True), reads=[sq_buf, self.const_buf], writes=[self.pbuf[7]])
        self.op(self.act, lambda: nc.scalar.activation(out=out[0:rows, 0:w], in_=self.psum[7][0:rows, 0:w], func=AF.Ln, scale=1.0 / nfeat, bias=EPS),
                reads=[self.pbuf[7]], writes=[out_buf])
        self.op(self.act, lambda: nc.scalar.activation(out=out[0:rows, 0:w], in_=out[0:rows, 0:w], func=AF.Exp, scale=-0.5),
                reads=[], writes=[out_buf])

    def rms_rstd(self, sq, sq_buf, nch, rows, nfeat, out, out_buf):
        nc = self.nc
        fns = [lambda c=c: nc.tensor.matmul(self.psum[7][0:rows, :], lhsT=self.ones_bf[0:rows, 0:rows], rhs=sq[0:rows, c, :],
                                            start=(c == 0), stop=(c == nch - 1)) for c in range(nch)]
        self.op(self.pe, fns, reads=[sq_buf, self.const_buf], writes=[self.pbuf[7]])
        self.op(self.act, lambda: nc.scalar.activation(out=out[0:rows, :], in_=self.psum[7][0:rows, :], func=AF.Ln, scale=1.0 / nfeat, bias=EPS),
                reads=[self.pbuf[7]], writes=[out_buf])
        self.op(self.act, lambda: nc.scalar.activation(out=out[0:rows, :], in_=out[0:rows, :], func=AF.Exp, scale=-0.5),
                reads=[], writes=[out_buf])

    def odd_mixer(self, l):
        nc = self.nc
        j = l // 2
        k = 1
        win = self.odd_w_in[j].rearrange("(kc p) f -> p kc f", p=128)
        wout = self.odd_w_out[j].rearrange("(c p) d -> p c d", p=128)
        with contextlib.ExitStack() as st:
            wv = self.sb("odd_wv", [128, KC, 2048], BF16, st)
            wv_buf = self.buf("wv")
            hT = self.sb("odd_hT", [128, KC, TN], BF16, st)
            h_bufs = [self.buf("h")]
            mix = self.sb("odd_mix", [128, 16, TN], BF16, st)
            mix_buf = self.buf("mix")
            gv = self.sb("odd_gv", [128, 2048], BF16, st)
            gv_buf = self.buf("gv")
            sqf = self.sb("odd_sqf", [128, 2, 512], F32, st)
            sq_bufs = [self.buf("sqf"), self.buf("sqf")]
            ss = self.sb("odd_ss", [128, 16], F32, st)
            ss_bufs = [self.buf("ss"), self.buf("ss")]
            wp = self.sb("odd_wp", [128, 4, 128], BF16, st)
            wp_buf = self.buf("wp")
            ut = self.sb("odd_u", [128, 2, TN], F32, st)
            u_bufs = [self.buf("u"), self.buf("u")]
            tmp, rstd = self.norm_tmp(st)
            for kc in range(KC):
                self.dma(self.pool, self.wv_sem, wv[:, kc, :], win[:, kc, 2048:4096], writes=[wv_buf] if kc == 0 else [])
            wv_buf.w = (self.wv_sem[0], self.wv_sem[1])
            bi = 0
            mi = 0
            for t in range(NT):
                jc = 0 if t < NT // 2 else 1
                self.norm_modulate(k, [t], hT, h_bufs, tmp, rstd)
                gvs = [gv, tmp["sq"][:].rearrange("p a b -> p (a b)")]
                gvb = [gv_buf, tmp["buf"]]

                def vfront(q4):
                    nonlocal bi
                    gq, gqb = gvs[q4 % 2], gvb[q4 % 2]
                    so = (q4 % 2) * 8
                    for g in range(4):
                        bank = bi % 4
                        bi += 1
                        fns = [lambda kc=kc, g=g, bank=bank: nc.tensor.matmul(
                            self.psum[bank][:], lhsT=hT[:, kc, q4 * 128:(q4 + 1) * 128],
                            rhs=wv[:, kc, g * 512:(g + 1) * 512], start=(kc == 0), stop=(kc == KC - 1)) for kc in range(KC)]
                        self.op(self.pe, fns, reads=[h_bufs[0], wv_buf], writes=[self.pbuf[bank]])
                        self.op(self.act, [
                            lambda g=g, bank=bank: nc.scalar.activation(out=gq[:, g * 512:(g + 1) * 512], in_=self.psum[bank][:],
                                                                        func=AF.Gelu_apprx_tanh),
                            lambda g=g, bank=bank: nc.scalar.activation(out=sqf[:, g % 2, :], in_=gq[:, g * 512:(g + 1) * 512],
                                                                        func=AF.Square)],
                            reads=[self.pbuf[bank]], writes=([gqb] if g == 0 else []) + [sq_bufs[g % 2]])
                        self.op(self.dve, lambda g=g: nc.vector.tensor_reduce(
                            out=ss[:, so + g:so + g + 1], in_=sqf[:, g % 2, :], axis=mybir.AxisListType.X, op=ALU.add),
                            reads=[sq_bufs[g % 2]], writes=[ss_bufs[q4 % 2]])
                    gqb.w = (self.act.sems[self.act.si], self.act.cnt)

                def vback(q4):
                    nonlocal mi
                    gq, gqb = gvs[q4 % 2], gvb[q4 % 2]
                    so = (q4 % 2) * 8
                    sb_ = ss_bufs[q4 % 2]
                    self.op(self.dve, lambda: nc.vector.tensor_reduce(out=ss[:, so + 4:so + 5], in_=ss[:, so:so + 4], axis=mybir.AxisListType.X, op=ALU.add),
                            reads=[], writes=[sb_], same=True)
                    self.op(self.act, lambda: nc.scalar.activation(out=ss[:, so + 5:so + 6], in_=ss[:, so + 4:so + 5], func=AF.Ln, scale=1.0 / 2048, bias=EPS),
                            reads=[], writes=[sb_], same=True)
                    self.op(self.act, lambda: nc.scalar.activation(out=ss[:, so + 6:so + 7], in_=ss[:, so + 5:so + 6], func=AF.Exp, scale=-0.5),
                            reads=[], writes=[sb_], same=True)
                    self.op(self.dve, lambda: nc.vector.tensor_scalar(
                        out=wp[:].rearrange("p a b -> p (a b)"), in0=self.wsT[:, j].rearrange("p a b -> p (a b)"),
                        scalar1=ss[:, so + 6:so + 7], scalar2=None, op0=ALU.mult), reads=[sb_], writes=[wp_buf])
                    for g in range(4):
                        bank = 4 + mi % 2
                        mi += 1
                        fns = [lambda g=g, cc=cc, bank=bank: nc.tensor.matmul(
                            self.psum[bank][:, cc * 128:(cc + 1) * 128], lhsT=gq[:, (g * 4 + cc) * 128:(g * 4 + cc + 1) * 128],
                            rhs=wp[:, g, :], start=True, stop=True) for cc in range(4)]
                        self.op(self.pe, fns, reads=[gqb, wp_buf], writes=[self.pbuf[bank]])
                        for cc in range(4):
                            c16 = g * 4 + cc
                            self.op(self.dve, lambda g=g, cc=cc, c16=c16, bank=bank: nc.vector.scalar_tensor_tensor(
                                out=mix[:, c16, q4 * 128:(q4 + 1) * 128], in0=self.psum[bank][:, cc * 128:(cc + 1) * 128],
                                scalar=self.oddvg[:, j, c16:c16 + 1], in1=self.bsb[:, j, g, :], op0=ALU.mult, op1=ALU.add),
                                reads=[self.pbuf[bank]], writes=[mix_buf] if (q4 == 0 and c16 == 0) else [])

                vfront(0)
                for q4 in range(4):
                    if q4 + 1 < 4:
                        vfront(q4 + 1)
                    vback(q4)
                mix_buf.w = (self.dve.sems[self.dve.si], self.dve.cnt)
                for sl in range(4):
                    s, sbuf_ = self.ring_load([(0, [KC, 512], win[:, :, sl * 512:(sl + 1) * 512])])
                    for cc in range(4):
                        c16 = sl * 4 + cc
                        bank = bi % 4
                        bi += 1
                        fns = [lambda kc=kc, cc=cc, s=s, bank=bank: nc.tensor.matmul(
                            self.psum[bank][:], lhsT=self.ring[:, s, kc * 512 + cc * 128: kc * 512 + (cc + 1) * 128],
                            rhs=hT[:, kc, :], start=(kc == 0), stop=(kc == KC - 1)) for kc in range(KC)]
                        self.op(self.pe, fns, reads=[sbuf_, h_bufs[0]], writes=[self.pbuf[bank]])
                        ub = c16 % 2
                        self.op(self.act, lambda ub=ub, bank=bank: nc.scalar.activation(
                            out=ut[:, ub, :], in_=self.psum[bank][:], func=AF.Gelu_apprx_tanh),
                            reads=[self.pbuf[bank]], writes=[u_bufs[ub]])
                        self.op(self.dve, lambda ub=ub, c16=c16: nc.vector.tensor_tensor(
                            out=mix[:, c16, :], in0=ut[:, ub, :], in1=mix[:, c16, :], op=ALU.mult),
                            reads=[u_bufs[ub]], writes=[mix_buf] if c16 == 0 else [])
                mix_buf.w = (self.dve.sems[self.dve.si], self.dve.cnt)
                mix_buf.r = {}
                for dc in range(KC):
                    if dc % 2 == 0:
                        so, sob = self.ring_load([(0, [16, 256], wout[:, :, dc * 128:(dc + 2) * 128])])
                    bank = 4 + mi % 3
                    mi += 1
                    fns = [lambda c16=c16, dc=dc, bank=bank, so=so: nc.tensor.matmul(
                        self.psum[bank][:], lhsT=self.ring[:, so, c16 * 256 + (dc % 2) * 128: c16 * 256 + (dc % 2 + 1) * 128],
                        rhs=mix[:, c16, :], start=(c16 == 0), stop=(c16 == 15)) for c16 in range(16)]
                    self.op(self.pe, fns, reads=[sob, mix_buf], writes=[self.pbuf[bank]])
                    xs = self.xT[:, dc, t * TN:(t + 1) * TN]
                    self.op(self.dve, lambda xs=xs, dc=dc, jc=jc, bank=bank: nc.vector.scalar_tensor_tensor(
                        out=xs, in0=self.psum[bank][:], scalar=self.modG[:, k, dc, jc:jc + 1], in1=xs,
                        op0=ALU.mult, op1=ALU.add), reads=[self.pbuf[bank], self.mod_buf], writes=[self.x_bufs[t]])

    def mod_blocks(self, l, par, cbs):
        nc = self.nc
        aw = self.ada_w[l].rearrange("(kc p) f -> p kc f", p=128)
        pb = self.pbuf[7]
        psv = self.psum[7][:, 0:144].rearrange("p (c j) -> p c j", j=2)
        for cb in cbs:
            s, sbuf_ = self.ring_load([(0, [KC, 512], aw[:, :, cb * 512:(cb + 1) * 512])])
            fns = []
            for cc in range(4):
                c = cb * 4 + cc
                for kc in range(KC):
                    fns.append(lambda c=c, cc=cc, kc=kc, s=s: nc.tensor.matmul(
                        psv[:, c, :], lhsT=self.ring[:, s, kc * 512 + cc * 128: kc * 512 + (cc + 1) * 128],
                        rhs=self.scond[:, kc, :], start=(kc == 0), stop=(kc == KC - 1)))
            first = (cb == 0 or cb == 7)
            self.op(self.pe, fns, reads=[sbuf_, self.scond_buf], writes=[pb] if first else [])
            pb.w = (self.pe.sems[self.pe.si], self.pe.cnt)

    def mod_finish(self, l, par, c0=0, c1=72, derive=True):
        nc = self.nc
        pb = self.pbuf[7]
        psv = self.psum[7][:, 0:144].rearrange("p (c j) -> p c j", j=2)
        mod, modA, modB, modG, mod_buf = self.mod2[par], self.modA2[par], self.modB2[par], self.modG2[par], self.mod_bufs[par]
        for j in range(2):
            self.op(self.dve, lambda j=j: nc.vector.tensor_tensor(
                out=mod[:, c0:c1, j], in0=psv[:, c0:c1, j], in1=self.adab[:, l, c0:c1], op=ALU.add),
                reads=[pb], writes=[mod_buf])
        if not derive:
            return
        for k in range(3):
            sh = mod[:, (3 * k) * 8:(3 * k) * 8 + 8, :]
            sc = mod[:, (3 * k + 1) * 8:(3 * k + 1) * 8 + 8, :]
            gt = mod[:, (3 * k + 2) * 8:(3 * k + 2) * 8 + 8, :]
            for j in range(2):
                self.op(self.dve, lambda k=k, j=j, sc=sc: nc.vector.scalar_tensor_tensor(
                    out=modA[:, k, :, j], in0=sc[:, :, j], scalar=1.0, in1=self.normg[:, l, k, :],
                    op0=ALU.add, op1=ALU.mult), reads=[mod_buf], same=True)
            self.op(self.dve, lambda k=k, sh=sh: nc.vector.tensor_copy(out=modB[:, k, :, :], in_=sh), writes=[])
            self.op(self.dve, lambda k=k, gt=gt: nc.vector.tensor_scalar(
                out=modG[:, k, :, :], in0=gt, scalar1=(1.0 if k == 1 else 0.5), scalar2=None, op0=ALU.mult),
                writes=[])
        mod_buf.w = (self.dve.sems[self.dve.si], self.dve.cnt)
        mod_buf.r = {}
        pb.r["dve"] = mod_buf.w

    def norm_modulate(self, k, tiles, hT, h_bufs, tmp, rstd):
        for li, t in enumerate(tiles):
            self.norm_modulate_tile(k, t, li, hT, h_bufs, tmp, rstd)

    def norm_modulate_tile(self, k, t, li, hT, h_bufs, tmp, rstd):
        nc = self.nc
        if True:
            j = 0 if t < NT // 2 else 1
            xs = self.xT[:, :, t * TN:(t + 1) * TN]
            xb = self.x_bufs[t]
            tb = tmp["buf"]
            self.op(self.act, lambda xs=xs: nc.scalar.activation(out=tmp["sq"][:], in_=xs, func=AF.Square),
                    reads=[xb], writes=[tb])
            fns = [lambda kc=kc: nc.tensor.matmul(self.psum[7][:], lhsT=self.ones_bf[:], rhs=tmp["sq"][:, kc, :],
                                                  start=(kc == 0), stop=(kc == KC - 1)) for kc in range(KC)]
            self.op(self.pe, fns, reads=[tb, self.const_buf], writes=[self.pbuf[7]])
            rb = rstd["buf"]
            self.op(self.act, [lambda: nc.scalar.activation(out=rstd["t"][:], in_=self.psum[7][:], func=AF.Ln,
                                                            scale=1.0 / D, bias=EPS),
                               lambda: nc.scalar.activation(out=rstd["t"][:], in_=rstd["t"][:], func=AF.Exp, scale=-0.5)],
                    reads=[self.pbuf[7]], writes=[rb])
            for kc in range(KC):
                t2 = tmp["t2buf"][kc % 2]
                self.op(self.dve, lambda kc=kc, t=t, j=j: nc.vector.scalar_tensor_tensor(
                    out=tmp["t2"][:, kc % 2, :], in0=self.xT[:, kc, t * TN:(t + 1) * TN],
                    scalar=self.modA[:, k, kc, j:j + 1], in1=rstd["t"][:], op0=ALU.mult, op1=ALU.mult),
                    reads=[xb, rb, self.mod_buf], writes=[t2])
                self.op(self.act, lambda kc=kc, li=li, j=j: nc.scalar.activation(
                    out=hT[:, kc, li * TN:(li + 1) * TN], in_=tmp["t2"][:, kc % 2, :], func=AF.Identity,
                    bias=self.modB[:, k, kc, j:j + 1], scale=1.0),
                    reads=[t2, self.mod_buf], writes=[h_bufs[li]] if kc == 0 else [])
            h_bufs[li].w = (self.act.sems[self.act.si], self.act.cnt)

    def norm_tmp(self, stack):
        tmp = {"sq": self.sb("n_sq", [128, KC, TN], BF16, stack), "buf": self.buf("nsq"),
               "t2": self.sb("n_t2", [128, 2, TN], F32, stack), "t2buf": [self.buf("nt2"), self.buf("nt2")]}
        rstd = {"t": self.sb("n_rstd", [128, TN], F32, stack), "buf": self.buf("rstd")}
        return tmp, rstd

    def ffn(self, l, k, wg, wu, wd, hook=None):
        nc = self.nc
        with contextlib.ExitStack() as st:
            hT = self.sb("ffn_hT", [128, KC, NTOK], BF16, st)
            h_bufs = [self.buf("h") for _ in range(NT)]
            aT = self.sb("ffn_aT", [128, 2, GCH, TN], BF16, st)
            a_bufs = [self.buf("a"), self.buf("a")]
            sg = self.sb("ffn_sg", [128, 2, TN], F32, st)
            sg_bufs = [self.buf("sg"), self.buf("sg")]
            tmp, rstd = self.norm_tmp(st)
            self.norm_modulate(k, list(range(NT)), hT, h_bufs, tmp, rstd)
            wgl = wg[l].rearrange("(kc p) f -> p kc f", p=128)
            wul = wu[l].rearrange("(kc p) f -> p kc f", p=128)
            wdl = wd[l].rearrange("(c p) d -> p c d", p=128)
            pend = None
            u = 0
            ci = 0
            dn = 0

            def down_step(pd, dc):
                nonlocal dn
                s, sbuf_, t, ab, au = pd
                j = 0 if t < NT // 2 else 1
                if True:
                    bank = 4 + dn % 3
                    dn += 1
                    fns = [lambda c=c, dc=dc, s=s, au=au, bank=bank: nc.tensor.matmul(
                        self.psum[bank][:], lhsT=self.ring[:, s, 4096 + c * 1024 + dc * 128: 4096 + c * 1024 + (dc + 1) * 128],
                        rhs=aT[:, au, c, :], start=(c == 0), stop=(c == GCH - 1)) for c in range(GCH)]
                    self.op(self.pe, fns, reads=[sbuf_, ab], writes=[self.pbuf[bank]])
                    xs = self.xT[:, dc, t * TN:(t + 1) * TN]
                    self.op(self.dve, lambda xs=xs, dc=dc, j=j, bank=bank: nc.vector.scalar_tensor_tensor(
                        out=xs, in0=self.psum[bank][:], scalar=self.modG[:, k, dc, j:j + 1], in1=xs,
                        op0=ALU.mult, op1=ALU.add), reads=[self.pbuf[bank], self.mod_buf], writes=[self.x_bufs[t]])

            psteps = []

            def run_down(n):
                for _ in range(n):
                    if psteps:
                        pd_, dc_ = psteps.pop(0)
                        down_step(pd_, dc_)

            for g in range(NG):
                s, sbuf_ = self.ring_load([
                    (0, [KC, GCH * 128], wgl[:, :, g * GCH * 128:(g + 1) * GCH * 128]),
                    (2048, [KC, GCH * 128], wul[:, :, g * GCH * 128:(g + 1) * GCH * 128]),
                    (4096, [GCH, D], wdl[:, g * GCH:(g + 1) * GCH, :]),
                ])
                for t in range(NT):
                    au = u % 2
                    ab = a_bufs[au]
                    for c in range(GCH):
                        gb = ci % 2
                        ub = 2 + ci % 2
                        ci += 1
                        fns = [lambda kc=kc, c=c, s=s, t=t, gb=gb: nc.tensor.matmul(
                            self.psum[gb][:], lhsT=self.ring[:, s, kc * 256 + c * 128: kc * 256 + (c + 1) * 128],
                            rhs=hT[:, kc, t * TN:(t + 1) * TN], start=(kc == 0), stop=(kc == KC - 1)) for kc in range(KC)]
                        self.op(self.pe, fns, reads=[sbuf_, h_bufs[t]], writes=[self.pbuf[gb]])
                        run_down(KC // (2 * GCH))
                        fns = [lambda kc=kc, c=c, s=s, t=t, ub=ub: nc.tensor.matmul(
                            self.psum[ub][:], lhsT=self.ring[:, s, 2048 + kc * 256 + c * 128: 2048 + kc * 256 + (c + 1) * 128],
                            rhs=hT[:, kc, t * TN:(t + 1) * TN], start=(kc == 0), stop=(kc == KC - 1)) for kc in range(KC)]
                        self.op(self.pe, fns, reads=[sbuf_, h_bufs[t]], writes=[self.pbuf[ub]])
                        run_down(KC // (2 * GCH))
                        self.op(self.act, lambda gb=gb: nc.scalar.activation(out=sg[:, gb, :], in_=self.psum[gb][:], func=AF.Silu),
                                reads=[self.pbuf[gb]], writes=[sg_bufs[gb]])
                        self.op(self.dve, lambda gb=gb, ub=ub, au=au, c=c: nc.vector.tensor_tensor(
                            out=aT[:, au, c, :], in0=self.psum[ub][:], in1=sg[:, gb, :], op=ALU.mult),
                            reads=[self.pbuf[ub], sg_bufs[gb]], writes=[ab] if c == 0 else [])
                    ab.w = (self.dve.sems[self.dve.si], self.dve.cnt)
                    run_down(KC)
                    pend = (s, sbuf_, t, ab, au)
                    psteps.extend((pend, dc) for dc in range(KC))
                    u += 1
                if hook is not None:
                    hook(g)
            run_down(KC)
            self.barrier()


def _host_layout(inputs, core):
    i = core
    xp = np.asarray(inputs["x_prompt"])[4 * i:4 * i + 4].reshape(1024, D)
    xs = np.asarray(inputs["x_sample"])[i]
    xT = np.ascontiguousarray(np.concatenate([xp, xs], axis=0).T)
    cond = np.stack([np.asarray(inputs["c_ctx"]), np.asarray(inputs["c"])[i]], axis=-1)
    condT = np.ascontiguousarray(cond.reshape(KC, 128, 2).transpose(1, 0, 2))
    return {"xT": xT, "condT": condT,
            "ckvcT": np.ascontiguousarray(np.asarray(inputs["cache_ckv"])[i].transpose(0, 2, 1)),
            "kropecT": np.ascontiguousarray(np.asarray(inputs["cache_krope"])[i].transpose(0, 2, 1)),
            "sgla": np.ascontiguousarray(np.asarray(inputs["state_gla"])[i])}


def _const_tables():
    idx = np.arange(128)
    same = (idx[:, None] // 64) == (idx[None, :] // 64)
    le = idx[:, None] <= idx[None, :]
    ge = idx[:, None] >= idx[None, :]
    gt = idx[:, None] > idx[None, :]
    lt = idx[:, None] < idx[None, :]
    c = np.float32(-1.0 / 16.0)
    gm = np.zeros((128, 6, 128), np.float32)
    gm[:, 0] = np.where(same & le, c, 0)
    gm[:, 1] = np.where(same & ge, c, 0)
    gm[:, 2] = np.where(same & gt, c, 0)
    gm[:, 3] = np.where(same & lt, c, 0)
    gm[:, 4] = np.where(same & le, 1, 0)
    gm[:, 5] = np.where(same & ge, 1, 0)
    m4 = np.zeros((128, 2, 4, 128), np.float32)
    m4[:, 0] = gm[:, 4][:, None, :]
    m4[:, 1] = gm[:, 5][:, None, :]
    pos = np.arange(1024)
    row = (pos // 64).astype(np.float32)
    col = (pos % 64).astype(np.float32)
    inv = (np.float32(10000.0) ** (-np.arange(8, dtype=np.float32) / np.float32(8))).astype(np.float32)
    ang = np.concatenate([row[:, None] * inv, col[:, None] * inv], axis=-1).astype(np.float32)
    cosv, sinv = np.cos(ang).astype(np.float32), np.sin(ang).astype(np.float32)
    rope = np.zeros((96, 2, 1024), np.float32)
    for f in range(32):
        rope[64 + f, 0] = cosv[:, f // 2]
        rope[64 + f, 1] = sinv[:, f // 2] * (-1.0 if f % 2 == 0 else 1.0)
    return gm, m4, rope


def _shared_layout(inputs):
    sh = {}
    A = lambda n: np.asarray(inputs[n], dtype=np.float32)
    ada_b = A("ada_b")
    sh["adab"] = np.ascontiguousarray(ada_b.reshape(DEPTH, 72, 128).transpose(2, 0, 1))
    ng = A("norm_g")
    sh["normg"] = np.ascontiguousarray(ng.reshape(DEPTH, 3, KC, 128).transpose(3, 0, 1, 2))
    vg = A("odd_v_g")
    sh["oddvg"] = np.ascontiguousarray(vg.reshape(2, 16, 128).transpose(2, 0, 1))
    sh["wsT"] = np.ascontiguousarray(A("odd_ws").transpose(3, 0, 1, 2))
    sh["bsb"] = np.ascontiguousarray(np.broadcast_to(A("odd_bs")[None], (128, 2, 4, 128)))
    for n in ("odd_w_in", "odd_w_out", "ada_w", "ffn1_wg", "ffn1_wu", "ffn1_wd", "ffn2_wg", "ffn2_wu", "ffn2_wd",
              "even_w_in", "even_w_out"):
        sh[n] = np.ascontiguousarray(A(n))
    swap = np.arange(32) ^ 1
    win = A("even_w_in")
    wsw = np.zeros((2, D, 96), np.float32)
    wsw[:, :, 64:96] = win[:, :, 2208:2240][:, :, swap]
    sh["win_sw"] = wsw
    qb = A("mla_qb_w")
    sh["qbw"] = np.ascontiguousarray(qb.reshape(2, 384, 768))
    qs = qb.copy()
    qs[..., 64:96] = qb[..., 64:96][..., swap]
    sh["qbw_sw"] = np.ascontiguousarray(qs.reshape(2, 384, 768))
    kvb = A("mla_kvb_w")
    sh["kvbK"] = np.ascontiguousarray(kvb[..., :64].reshape(2, 256, 512))
    sh["kvbV"] = np.ascontiguousarray(kvb[..., 64:].reshape(2, 256, 512))
    gm, m4, rope = _const_tables()
    sh["gmask"], sh["mask4"], sh["rope"] = np.ascontiguousarray(gm[:, 0:4]), m4, rope
    qn, kn = A("mla_qn_g"), A("mla_kn_g")
    sw96 = np.arange(96)
    sw96[64:96] = 64 + swap
    sh["qkng"] = np.ascontiguousarray(np.stack([qn, qn[:, sw96], kn, kn[:, sw96]], axis=-1).transpose(1, 0, 2))
    sh["glag"] = np.ascontiguousarray(A("gla_norm_g").T)
    sh["qag"] = np.ascontiguousarray(A("mla_qa_g").reshape(2, 3, 128).transpose(2, 0, 1))
    sh["kvag"] = np.ascontiguousarray(A("mla_kva_g").reshape(2, 2, 128).transpose(2, 0, 1))
    w2 = A("gla_gate_w2")
    gb = A("gla_gate_b")
    w2aug = np.zeros((33, 2, 2, 256), np.float32)
    w2aug[0:16, :, 0, :] = w2[:, 0].transpose(1, 0, 2)
    w2aug[16:32, :, 1, :] = w2[:, 1].transpose(1, 0, 2)
    w2aug[32] = gb
    sh["w2aug"] = w2aug
    return sh


def run(inputs, cfg, cores=8, trace=False):
    b = Builder(cfg)
    nc = b.build()
    sh = _shared_layout(inputs)
    in_maps = []
    for i in range(cores):
        m = dict(sh)
        m.update(_host_layout(inputs, i))
        in_maps.append(m)
    res = run_bass_kernel_spmd(nc, in_maps, core_ids=list(range(cores)), trace=trace)
    return res


def kernel(**inputs):
    res = run(inputs, {})
    B, S, DB, DS = 32, 256, 8, 1024
    y_prompt = np.zeros((B, S, D), np.float32)
    y_sample = np.zeros((DB, DS, D), np.float32)
    new_ckv = np.zeros((B, 2, S, 256), np.float32)
    new_krope = np.zeros((B, 2, S, 32), np.float32)
    new_gla = np.zeros((B, 2, 2, 4, 64, 128), np.float32)
    for i, r in enumerate(res.results):
        yT = np.asarray(r["yT"])
        y_prompt[4 * i:4 * i + 4] = yT[:, :1024].T.reshape(4, S, D)
        y_sample[i] = yT[:, 1024:].T
        ck = np.asarray(r["ckvT_o"])
        new_ckv[4 * i:4 * i + 4] = ck.reshape(2, 256, 4, S).transpose(2, 0, 3, 1)
        kr = np.asarray(r["kropeT_o"])
        new_krope[4 * i:4 * i + 4] = kr.reshape(2, 32, 4, S).transpose(2, 0, 3, 1)
        new_gla[4 * i:4 * i + 4] = np.asarray(r["gla_o"])
    return (y_prompt, y_sample, new_ckv, new_krope, new_gla)


def check_states(r, states, inp):
    for (l, st) in states:
        j = l // 2
        ck = np.asarray(r["ckvT_o"])[j].reshape(256, 4, 256).transpose(1, 2, 0)
        kr = np.asarray(r["kropeT_o"])[j].reshape(32, 4, 256).transpose(1, 2, 0)
        gl = np.asarray(r["gla_o"])[:, j]
        for nm, a, b in (("ckv", ck, np.asarray(st[0])), ("krope", kr, np.asarray(st[1])), ("gla", gl, np.asarray(st[2]))):
            print("state", nm, "layer", l, "relvar", ((a - b) ** 2).mean() / (b ** 2).mean())
```

```python
import contextlib
import numpy as np
import concourse.bass as bass
import concourse.mybir as mybir
from concourse.bass_utils import run_bass_kernel_spmd

F32 = mybir.dt.float32
BF16 = mybir.dt.bfloat16
AF = mybir.ActivationFunctionType
ALU = mybir.AluOpType

D = 1024
KC = 8
NTOK = 2048
TN = 512
NT = NTOK // TN
DEPTH = 4
FH = 2816
FC = FH // 128
GCH = 2
NG = FC // GCH
EPS = 1e-6
SLOT = 6144
NSLOT = 3
SEM_LIMIT = 30000


class Buf:
    __slots__ = ("name", "w", "r")

    def __init__(self, name):
        self.name = name
        self.w = None
        self.r = {}


class Eng:
    def __init__(self, nc, h, name, nsem):
        self.nc = nc
        self.h = h
        self.name = name
        self.sems = [nc.alloc_semaphore(name=f"e_{name}_{i}") for i in range(nsem)]
        self.si = 0
        self.cnt = 0
        self.seen = {}
        self.own = set(id(s) for s in self.sems)

    def wait(self, tok, same=False):
        if tok is None:
            return
        sem, val = tok
        if id(sem) in self.own and not same:
            return
        k = id(sem)
        if self.seen.get(k, 0) >= val:
            return
        self.h.wait_ge(sem, val)
        self.seen[k] = val

    def signal(self, ins):
        sem = self.sems[self.si]
        ins.then_inc(sem, 1)
        self.cnt += 1
        tok = (sem, self.cnt)
        if self.cnt >= SEM_LIMIT:
            self.si += 1
            self.cnt = 0
        return tok


class Builder:
    def __init__(self, cfg):
        self.cfg = cfg
        nc = bass.Bass("TRN2", target_bir_lowering=False)
        self.nc = nc
        self.pe = Eng(nc, nc.tensor, "pe", 1)
        self.act = Eng(nc, nc.scalar, "act", 2)
        self.dve = Eng(nc, nc.vector, "dve", 2)
        self.pool = Eng(nc, nc.gpsimd, "pool", 0)
        self.sp = Eng(nc, nc.sync, "sp", 0)
        self.es = contextlib.ExitStack()
        self.dram = {}
        self.nbuf = 0

    def din(self, name, shape, dt=F32):
        t = self.nc.dram_tensor(name, list(shape), dt, kind="ExternalInput").ap()
        self.dram[name] = t
        return t

    def dout(self, name, shape, dt=F32):
        t = self.nc.dram_tensor(name, list(shape), dt, kind="ExternalOutput").ap()
        self.dram[name] = t
        return t

    def sb(self, name, shape, dt, stack=None):
        self.nbuf += 1
        return (stack or self.es).enter_context(self.nc.sbuf_tensor(f"{name}_{self.nbuf}", list(shape), dt))

    def buf(self, name="b"):
        self.nbuf += 1
        return Buf(f"{name}{self.nbuf}")

    def op(self, eng, fns, reads=(), writes=(), same=False):
        for b in reads:
            eng.wait(b.w, same)
        for b in writes:
            eng.wait(b.w, same)
            for t in b.r.values():
                eng.wait(t, same)
        if not isinstance(fns, (list, tuple)):
            fns = [fns]
        ins = None
        for f in fns:
            ins = f()
        tok = eng.signal(ins)
        for b in reads:
            b.r[eng.name] = tok
        for b in writes:
            b.w = tok
            b.r = {}
        return tok

    def dma(self, q, sem_state, out, in_, reads=(), writes=(), **kw):
        for t in getattr(self, "last_bar", []):
            q.wait(t)
        for b in reads:
            q.wait(b.w)
        for b in writes:
            q.wait(b.w)
            for t in b.r.values():
                q.wait(t)
        q.h.dma_start(out=out, in_=in_, **kw).then_inc(sem_state[0], 16)
        sem_state[1] += 16
        tok = (sem_state[0], sem_state[1])
        for b in reads:
            b.r["dma_" + q.name] = tok
        for b in writes:
            b.w = tok
            b.r = {}
        return tok

    def newsem(self, name):
        return [self.nc.alloc_semaphore(name=name), 0]

    def barrier(self, engs=None):
        engs = engs or [self.pe, self.act, self.dve]
        toks = []
        for e in engs:
            if e.cnt > 0:
                toks.append((e.sems[e.si], e.cnt))
        for e in engs:
            for t in toks:
                e.wait(t)
        self.last_bar = toks

    def ring_init(self):
        self.ring = self.sb("ring", [128, NSLOT, SLOT], BF16)
        self.ring_bufs = [self.buf("slot") for _ in range(NSLOT)]
        self.ring_sems = [self.newsem(f"ring{i}") for i in range(NSLOT)]
        self.ring_i = 0

    def ring_load(self, pieces):
        s = self.ring_i % NSLOT
        self.ring_i += 1
        b = self.ring_bufs[s]
        for piece in pieces:
            off, dims, src = piece[0], piece[1], piece[2]
            npart = piece[3] if len(piece) > 3 else 128
            n = int(np.prod(dims))
            dst = self.ring[0:npart, s, off:off + n]
            if len(dims) == 2:
                dst = dst.rearrange("p (a b) -> p a b", a=dims[0])
            for t in b.r.values():
                self.pool.wait(t)
            self.pool.h.dma_start(out=dst, in_=src).then_inc(self.ring_sems[s][0], 16)
            self.ring_sems[s][1] += 16
        b.w = (self.ring_sems[s][0], self.ring_sems[s][1])
        b.r = {}
        return s, b

    def build(self):
        cfg = self.cfg
        nc = self.nc
        depth = cfg.get("depth", DEPTH)
        xT_d = self.din("xT", [D, NTOK])
        condT_d = self.din("condT", [128, KC, 2])
        adab_d = self.din("adab", [128, DEPTH, 72])
        normg_d = self.din("normg", [128, DEPTH, 3, KC])
        ada_w = self.din("ada_w", [DEPTH, D, 9 * D])
        wg = [self.din("ffn1_wg", [DEPTH, D, FH]), self.din("ffn2_wg", [DEPTH, D, FH])]
        wu = [self.din("ffn1_wu", [DEPTH, D, FH]), self.din("ffn2_wu", [DEPTH, D, FH])]
        wd = [self.din("ffn1_wd", [DEPTH, FH, D]), self.din("ffn2_wd", [DEPTH, FH, D])]
        yT_d = self.dout("yT", [D, NTOK])

        self.xT = self.sb("xT_sb", [128, KC, NTOK], F32)
        self.x_bufs = [self.buf("x") for _ in range(NT)]
        self.ones_bf = self.sb("ones_bf", [128, 128], BF16)
        self.condT = self.sb("condT_sb", [128, KC, 2], F32)
        self.scond = self.sb("scond", [128, KC, 2], BF16)
        self.adab = self.sb("adab_sb", [128, DEPTH, 72], F32)
        self.normg = self.sb("normg_sb", [128, DEPTH, 3, KC], F32)
        self.mod2 = [self.sb("mod_sb", [128, 72, 2], F32)] * 2
        self.modA2 = [self.sb("modA", [128, 3, KC, 2], F32)] * 2
        self.modB2 = [self.sb("modB", [128, 3, KC, 2], F32)] * 2
        self.modG2 = [self.sb("modG", [128, 3, KC, 2], F32)] * 2
        self.mod_bufs = [self.buf("mod")] * 2
        self.mod_first = [True, True]
        self.ring_init()
        self.psum = [self.es.enter_context(nc.psum_tensor(f"ps{i}", [128, TN], F32)) for i in range(8)]
        self.pbuf = [self.buf("ps") for _ in range(8)]
        self.setup_sem = self.newsem("setup")
        self.setup2_sem = self.newsem("setup2")
        self.const_buf = self.buf("const")

        for kc in range(KC):
            nc.sync.dma_start(out=self.xT[:, kc, :], in_=xT_d[kc * 128:(kc + 1) * 128, :]).then_inc(self.setup_sem[0], 16)
            self.setup_sem[1] += 16
        for dst, src in ((self.condT, condT_d), (self.adab, adab_d), (self.normg, normg_d)):
            nc.sync.dma_start(out=dst[:], in_=src).then_inc(self.setup_sem[0], 16)
            self.setup_sem[1] += 16
        self.extra_setup()
        setup_tok = (self.setup_sem[0], self.setup_sem[1])
        setup2_tok = (self.setup2_sem[0], self.setup2_sem[1])
        for e in (self.pe, self.act, self.dve):
            e.wait(setup_tok)
            e.wait(setup2_tok)
        self.op(self.dve, lambda: nc.vector.memset(self.ones_bf[:], 1.0), writes=[self.const_buf])
        self.scond_buf = self.buf("scond")
        self.op(self.act, lambda: nc.scalar.activation(out=self.scond[:], in_=self.condT[:], func=AF.Silu),
                writes=[self.scond_buf])
        self.op(self.dve, lambda: nc.vector.tensor_copy(out=self.w2aug_bf[:], in_=self.cs["w2aug"][:]), writes=[self.const_buf])

        layers = list(cfg.get("layers", range(depth)))
        self.ada_w = ada_w
        pre_done = False
        for i, l in enumerate(layers):
            par = i % 2
            if not pre_done:
                self.mod_blocks(l, par, range(18))
                self.mod_finish(l, par)
            self.mod, self.modA, self.modB, self.modG = self.mod2[par], self.modA2[par], self.modB2[par], self.modG2[par]
            self.mod_buf = self.mod_bufs[par]
            pre_done = False
            overlap = (i + 1 < len(layers)) and cfg.get("mod_overlap", True) and cfg.get("ffn", True) and cfg.get("ffn2", True)
            if overlap:
                nl, npar = layers[i + 1], (i + 1) % 2
            if cfg.get("ffn", True):
                hook = None
                if overlap:
                    hook = lambda g, nl=nl, npar=npar: self.mod_blocks(nl, npar, [g] if g < 7 else [])
                self.ffn(l, 0, wg[0], wu[0], wd[0], hook=hook)
                if overlap:
                    self.mod_finish(nl, npar, 0, 28, derive=False)
            if cfg.get("mixer", True):
                self.mixer(l)
            if cfg.get("ffn", True) and cfg.get("ffn2", True):
                hook = None
                if overlap:
                    hook = lambda g, nl=nl, npar=npar: self.mod_blocks(nl, npar, [7 + g])
                self.ffn(l, 2, wg[1], wu[1], wd[1], hook=hook)
                if overlap:
                    self.mod_finish(nl, npar, 28, 72, derive=True)
                    pre_done = True

        self.finish_outputs()
        osem = self.newsem("out")
        for b in self.x_bufs:
            self.sp.wait(b.w)
        for kc in range(KC):
            nc.sync.dma_start(out=yT_d[kc * 128:(kc + 1) * 128, :], in_=self.xT[:, kc, :]).then_inc(osem[0], 16)
            osem[1] += 16
        for s in self.out_sems:
            nc.sync.wait_ge(s[0], s[1])
        nc.sync.wait_ge(osem[0], osem[1])
        self.es.close()
        return nc

    def extra_setup(self):
        nc = self.nc
        self.out_sems = []
        self.odd_w_in = self.din("odd_w_in", [2, D, 4096])
        self.odd_w_out = self.din("odd_w_out", [2, 2048, D])
        oddvg_d = self.din("oddvg", [128, 2, 16])
        wsT_d = self.din("wsT", [128, 2, 4, 128])
        bsb_d = self.din("bsb", [128, 2, 4, 128])
        self.oddvg = self.sb("oddvg_sb", [128, 2, 16], F32)
        self.wsT = self.sb("wsT_sb", [128, 2, 4, 128], F32)
        self.bsb = self.sb("bsb_sb", [128, 2, 4, 128], F32)
        for dst, src in ((self.oddvg, oddvg_d), (self.wsT, wsT_d), (self.bsb, bsb_d)):
            nc.sync.dma_start(out=dst[:], in_=src).then_inc(self.setup_sem[0], 16)
            self.setup_sem[1] += 16
        self.wv_sem = self.newsem("wv")
        self.even_w_in = self.din("even_w_in", [2, D, 2240])
        self.win_sw = self.din("win_sw", [2, D, 96])
        self.even_w_out = self.din("even_w_out", [2, D, D])
        self.qbw = self.din("qbw", [2, 384, 768])
        self.qbw_sw = self.din("qbw_sw", [2, 384, 768])
        self.kvbK = self.din("kvbK", [2, 256, 512])
        self.kvbV = self.din("kvbV", [2, 256, 512])
        self.ckvcT_d = self.din("ckvcT", [2, 256, 256])
        self.kropecT_d = self.din("kropecT", [2, 32, 256])
        self.sgla_d = self.din("sgla", [2, 2, 4, 64, 128])
        self.ckvT_o = self.dout("ckvT_o", [2, 256, 1024])
        self.kropeT_o = self.dout("kropeT_o", [2, 32, 1024])
        self.gla_o = self.dout("gla_o", [4, 2, 2, 4, 64, 128])
        cs = {}
        self.cs = cs
        for nm, shp in (("gmask", [128, 4, 128]), ("mask4", [128, 2, 4, 128]), ("rope", [96, 2, 1024])):
            dd = self.din(nm, shp)
            t = self.sb(nm + "_sb", shp, BF16)
            nc.gpsimd.dma_start(out=t[:], in_=dd).then_inc(self.setup2_sem[0], 16)
            self.setup2_sem[1] += 16
            cs[nm] = t
        for nm, shp in (("qkng", [96, 2, 4]), ("glag", [128, 2]), ("qag", [128, 2, 3]), ("kvag", [128, 2, 2]),
                        ("w2aug", [33, 2, 2, 256])):
            dd = self.din(nm, shp)
            t = self.sb(nm + "_sb", shp, F32)
            nc.sync.dma_start(out=t[:], in_=dd).then_inc(self.setup_sem[0], 16)
            self.setup_sem[1] += 16
            cs[nm] = t
        self.cs = cs
        self.gmask_bf = cs["gmask"]
        self.w2aug_bf = self.sb("w2aug_bf", [33, 2, 2, 256], BF16)
        self.ld1 = self.newsem("ld1")
        self.ld2 = self.newsem("ld2")
        self.st1 = self.newsem("st1")
        self.st2 = self.newsem("st2")
        self.st3 = self.newsem("st3")
        self.out_sems += [self.st1, self.st2, self.st3]

    def finish_outputs(self):
        pass

    def mixer(self, l):
        if l % 2 == 1:
            self.odd_mixer(l)
        else:
            self.even_mixer(l)
        self.barrier()

    def even_mixer(self, l):
        for hf in range(2):
            self.even_half(l, hf)

    def _tok(self, eng):
        return (eng.sems[eng.si], eng.cnt)

    def even_half(self, l, hf):
        nc = self.nc
        j = l // 2
        k = 1
        L = 1024
        T0 = hf * L
        tiles = [2 * hf, 2 * hf + 1]
        win = self.even_w_in[j].rearrange("(kc p) f -> p kc f", p=128)
        winsw = self.win_sw[j].rearrange("(kc p) f -> p kc f", p=128)
        wout = self.even_w_out[j].rearrange("(c p) d -> p c d", p=128)
        cs = self.cs
        X = mybir.AxisListType.X
        pe, act, dve = self.pe, self.act, self.dve
        PS = self.psum
        PB = self.pbuf
        with contextlib.ExitStack() as hs:
            ogT = self.sb("e_ogT", [128, 4, L], BF16, hs); og_buf = self.buf("og")
            cqn = self.sb("e_cqn", [128, 3, L], BF16, hs); cqn_buf = self.buf("cqn")
            ckvn = self.sb("e_ckvn", [128, 2, L], BF16, hs); ckvn_buf = self.buf("ckvn")
            krT = self.sb("e_krT", [96, L], F32, hs); kr_buf = self.buf("kr")
            krsw = self.sb("e_krsw", [96, L], F32, hs); krsw_buf = self.buf("krsw")
            with contextlib.ExitStack() as gs:
                qT = self.sb("e_qT", [128, 2, L], BF16, gs); q_buf = self.buf("q")
                kT = self.sb("e_kT", [128, 2, L], BF16, gs); k_buf = self.buf("k")
                gl = self.sb("e_gl", [33, L], BF16, gs); gl_buf = self.buf("gl")
                ktok = self.sb("e_ktok", [128, 8, 256], BF16, gs); ktok_buf = self.buf("ktok")
                vtok = self.sb("e_vtok", [128, 8, 512], BF16, gs); vtok_buf = self.buf("vtok")
                with contextlib.ExitStack() as ps_:
                    hT = self.sb("e_hT", [128, KC, L], BF16, ps_)
                    h_bufs = [self.buf("h"), self.buf("h")]
                    tmp, rstd = self.norm_tmp(ps_)
                    self.norm_modulate(k, tiles, hT, h_bufs, tmp, rstd)
                    self.barrier()
                    stg = tmp["t2"]; stg_buf = self.buf("stg")
                    sq3 = tmp["sq"]; sq3_buf = self.buf("sq3")
                    rs2 = rstd["t"]; rs2_buf = self.buf("rs2")
                    self.op(dve, lambda: nc.vector.memset(gl[:], 1.0), writes=[gl_buf])
                    bi = [0]

                    def fm_group(s, sbuf_, ncols, c0, m, li, bank):
                        fns = [lambda kc=kc: nc.tensor.matmul(
                            PS[bank][0:m, :], lhsT=self.ring[:, s, kc * ncols + c0: kc * ncols + c0 + m],
                            rhs=hT[:, kc, li * TN:(li + 1) * TN], start=(kc == 0), stop=(kc == KC - 1)) for kc in range(KC)]
                        self.op(pe, fns, reads=[sbuf_, h_bufs[li]], writes=[PB[bank]])

                    def nb():
                        b = bi[0] % 4
                        bi[0] += 1
                        return b

                    s, sb_ = self.ring_load([(0, [KC, 512], win[:, :, 0:512])])
                    for li in range(2):
                        for c in range(2):
                            b = nb()
                            fm_group(s, sb_, 512, c * 128, 128, li, b)
                            self.op(act, lambda c=c, li=li, b=b: nc.scalar.activation(
                                out=qT[:, c, li * TN:(li + 1) * TN], in_=PS[b][:], func=AF.Copy, scale=0.125),
                                reads=[PB[b]], writes=[q_buf])
                        for c in range(2):
                            b = nb()
                            fm_group(s, sb_, 512, 256 + c * 128, 128, li, b)
                            self.op(dve, lambda c=c, li=li, b=b: nc.vector.tensor_copy(
                                out=kT[:, c, li * TN:(li + 1) * TN], in_=PS[b][:]), reads=[PB[b]], writes=[k_buf])
                    for blk in range(8):
                        b = nb()
                        fns = [lambda kc=kc, blk=blk, b=b: nc.tensor.matmul(
                            PS[b][:, 0:256], lhsT=hT[:, kc, blk * 128:(blk + 1) * 128],
                            rhs=self.ring[:, s, kc * 512 + 256: kc * 512 + 512], start=(kc == 0), stop=(kc == KC - 1)) for kc in range(KC)]
                        self.op(pe, fns, reads=[sb_, h_bufs[blk // 4]], writes=[PB[b]])
                        self.op(act, lambda blk=blk, b=b: nc.scalar.activation(out=ktok[:, blk, :], in_=PS[b][:, 0:256], func=AF.Copy),
                                reads=[PB[b]], writes=[ktok_buf])
                    s, sb_ = self.ring_load([(0, [KC, 512], win[:, :, 512:1024])])
                    for blk in range(8):
                        b = nb()
                        fns = [lambda kc=kc, blk=blk, b=b: nc.tensor.matmul(
                            PS[b][:], lhsT=hT[:, kc, blk * 128:(blk + 1) * 128],
                            rhs=self.ring[:, s, kc * 512: kc * 512 + 512], start=(kc == 0), stop=(kc == KC - 1)) for kc in range(KC)]
                        self.op(pe, fns, reads=[sb_, h_bufs[blk // 4]], writes=[PB[b]])
                        self.op(dve, lambda blk=blk, b=b: nc.vector.tensor_copy(out=vtok[:, blk, :], in_=PS[b][:]),
                                reads=[PB[b]], writes=[vtok_buf])
                    s, sb_ = self.ring_load([(0, [KC, 512], win[:, :, 1024:1536])])
                    for li in range(2):
                        for c in range(4):
                            b = nb()
                            fm_group(s, sb_, 512, c * 128, 128, li, b)
                            self.op(act, lambda c=c, li=li, b=b: nc.scalar.activation(
                                out=ogT[:, c, li * TN:(li + 1) * TN], in_=PS[b][:], func=AF.Silu),
                                reads=[PB[b]], writes=[og_buf])
                    s, sb_ = self.ring_load([(0, [KC, 416], win[:, :, 1536:1952])])
                    for li in range(2):
                        b = nb()
                        fm_group(s, sb_, 416, 0, 32, li, b)
                        self.op(act, lambda li=li, b=b: nc.scalar.activation(
                            out=gl[0:32, li * TN:(li + 1) * TN], in_=PS[b][0:32, :], func=AF.Copy),
                            reads=[PB[b]], writes=[gl_buf])
                        bs = [nb() for _ in range(3)]
                        for c in range(3):
                            fm_group(s, sb_, 416, 32 + c * 128, 128, li, bs[c])
                            self.op(act, lambda c=c, b=bs[c]: nc.scalar.activation(out=sq3[:, c, :], in_=PS[b][:], func=AF.Square),
                                    reads=[PB[bs[c]]], writes=[sq3_buf])
                        self.rms_rstd(sq3, sq3_buf, 3, 128, 384, rs2, rs2_buf)
                        for c in range(3):
                            self.op(dve, lambda c=c, li=li, b=bs[c]: nc.vector.scalar_tensor_tensor(
                                out=cqn[:, c, li * TN:(li + 1) * TN], in0=PS[b][:], scalar=cs["qag"][:, j, c:c + 1], in1=rs2[:],
                                op0=ALU.mult, op1=ALU.mult), reads=[PB[bs[c]], rs2_buf], writes=[cqn_buf])
                    s, sb_ = self.ring_load([(0, [KC, 288], win[:, :, 1952:2240]), (2304, [KC, 96], winsw[:, :, :])])
                    for li in range(2):
                        bs = [nb() for _ in range(2)]
                        for c in range(2):
                            fm_group(s, sb_, 288, c * 128, 128, li, bs[c])
                            self.op(act, lambda c=c, b=bs[c]: nc.scalar.activation(out=sq3[:, c, :], in_=PS[b][:], func=AF.Square),
                                    reads=[PB[bs[c]]], writes=[sq3_buf])
                        self.rms_rstd(sq3, sq3_buf, 2, 128, 256, rs2, rs2_buf)
                        for c in range(2):
                            self.op(dve, lambda c=c, li=li, b=bs[c]: nc.vector.scalar_tensor_tensor(
                                out=stg[:, c, :], in0=PS[b][:], scalar=cs["kvag"][:, j, c:c + 1], in1=rs2[:],
                                op0=ALU.mult, op1=ALU.mult), reads=[PB[bs[c]], rs2_buf], writes=[stg_buf])
                        self.op(act, lambda li=li: nc.scalar.activation(out=ckvn[:, :, li * TN:(li + 1) * TN], in_=stg[:], func=AF.Copy),
                                reads=[stg_buf], writes=[ckvn_buf])
                        if hf == 0:
                            self.dma(self.sp, self.st1, self.ckvT_o[j].rearrange("(c p) t -> p c t", p=128)[:, :, li * TN:(li + 1) * TN],
                                     stg[:], reads=[stg_buf])
                        b = nb()
                        fm_group(s, sb_, 288, 192, 96, li, b)
                        self.op(act, lambda li=li, b=b: nc.scalar.activation(
                            out=krT[64:96, li * TN:(li + 1) * TN], in_=PS[b][64:96, :], func=AF.Copy),
                            reads=[PB[b]], writes=[kr_buf])
                        if hf == 1:
                            b = nb()
                            fns = [lambda kc=kc, li=li, b=b: nc.tensor.matmul(
                                PS[b][0:96, :], lhsT=self.ring[:, s, 2304 + kc * 96: 2304 + (kc + 1) * 96],
                                rhs=hT[:, kc, li * TN:(li + 1) * TN], start=(kc == 0), stop=(kc == KC - 1)) for kc in range(KC)]
                            self.op(pe, fns, reads=[sb_, h_bufs[li]], writes=[PB[b]])
                            self.op(act, lambda li=li, b=b: nc.scalar.activation(
                                out=krsw[64:96, li * TN:(li + 1) * TN], in_=PS[b][64:96, :], func=AF.Copy),
                                reads=[PB[b]], writes=[krsw_buf])
                    if hf == 0:
                        self.dma(self.sp, self.st3, self.kropeT_o[j], krT[64:96, :], reads=[kr_buf])
                    self.barrier()
                    for e_ in (pe, act, dve):
                        e_.wait((self.st1[0], self.st1[1]))
                if self.cfg.get("even_stop", 9) <= 1:
                    return
                with contextlib.ExitStack() as ws:
                    oT = self.sb("g_oT", [128, 4, L], F32, ws); o_buf = self.buf("o")
                    ez = self.sb("g_ez", [128, 256], F32, ws)
                    lg = self.sb("g_l", [128, 256], BF16, ws); l_buf = self.buf("l")
                    Eq = self.sb("g_Eq", [128, 2, 2, 128], F32, ws); Eq_bufs = [self.buf("Eq"), self.buf("Eq")]
                    Ek = self.sb("g_Ek", [128, 2, 128], F32, ws)
                    Er = self.sb("g_Er", [128, 256], F32, ws); E_buf = self.buf("E")
                    qe = self.sb("g_qe", [128, 2, 2, 128], BF16, ws); qe_bufs = [self.buf("qe"), self.buf("qe")]
                    ke = self.sb("g_ke", [128, 2, 128], BF16, ws)
                    kl = self.sb("g_kl", [128, 2, 2, 128], BF16, ws); qk_buf = self.buf("qk")
                    self.op(dve, lambda: nc.vector.memset(kl[:], 0.0), writes=[qk_buf])
                    attm = self.sb("g_attm", [128, 2, 4, 128], BF16, ws); attm_bufs = [self.buf("attm"), self.buf("attm")]
                    S = self.sb("g_S", [128, 2, 2, 128], F32, ws); S_bufs = [self.buf("S"), self.buf("S")]
                    Sbf = self.sb("g_Sbf", [128, 3, 2, 2, 128], BF16, ws); Sbf_bufs = [self.buf("Sbf") for _ in range(3)]
                    self.op(dve, lambda: nc.vector.memset(Sbf[:], 0.0), writes=Sbf_bufs)
                    nseq = 4 if hf == 0 else 1
                    bps = 8 // nseq
                    sgl = self.sgla_d[j]
                    items = []
                    for dr in range(2):
                        for sq in range(nseq):
                            blks = list(range(sq * bps, (sq + 1) * bps))
                            if dr == 1:
                                blks = blks[::-1]
                            for bi_, blk in enumerate(blks):
                                items.append(dict(dr=dr, sq=sq, blk=blk, first=(bi_ == 0), last=(bi_ == len(blks) - 1), par=len(items) % 2))
                    st_ = {"sv": 0, "p": 0}

                    def P1(it):
                        dr, tsl = it["dr"], slice(it["blk"] * 128, (it["blk"] + 1) * 128)
                        self.op(pe, lambda: nc.tensor.matmul(PS[0][:, 0:256], lhsT=gl[0:33, tsl], rhs=self.w2aug_bf[0:33, j, dr, :],
                                                             start=True, stop=True), reads=[gl_buf, self.const_buf], writes=[PB[0]])
                        self.op(act, lambda: nc.scalar.activation(out=ez[:], in_=PS[0][:, 0:256], func=AF.Exp, scale=-1.0),
                                reads=[PB[0]], writes=[l_buf])
                        self.op(act, lambda: nc.scalar.activation(out=lg[:], in_=ez[:], func=AF.Ln, bias=1.0),
                                reads=[], writes=[l_buf])

                    def P2(it):
                        dr, par, blk = it["dr"], it["par"], it["blk"]
                        tsl = slice(blk * 128, (blk + 1) * 128)
                        fns = [lambda hp=hp: nc.tensor.matmul(PS[1][:, hp * 128:(hp + 1) * 128], lhsT=lg[:, hp * 128:(hp + 1) * 128],
                                                              rhs=self.gmask_bf[:, dr, :], start=True, stop=True) for hp in range(2)]
                        self.op(pe, fns, reads=[l_buf], writes=[PB[1]])
                        self.op(pe, lambda: nc.tensor.matmul(PS[2][:, 0:256], lhsT=self.gmask_bf[:, 2 + dr, :], rhs=lg[:],
                                                             start=True, stop=True), reads=[l_buf], writes=[PB[2]])
                        self.op(act, lambda: nc.scalar.activation(out=Eq[:, par].rearrange("p a b -> p (a b)"), in_=PS[1][:, 0:256], func=AF.Exp),
                                reads=[PB[1]], writes=[Eq_bufs[par]])
                        self.op(act, lambda: nc.scalar.activation(out=Ek[:].rearrange("p a b -> p (a b)"), in_=PS[1][:, 0:256], func=AF.Exp, scale=-1.0),
                                reads=[PB[1]], writes=[E_buf])
                        self.op(act, lambda: nc.scalar.activation(out=Er[:], in_=PS[2][:, 0:256], func=AF.Exp),
                                reads=[PB[2]], writes=[])
                        E_buf.w = self._tok(act)
                        self.op(dve, lambda: nc.vector.tensor_tensor(out=qe[:, par], in0=qT[:, :, tsl], in1=Eq[:, par], op=ALU.mult),
                                reads=[Eq_bufs[par], q_buf], writes=[qe_bufs[par]])
                        self.op(dve, lambda: nc.vector.tensor_tensor(out=ke[:], in0=kT[:, :, tsl], in1=Ek[:], op=ALU.mult),
                                reads=[k_buf, E_buf], writes=[qk_buf])
                        for hh in range(2):
                            self.op(dve, lambda hh=hh: nc.vector.tensor_tensor(
                                out=kl[:, :, hh, hh * 64:(hh + 1) * 64],
                                in0=ktok[:, blk, :].rearrange("p (a b) -> p a b", a=2)[:, :, hh * 64:(hh + 1) * 64],
                                in1=Er[:].rearrange("p (a b) -> p a b", a=2)[:, :, hh * 64:(hh + 1) * 64], op=ALU.mult),
                                reads=[ktok_buf], writes=[])
                        qk_buf.w = self._tok(dve)

                    def P3a(it):
                        dr, par = it["dr"], it["par"]
                        for hh in range(2):
                            bank = 3 if hh == 0 else 0
                            fns = [lambda hp=hp, hh=hh, bank=bank: nc.tensor.matmul(
                                PS[bank][:, hp * 128:(hp + 1) * 128], lhsT=ke[hh * 64:hh * 64 + 64, hp, :],
                                rhs=qe[hh * 64:hh * 64 + 64, par, hp, :], start=True, stop=True) for hp in range(2)]
                            self.op(pe, fns, reads=[qk_buf, qe_bufs[par]], writes=[PB[bank]])
                            self.op(dve, lambda hh=hh, bank=bank: nc.vector.tensor_tensor(
                                out=attm[:, par, hh::2, :], in0=PS[bank][:, 0:256].rearrange("p (a b) -> p a b", a=2),
                                in1=cs["mask4"][:, dr, 0:2, :], op=ALU.mult), reads=[PB[bank]], writes=[attm_bufs[par]] if hh == 0 else [])
                        attm_bufs[par].w = self._tok(dve)

                    def P3b(it):
                        blk = it["blk"]
                        for c2 in range(2):
                            fns = [lambda c2=c2, hp=hp, hh=hh: nc.tensor.matmul(
                                PS[4 + c2][:, hp * 128:(hp + 1) * 128], lhsT=kl[c2 * 64:(c2 + 1) * 64, hp, hh, :],
                                rhs=vtok[c2 * 64:(c2 + 1) * 64, blk, (hp * 2 + hh) * 128:(hp * 2 + hh + 1) * 128],
                                start=(hh == 0), stop=(hh == 1)) for hp in range(2) for hh in range(2)]
                            self.op(pe, fns, reads=[qk_buf, vtok_buf], writes=[PB[4 + c2]])

                    def copy_S(ver):
                        p = st_["p"]
                        self.op(act, [lambda hh=hh, p=p: nc.scalar.activation(out=Sbf[hh * 64:hh * 64 + 64, ver, :, hh, :],
                                                                              in_=S[hh * 64:hh * 64 + 64, p, :, :], func=AF.Copy) for hh in range(2)],
                                reads=[S_bufs[p]], writes=[Sbf_bufs[ver]])

                    def P4(it):
                        dr, par, sq = it["dr"], it["par"], it["sq"]
                        if it["first"]:
                            p = st_["p"]
                            if hf == 0:
                                self.op(dve, lambda p=p: nc.vector.memset(S[:, p], 0.0), writes=[S_bufs[p]])
                            else:
                                self.dma(self.sp, self.ld2, S[:, p], sgl[dr].rearrange("(hp hh) d e -> (hh d) hp e", hh=2), writes=[S_bufs[p]])
                            copy_S(st_["sv"])
                        corder = [0, 1] if dr == 0 else [1, 0]
                        svs = [st_["sv"]]
                        for c2 in corder:
                            last = (c2 * 64 + 63) if dr == 0 else (c2 * 64)
                            p = st_["p"]
                            q_ = 1 - p
                            for hp in range(2):
                                self.op(dve, lambda hp=hp, c2=c2, last=last, p=p, q_=q_: nc.vector.scalar_tensor_tensor(
                                    out=S[:, q_, hp, :], in0=S[:, p, hp, :], scalar=Eq[:, par, hp, last:last + 1],
                                    in1=PS[4 + c2][:, hp * 128:(hp + 1) * 128],
                                    op0=ALU.mult, op1=ALU.add), reads=[PB[4 + c2], Eq_bufs[par], S_bufs[p]],
                                    writes=[S_bufs[q_]] if hp == 0 else [])
                            S_bufs[q_].w = self._tok(dve)
                            st_["p"] = q_
                            nsv = (svs[-1] + 1) % 3
                            copy_S(nsv)
                            svs.append(nsv)
                        it["svs"] = svs
                        it["corder"] = corder
                        st_["sv"] = svs[2]
                        if it["last"] and hf == 0:
                            p = st_["p"]
                            self.dma(self.sp, self.st2, self.gla_o[sq, j, dr].rearrange("(hp hh) d e -> (hh d) hp e", hh=2), S[:, p],
                                     reads=[S_bufs[p]])

                    def P5(it):
                        dr, par, blk, svs, corder = it["dr"], it["par"], it["blk"], it["svs"], it["corder"]
                        tsl = slice(blk * 128, (blk + 1) * 128)
                        bank = 6 + par
                        fns = []
                        for h in range(4):
                            hp, r0 = h // 2, (h % 2) * 64
                            fns.append(lambda h=h: nc.tensor.matmul(PS[bank][:, h * 128:(h + 1) * 128], lhsT=vtok[:, blk, h * 128:(h + 1) * 128],
                                                                    rhs=attm[:, par, h, :], start=True, stop=False))
                            for ci_, c2 in enumerate(corder):
                                fns.append(lambda h=h, c2=c2, ci_=ci_, hp=hp, r0=r0: nc.tensor.matmul(
                                    PS[bank][:, h * 128 + c2 * 64: h * 128 + (c2 + 1) * 64], lhsT=Sbf[:, svs[ci_], hp, r0 // 64, :],
                                    rhs=qe[:, par, hp, c2 * 64:(c2 + 1) * 64], start=False, stop=(ci_ == 1)))
                        self.op(pe, fns, reads=[vtok_buf, attm_bufs[par], qe_bufs[par], Sbf_bufs[svs[0]], Sbf_bufs[svs[1]]], writes=[PB[bank]])
                        pv = PS[bank][:].rearrange("p (h t) -> p h t", h=4)
                        if dr == 0:
                            self.op(act, lambda: nc.scalar.activation(out=oT[:, :, tsl], in_=pv, func=AF.Copy),
                                    reads=[PB[bank]], writes=[o_buf])
                        else:
                            self.op(dve, lambda: nc.vector.tensor_tensor(out=oT[:, :, tsl], in0=pv, in1=oT[:, :, tsl], op=ALU.add),
                                    reads=[PB[bank]], writes=[o_buf])

                    P1(items[0]); P2(items[0]); P3a(items[0])
                    for i_, it in enumerate(items):
                        nxt = items[i_ + 1] if i_ + 1 < len(items) else None
                        if nxt is not None:
                            P1(nxt)
                        P3b(it)
                        if nxt is not None:
                            P2(nxt)
                        P4(it)
                        if nxt is not None:
                            P3a(nxt)
                        P5(it)
                    with contextlib.ExitStack() as ns:
                        sqh = self.sb("g_sq", [128, 1, TN], BF16, ns); sqh_buf = self.buf("sqh")
                        rs = self.sb("g_rs", [128, TN], F32, ns); rs_buf = self.buf("rs")
                        t3 = Eq[:].rearrange("p a b c -> p (a b c)"); t3_buf = self.buf("t3")
                        for li in range(2):
                            for h in range(4):
                                sl = slice(li * TN, (li + 1) * TN)
                                self.op(act, lambda h=h, sl=sl: nc.scalar.activation(out=sqh[:, 0, :], in_=oT[:, h, sl], func=AF.Square),
                                        reads=[o_buf], writes=[sqh_buf])
                                self.rms_rstd(sqh, sqh_buf, 1, 128, 128, rs, rs_buf)
                                self.op(dve, lambda h=h, sl=sl: nc.vector.scalar_tensor_tensor(
                                    out=t3, in0=oT[:, h, sl], scalar=cs["glag"][:, j:j + 1], in1=rs[:], op0=ALU.mult, op1=ALU.mult),
                                    reads=[o_buf, rs_buf], writes=[t3_buf])
                                self.op(dve, lambda h=h, sl=sl: nc.vector.tensor_tensor(out=ogT[:, h, sl], in0=t3, in1=ogT[:, h, sl], op=ALU.mult),
                                        reads=[t3_buf], writes=[og_buf])
                    self.barrier()
                    for e_ in (pe, act, dve):
                        e_.wait((self.st2[0], self.st2[1]))
                    if self.cfg.get("dump_o3") and hf == 1:
                        dsem = self.newsem("dbg")
                        self.out_sems.append(dsem)
                        for nm_, t_, shp_, dt_ in (("og3", ogT, [128, 4, L], BF16), ("oT3", oT, [128, 4, L], F32)):
                            self.sp.wait(self._tok(pe)); self.sp.wait(self._tok(act)); self.sp.wait(self._tok(dve))
                            nc.sync.dma_start(out=self.dout("dbg_" + nm_, shp_, dt_), in_=t_[:]).then_inc(dsem[0], 16)
                            dsem[1] += 16
                        for e_ in (pe, act, dve):
                            e_.wait((dsem[0], dsem[1]))
            if self.cfg.get("even_stop", 9) <= 2:
                return
            self.mla_half(l, hf, ogT, og_buf, cqn, cqn_buf, ckvn, ckvn_buf, krT, kr_buf, krsw, krsw_buf, wout)

    def mla_half(self, l, hf, ogT, og_buf, cqn, cqn_buf, ckvn, ckvn_buf, krT, kr_buf, krsw, krsw_buf, wout):
        nc = self.nc
        j = l // 2
        k = 1
        L = 1024
        cs = self.cs
        pe, act, dve = self.pe, self.act, self.dve
        PS, PB = self.psum, self.pbuf
        X = mybir.AxisListType.X
        koff = 256 if hf == 1 else 0
        nk = L + koff
        nkt = nk // 128
        SC = 96 ** -0.5
        g_q, g_qs, g_k, g_ks = (cs["qkng"][:, j, i:i + 1] for i in range(4))
        with contextlib.ExitStack() as ms:
            mlaT = self.sb("m_mlaT", [64, 8, L], BF16, ms); mla_buf = self.buf("mla")
            Kb = self.sb("m_Kb", [96, 2, nk], BF16, ms); kb_bufs = [self.buf("kb"), self.buf("kb")]
            Vt = self.sb("m_Vt", [128, nkt, 512], BF16, ms); vt_buf = self.buf("vt")
            ckvc = self.sb("m_ckvc", [128, 2, 256], BF16, ms); ckvc_buf = self.buf("ckvc")
            ssn = self.sb("m_ssn", [128, nkt, 8], F32, ms); ssn_buf = self.buf("ssn")
            ssr = self.sb("m_ssr", [128, nkt], F32, ms); ssr_buf = self.buf("ssr")
            rk = self.sb("m_rk", [128, nkt, 8], F32, ms); rk_buf = self.buf("rk")
            gq2 = self.sb("m_gq2", [96, 1], F32, ms); gq2_buf = self.buf("gq2")
            sq_, sqb = self.ring_load([(0, [3, 768], self.qbw[j].rearrange("(c p) f -> p c f", p=128)),
                                       (2304, [3, 768], self.qbw_sw[j].rearrange("(c p) f -> p c f", p=128))])
            sk_, skb = self.ring_load([(0, [2, 512], self.kvbK[j].rearrange("(c p) f -> p c f", p=128)),
                                       (1024, [2, 512], self.kvbV[j].rearrange("(c p) f -> p c f", p=128))])
            self.op(dve, lambda: nc.vector.tensor_tensor(out=gq2[:], in0=g_q, in1=g_k, op=ALU.mult), writes=[gq2_buf])
            with contextlib.ExitStack() as sa:
                krg = self.sb("m_krg", [96, nk], F32, sa); krg_buf = self.buf("krg")
                krr = self.sb("m_krr", [96, nk], BF16, sa); krr_buf = self.buf("krr")
                krc = self.sb("m_krc", [96, 256], F32, sa); krc_buf = self.buf("krc")
                ckvf = self.sb("m_ckvf", [128, 2, 256], F32, sa); ckvf_buf = self.buf("ckvf")
                sqt = self.sb("m_sqt", [128, 2, 512], F32, sa); sqt_bufs = [self.buf("sqt"), self.buf("sqt")]
                t1 = self.sb("m_t1a", [96, TN], F32, sa)
                t2 = self.sb("m_t2a", [96, TN], F32, sa); t_buf = self.buf("t12")
                if hf == 1:
                    for c in range(2):
                        self.dma(self.sp, self.ld1, ckvf[:, c, :], self.ckvcT_d[j, c * 128:(c + 1) * 128, :], writes=[ckvf_buf] if c == 0 else [])
                    self.dma(self.sp, self.ld1, krc[64:96, :], self.kropecT_d[j], writes=[krc_buf])
                    tok = (self.ld1[0], self.ld1[1])
                    ckvf_buf.w = tok
                    krc_buf.w = tok
                    self.op(dve, lambda: nc.vector.tensor_copy(out=ckvc[:], in_=ckvf[:]), reads=[ckvf_buf], writes=[ckvc_buf])
                    self.op(dve, lambda: nc.vector.tensor_scalar(out=krg[64:96, 0:256], in0=krc[64:96, :], scalar1=g_k[64:96, :],
                                                                 scalar2=None, op0=ALU.mult), reads=[krc_buf], writes=[krg_buf])
                    self.op(act, lambda: nc.scalar.activation(out=krr[64:96, 0:256], in_=krc[64:96, :], func=AF.Square),
                            reads=[krc_buf], writes=[krr_buf])
                    for li in range(2):
                        sl = slice(li * TN, (li + 1) * TN)
                        self.op(dve, lambda sl=sl: nc.vector.scalar_tensor_tensor(
                            out=t1[64:96, :], in0=krT[64:96, sl], scalar=g_k[64:96, :], in1=cs["rope"][64:96, 0, sl],
                            op0=ALU.mult, op1=ALU.mult), reads=[kr_buf], writes=[t_buf])
                        self.op(dve, lambda sl=sl: nc.vector.scalar_tensor_tensor(
                            out=t2[64:96, :], in0=krsw[64:96, sl], scalar=g_ks[64:96, :], in1=cs["rope"][64:96, 1, sl],
                            op0=ALU.mult, op1=ALU.mult), reads=[krsw_buf], writes=[])
                        self.op(dve, lambda li=li: nc.vector.tensor_tensor(
                            out=krg[64:96, koff + li * TN: koff + (li + 1) * TN], in0=t1[64:96, :], in1=t2[64:96, :], op=ALU.add),
                            reads=[], writes=[krg_buf])
                else:
                    self.op(dve, lambda: nc.vector.tensor_scalar(out=krg[64:96, :], in0=krT[64:96, :], scalar1=g_k[64:96, :],
                                                                 scalar2=None, op0=ALU.mult), reads=[kr_buf], writes=[krg_buf])
                self.op(act, lambda: nc.scalar.activation(out=krr[64:96, koff:nk], in_=krT[64:96, :], func=AF.Square),
                        reads=[kr_buf], writes=[krr_buf])
                for bb in range(2):
                    self.op(act if bb == 0 else dve,
                            (lambda bb=bb: nc.scalar.activation(out=Kb[64:96, bb, :], in_=krg[64:96, :], func=AF.Copy)) if bb == 0 else
                            (lambda bb=bb: nc.vector.tensor_copy(out=Kb[64:96, bb, :], in_=krg[64:96, :])),
                            reads=[krg_buf], writes=[kb_bufs[bb]])
                fns = [lambda kt=kt: nc.tensor.matmul(PS[6][:, kt:kt + 1], lhsT=krr[64:96, kt * 128:(kt + 1) * 128],
                                                      rhs=self.ones_bf[64:96, 0:1], start=True, stop=True) for kt in range(nkt)]
                self.op(pe, fns, reads=[krr_buf, self.const_buf], writes=[PB[6]])
                self.op(act, lambda: nc.scalar.activation(out=ssr[:], in_=PS[6][:, 0:nkt], func=AF.Copy), reads=[PB[6]], writes=[ssr_buf])
                for kt in range(nkt):
                    if kt * 128 < koff:
                        src, srcb, s0 = ckvc, ckvc_buf, kt * 128
                    else:
                        src, srcb, s0 = ckvn, ckvn_buf, kt * 128 - koff
                    b0 = kt % 2
                    fns = [lambda rc=rc, src=src, s0=s0, b0=b0: nc.tensor.matmul(
                        PS[b0][:], lhsT=src[:, rc, s0:s0 + 128], rhs=self.ring[:, sk_, rc * 512:(rc + 1) * 512],
                        start=(rc == 0), stop=(rc == 1)) for rc in range(2)]
                    self.op(pe, fns, reads=[skb, srcb], writes=[PB[b0]])
                    self.op(act, lambda b0=b0: nc.scalar.activation(out=sqt[:, b0, :], in_=PS[b0][:], func=AF.Square),
                            reads=[PB[b0]], writes=[sqt_bufs[b0]])
                    self.op(dve, lambda kt=kt, b0=b0: nc.vector.tensor_reduce(
                        out=ssn[:, kt, :], in_=sqt[:, b0, :].rearrange("p (h e) -> p h e", e=64), axis=X, op=ALU.add),
                        reads=[sqt_bufs[b0]], writes=[ssn_buf])
                    b1 = 2 + kt % 2
                    fns = [lambda rc=rc, src=src, s0=s0, b1=b1: nc.tensor.matmul(
                        PS[b1][:], lhsT=src[:, rc, s0:s0 + 128], rhs=self.ring[:, sk_, 1024 + rc * 512: 1024 + (rc + 1) * 512],
                        start=(rc == 0), stop=(rc == 1)) for rc in range(2)]
                    self.op(pe, fns, reads=[skb, srcb], writes=[PB[b1]])
                    self.op(dve, lambda kt=kt, b1=b1: nc.vector.tensor_copy(out=Vt[:, kt, :], in_=PS[b1][:]), reads=[PB[b1]], writes=[vt_buf])
                for kt in range(nkt):
                    self.op(dve, lambda kt=kt: nc.vector.tensor_scalar(out=rk[:, kt, :], in0=ssn[:, kt, :], scalar1=ssr[:, kt:kt + 1],
                                                                       scalar2=None, op0=ALU.add), reads=[ssn_buf, ssr_buf], writes=[rk_buf], same=True)
                self.op(act, lambda: nc.scalar.activation(out=rk[:].rearrange("p a b -> p (a b)"), in_=rk[:].rearrange("p a b -> p (a b)"),
                                                          func=AF.Ln, scale=1.0 / 96, bias=EPS), reads=[], writes=[rk_buf], same=True)
                self.op(act, lambda: nc.scalar.activation(out=rk[:].rearrange("p a b -> p (a b)"), in_=rk[:].rearrange("p a b -> p (a b)"),
                                                          func=AF.Exp, scale=-0.5, bias=float(np.log(SC))), reads=[], writes=[rk_buf], same=True)
                self.barrier()
            with contextlib.ExitStack() as sbk_:
                Qn = self.sb("m_Qn", [96, 8, TN], BF16, sbk_); qn_bufs = [self.buf("qn") for _ in range(8)]
                sq96 = self.sb("m_sq", [96, 2, TN], BF16, sbk_); sq_bufs = [self.buf("sq96"), self.buf("sq96")]
                rs96 = self.sb("m_rs", [96, 2, TN], F32, sbk_); rs_bufs = [self.buf("rs96"), self.buf("rs96")]
                t1 = self.sb("m_t1", [96, TN], F32, sbk_)
                t2 = self.sb("m_t2", [96, TN], F32, sbk_); t_buf = self.buf("t12")
                PT = self.sb("m_PT", [128, 2, TN], BF16, sbk_); pt_bufs = [self.buf("pt"), self.buf("pt")]
                rden = self.sb("m_rden", [64, 1, TN], F32, sbk_); rden_bufs = [self.buf("rden")]
                si = 0
                ei = 0
                pairs = [(li, h) for li in range(2) for h in range(8)]

                def q_steps(pi):
                    li, h = pairs[pi]
                    sl = slice(li * TN, (li + 1) * TN)
                    qb_ = 4 + pi % 2
                    db_ = pi % 2
                    steps = []

                    def s1():
                        fns = [lambda c3=c3: nc.tensor.matmul(
                            PS[qb_][0:96, :], lhsT=self.ring[:, sq_, c3 * 768 + h * 96: c3 * 768 + (h + 1) * 96],
                            rhs=cqn[:, c3, sl], start=(c3 == 0), stop=(c3 == 2)) for c3 in range(3)]
                        self.op(pe, fns, reads=[sqb, cqn_buf], writes=[PB[qb_]])
                        if hf == 1:
                            fns = [lambda c3=c3: nc.tensor.matmul(
                                PS[6][0:96, :], lhsT=self.ring[:, sq_, 2304 + c3 * 768 + h * 96: 2304 + c3 * 768 + (h + 1) * 96],
                                rhs=cqn[:, c3, sl], start=(c3 == 0), stop=(c3 == 2)) for c3 in range(3)]
                            self.op(pe, fns, reads=[sqb, cqn_buf], writes=[PB[6]])
                    steps.append(s1)
                    steps.append(lambda: self.op(act, lambda: nc.scalar.activation(out=sq96[:, db_, :], in_=PS[qb_][0:96, :], func=AF.Square),
                                                 reads=[PB[qb_]], writes=[sq_bufs[db_]]))
                    steps.append(lambda: self.op(pe, lambda: nc.tensor.matmul(PS[7][0:96, :], lhsT=self.ones_bf[0:96, 0:96], rhs=sq96[0:96, db_, :],
                                                                              start=True, stop=True), reads=[sq_bufs[db_], self.const_buf], writes=[PB[7]]))
                    steps.append(lambda: self.op(act, lambda: nc.scalar.activation(out=rs96[:, db_, :], in_=PS[7][0:96, :], func=AF.Ln, scale=1.0 / 96, bias=EPS),
                                                 reads=[PB[7]], writes=[rs_bufs[db_]]))
                    steps.append(lambda: self.op(act, lambda: nc.scalar.activation(out=rs96[:, db_, :], in_=rs96[:, db_, :], func=AF.Exp, scale=-0.5),
                                                 reads=[], writes=[rs_bufs[db_]]))
                    steps.append(lambda: self.op(dve, lambda: nc.vector.scalar_tensor_tensor(
                        out=Qn[0:64, h, :], in0=PS[qb_][0:64, :], scalar=gq2[0:64, :], in1=rs96[0:64, db_, :], op0=ALU.mult, op1=ALU.mult),
                        reads=[PB[qb_], rs_bufs[db_], gq2_buf], writes=[qn_bufs[h]]))

                    def s7():
                        if hf == 0:
                            self.op(dve, lambda: nc.vector.scalar_tensor_tensor(
                                out=Qn[64:96, h, :], in0=PS[qb_][64:96, :], scalar=g_q[64:96, :], in1=rs96[64:96, db_, :], op0=ALU.mult, op1=ALU.mult),
                                reads=[PB[qb_], rs_bufs[db_]], writes=[])
                        else:
                            self.op(dve, lambda: nc.vector.scalar_tensor_tensor(
                                out=t1[64:96, :], in0=PS[qb_][64:96, :], scalar=g_q[64:96, :], in1=cs["rope"][64:96, 0, sl],
                                op0=ALU.mult, op1=ALU.mult), reads=[PB[qb_]], writes=[t_buf])
                            self.op(dve, lambda: nc.vector.scalar_tensor_tensor(
                                out=t2[64:96, :], in0=PS[6][64:96, :], scalar=g_qs[64:96, :], in1=cs["rope"][64:96, 1, sl],
                                op0=ALU.mult, op1=ALU.mult), reads=[PB[6]], writes=[])
                            self.op(dve, lambda: nc.vector.tensor_tensor(out=t1[64:96, :], in0=t1[64:96, :], in1=t2[64:96, :], op=ALU.add),
                                    reads=[], writes=[])
                            self.op(dve, lambda: nc.vector.tensor_tensor(out=Qn[64:96, h, :], in0=t1[64:96, :], in1=rs96[64:96, db_, :], op=ALU.mult),
                                    reads=[rs_bufs[db_]], writes=[])
                        qn_bufs[h].w = self._tok(dve)
                    steps.append(s7)
                    return steps

                iters = []
                for pi, (li, h) in enumerate(pairs):
                    if hf == 1:
                        groups = [(0, TN, list(range(nkt)))]
                    else:
                        groups = [(s2 * 256, 256, [li * 4 + s2 * 2, li * 4 + s2 * 2 + 1]) for s2 in range(2)]
                    for gi, (q0, nq, kts) in enumerate(groups):
                        for ki, kt in enumerate(kts):
                            iters.append((pi, q0, nq, kt, ki == 0, ki == len(kts) - 1, gi == 0 and ki == 0))
                kb_done = {}
                qpend = {}

                def run_q(pi, n):
                    st = qpend.get(pi)
                    while st and n > 0:
                        st.pop(0)()
                        n -= 1

                def emit_k(pi):
                    nonlocal ei
                    li, h = pairs[pi]
                    kbi = pi % 2
                    if hf == 1:
                        kcols = [(0, 256, ckvc, ckvc_buf, 0), (256, 512, ckvn, ckvn_buf, 0), (768, 512, ckvn, ckvn_buf, 512)]
                    else:
                        kcols = [(li * TN, TN, ckvn, ckvn_buf, li * TN)]
                    for ci_, (c0, w, src, srcb, s0) in enumerate(kcols):
                        fns = [lambda rc=rc, src=src, s0=s0, w=w, h=h: nc.tensor.matmul(
                            PS[6][0:64, 0:w], lhsT=self.ring[:, sk_, rc * 512 + h * 64: rc * 512 + (h + 1) * 64],
                            rhs=src[:, rc, s0:s0 + w], start=(rc == 0), stop=(rc == 1)) for rc in range(2)]
                        self.op(pe, fns, reads=[skb, srcb], writes=[PB[6]])
                        if ei % 2 == 0:
                            self.op(act, lambda c0=c0, w=w, kbi=kbi: nc.scalar.activation(out=Kb[0:64, kbi, c0:c0 + w], in_=PS[6][0:64, 0:w], func=AF.Copy),
                                    reads=[PB[6]], writes=[kb_bufs[kbi]] if ci_ == 0 else [])
                        else:
                            self.op(dve, lambda c0=c0, w=w, kbi=kbi: nc.vector.tensor_copy(out=Kb[0:64, kbi, c0:c0 + w], in_=PS[6][0:64, 0:w]),
                                    reads=[PB[6]], writes=[kb_bufs[kbi]] if ci_ == 0 else [])
                        ei += 1
                    kb_done[pi] = [self._tok(act), self._tok(dve)]

                def emit_s(it):
                    nonlocal si
                    pi, q0, nq, kt, first, lastk, newhead = it
                    li, h = pairs[pi]
                    if newhead:
                        run_q(pi, 99)
                        emit_k(pi)
                        if pi + 1 < len(pairs):
                            qpend[pi + 1] = q_steps(pi + 1)
                    sbk = si % 2
                    si += 1
                    for tk_ in kb_done[pi]:
                        pe.wait(tk_)
                    self.op(pe, lambda: nc.tensor.matmul(
                        PS[sbk][:, 0:nq], lhsT=Kb[0:96, pi % 2, kt * 128:(kt + 1) * 128], rhs=Qn[0:96, h, q0:q0 + nq], start=True, stop=True),
                        reads=[kb_bufs[pi % 2], qn_bufs[h]], writes=[PB[sbk]])
                    return sbk

                qpend[0] = q_steps(0)
                nstep = 1 if hf == 1 else 2
                sb_next = emit_s(iters[0])
                for idx, it in enumerate(iters):
                    pi, q0, nq, kt, first, lastk, newhead = it
                    li, h = pairs[pi]
                    sbk = sb_next
                    self.op(act, lambda: nc.scalar.activation(
                        out=PT[:, sbk, 0:nq], in_=PS[sbk][:, 0:nq], func=AF.Exp, scale=rk[:, kt, h:h + 1]),
                        reads=[PB[sbk], rk_buf], writes=[pt_bufs[sbk]])
                    if idx + 1 < len(iters):
                        sb_next = emit_s(iters[idx + 1])
                    ob, dbk = 2, 3
                    self.op(pe, [lambda: nc.tensor.matmul(
                        PS[ob][0:64, 0:nq], lhsT=Vt[:, kt, h * 64:(h + 1) * 64], rhs=PT[:, sbk, 0:nq], start=first, stop=lastk),
                        lambda: nc.tensor.matmul(
                        PS[dbk][0:64, 0:nq], lhsT=self.ones_bf[:, 0:64], rhs=PT[:, sbk, 0:nq], start=first, stop=lastk)],
                        reads=[vt_buf, pt_bufs[sbk], self.const_buf], writes=[PB[ob], PB[dbk]] if first else [])
                    run_q(pi + 1, nstep)
                    if lastk:
                        tk = self._tok(pe)
                        PB[ob].w = tk
                        PB[dbk].w = tk
                        self.op(act, lambda: nc.scalar.activation(out=rden[:, 0, 0:nq], in_=PS[dbk][0:64, 0:nq], func=AF.Ln),
                                reads=[PB[dbk]], writes=[rden_bufs[0]])
                        self.op(act, lambda: nc.scalar.activation(out=rden[:, 0, 0:nq], in_=rden[:, 0, 0:nq], func=AF.Exp, scale=-1.0),
                                reads=[], writes=[rden_bufs[0]])
                        self.op(dve, lambda: nc.vector.tensor_tensor(
                            out=mlaT[0:64, h, li * TN + q0: li * TN + q0 + nq], in0=PS[ob][0:64, 0:nq], in1=rden[:, 0, 0:nq], op=ALU.mult),
                            reads=[PB[ob], rden_bufs[0]], writes=[mla_buf])
                self.barrier()
            wo = self.even_w_out[j]
            sa_, sab = self.ring_load([(0, [4, D], wout[:, 0:4, :])])
            sb1, sbb1 = self.ring_load([(0, [4, D], wo[512:768, :].rearrange("(h p) d -> p h d", p=64), 64)])
            sb2, sbb2 = self.ring_load([(0, [4, D], wo[768:1024, :].rearrange("(h p) d -> p h d", p=64), 64)])
            di = 0
            for li in range(2):
                t = 2 * hf + li
                jc = hf
                sl = slice(li * TN, (li + 1) * TN)
                for dc in range(KC):
                    bank = 4 + di % 3
                    di += 1
                    fns = [lambda c=c, dc=dc, bank=bank: nc.tensor.matmul(
                        PS[bank][:], lhsT=self.ring[:, sa_, c * 1024 + dc * 128: c * 1024 + (dc + 1) * 128], rhs=ogT[:, c, sl],
                        start=(c == 0), stop=False) for c in range(4)]
                    for hh_ in range(8):
                        sx = sb1 if hh_ < 4 else sb2
                        fns.append(lambda hh_=hh_, sx=sx, dc=dc, bank=bank: nc.tensor.matmul(
                            PS[bank][:], lhsT=self.ring[0:64, sx, (hh_ % 4) * 1024 + dc * 128: (hh_ % 4) * 1024 + (dc + 1) * 128],
                            rhs=mlaT[0:64, hh_, sl], start=False, stop=(hh_ == 7)))
                    self.op(pe, fns, reads=[sab, sbb1, sbb2, og_buf, mla_buf], writes=[PB[bank]])
                    xs = self.xT[:, dc, t * TN:(t + 1) * TN]
                    self.op(dve, lambda xs=xs, dc=dc, jc=jc, bank=bank: nc.vector.scalar_tensor_tensor(
                        out=xs, in0=PS[bank][:], scalar=self.modG[:, k, dc, jc:jc + 1], in1=xs,
                        op0=ALU.mult, op1=ALU.add), reads=[PB[bank], self.mod_buf], writes=[self.x_bufs[t]])
            self.barrier()
            for e_ in (pe, act, dve):
                e_.wait((self.st3[0], self.st3[1]))

    def rms_rstd_w(self, sq, sq_buf, rows, nfeat, out, out_buf, w):
        nc = self.nc
        self.op(self.pe, lambda: nc.tensor.matmul(self.psum[7][0:rows, 0:w], lhsT=self.ones_bf[0:rows, 0:rows], rhs=sq[0:rows, 0, 0:w],
                                                  start=True, stop=True), reads=[sq_buf, self.const_buf], writes=[self.pbuf[7]])
        self.op(self.act, lambda: nc.scalar.activation(out=out[0:rows, 0:w], in_=self.psum[7][0:rows, 0:w], func=AF.Ln, scale=1.0 / nfeat, bias=EPS),
                reads=[self.pbuf[7]], writes=[out_buf])
        self.op(self.act, lambda: nc.scalar.activation(out=out[0:rows, 0:w], in_=out[0:rows, 0:w], func=AF.Exp, scale=-0.5),
                reads=[], writes=[out_buf])

    def rms_rstd(self, sq, sq_buf, nch, rows, nfeat, out, out_buf):
        nc = self.nc
        fns = [lambda c=c: nc.tensor.matmul(self.psum[7][0:rows, :], lhsT=self.ones_bf[0:rows, 0:rows], rhs=sq[0:rows, c, :],
                                            start=(c == 0), stop=(c == nch - 1)) for c in range(nch)]
        self.op(self.pe, fns, reads=[sq_buf, self.const_buf], writes=[self.pbuf[7]])
        self.op(self.act, lambda: nc.scalar.activation(out=out[0:rows, :], in_=self.psum[7][0:rows, :], func=AF.Ln, scale=1.0 / nfeat, bias=EPS),
                reads=[self.pbuf[7]], writes=[out_buf])
        self.op(self.act, lambda: nc.scalar.activation(out=out[0:rows, :], in_=out[0:rows, :], func=AF.Exp, scale=-0.5),
                reads=[], writes=[out_buf])

    def odd_mixer(self, l):
        nc = self.nc
        j = l // 2
        k = 1
        win = self.odd_w_in[j].rearrange("(kc p) f -> p kc f", p=128)
        wout = self.odd_w_out[j].rearrange("(c p) d -> p c d", p=128)
        with contextlib.ExitStack() as st:
            wv = self.sb("odd_wv", [128, KC, 2048], BF16, st)
            wv_buf = self.buf("wv")
            hT = self.sb("odd_hT", [128, KC, TN], BF16, st)
            h_bufs = [self.buf("h")]
            mix = self.sb("odd_mix", [128, 16, TN], BF16, st)
            mix_buf = self.buf("mix")
            gv = self.sb("odd_gv", [128, 2048], BF16, st)
            gv_buf = self.buf("gv")
            sqf = self.sb("odd_sqf", [128, 2, 512], F32, st)
            sq_bufs = [self.buf("sqf"), self.buf("sqf")]
            ss = self.sb("odd_ss", [128, 16], F32, st)
            ss_bufs = [self.buf("ss"), self.buf("ss")]
            wp = self.sb("odd_wp", [128, 4, 128], BF16, st)
            wp_buf = self.buf("wp")
            ut = self.sb("odd_u", [128, 2, TN], F32, st)
            u_bufs = [self.buf("u"), self.buf("u")]
            tmp, rstd = self.norm_tmp(st)
            for kc in range(KC):
                self.dma(self.pool, self.wv_sem, wv[:, kc, :], win[:, kc, 2048:4096], writes=[wv_buf] if kc == 0 else [])
            wv_buf.w = (self.wv_sem[0], self.wv_sem[1])
            bi = 0
            mi = 0
            for t in range(NT):
                jc = 0 if t < NT // 2 else 1
                self.norm_modulate(k, [t], hT, h_bufs, tmp, rstd)
                gvs = [gv, tmp["sq"][:].rearrange("p a b -> p (a b)")]
                gvb = [gv_buf, tmp["buf"]]

                def vfront(q4):
                    nonlocal bi
                    gq, gqb = gvs[q4 % 2], gvb[q4 % 2]
                    so = (q4 % 2) * 8
                    for g in range(4):
                        bank = bi % 4
                        bi += 1
                        fns = [lambda kc=kc, g=g, bank=bank: nc.tensor.matmul(
                            self.psum[bank][:], lhsT=hT[:, kc, q4 * 128:(q4 + 1) * 128],
                            rhs=wv[:, kc, g * 512:(g + 1) * 512], start=(kc == 0), stop=(kc == KC - 1)) for kc in range(KC)]
                        self.op(self.pe, fns, reads=[h_bufs[0], wv_buf], writes=[self.pbuf[bank]])
                        self.op(self.act, [
                            lambda g=g, bank=bank: nc.scalar.activation(out=gq[:, g * 512:(g + 1) * 512], in_=self.psum[bank][:],
                                                                        func=AF.Gelu_apprx_tanh),
                            lambda g=g, bank=bank: nc.scalar.activation(out=sqf[:, g % 2, :], in_=gq[:, g * 512:(g + 1) * 512],
                                                                        func=AF.Square)],
                            reads=[self.pbuf[bank]], writes=([gqb] if g == 0 else []) + [sq_bufs[g % 2]])
                        self.op(self.dve, lambda g=g: nc.vector.tensor_reduce(
                            out=ss[:, so + g:so + g + 1], in_=sqf[:, g % 2, :], axis=mybir.AxisListType.X, op=ALU.add),
                            reads=[sq_bufs[g % 2]], writes=[ss_bufs[q4 % 2]])
                    gqb.w = (self.act.sems[self.act.si], self.act.cnt)

                def vback(q4):
                    nonlocal mi
                    gq, gqb = gvs[q4 % 2], gvb[q4 % 2]
                    so = (q4 % 2) * 8
                    sb_ = ss_bufs[q4 % 2]
                    self.op(self.dve, lambda: nc.vector.tensor_reduce(out=ss[:, so + 4:so + 5], in_=ss[:, so:so + 4], axis=mybir.AxisListType.X, op=ALU.add),
                            reads=[], writes=[sb_], same=True)
                    self.op(self.act, lambda: nc.scalar.activation(out=ss[:, so + 5:so + 6], in_=ss[:, so + 4:so + 5], func=AF.Ln, scale=1.0 / 2048, bias=EPS),
                            reads=[], writes=[sb_], same=True)
                    self.op(self.act, lambda: nc.scalar.activation(out=ss[:, so + 6:so + 7], in_=ss[:, so + 5:so + 6], func=AF.Exp, scale=-0.5),
                            reads=[], writes=[sb_], same=True)
                    self.op(self.dve, lambda: nc.vector.tensor_scalar(
                        out=wp[:].rearrange("p a b -> p (a b)"), in0=self.wsT[:, j].rearrange("p a b -> p (a b)"),
                        scalar1=ss[:, so + 6:so + 7], scalar2=None, op0=ALU.mult), reads=[sb_], writes=[wp_buf])
                    for g in range(4):
                        bank = 4 + mi % 2
                        mi += 1
                        fns = [lambda g=g, cc=cc, bank=bank: nc.tensor.matmul(
                            self.psum[bank][:, cc * 128:(cc + 1) * 128], lhsT=gq[:, (g * 4 + cc) * 128:(g * 4 + cc + 1) * 128],
                            rhs=wp[:, g, :], start=True, stop=True) for cc in range(4)]
                        self.op(self.pe, fns, reads=[gqb, wp_buf], writes=[self.pbuf[bank]])
                        for cc in range(4):
                            c16 = g * 4 + cc
                            self.op(self.dve, lambda g=g, cc=cc, c16=c16, bank=bank: nc.vector.scalar_tensor_tensor(
                                out=mix[:, c16, q4 * 128:(q4 + 1) * 128], in0=self.psum[bank][:, cc * 128:(cc + 1) * 128],
                                scalar=self.oddvg[:, j, c16:c16 + 1], in1=self.bsb[:, j, g, :], op0=ALU.mult, op1=ALU.add),
                                reads=[self.pbuf[bank]], writes=[mix_buf] if (q4 == 0 and c16 == 0) else [])

                vfront(0)
                for q4 in range(4):
                    if q4 + 1 < 4:
                        vfront(q4 + 1)
                    vback(q4)
                mix_buf.w = (self.dve.sems[self.dve.si], self.dve.cnt)
                for sl in range(4):
                    s, sbuf_ = self.ring_load([(0, [KC, 512], win[:, :, sl * 512:(sl + 1) * 512])])
                    for cc in range(4):
                        c16 = sl * 4 + cc
                        bank = bi % 4
                        bi += 1
                        fns = [lambda kc=kc, cc=cc, s=s, bank=bank: nc.tensor.matmul(
                            self.psum[bank][:], lhsT=self.ring[:, s, kc * 512 + cc * 128: kc * 512 + (cc + 1) * 128],
                            rhs=hT[:, kc, :], start=(kc == 0), stop=(kc == KC - 1)) for kc in range(KC)]
                        self.op(self.pe, fns, reads=[sbuf_, h_bufs[0]], writes=[self.pbuf[bank]])
                        ub = c16 % 2
                        self.op(self.act, lambda ub=ub, bank=bank: nc.scalar.activation(
                            out=ut[:, ub, :], in_=self.psum[bank][:], func=AF.Gelu_apprx_tanh),
                            reads=[self.pbuf[bank]], writes=[u_bufs[ub]])
                        self.op(self.dve, lambda ub=ub, c16=c16: nc.vector.tensor_tensor(
                            out=mix[:, c16, :], in0=ut[:, ub, :], in1=mix[:, c16, :], op=ALU.mult),
                            reads=[u_bufs[ub]], writes=[mix_buf] if c16 == 0 else [])
                mix_buf.w = (self.dve.sems[self.dve.si], self.dve.cnt)
                mix_buf.r = {}
                for dc in range(KC):
                    if dc % 2 == 0:
                        so, sob = self.ring_load([(0, [16, 256], wout[:, :, dc * 128:(dc + 2) * 128])])
                    bank = 4 + mi % 3
                    mi += 1
                    fns = [lambda c16=c16, dc=dc, bank=bank, so=so: nc.tensor.matmul(
                        self.psum[bank][:], lhsT=self.ring[:, so, c16 * 256 + (dc % 2) * 128: c16 * 256 + (dc % 2 + 1) * 128],
                        rhs=mix[:, c16, :], start=(c16 == 0), stop=(c16 == 15)) for c16 in range(16)]
                    self.op(self.pe, fns, reads=[sob, mix_buf], writes=[self.pbuf[bank]])
                    xs = self.xT[:, dc, t * TN:(t + 1) * TN]
                    self.op(self.dve, lambda xs=xs, dc=dc, jc=jc, bank=bank: nc.vector.scalar_tensor_tensor(
                        out=xs, in0=self.psum[bank][:], scalar=self.modG[:, k, dc, jc:jc + 1], in1=xs,
                        op0=ALU.mult, op1=ALU.add), reads=[self.pbuf[bank], self.mod_buf], writes=[self.x_bufs[t]])

    def mod_blocks(self, l, par, cbs):
        nc = self.nc
        aw = self.ada_w[l].rearrange("(kc p) f -> p kc f", p=128)
        pb = self.pbuf[7]
        psv = self.psum[7][:, 0:144].rearrange("p (c j) -> p c j", j=2)
        for cb in cbs:
            s, sbuf_ = self.ring_load([(0, [KC, 512], aw[:, :, cb * 512:(cb + 1) * 512])])
            fns = []
            for cc in range(4):
                c = cb * 4 + cc
                for kc in range(KC):
                    fns.append(lambda c=c, cc=cc, kc=kc, s=s: nc.tensor.matmul(
                        psv[:, c, :], lhsT=self.ring[:, s, kc * 512 + cc * 128: kc * 512 + (cc + 1) * 128],
                        rhs=self.scond[:, kc, :], start=(kc == 0), stop=(kc == KC - 1)))
            first = (cb == 0 or cb == 7)
            self.op(self.pe, fns, reads=[sbuf_, self.scond_buf], writes=[pb] if first else [])
            pb.w = (self.pe.sems[self.pe.si], self.pe.cnt)

    def mod_finish(self, l, par, c0=0, c1=72, derive=True):
        nc = self.nc
        pb = self.pbuf[7]
        psv = self.psum[7][:, 0:144].rearrange("p (c j) -> p c j", j=2)
        mod, modA, modB, modG, mod_buf = self.mod2[par], self.modA2[par], self.modB2[par], self.modG2[par], self.mod_bufs[par]
        for j in range(2):
            self.op(self.dve, lambda j=j: nc.vector.tensor_tensor(
                out=mod[:, c0:c1, j], in0=psv[:, c0:c1, j], in1=self.adab[:, l, c0:c1], op=ALU.add),
                reads=[pb], writes=[mod_buf])
        if not derive:
            return
        for k in range(3):
            sh = mod[:, (3 * k) * 8:(3 * k) * 8 + 8, :]
            sc = mod[:, (3 * k + 1) * 8:(3 * k + 1) * 8 + 8, :]
            gt = mod[:, (3 * k + 2) * 8:(3 * k + 2) * 8 + 8, :]
            for j in range(2):
                self.op(self.dve, lambda k=k, j=j, sc=sc: nc.vector.scalar_tensor_tensor(
                    out=modA[:, k, :, j], in0=sc[:, :, j], scalar=1.0, in1=self.normg[:, l, k, :],
                    op0=ALU.add, op1=ALU.mult), reads=[mod_buf], same=True)
            self.op(self.dve, lambda k=k, sh=sh: nc.vector.tensor_copy(out=modB[:, k, :, :], in_=sh), writes=[])
            self.op(self.dve, lambda k=k, gt=gt: nc.vector.tensor_scalar(
                out=modG[:, k, :, :], in0=gt, scalar1=(1.0 if k == 1 else 0.5), scalar2=None, op0=ALU.mult),
                writes=[])
        mod_buf.w = (self.dve.sems[self.dve.si], self.dve.cnt)
        mod_buf.r = {}
        pb.r["dve"] = mod_buf.w

    def norm_modulate(self, k, tiles, hT, h_bufs, tmp, rstd):
        for li, t in enumerate(tiles):
            self.norm_modulate_tile(k, t, li, hT, h_bufs, tmp, rstd)

    def norm_modulate_tile(self, k, t, li, hT, h_bufs, tmp, rstd):
        nc = self.nc
        if True:
            j = 0 if t < NT // 2 else 1
            xs = self.xT[:, :, t * TN:(t + 1) * TN]
            xb = self.x_bufs[t]
            tb = tmp["buf"]
            self.op(self.act, lambda xs=xs: nc.scalar.activation(out=tmp["sq"][:], in_=xs, func=AF.Square),
                    reads=[xb], writes=[tb])
            fns = [lambda kc=kc: nc.tensor.matmul(self.psum[7][:], lhsT=self.ones_bf[:], rhs=tmp["sq"][:, kc, :],
                                                  start=(kc == 0), stop=(kc == KC - 1)) for kc in range(KC)]
            self.op(self.pe, fns, reads=[tb, self.const_buf], writes=[self.pbuf[7]])
            rb = rstd["buf"]
            self.op(self.act, [lambda: nc.scalar.activation(out=rstd["t"][:], in_=self.psum[7][:], func=AF.Ln,
                                                            scale=1.0 / D, bias=EPS),
                               lambda: nc.scalar.activation(out=rstd["t"][:], in_=rstd["t"][:], func=AF.Exp, scale=-0.5)],
                    reads=[self.pbuf[7]], writes=[rb])
            for kc in range(KC):
                t2 = tmp["t2buf"][kc % 2]
                self.op(self.dve, lambda kc=kc, t=t, j=j: nc.vector.scalar_tensor_tensor(
                    out=tmp["t2"][:, kc % 2, :], in0=self.xT[:, kc, t * TN:(t + 1) * TN],
                    scalar=self.modA[:, k, kc, j:j + 1], in1=rstd["t"][:], op0=ALU.mult, op1=ALU.mult),
                    reads=[xb, rb, self.mod_buf], writes=[t2])
                self.op(self.act, lambda kc=kc, li=li, j=j: nc.scalar.activation(
                    out=hT[:, kc, li * TN:(li + 1) * TN], in_=tmp["t2"][:, kc % 2, :], func=AF.Identity,
                    bias=self.modB[:, k, kc, j:j + 1], scale=1.0),
                    reads=[t2, self.mod_buf], writes=[h_bufs[li]] if kc == 0 else [])
            h_bufs[li].w = (self.act.sems[self.act.si], self.act.cnt)

    def norm_tmp(self, stack):
        tmp = {"sq": self.sb("n_sq", [128, KC, TN], BF16, stack), "buf": self.buf("nsq"),
               "t2": self.sb("n_t2", [128, 2, TN], F32, stack), "t2buf": [self.buf("nt2"), self.buf("nt2")]}
        rstd = {"t": self.sb("n_rstd", [128, TN], F32, stack), "buf": self.buf("rstd")}
        return tmp, rstd

    def ffn(self, l, k, wg, wu, wd, hook=None):
        nc = self.nc
        with contextlib.ExitStack() as st:
            hT = self.sb("ffn_hT", [128, KC, NTOK], BF16, st)
            h_bufs = [self.buf("h") for _ in range(NT)]
            aT = self.sb("ffn_aT", [128, 2, GCH, TN], BF16, st)
            a_bufs = [self.buf("a"), self.buf("a")]
            sg = self.sb("ffn_sg", [128, 2, TN], F32, st)
            sg_bufs = [self.buf("sg"), self.buf("sg")]
            tmp, rstd = self.norm_tmp(st)
            self.norm_modulate(k, list(range(NT)), hT, h_bufs, tmp, rstd)
            wgl = wg[l].rearrange("(kc p) f -> p kc f", p=128)
            wul = wu[l].rearrange("(kc p) f -> p kc f", p=128)
            wdl = wd[l].rearrange("(c p) d -> p c d", p=128)
            pend = None
            u = 0
            ci = 0
            dn = 0

            def down_step(pd, dc):
                nonlocal dn
                s, sbuf_, t, ab, au = pd
                j = 0 if t < NT // 2 else 1
                if True:
                    bank = 4 + dn % 3
                    dn += 1
                    fns = [lambda c=c, dc=dc, s=s, au=au, bank=bank: nc.tensor.matmul(
                        self.psum[bank][:], lhsT=self.ring[:, s, 4096 + c * 1024 + dc * 128: 4096 + c * 1024 + (dc + 1) * 128],
                        rhs=aT[:, au, c, :], start=(c == 0), stop=(c == GCH - 1)) for c in range(GCH)]
                    self.op(self.pe, fns, reads=[sbuf_, ab], writes=[self.pbuf[bank]])
                    xs = self.xT[:, dc, t * TN:(t + 1) * TN]
                    self.op(self.dve, lambda xs=xs, dc=dc, j=j, bank=bank: nc.vector.scalar_tensor_tensor(
                        out=xs, in0=self.psum[bank][:], scalar=self.modG[:, k, dc, j:j + 1], in1=xs,
                        op0=ALU.mult, op1=ALU.add), reads=[self.pbuf[bank], self.mod_buf], writes=[self.x_bufs[t]])

            psteps = []

            def run_down(n):
                for _ in range(n):
                    if psteps:
                        pd_, dc_ = psteps.pop(0)
                        down_step(pd_, dc_)

            for g in range(NG):
                s, sbuf_ = self.ring_load([
                    (0, [KC, GCH * 128], wgl[:, :, g * GCH * 128:(g + 1) * GCH * 128]),
                    (2048, [KC, GCH * 128], wul[:, :, g * GCH * 128:(g + 1) * GCH * 128]),
                    (4096, [GCH, D], wdl[:, g * GCH:(g + 1) * GCH, :]),
                ])
                for t in range(NT):
                    au = u % 2
                    ab = a_bufs[au]
                    for c in range(GCH):
                        gb = ci % 2
                        ub = 2 + ci % 2
                        ci += 1
                        fns = [lambda kc=kc, c=c, s=s, t=t, gb=gb: nc.tensor.matmul(
                            self.psum[gb][:], lhsT=self.ring[:, s, kc * 256 + c * 128: kc * 256 + (c + 1) * 128],
                            rhs=hT[:, kc, t * TN:(t + 1) * TN], start=(kc == 0), stop=(kc == KC - 1)) for kc in range(KC)]
                        self.op(self.pe, fns, reads=[sbuf_, h_bufs[t]], writes=[self.pbuf[gb]])
                        run_down(KC // (2 * GCH))
                        fns = [lambda kc=kc, c=c, s=s, t=t, ub=ub: nc.tensor.matmul(
                            self.psum[ub][:], lhsT=self.ring[:, s, 2048 + kc * 256 + c * 128: 2048 + kc * 256 + (c + 1) * 128],
                            rhs=hT[:, kc, t * TN:(t + 1) * TN], start=(kc == 0), stop=(kc == KC - 1)) for kc in range(KC)]
                        self.op(self.pe, fns, reads=[sbuf_, h_bufs[t]], writes=[self.pbuf[ub]])
                        run_down(KC // (2 * GCH))
                        self.op(self.act, lambda gb=gb: nc.scalar.activation(out=sg[:, gb, :], in_=self.psum[gb][:], func=AF.Silu),
                                reads=[self.pbuf[gb]], writes=[sg_bufs[gb]])
                        self.op(self.dve, lambda gb=gb, ub=ub, au=au, c=c: nc.vector.tensor_tensor(
                            out=aT[:, au, c, :], in0=self.psum[ub][:], in1=sg[:, gb, :], op=ALU.mult),
                            reads=[self.pbuf[ub], sg_bufs[gb]], writes=[ab] if c == 0 else [])
                    ab.w = (self.dve.sems[self.dve.si], self.dve.cnt)
                    run_down(KC)
                    pend = (s, sbuf_, t, ab, au)
                    psteps.extend((pend, dc) for dc in range(KC))
                    u += 1
                if hook is not None:
                    hook(g)
            run_down(KC)
            self.barrier()


def _host_layout(inputs, core):
    i = core
    xp = np.asarray(inputs["x_prompt"])[4 * i:4 * i + 4].reshape(1024, D)
    xs = np.asarray(inputs["x_sample"])[i]
    xT = np.ascontiguousarray(np.concatenate([xp, xs], axis=0).T)
    cond = np.stack([np.asarray(inputs["c_ctx"]), np.asarray(inputs["c"])[i]], axis=-1)
    condT = np.ascontiguousarray(cond.reshape(KC, 128, 2).transpose(1, 0, 2))
    return {"xT": xT, "condT": condT,
            "ckvcT": np.ascontiguousarray(np.asarray(inputs["cache_ckv"])[i].transpose(0, 2, 1)),
            "kropecT": np.ascontiguousarray(np.asarray(inputs["cache_krope"])[i].transpose(0, 2, 1)),
            "sgla": np.ascontiguousarray(np.asarray(inputs["state_gla"])[i])}


def _const_tables():
    idx = np.arange(128)
    same = (idx[:, None] // 64) == (idx[None, :] // 64)
    le = idx[:, None] <= idx[None, :]
    ge = idx[:, None] >= idx[None, :]
    gt = idx[:, None] > idx[None, :]
    lt = idx[:, None] < idx[None, :]
    c = np.float32(-1.0 / 16.0)
    gm = np.zeros((128, 6, 128), np.float32)
    gm[:, 0] = np.where(same & le, c, 0)
    gm[:, 1] = np.where(same & ge, c, 0)
    gm[:, 2] = np.where(same & gt, c, 0)
    gm[:, 3] = np.where(same & lt, c, 0)
    gm[:, 4] = np.where(same & le, 1, 0)
    gm[:, 5] = np.where(same & ge, 1, 0)
    m4 = np.zeros((128, 2, 4, 128), np.float32)
    m4[:, 0] = gm[:, 4][:, None, :]
    m4[:, 1] = gm[:, 5][:, None, :]
    pos = np.arange(1024)
    row = (pos // 64).astype(np.float32)
    col = (pos % 64).astype(np.float32)
    inv = (np.float32(10000.0) ** (-np.arange(8, dtype=np.float32) / np.float32(8))).astype(np.float32)
    ang = np.concatenate([row[:, None] * inv, col[:, None] * inv], axis=-1).astype(np.float32)
    cosv, sinv = np.cos(ang).astype(np.float32), np.sin(ang).astype(np.float32)
    rope = np.zeros((96, 2, 1024), np.float32)
    for f in range(32):
        rope[64 + f, 0] = cosv[:, f // 2]
        rope[64 + f, 1] = sinv[:, f // 2] * (-1.0 if f % 2 == 0 else 1.0)
    return gm, m4, rope


def _shared_layout(inputs):
    sh = {}
    A = lambda n: np.asarray(inputs[n], dtype=np.float32)
    ada_b = A("ada_b")
    sh["adab"] = np.ascontiguousarray(ada_b.reshape(DEPTH, 72, 128).transpose(2, 0, 1))
    ng = A("norm_g")
    sh["normg"] = np.ascontiguousarray(ng.reshape(DEPTH, 3, KC, 128).transpose(3, 0, 1, 2))
    vg = A("odd_v_g")
    sh["oddvg"] = np.ascontiguousarray(vg.reshape(2, 16, 128).transpose(2, 0, 1))
    sh["wsT"] = np.ascontiguousarray(A("odd_ws").transpose(3, 0, 1, 2))
    sh["bsb"] = np.ascontiguousarray(np.broadcast_to(A("odd_bs")[None], (128, 2, 4, 128)))
    for n in ("odd_w_in", "odd_w_out", "ada_w", "ffn1_wg", "ffn1_wu", "ffn1_wd", "ffn2_wg", "ffn2_wu", "ffn2_wd",
              "even_w_in", "even_w_out"):
        sh[n] = np.ascontiguousarray(A(n))
    swap = np.arange(32) ^ 1
    win = A("even_w_in")
    wsw = np.zeros((2, D, 96), np.float32)
    wsw[:, :, 64:96] = win[:, :, 2208:2240][:, :, swap]
    sh["win_sw"] = wsw
    qb = A("mla_qb_w")
    sh["qbw"] = np.ascontiguousarray(qb.reshape(2, 384, 768))
    qs = qb.copy()
    qs[..., 64:96] = qb[..., 64:96][..., swap]
    sh["qbw_sw"] = np.ascontiguousarray(qs.reshape(2, 384, 768))
    kvb = A("mla_kvb_w")
    sh["kvbK"] = np.ascontiguousarray(kvb[..., :64].reshape(2, 256, 512))
    sh["kvbV"] = np.ascontiguousarray(kvb[..., 64:].reshape(2, 256, 512))
    gm, m4, rope = _const_tables()
    sh["gmask"], sh["mask4"], sh["rope"] = np.ascontiguousarray(gm[:, 0:4]), m4, rope
    qn, kn = A("mla_qn_g"), A("mla_kn_g")
    sw96 = np.arange(96)
    sw96[64:96] = 64 + swap
    sh["qkng"] = np.ascontiguousarray(np.stack([qn, qn[:, sw96], kn, kn[:, sw96]], axis=-1).transpose(1, 0, 2))
    sh["glag"] = np.ascontiguousarray(A("gla_norm_g").T)
    sh["qag"] = np.ascontiguousarray(A("mla_qa_g").reshape(2, 3, 128).transpose(2, 0, 1))
    sh["kvag"] = np.ascontiguousarray(A("mla_kva_g").reshape(2, 2, 128).transpose(2, 0, 1))
    w2 = A("gla_gate_w2")
    gb = A("gla_gate_b")
    w2aug = np.zeros((33, 2, 2, 256), np.float32)
    w2aug[0:16, :, 0, :] = w2[:, 0].transpose(1, 0, 2)
    w2aug[16:32, :, 1, :] = w2[:, 1].transpose(1, 0, 2)
    w2aug[32] = gb
    sh["w2aug"] = w2aug
    return sh


def run(inputs, cfg, cores=8, trace=False):
    b = Builder(cfg)
    nc = b.build()
    sh = _shared_layout(inputs)
    in_maps = []
    for i in range(cores):
        m = dict(sh)
        m.update(_host_layout(inputs, i))
        in_maps.append(m)
    res = run_bass_kernel_spmd(nc, in_maps, core_ids=list(range(cores)), trace=trace)
    return res


def kernel(**inputs):
    res = run(inputs, {})
    B, S, DB, DS = 32, 256, 8, 1024
    y_prompt = np.zeros((B, S, D), np.float32)
    y_sample = np.zeros((DB, DS, D), np.float32)
    new_ckv = np.zeros((B, 2, S, 256), np.float32)
    new_krope = np.zeros((B, 2, S, 32), np.float32)
    new_gla = np.zeros((B, 2, 2, 4, 64, 128), np.float32)
    for i, r in enumerate(res.results):
        yT = np.asarray(r["yT"])
        y_prompt[4 * i:4 * i + 4] = yT[:, :1024].T.reshape(4, S, D)
        y_sample[i] = yT[:, 1024:].T
        ck = np.asarray(r["ckvT_o"])
        new_ckv[4 * i:4 * i + 4] = ck.reshape(2, 256, 4, S).transpose(2, 0, 3, 1)
        kr = np.asarray(r["kropeT_o"])
        new_krope[4 * i:4 * i + 4] = kr.reshape(2, 32, 4, S).transpose(2, 0, 3, 1)
        new_gla[4 * i:4 * i + 4] = np.asarray(r["gla_o"])
    return (y_prompt, y_sample, new_ckv, new_krope, new_gla)


def check_states(r, states, inp):
    for (l, st) in states:
        j = l // 2
        ck = np.asarray(r["ckvT_o"])[j].reshape(256, 4, 256).transpose(1, 2, 0)
        kr = np.asarray(r["kropeT_o"])[j].reshape(32, 4, 256).transpose(1, 2, 0)
        gl = np.asarray(r["gla_o"])[:, j]
        for nm, a, b in (("ckv", ck, np.asarray(st[0])), ("krope", kr, np.asarray(st[1])), ("gla", gl, np.asarray(st[2]))):
            print("state", nm, "layer", l, "relvar", ((a - b) ** 2).mean() / (b ** 2).mean())
```

```python
import contextlib
import numpy as np
import concourse.bass as bass
import concourse.mybir as mybir
from concourse.bass_utils import run_bass_kernel_spmd

F32 = mybir.dt.float32
BF16 = mybir.dt.bfloat16
AF = mybir.ActivationFunctionType
ALU = mybir.AluOpType

D = 1024
KC = 8
NTOK = 2048
TN = 512
NT = NTOK // TN
DEPTH = 4
FH = 2816
FC = FH // 128
GCH = 2
NG = FC // GCH
EPS = 1e-6
SLOT = 6144
NSLOT = 3
SEM_LIMIT = 30000


class Buf:
    __slots__ = ("name", "w", "r")

    def __init__(self, name):
        self.name = name
        self.w = None
        self.r = {}


class Eng:
    def __init__(self, nc, h, name, nsem):
        self.nc = nc
        self.h = h
        self.name = name
        self.sems = [nc.alloc_semaphore(name=f"e_{name}_{i}") for i in range(nsem)]
        self.si = 0
        self.cnt = 0
        self.seen = {}
        self.own = set(id(s) for s in self.sems)

    def wait(self, tok, same=False):
        if tok is None:
            return
        sem, val = tok
        if id(sem) in self.own and not same:
            return
        k = id(sem)
        if self.seen.get(k, 0) >= val:
            return
        self.h.wait_ge(sem, val)
        self.seen[k] = val

    def signal(self, ins):
        sem = self.sems[self.si]
        ins.then_inc(sem, 1)
        self.cnt += 1
        tok = (sem, self.cnt)
        if self.cnt >= SEM_LIMIT:
            self.si += 1
            self.cnt = 0
        return tok


class Builder:
    def __init__(self, cfg):
        self.cfg = cfg
        nc = bass.Bass("TRN2", target_bir_lowering=False)
        self.nc = nc
        self.pe = Eng(nc, nc.tensor, "pe", 1)
        self.act = Eng(nc, nc.scalar, "act", 2)
        self.dve = Eng(nc, nc.vector, "dve", 2)
        self.pool = Eng(nc, nc.gpsimd, "pool", 0)
        self.sp = Eng(nc, nc.sync, "sp", 0)
        self.es = contextlib.ExitStack()
        self.dram = {}
        self.nbuf = 0

    def din(self, name, shape, dt=F32):
        t = self.nc.dram_tensor(name, list(shape), dt, kind="ExternalInput").ap()
        self.dram[name] = t
        return t

    def dout(self, name, shape, dt=F32):
        t = self.nc.dram_tensor(name, list(shape), dt, kind="ExternalOutput").ap()
        self.dram[name] = t
        return t

    def sb(self, name, shape, dt, stack=None):
        self.nbuf += 1
        return (stack or self.es).enter_context(self.nc.sbuf_tensor(f"{name}_{self.nbuf}", list(shape), dt))

    def buf(self, name="b"):
        self.nbuf += 1
        return Buf(f"{name}{self.nbuf}")

    def op(self, eng, fns, reads=(), writes=(), same=False):
        for b in reads:
            eng.wait(b.w, same)
        for b in writes:
            eng.wait(b.w, same)
            for t in b.r.values():
                eng.wait(t, same)
        if not isinstance(fns, (list, tuple)):
            fns = [fns]
        ins = None
        for f in fns:
            ins = f()
        tok = eng.signal(ins)
        for b in reads:
            b.r[eng.name] = tok
        for b in writes:
            b.w = tok
            b.r = {}
        return tok

    def dma(self, q, sem_state, out, in_, reads=(), writes=(), **kw):
        for t in getattr(self, "last_bar", []):
            q.wait(t)
        for b in reads:
            q.wait(b.w)
        for b in writes:
            q.wait(b.w)
            for t in b.r.values():
                q.wait(t)
        q.h.dma_start(out=out, in_=in_, **kw).then_inc(sem_state[0], 16)
        sem_state[1] += 16
        tok = (sem_state[0], sem_state[1])
        for b in reads:
            b.r["dma_" + q.name] = tok
        for b in writes:
            b.w = tok
            b.r = {}
        return tok

    def newsem(self, name):
        return [self.nc.alloc_semaphore(name=name), 0]

    def barrier(self, engs=None):
        engs = engs or [self.pe, self.act, self.dve]
        toks = []
        for e in engs:
            if e.cnt > 0:
                toks.append((e.sems[e.si], e.cnt))
        for e in engs:
            for t in toks:
                e.wait(t)
        self.last_bar = toks

    def ring_init(self):
        self.ring = self.sb("ring", [128, NSLOT, SLOT], BF16)
        self.ring_bufs = [self.buf("slot") for _ in range(NSLOT)]
        self.ring_sems = [self.newsem(f"ring{i}") for i in range(NSLOT)]
        self.ring_i = 0

    def ring_load(self, pieces):
        s = self.ring_i % NSLOT
        self.ring_i += 1
        b = self.ring_bufs[s]
        for piece in pieces:
            off, dims, src = piece[0], piece[1], piece[2]
            npart = piece[3] if len(piece) > 3 else 128
            n = int(np.prod(dims))
            dst = self.ring[0:npart, s, off:off + n]
            if len(dims) == 2:
                dst = dst.rearrange("p (a b) -> p a b", a=dims[0])
            for t in b.r.values():
                self.pool.wait(t)
            self.pool.h.dma_start(out=dst, in_=src).then_inc(self.ring_sems[s][0], 16)
            self.ring_sems[s][1] += 16
        b.w = (self.ring_sems[s][0], self.ring_sems[s][1])
        b.r = {}
        return s, b

    def build(self):
        cfg = self.cfg
        nc = self.nc
        depth = cfg.get("depth", DEPTH)
        xT_d = self.din("xT", [D, NTOK])
        condT_d = self.din("condT", [128, KC, 2])
        adab_d = self.din("adab", [128, DEPTH, 72])
        normg_d = self.din("normg", [128, DEPTH, 3, KC])
        ada_w = self.din("ada_w", [DEPTH, D, 9 * D])
        wg = [self.din("ffn1_wg", [DEPTH, D, FH]), self.din("ffn2_wg", [DEPTH, D, FH])]
        wu = [self.din("ffn1_wu", [DEPTH, D, FH]), self.din("ffn2_wu", [DEPTH, D, FH])]
        wd = [self.din("ffn1_wd", [DEPTH, FH, D]), self.din("ffn2_wd", [DEPTH, FH, D])]
        yT_d = self.dout("yT", [D, NTOK])

        self.xT = self.sb("xT_sb", [128, KC, NTOK], F32)
        self.x_bufs = [self.buf("x") for _ in range(NT)]
        self.ones_bf = self.sb("ones_bf", [128, 128], BF16)
        self.condT = self.sb("condT_sb", [128, KC, 2], F32)
        self.scond = self.sb("scond", [128, KC, 2], BF16)
        self.adab = self.sb("adab_sb", [128, DEPTH, 72], F32)
        self.normg = self.sb("normg_sb", [128, DEPTH, 3, KC], F32)
        self.mod2 = [self.sb("mod_sb", [128, 72, 2], F32)] * 2
        self.modA2 = [self.sb("modA", [128, 3, KC, 2], F32)] * 2
        self.modB2 = [self.sb("modB", [128, 3, KC, 2], F32)] * 2
        self.modG2 = [self.sb("modG", [128, 3, KC, 2], F32)] * 2
        self.mod_bufs = [self.buf("mod")] * 2
        self.mod_first = [True, True]
        self.ring_init()
        self.psum = [self.es.enter_context(nc.psum_tensor(f"ps{i}", [128, TN], F32)) for i in range(8)]
        self.pbuf = [self.buf("ps") for _ in range(8)]
        self.setup_sem = self.newsem("setup")
        self.setup2_sem = self.newsem("setup2")
        self.const_buf = self.buf("const")

        for kc in range(KC):
            nc.sync.dma_start(out=self.xT[:, kc, :], in_=xT_d[kc * 128:(kc + 1) * 128, :]).then_inc(self.setup_sem[0], 16)
            self.setup_sem[1] += 16
        for dst, src in ((self.condT, condT_d), (self.adab, adab_d), (self.normg, normg_d)):
            nc.sync.dma_start(out=dst[:], in_=src).then_inc(self.setup_sem[0], 16)
            self.setup_sem[1] += 16
        self.extra_setup()
        setup_tok = (self.setup_sem[0], self.setup_sem[1])
        setup2_tok = (self.setup2_sem[0], self.setup2_sem[1])
        for e in (self.pe, self.act, self.dve):
            e.wait(setup_tok)
            e.wait(setup2_tok)
        self.op(self.dve, lambda: nc.vector.memset(self.ones_bf[:], 1.0), writes=[self.const_buf])
        self.scond_buf = self.buf("scond")
        self.op(self.act, lambda: nc.scalar.activation(out=self.scond[:], in_=self.condT[:], func=AF.Silu),
                writes=[self.scond_buf])
        self.op(self.dve, lambda: nc.vector.tensor_copy(out=self.w2aug_bf[:], in_=self.cs["w2aug"][:]), writes=[self.const_buf])

        layers = list(cfg.get("layers", range(depth)))
        self.ada_w = ada_w
        PSET = [0, 1, 2, 3, 4, 5, 17]
        pre_done = False
        for i, l in enumerate(layers):
            par = i % 2
            self.mod, self.modA, self.modB, self.modG = self.mod2[par], self.modA2[par], self.modB2[par], self.modG2[par]
            self.mod_buf = self.mod_bufs[par]
            ride = cfg.get("mod_overlap", True) and cfg.get("ffn", True) and cfg.get("ffn2", True)
            if not ride:
                self.mod_blocks(l, par, range(18), first=True)
                self.mod_raw(l, par, [(0, 72)])
                self.mod_derive(l, par, [0, 1, 2])
            elif not pre_done:
                self.mod_blocks(l, par, PSET, first=True)
                self.mod_raw(l, par, [(0, 24), (68, 72)])
                self.mod_derive(l, par, [0])
            pre_done = False
            if cfg.get("ffn", True):
                hook = None
                if ride:
                    hook = lambda g, l=l, par=par: self.mod_blocks(l, par, [6 + g], first=(g == 0))
                self.ffn(l, 0, wg[0], wu[0], wd[0], hook=hook)
                if ride:
                    self.mod_raw(l, par, [(24, 68)])
                    self.mod_derive(l, par, [1, 2])
            if cfg.get("mixer", True):
                self.mixer(l)
            if cfg.get("ffn", True) and cfg.get("ffn2", True):
                hook = None
                nxt = ride and (i + 1 < len(layers))
                if nxt:
                    nl, npar = layers[i + 1], (i + 1) % 2
                    hook = lambda g, nl=nl, npar=npar: self.mod_blocks(nl, npar, [PSET[g]] if g < len(PSET) else [], first=(g == 0))
                self.ffn(l, 2, wg[1], wu[1], wd[1], hook=hook)
                if nxt:
                    self.mod_raw(nl, npar, [(0, 24), (68, 72)])
                    self.mod_derive(nl, npar, [0])
                    pre_done = True

        self.finish_outputs()
        osem = self.newsem("out")
        for b in self.x_bufs:
            self.sp.wait(b.w)
        for kc in range(KC):
            nc.sync.dma_start(out=yT_d[kc * 128:(kc + 1) * 128, :], in_=self.xT[:, kc, :]).then_inc(osem[0], 16)
            osem[1] += 16
        for s in self.out_sems:
            nc.sync.wait_ge(s[0], s[1])
        nc.sync.wait_ge(osem[0], osem[1])
        self.es.close()
        return nc

    def extra_setup(self):
        nc = self.nc
        self.out_sems = []
        self.odd_w_in = self.din("odd_w_in", [2, D, 4096])
        self.odd_w_out = self.din("odd_w_out", [2, 2048, D])
        oddvg_d = self.din("oddvg", [128, 2, 16])
        wsT_d = self.din("wsT", [128, 2, 4, 128])
        bsb_d = self.din("bsb", [128, 2, 4, 128])
        self.oddvg = self.sb("oddvg_sb", [128, 2, 16], F32)
        self.wsT = self.sb("wsT_sb", [128, 2, 4, 128], F32)
        self.bsb = self.sb("bsb_sb", [128, 2, 4, 128], F32)
        for dst, src in ((self.oddvg, oddvg_d), (self.wsT, wsT_d), (self.bsb, bsb_d)):
            nc.sync.dma_start(out=dst[:], in_=src).then_inc(self.setup_sem[0], 16)
            self.setup_sem[1] += 16
        self.wv_sem = self.newsem("wv")
        self.even_w_in = self.din("even_w_in", [2, D, 2240])
        self.win_sw = self.din("win_sw", [2, D, 96])
        self.even_w_out = self.din("even_w_out", [2, D, D])
        self.qbw = self.din("qbw", [2, 384, 768])
        self.qbw_sw = self.din("qbw_sw", [2, 384, 768])
        self.kvbK = self.din("kvbK", [2, 256, 512])
        self.kvbV = self.din("kvbV", [2, 256, 512])
        self.ckvcT_d = self.din("ckvcT", [2, 256, 256])
        self.kropecT_d = self.din("kropecT", [2, 32, 256])
        self.sgla_d = self.din("sgla", [2, 2, 4, 64, 128])
        self.ckvT_o = self.dout("ckvT_o", [2, 256, 1024])
        self.kropeT_o = self.dout("kropeT_o", [2, 32, 1024])
        self.gla_o = self.dout("gla_o", [4, 2, 2, 4, 64, 128])
        cs = {}
        self.cs = cs
        for nm, shp in (("gmask", [128, 4, 128]), ("mask4", [128, 2, 4, 128]), ("rope", [96, 2, 1024])):
            dd = self.din(nm, shp)
            t = self.sb(nm + "_sb", shp, BF16)
            nc.gpsimd.dma_start(out=t[:], in_=dd).then_inc(self.setup2_sem[0], 16)
            self.setup2_sem[1] += 16
            cs[nm] = t
        for nm, shp in (("qkng", [96, 2, 4]), ("glag", [128, 2]), ("qag", [128, 2, 3]), ("kvag", [128, 2, 2]),
                        ("w2aug", [33, 2, 2, 256])):
            dd = self.din(nm, shp)
            t = self.sb(nm + "_sb", shp, F32)
            nc.sync.dma_start(out=t[:], in_=dd).then_inc(self.setup_sem[0], 16)
            self.setup_sem[1] += 16
            cs[nm] = t
        self.cs = cs
        self.gmask_bf = cs["gmask"]
        self.w2aug_bf = self.sb("w2aug_bf", [33, 2, 2, 256], BF16)
        self.ld1 = self.newsem("ld1")
        self.ld2 = self.newsem("ld2")
        self.st1 = self.newsem("st1")
        self.st2 = self.newsem("st2")
        self.st3 = self.newsem("st3")
        self.out_sems += [self.st1, self.st2, self.st3]

    def finish_outputs(self):
        pass

    def mixer(self, l):
        if l % 2 == 1:
            self.odd_mixer(l)
        else:
            self.even_mixer(l)
        self.barrier()

    def even_mixer(self, l):
        for hf in range(2):
            self.even_half(l, hf)

    def _tok(self, eng):
        return (eng.sems[eng.si], eng.cnt)

    def even_half(self, l, hf):
        nc = self.nc
        j = l // 2
        k = 1
        L = 1024
        T0 = hf * L
        tiles = [2 * hf, 2 * hf + 1]
        win = self.even_w_in[j].rearrange("(kc p) f -> p kc f", p=128)
        winsw = self.win_sw[j].rearrange("(kc p) f -> p kc f", p=128)
        wout = self.even_w_out[j].rearrange("(c p) d -> p c d", p=128)
        cs = self.cs
        X = mybir.AxisListType.X
        pe, act, dve = self.pe, self.act, self.dve
        PS = self.psum
        PB = self.pbuf
        with contextlib.ExitStack() as hs:
            ogT = self.sb("e_ogT", [128, 4, L], BF16, hs); og_buf = self.buf("og")
            cqn = self.sb("e_cqn", [128, 3, L], BF16, hs); cqn_buf = self.buf("cqn")
            ckvn = self.sb("e_ckvn", [128, 2, L], BF16, hs); ckvn_buf = self.buf("ckvn")
            krT = self.sb("e_krT", [96, L], F32, hs); kr_buf = self.buf("kr")
            krsw = self.sb("e_krsw", [96, L], F32, hs); krsw_buf = self.buf("krsw")
            with contextlib.ExitStack() as gs:
                qT = self.sb("e_qT", [128, 2, L], BF16, gs); q_buf = self.buf("q")
                kT = self.sb("e_kT", [128, 2, L], BF16, gs); k_buf = self.buf("k")
                gl = self.sb("e_gl", [33, L], BF16, gs); gl_buf = self.buf("gl")
                ktok = self.sb("e_ktok", [128, 8, 256], BF16, gs); ktok_buf = self.buf("ktok")
                vtok = self.sb("e_vtok", [128, 8, 512], BF16, gs); vtok_buf = self.buf("vtok")
                with contextlib.ExitStack() as ps_:
                    hT = self.sb("e_hT", [128, KC, L], BF16, ps_)
                    h_bufs = [self.buf("h"), self.buf("h")]
                    tmp, rstd = self.norm_tmp(ps_)
                    self.norm_modulate(k, tiles, hT, h_bufs, tmp, rstd)
                    self.barrier()
                    stg = tmp["t2"]; stg_buf = self.buf("stg")
                    sq3 = tmp["sq"]; sq3_buf = self.buf("sq3")
                    rs2 = rstd["t"]; rs2_buf = self.buf("rs2")
                    self.op(dve, lambda: nc.vector.memset(gl[:], 1.0), writes=[gl_buf])
                    bi = [0]

                    def fm_group(s, sbuf_, ncols, c0, m, li, bank):
                        fns = [lambda kc=kc: nc.tensor.matmul(
                            PS[bank][0:m, :], lhsT=self.ring[:, s, kc * ncols + c0: kc * ncols + c0 + m],
                            rhs=hT[:, kc, li * TN:(li + 1) * TN], start=(kc == 0), stop=(kc == KC - 1)) for kc in range(KC)]
                        self.op(pe, fns, reads=[sbuf_, h_bufs[li]], writes=[PB[bank]])

                    def nb():
                        b = bi[0] % 4
                        bi[0] += 1
                        return b

                    s, sb_ = self.ring_load([(0, [KC, 512], win[:, :, 0:512])])
                    for li in range(2):
                        for c in range(2):
                            b = nb()
                            fm_group(s, sb_, 512, c * 128, 128, li, b)
                            self.op(act, lambda c=c, li=li, b=b: nc.scalar.activation(
                                out=qT[:, c, li * TN:(li + 1) * TN], in_=PS[b][:], func=AF.Copy, scale=0.125),
                                reads=[PB[b]], writes=[q_buf])
                        for c in range(2):
                            b = nb()
                            fm_group(s, sb_, 512, 256 + c * 128, 128, li, b)
                            self.op(dve, lambda c=c, li=li, b=b: nc.vector.tensor_copy(
                                out=kT[:, c, li * TN:(li + 1) * TN], in_=PS[b][:]), reads=[PB[b]], writes=[k_buf])
                    for blk in range(8):
                        b = nb()
                        fns = [lambda kc=kc, blk=blk, b=b: nc.tensor.matmul(
                            PS[b][:, 0:256], lhsT=hT[:, kc, blk * 128:(blk + 1) * 128],
                            rhs=self.ring[:, s, kc * 512 + 256: kc * 512 + 512], start=(kc == 0), stop=(kc == KC - 1)) for kc in range(KC)]
                        self.op(pe, fns, reads=[sb_, h_bufs[blk // 4]], writes=[PB[b]])
                        self.op(act, lambda blk=blk, b=b: nc.scalar.activation(out=ktok[:, blk, :], in_=PS[b][:, 0:256], func=AF.Copy),
                                reads=[PB[b]], writes=[ktok_buf])
                    s, sb_ = self.ring_load([(0, [KC, 512], win[:, :, 512:1024])])
                    for blk in range(8):
                        b = nb()
                        fns = [lambda kc=kc, blk=blk, b=b: nc.tensor.matmul(
                            PS[b][:], lhsT=hT[:, kc, blk * 128:(blk + 1) * 128],
                            rhs=self.ring[:, s, kc * 512: kc * 512 + 512], start=(kc == 0), stop=(kc == KC - 1)) for kc in range(KC)]
                        self.op(pe, fns, reads=[sb_, h_bufs[blk // 4]], writes=[PB[b]])
                        self.op(dve, lambda blk=blk, b=b: nc.vector.tensor_copy(out=vtok[:, blk, :], in_=PS[b][:]),
                                reads=[PB[b]], writes=[vtok_buf])
                    s, sb_ = self.ring_load([(0, [KC, 512], win[:, :, 1024:1536])])
                    for li in range(2):
                        for c in range(4):
                            b = nb()
                            fm_group(s, sb_, 512, c * 128, 128, li, b)
                            self.op(act, lambda c=c, li=li, b=b: nc.scalar.activation(
                                out=ogT[:, c, li * TN:(li + 1) * TN], in_=PS[b][:], func=AF.Silu),
                                reads=[PB[b]], writes=[og_buf])
                    s, sb_ = self.ring_load([(0, [KC, 416], win[:, :, 1536:1952])])
                    for li in range(2):
                        b = nb()
                        fm_group(s, sb_, 416, 0, 32, li, b)
                        self.op(act, lambda li=li, b=b: nc.scalar.activation(
                            out=gl[0:32, li * TN:(li + 1) * TN], in_=PS[b][0:32, :], func=AF.Copy),
                            reads=[PB[b]], writes=[gl_buf])
                        bs = [nb() for _ in range(3)]
                        for c in range(3):
                            fm_group(s, sb_, 416, 32 + c * 128, 128, li, bs[c])
                            self.op(act, lambda c=c, b=bs[c]: nc.scalar.activation(out=sq3[:, c, :], in_=PS[b][:], func=AF.Square),
                                    reads=[PB[bs[c]]], writes=[sq3_buf])
                        self.rms_rstd(sq3, sq3_buf, 3, 128, 384, rs2, rs2_buf)
                        for c in range(3):
                            self.op(dve, lambda c=c, li=li, b=bs[c]: nc.vector.scalar_tensor_tensor(
                                out=cqn[:, c, li * TN:(li + 1) * TN], in0=PS[b][:], scalar=cs["qag"][:, j, c:c + 1], in1=rs2[:],
                                op0=ALU.mult, op1=ALU.mult), reads=[PB[bs[c]], rs2_buf], writes=[cqn_buf])
                    s, sb_ = self.ring_load([(0, [KC, 288], win[:, :, 1952:2240]), (2304, [KC, 96], winsw[:, :, :])])
                    for li in range(2):
                        bs = [nb() for _ in range(2)]
                        for c in range(2):
                            fm_group(s, sb_, 288, c * 128, 128, li, bs[c])
                            self.op(act, lambda c=c, b=bs[c]: nc.scalar.activation(out=sq3[:, c, :], in_=PS[b][:], func=AF.Square),
                                    reads=[PB[bs[c]]], writes=[sq3_buf])
                        self.rms_rstd(sq3, sq3_buf, 2, 128, 256, rs2, rs2_buf)
                        for c in range(2):
                            self.op(dve, lambda c=c, li=li, b=bs[c]: nc.vector.scalar_tensor_tensor(
                                out=stg[:, c, :], in0=PS[b][:], scalar=cs["kvag"][:, j, c:c + 1], in1=rs2[:],
                                op0=ALU.mult, op1=ALU.mult), reads=[PB[bs[c]], rs2_buf], writes=[stg_buf])
                        self.op(act, lambda li=li: nc.scalar.activation(out=ckvn[:, :, li * TN:(li + 1) * TN], in_=stg[:], func=AF.Copy),
                                reads=[stg_buf], writes=[ckvn_buf])
                        if hf == 0:
                            self.dma(self.sp, self.st1, self.ckvT_o[j].rearrange("(c p) t -> p c t", p=128)[:, :, li * TN:(li + 1) * TN],
                                     stg[:], reads=[stg_buf])
                        b = nb()
                        fm_group(s, sb_, 288, 192, 96, li, b)
                        self.op(act, lambda li=li, b=b: nc.scalar.activation(
                            out=krT[64:96, li * TN:(li + 1) * TN], in_=PS[b][64:96, :], func=AF.Copy),
                            reads=[PB[b]], writes=[kr_buf])
                        if hf == 1:
                            b = nb()
                            fns = [lambda kc=kc, li=li, b=b: nc.tensor.matmul(
                                PS[b][0:96, :], lhsT=self.ring[:, s, 2304 + kc * 96: 2304 + (kc + 1) * 96],
                                rhs=hT[:, kc, li * TN:(li + 1) * TN], start=(kc == 0), stop=(kc == KC - 1)) for kc in range(KC)]
                            self.op(pe, fns, reads=[sb_, h_bufs[li]], writes=[PB[b]])
                            self.op(act, lambda li=li, b=b: nc.scalar.activation(
                                out=krsw[64:96, li * TN:(li + 1) * TN], in_=PS[b][64:96, :], func=AF.Copy),
                                reads=[PB[b]], writes=[krsw_buf])
                    if hf == 0:
                        self.dma(self.sp, self.st3, self.kropeT_o[j], krT[64:96, :], reads=[kr_buf])
                    self.barrier()
                    for e_ in (pe, act, dve):
                        e_.wait((self.st1[0], self.st1[1]))
                if self.cfg.get("even_stop", 9) <= 1:
                    return
                with contextlib.ExitStack() as ws:
                    oT = self.sb("g_oT", [128, 4, L], F32, ws); o_buf = self.buf("o")
                    ez = self.sb("g_ez", [128, 256], F32, ws)
                    lg = self.sb("g_l", [128, 256], BF16, ws); l_buf = self.buf("l")
                    Eq = self.sb("g_Eq", [128, 2, 2, 128], F32, ws); Eq_bufs = [self.buf("Eq"), self.buf("Eq")]
                    Ek = self.sb("g_Ek", [128, 2, 128], F32, ws)
                    Er = self.sb("g_Er", [128, 256], F32, ws); E_buf = self.buf("E")
                    qe = self.sb("g_qe", [128, 2, 2, 128], BF16, ws); qe_bufs = [self.buf("qe"), self.buf("qe")]
                    ke = self.sb("g_ke", [128, 2, 128], BF16, ws)
                    kl = self.sb("g_kl", [128, 2, 2, 128], BF16, ws); qk_buf = self.buf("qk")
                    self.op(dve, lambda: nc.vector.memset(kl[:], 0.0), writes=[qk_buf])
                    attm = self.sb("g_attm", [128, 2, 4, 128], BF16, ws); attm_bufs = [self.buf("attm"), self.buf("attm")]
                    S = self.sb("g_S", [128, 2, 2, 128], F32, ws); S_bufs = [self.buf("S"), self.buf("S")]
                    Sbf = self.sb("g_Sbf", [128, 3, 2, 2, 128], BF16, ws); Sbf_bufs = [self.buf("Sbf") for _ in range(3)]
                    self.op(dve, lambda: nc.vector.memset(Sbf[:], 0.0), writes=Sbf_bufs)
                    nseq = 4 if hf == 0 else 1
                    bps = 8 // nseq
                    sgl = self.sgla_d[j]
                    items = []
                    for dr in range(2):
                        for sq in range(nseq):
                            blks = list(range(sq * bps, (sq + 1) * bps))
                            if dr == 1:
                                blks = blks[::-1]
                            for bi_, blk in enumerate(blks):
                                items.append(dict(dr=dr, sq=sq, blk=blk, first=(bi_ == 0), last=(bi_ == len(blks) - 1), par=len(items) % 2))
                    st_ = {"sv": 0, "p": 0}

                    def P1(it):
                        dr, tsl = it["dr"], slice(it["blk"] * 128, (it["blk"] + 1) * 128)
                        self.op(pe, lambda: nc.tensor.matmul(PS[0][:, 0:256], lhsT=gl[0:33, tsl], rhs=self.w2aug_bf[0:33, j, dr, :],
                                                             start=True, stop=True), reads=[gl_buf, self.const_buf], writes=[PB[0]])
                        self.op(act, lambda: nc.scalar.activation(out=ez[:], in_=PS[0][:, 0:256], func=AF.Exp, scale=-1.0),
                                reads=[PB[0]], writes=[l_buf])
                        self.op(act, lambda: nc.scalar.activation(out=lg[:], in_=ez[:], func=AF.Ln, bias=1.0),
                                reads=[], writes=[l_buf])

                    def P2(it):
                        dr, par, blk = it["dr"], it["par"], it["blk"]
                        tsl = slice(blk * 128, (blk + 1) * 128)
                        fns = [lambda hp=hp: nc.tensor.matmul(PS[1][:, hp * 128:(hp + 1) * 128], lhsT=lg[:, hp * 128:(hp + 1) * 128],
                                                              rhs=self.gmask_bf[:, dr, :], start=True, stop=True) for hp in range(2)]
                        self.op(pe, fns, reads=[l_buf], writes=[PB[1]])
                        self.op(pe, lambda: nc.tensor.matmul(PS[2][:, 0:256], lhsT=self.gmask_bf[:, 2 + dr, :], rhs=lg[:],
                                                             start=True, stop=True), reads=[l_buf], writes=[PB[2]])
                        self.op(act, lambda: nc.scalar.activation(out=Eq[:, par].rearrange("p a b -> p (a b)"), in_=PS[1][:, 0:256], func=AF.Exp),
                                reads=[PB[1]], writes=[Eq_bufs[par]])
                        self.op(act, lambda: nc.scalar.activation(out=Ek[:].rearrange("p a b -> p (a b)"), in_=PS[1][:, 0:256], func=AF.Exp, scale=-1.0),
                                reads=[PB[1]], writes=[E_buf])
                        self.op(act, lambda: nc.scalar.activation(out=Er[:], in_=PS[2][:, 0:256], func=AF.Exp),
                                reads=[PB[2]], writes=[])
                        E_buf.w = self._tok(act)
                        self.op(dve, lambda: nc.vector.tensor_tensor(out=qe[:, par], in0=qT[:, :, tsl], in1=Eq[:, par], op=ALU.mult),
                                reads=[Eq_bufs[par], q_buf], writes=[qe_bufs[par]])
                        self.op(dve, lambda: nc.vector.tensor_tensor(out=ke[:], in0=kT[:, :, tsl], in1=Ek[:], op=ALU.mult),
                                reads=[k_buf, E_buf], writes=[qk_buf])
                        for hh in range(2):
                            self.op(dve, lambda hh=hh: nc.vector.tensor_tensor(
                                out=kl[:, :, hh, hh * 64:(hh + 1) * 64],
                                in0=ktok[:, blk, :].rearrange("p (a b) -> p a b", a=2)[:, :, hh * 64:(hh + 1) * 64],
                                in1=Er[:].rearrange("p (a b) -> p a b", a=2)[:, :, hh * 64:(hh + 1) * 64], op=ALU.mult),
                                reads=[ktok_buf], writes=[])
                        qk_buf.w = self._tok(dve)

                    def P3a(it):
                        dr, par = it["dr"], it["par"]
                        for hh in range(2):
                            bank = 3 if hh == 0 else 0
                            fns = [lambda hp=hp, hh=hh, bank=bank: nc.tensor.matmul(
                                PS[bank][:, hp * 128:(hp + 1) * 128], lhsT=ke[hh * 64:hh * 64 + 64, hp, :],
                                rhs=qe[hh * 64:hh * 64 + 64, par, hp, :], start=True, stop=True) for hp in range(2)]
                            self.op(pe, fns, reads=[qk_buf, qe_bufs[par]], writes=[PB[bank]])
                            self.op(dve, lambda hh=hh, bank=bank: nc.vector.tensor_tensor(
                                out=attm[:, par, hh::2, :], in0=PS[bank][:, 0:256].rearrange("p (a b) -> p a b", a=2),
                                in1=cs["mask4"][:, dr, 0:2, :], op=ALU.mult), reads=[PB[bank]], writes=[attm_bufs[par]] if hh == 0 else [])
                        attm_bufs[par].w = self._tok(dve)

                    def P3b(it):
                        blk = it["blk"]
                        for c2 in range(2):
                            fns = [lambda c2=c2, hp=hp, hh=hh: nc.tensor.matmul(
                                PS[4 + c2][:, hp * 128:(hp + 1) * 128], lhsT=kl[c2 * 64:(c2 + 1) * 64, hp, hh, :],
                                rhs=vtok[c2 * 64:(c2 + 1) * 64, blk, (hp * 2 + hh) * 128:(hp * 2 + hh + 1) * 128],
                                start=(hh == 0), stop=(hh == 1)) for hp in range(2) for hh in range(2)]
                            self.op(pe, fns, reads=[qk_buf, vtok_buf], writes=[PB[4 + c2]])

                    def copy_S(ver):
                        p = st_["p"]
                        self.op(act, [lambda hh=hh, p=p: nc.scalar.activation(out=Sbf[hh * 64:hh * 64 + 64, ver, :, hh, :],
                                                                              in_=S[hh * 64:hh * 64 + 64, p, :, :], func=AF.Copy) for hh in range(2)],
                                reads=[S_bufs[p]], writes=[Sbf_bufs[ver]])

                    def P4(it):
                        dr, par, sq = it["dr"], it["par"], it["sq"]
                        if it["first"]:
                            p = st_["p"]
                            if hf == 0:
                                self.op(dve, lambda p=p: nc.vector.memset(S[:, p], 0.0), writes=[S_bufs[p]])
                            else:
                                self.dma(self.sp, self.ld2, S[:, p], sgl[dr].rearrange("(hp hh) d e -> (hh d) hp e", hh=2), writes=[S_bufs[p]])
                            copy_S(st_["sv"])
                        corder = [0, 1] if dr == 0 else [1, 0]
                        svs = [st_["sv"]]
                        for c2 in corder:
                            last = (c2 * 64 + 63) if dr == 0 else (c2 * 64)
                            p = st_["p"]
                            q_ = 1 - p
                            for hp in range(2):
                                self.op(dve, lambda hp=hp, c2=c2, last=last, p=p, q_=q_: nc.vector.scalar_tensor_tensor(
                                    out=S[:, q_, hp, :], in0=S[:, p, hp, :], scalar=Eq[:, par, hp, last:last + 1],
                                    in1=PS[4 + c2][:, hp * 128:(hp + 1) * 128],
                                    op0=ALU.mult, op1=ALU.add), reads=[PB[4 + c2], Eq_bufs[par], S_bufs[p]],
                                    writes=[S_bufs[q_]] if hp == 0 else [])
                            S_bufs[q_].w = self._tok(dve)
                            st_["p"] = q_
                            nsv = (svs[-1] + 1) % 3
                            copy_S(nsv)
                            svs.append(nsv)
                        it["svs"] = svs
                        it["corder"] = corder
                        st_["sv"] = svs[2]
                        if it["last"] and hf == 0:
                            p = st_["p"]
                            self.dma(self.sp, self.st2, self.gla_o[sq, j, dr].rearrange("(hp hh) d e -> (hh d) hp e", hh=2), S[:, p],
                                     reads=[S_bufs[p]])

                    def P5(it):
                        dr, par, blk, svs, corder = it["dr"], it["par"], it["blk"], it["svs"], it["corder"]
                        tsl = slice(blk * 128, (blk + 1) * 128)
                        bank = 6 + par
                        fns = []
                        for h in range(4):
                            hp, r0 = h // 2, (h % 2) * 64
                            fns.append(lambda h=h: nc.tensor.matmul(PS[bank][:, h * 128:(h + 1) * 128], lhsT=vtok[:, blk, h * 128:(h + 1) * 128],
                                                                    rhs=attm[:, par, h, :], start=True, stop=False))
                            for ci_, c2 in enumerate(corder):
                                fns.append(lambda h=h, c2=c2, ci_=ci_, hp=hp, r0=r0: nc.tensor.matmul(
                                    PS[bank][:, h * 128 + c2 * 64: h * 128 + (c2 + 1) * 64], lhsT=Sbf[:, svs[ci_], hp, r0 // 64, :],
                                    rhs=qe[:, par, hp, c2 * 64:(c2 + 1) * 64], start=False, stop=(ci_ == 1)))
                        self.op(pe, fns, reads=[vtok_buf, attm_bufs[par], qe_bufs[par], Sbf_bufs[svs[0]], Sbf_bufs[svs[1]]], writes=[PB[bank]])
                        pv = PS[bank][:].rearrange("p (h t) -> p h t", h=4)
                        if dr == 0:
                            self.op(act, lambda: nc.scalar.activation(out=oT[:, :, tsl], in_=pv, func=AF.Copy),
                                    reads=[PB[bank]], writes=[o_buf])
                        else:
                            self.op(dve, lambda: nc.vector.tensor_tensor(out=oT[:, :, tsl], in0=pv, in1=oT[:, :, tsl], op=ALU.add),
                                    reads=[PB[bank]], writes=[o_buf])

                    P1(items[0]); P2(items[0]); P3a(items[0])
                    for i_, it in enumerate(items):
                        nxt = items[i_ + 1] if i_ + 1 < len(items) else None
                        if nxt is not None:
                            P1(nxt)
                        P3b(it)
                        if nxt is not None:
                            P2(nxt)
                        P4(it)
                        if nxt is not None:
                            P3a(nxt)
                        P5(it)
                    with contextlib.ExitStack() as ns:
                        sqh = self.sb("g_sq", [128, 1, TN], BF16, ns); sqh_buf = self.buf("sqh")
                        rs = self.sb("g_rs", [128, TN], F32, ns); rs_buf = self.buf("rs")
                        t3 = Eq[:].rearrange("p a b c -> p (a b c)"); t3_buf = self.buf("t3")
                        for li in range(2):
                            for h in range(4):
                                sl = slice(li * TN, (li + 1) * TN)
                                self.op(act, lambda h=h, sl=sl: nc.scalar.activation(out=sqh[:, 0, :], in_=oT[:, h, sl], func=AF.Square),
                                        reads=[o_buf], writes=[sqh_buf])
                                self.rms_rstd(sqh, sqh_buf, 1, 128, 128, rs, rs_buf)
                                self.op(dve, lambda h=h, sl=sl: nc.vector.scalar_tensor_tensor(
                                    out=t3, in0=oT[:, h, sl], scalar=cs["glag"][:, j:j + 1], in1=rs[:], op0=ALU.mult, op1=ALU.mult),
                                    reads=[o_buf, rs_buf], writes=[t3_buf])
                                self.op(dve, lambda h=h, sl=sl: nc.vector.tensor_tensor(out=ogT[:, h, sl], in0=t3, in1=ogT[:, h, sl], op=ALU.mult),
                                        reads=[t3_buf], writes=[og_buf])
                    self.barrier()
                    for e_ in (pe, act, dve):
                        e_.wait((self.st2[0], self.st2[1]))
                    if self.cfg.get("dump_o3") and hf == 1:
                        dsem = self.newsem("dbg")
                        self.out_sems.append(dsem)
                        for nm_, t_, shp_, dt_ in (("og3", ogT, [128, 4, L], BF16), ("oT3", oT, [128, 4, L], F32)):
                            self.sp.wait(self._tok(pe)); self.sp.wait(self._tok(act)); self.sp.wait(self._tok(dve))
                            nc.sync.dma_start(out=self.dout("dbg_" + nm_, shp_, dt_), in_=t_[:]).then_inc(dsem[0], 16)
                            dsem[1] += 16
                        for e_ in (pe, act, dve):
                            e_.wait((dsem[0], dsem[1]))
            if self.cfg.get("even_stop", 9) <= 2:
                return
            self.mla_half(l, hf, ogT, og_buf, cqn, cqn_buf, ckvn, ckvn_buf, krT, kr_buf, krsw, krsw_buf, wout)

    def mla_half(self, l, hf, ogT, og_buf, cqn, cqn_buf, ckvn, ckvn_buf, krT, kr_buf, krsw, krsw_buf, wout):
        nc = self.nc
        j = l // 2
        k = 1
        L = 1024
        cs = self.cs
        pe, act, dve = self.pe, self.act, self.dve
        PS, PB = self.psum, self.pbuf
        X = mybir.AxisListType.X
        koff = 256 if hf == 1 else 0
        nk = L + koff
        nkt = nk // 128
        SC = 96 ** -0.5
        g_q, g_qs, g_k, g_ks = (cs["qkng"][:, j, i:i + 1] for i in range(4))
        with contextlib.ExitStack() as ms:
            mlaT = self.sb("m_mlaT", [64, 8, L], BF16, ms); mla_buf = self.buf("mla")
            Kb = self.sb("m_Kb", [96, 2, nk], BF16, ms); kb_bufs = [self.buf("kb"), self.buf("kb")]
            Vt = self.sb("m_Vt", [128, nkt, 512], BF16, ms); vt_buf = self.buf("vt")
            ckvc = self.sb("m_ckvc", [128, 2, 256], BF16, ms); ckvc_buf = self.buf("ckvc")
            ssn = self.sb("m_ssn", [128, nkt, 8], F32, ms); ssn_buf = self.buf("ssn")
            ssr = self.sb("m_ssr", [128, nkt], F32, ms); ssr_buf = self.buf("ssr")
            rk = self.sb("m_rk", [128, nkt, 8], F32, ms); rk_buf = self.buf("rk")
            gq2 = self.sb("m_gq2", [96, 1], F32, ms); gq2_buf = self.buf("gq2")
            sq_, sqb = self.ring_load([(0, [3, 768], self.qbw[j].rearrange("(c p) f -> p c f", p=128)),
                                       (2304, [3, 768], self.qbw_sw[j].rearrange("(c p) f -> p c f", p=128))])
            sk_, skb = self.ring_load([(0, [2, 512], self.kvbK[j].rearrange("(c p) f -> p c f", p=128)),
                                       (1024, [2, 512], self.kvbV[j].rearrange("(c p) f -> p c f", p=128))])
            self.op(dve, lambda: nc.vector.tensor_tensor(out=gq2[:], in0=g_q, in1=g_k, op=ALU.mult), writes=[gq2_buf])
            with contextlib.ExitStack() as sa:
                krg = self.sb("m_krg", [96, nk], F32, sa); krg_buf = self.buf("krg")
                krr = self.sb("m_krr", [96, nk], BF16, sa); krr_buf = self.buf("krr")
                krc = self.sb("m_krc", [96, 256], F32, sa); krc_buf = self.buf("krc")
                ckvf = self.sb("m_ckvf", [128, 2, 256], F32, sa); ckvf_buf = self.buf("ckvf")
                sqt = self.sb("m_sqt", [128, 2, 512], F32, sa); sqt_bufs = [self.buf("sqt"), self.buf("sqt")]
                t1 = self.sb("m_t1a", [96, TN], F32, sa)
                t2 = self.sb("m_t2a", [96, TN], F32, sa); t_buf = self.buf("t12")
                if hf == 1:
                    for c in range(2):
                        self.dma(self.sp, self.ld1, ckvf[:, c, :], self.ckvcT_d[j, c * 128:(c + 1) * 128, :], writes=[ckvf_buf] if c == 0 else [])
                    self.dma(self.sp, self.ld1, krc[64:96, :], self.kropecT_d[j], writes=[krc_buf])
                    tok = (self.ld1[0], self.ld1[1])
                    ckvf_buf.w = tok
                    krc_buf.w = tok
                    self.op(dve, lambda: nc.vector.tensor_copy(out=ckvc[:], in_=ckvf[:]), reads=[ckvf_buf], writes=[ckvc_buf])
                    self.op(dve, lambda: nc.vector.tensor_scalar(out=krg[64:96, 0:256], in0=krc[64:96, :], scalar1=g_k[64:96, :],
                                                                 scalar2=None, op0=ALU.mult), reads=[krc_buf], writes=[krg_buf])
                    self.op(act, lambda: nc.scalar.activation(out=krr[64:96, 0:256], in_=krc[64:96, :], func=AF.Square),
                            reads=[krc_buf], writes=[krr_buf])
                    for li in range(2):
                        sl = slice(li * TN, (li + 1) * TN)
                        self.op(dve, lambda sl=sl: nc.vector.scalar_tensor_tensor(
                            out=t1[64:96, :], in0=krT[64:96, sl], scalar=g_k[64:96, :], in1=cs["rope"][64:96, 0, sl],
                            op0=ALU.mult, op1=ALU.mult), reads=[kr_buf], writes=[t_buf])
                        self.op(dve, lambda sl=sl: nc.vector.scalar_tensor_tensor(
                            out=t2[64:96, :], in0=krsw[64:96, sl], scalar=g_ks[64:96, :], in1=cs["rope"][64:96, 1, sl],
                            op0=ALU.mult, op1=ALU.mult), reads=[krsw_buf], writes=[])
                        self.op(dve, lambda li=li: nc.vector.tensor_tensor(
                            out=krg[64:96, koff + li * TN: koff + (li + 1) * TN], in0=t1[64:96, :], in1=t2[64:96, :], op=ALU.add),
                            reads=[], writes=[krg_buf])
                else:
                    self.op(dve, lambda: nc.vector.tensor_scalar(out=krg[64:96, :], in0=krT[64:96, :], scalar1=g_k[64:96, :],
                                                                 scalar2=None, op0=ALU.mult), reads=[kr_buf], writes=[krg_buf])
                self.op(act, lambda: nc.scalar.activation(out=krr[64:96, koff:nk], in_=krT[64:96, :], func=AF.Square),
                        reads=[kr_buf], writes=[krr_buf])
                for bb in range(2):
                    self.op(act if bb == 0 else dve,
                            (lambda bb=bb: nc.scalar.activation(out=Kb[64:96, bb, :], in_=krg[64:96, :], func=AF.Copy)) if bb == 0 else
                            (lambda bb=bb: nc.vector.tensor_copy(out=Kb[64:96, bb, :], in_=krg[64:96, :])),
                            reads=[krg_buf], writes=[kb_bufs[bb]])
                fns = [lambda kt=kt: nc.tensor.matmul(PS[6][:, kt:kt + 1], lhsT=krr[64:96, kt * 128:(kt + 1) * 128],
                                                      rhs=self.ones_bf[64:96, 0:1], start=True, stop=True) for kt in range(nkt)]
                self.op(pe, fns, reads=[krr_buf, self.const_buf], writes=[PB[6]])
                self.op(act, lambda: nc.scalar.activation(out=ssr[:], in_=PS[6][:, 0:nkt], func=AF.Copy), reads=[PB[6]], writes=[ssr_buf])
                for kt in range(nkt):
                    if kt * 128 < koff:
                        src, srcb, s0 = ckvc, ckvc_buf, kt * 128
                    else:
                        src, srcb, s0 = ckvn, ckvn_buf, kt * 128 - koff
                    b0 = kt % 2
                    fns = [lambda rc=rc, src=src, s0=s0, b0=b0: nc.tensor.matmul(
                        PS[b0][:], lhsT=src[:, rc, s0:s0 + 128], rhs=self.ring[:, sk_, rc * 512:(rc + 1) * 512],
                        start=(rc == 0), stop=(rc == 1)) for rc in range(2)]
                    self.op(pe, fns, reads=[skb, srcb], writes=[PB[b0]])
                    self.op(act, lambda b0=b0: nc.scalar.activation(out=sqt[:, b0, :], in_=PS[b0][:], func=AF.Square),
                            reads=[PB[b0]], writes=[sqt_bufs[b0]])
                    self.op(dve, lambda kt=kt, b0=b0: nc.vector.tensor_reduce(
                        out=ssn[:, kt, :], in_=sqt[:, b0, :].rearrange("p (h e) -> p h e", e=64), axis=X, op=ALU.add),
                        reads=[sqt_bufs[b0]], writes=[ssn_buf])
                    b1 = 2 + kt % 2
                    fns = [lambda rc=rc, src=src, s0=s0, b1=b1: nc.tensor.matmul(
                        PS[b1][:], lhsT=src[:, rc, s0:s0 + 128], rhs=self.ring[:, sk_, 1024 + rc * 512: 1024 + (rc + 1) * 512],
                        start=(rc == 0), stop=(rc == 1)) for rc in range(2)]
                    self.op(pe, fns, reads=[skb, srcb], writes=[PB[b1]])
                    self.op(dve, lambda kt=kt, b1=b1: nc.vector.tensor_copy(out=Vt[:, kt, :], in_=PS[b1][:]), reads=[PB[b1]], writes=[vt_buf])
                for kt in range(nkt):
                    self.op(dve, lambda kt=kt: nc.vector.tensor_scalar(out=rk[:, kt, :], in0=ssn[:, kt, :], scalar1=ssr[:, kt:kt + 1],
                                                                       scalar2=None, op0=ALU.add), reads=[ssn_buf, ssr_buf], writes=[rk_buf], same=True)
                self.op(act, lambda: nc.scalar.activation(out=rk[:].rearrange("p a b -> p (a b)"), in_=rk[:].rearrange("p a b -> p (a b)"),
                                                          func=AF.Ln, scale=1.0 / 96, bias=EPS), reads=[], writes=[rk_buf], same=True)
                self.op(act, lambda: nc.scalar.activation(out=rk[:].rearrange("p a b -> p (a b)"), in_=rk[:].rearrange("p a b -> p (a b)"),
                                                          func=AF.Exp, scale=-0.5, bias=float(np.log(SC))), reads=[], writes=[rk_buf], same=True)
                self.barrier()
            with contextlib.ExitStack() as sbk_:
                Qn = self.sb("m_Qn", [96, 8, TN], BF16, sbk_); qn_bufs = [self.buf("qn") for _ in range(8)]
                sq96 = self.sb("m_sq", [96, 2, TN], BF16, sbk_); sq_bufs = [self.buf("sq96"), self.buf("sq96")]
                rs96 = self.sb("m_rs", [96, 2, TN], F32, sbk_); rs_bufs = [self.buf("rs96"), self.buf("rs96")]
                t1 = self.sb("m_t1", [96, TN], F32, sbk_)
                t2 = self.sb("m_t2", [96, TN], F32, sbk_); t_buf = self.buf("t12")
                PT = self.sb("m_PT", [128, 2, TN], BF16, sbk_); pt_bufs = [self.buf("pt"), self.buf("pt")]
                rden = self.sb("m_rden", [64, 1, TN], F32, sbk_); rden_bufs = [self.buf("rden")]
                si = 0
                ei = 0
                pairs = [(li, h) for li in range(2) for h in range(8)]

                def q_steps(pi):
                    li, h = pairs[pi]
                    sl = slice(li * TN, (li + 1) * TN)
                    qb_ = 4 + pi % 2
                    db_ = pi % 2
                    steps = []

                    def s1():
                        fns = [lambda c3=c3: nc.tensor.matmul(
                            PS[qb_][0:96, :], lhsT=self.ring[:, sq_, c3 * 768 + h * 96: c3 * 768 + (h + 1) * 96],
                            rhs=cqn[:, c3, sl], start=(c3 == 0), stop=(c3 == 2)) for c3 in range(3)]
                        self.op(pe, fns, reads=[sqb, cqn_buf], writes=[PB[qb_]])
                        if hf == 1:
                            fns = [lambda c3=c3: nc.tensor.matmul(
                                PS[6][0:96, :], lhsT=self.ring[:, sq_, 2304 + c3 * 768 + h * 96: 2304 + c3 * 768 + (h + 1) * 96],
                                rhs=cqn[:, c3, sl], start=(c3 == 0), stop=(c3 == 2)) for c3 in range(3)]
                            self.op(pe, fns, reads=[sqb, cqn_buf], writes=[PB[6]])
                    steps.append(s1)
                    steps.append(lambda: self.op(act, lambda: nc.scalar.activation(out=sq96[:, db_, :], in_=PS[qb_][0:96, :], func=AF.Square),
                                                 reads=[PB[qb_]], writes=[sq_bufs[db_]]))
                    steps.append(lambda: self.op(pe, lambda: nc.tensor.matmul(PS[7][0:96, :], lhsT=self.ones_bf[0:96, 0:96], rhs=sq96[0:96, db_, :],
                                                                              start=True, stop=True), reads=[sq_bufs[db_], self.const_buf], writes=[PB[7]]))
                    steps.append(lambda: self.op(act, lambda: nc.scalar.activation(out=rs96[:, db_, :], in_=PS[7][0:96, :], func=AF.Ln, scale=1.0 / 96, bias=EPS),
                                                 reads=[PB[7]], writes=[rs_bufs[db_]]))
                    steps.append(lambda: self.op(act, lambda: nc.scalar.activation(out=rs96[:, db_, :], in_=rs96[:, db_, :], func=AF.Exp, scale=-0.5),
                                                 reads=[], writes=[rs_bufs[db_]]))
                    steps.append(lambda: self.op(dve, lambda: nc.vector.scalar_tensor_tensor(
                        out=Qn[0:64, h, :], in0=PS[qb_][0:64, :], scalar=gq2[0:64, :], in1=rs96[0:64, db_, :], op0=ALU.mult, op1=ALU.mult),
                        reads=[PB[qb_], rs_bufs[db_], gq2_buf], writes=[qn_bufs[h]]))

                    def s7():
                        if hf == 0:
                            self.op(dve, lambda: nc.vector.scalar_tensor_tensor(
                                out=Qn[64:96, h, :], in0=PS[qb_][64:96, :], scalar=g_q[64:96, :], in1=rs96[64:96, db_, :], op0=ALU.mult, op1=ALU.mult),
                                reads=[PB[qb_], rs_bufs[db_]], writes=[])
                        else:
                            self.op(dve, lambda: nc.vector.scalar_tensor_tensor(
                                out=t1[64:96, :], in0=PS[qb_][64:96, :], scalar=g_q[64:96, :], in1=cs["rope"][64:96, 0, sl],
                                op0=ALU.mult, op1=ALU.mult), reads=[PB[qb_]], writes=[t_buf])
                            self.op(dve, lambda: nc.vector.scalar_tensor_tensor(
                                out=t2[64:96, :], in0=PS[6][64:96, :], scalar=g_qs[64:96, :], in1=cs["rope"][64:96, 1, sl],
                                op0=ALU.mult, op1=ALU.mult), reads=[PB[6]], writes=[])
                            self.op(dve, lambda: nc.vector.tensor_tensor(out=t1[64:96, :], in0=t1[64:96, :], in1=t2[64:96, :], op=ALU.add),
                                    reads=[], writes=[])
                            self.op(dve, lambda: nc.vector.tensor_tensor(out=Qn[64:96, h, :], in0=t1[64:96, :], in1=rs96[64:96, db_, :], op=ALU.mult),
                                    reads=[rs_bufs[db_]], writes=[])
                        qn_bufs[h].w = self._tok(dve)
                    steps.append(s7)
                    return steps

                iters = []
                for pi, (li, h) in enumerate(pairs):
                    if hf == 1:
                        groups = [(0, TN, list(range(nkt)))]
                    else:
                        groups = [(s2 * 256, 256, [li * 4 + s2 * 2, li * 4 + s2 * 2 + 1]) for s2 in range(2)]
                    for gi, (q0, nq, kts) in enumerate(groups):
                        for ki, kt in enumerate(kts):
                            iters.append((pi, q0, nq, kt, ki == 0, ki == len(kts) - 1, gi == 0 and ki == 0))
                kb_done = {}
                qpend = {}

                def run_q(pi, n):
                    st = qpend.get(pi)
                    while st and n > 0:
                        st.pop(0)()
                        n -= 1

                def emit_k(pi):
                    nonlocal ei
                    li, h = pairs[pi]
                    kbi = pi % 2
                    if hf == 1:
                        kcols = [(0, 256, ckvc, ckvc_buf, 0), (256, 512, ckvn, ckvn_buf, 0), (768, 512, ckvn, ckvn_buf, 512)]
                    else:
                        kcols = [(li * TN, TN, ckvn, ckvn_buf, li * TN)]
                    for ci_, (c0, w, src, srcb, s0) in enumerate(kcols):
                        fns = [lambda rc=rc, src=src, s0=s0, w=w, h=h: nc.tensor.matmul(
                            PS[6][0:64, 0:w], lhsT=self.ring[:, sk_, rc * 512 + h * 64: rc * 512 + (h + 1) * 64],
                            rhs=src[:, rc, s0:s0 + w], start=(rc == 0), stop=(rc == 1)) for rc in range(2)]
                        self.op(pe, fns, reads=[skb, srcb], writes=[PB[6]])
                        if ei % 2 == 0:
                            self.op(act, lambda c0=c0, w=w, kbi=kbi: nc.scalar.activation(out=Kb[0:64, kbi, c0:c0 + w], in_=PS[6][0:64, 0:w], func=AF.Copy),
                                    reads=[PB[6]], writes=[kb_bufs[kbi]] if ci_ == 0 else [])
                        else:
                            self.op(dve, lambda c0=c0, w=w, kbi=kbi: nc.vector.tensor_copy(out=Kb[0:64, kbi, c0:c0 + w], in_=PS[6][0:64, 0:w]),
                                    reads=[PB[6]], writes=[kb_bufs[kbi]] if ci_ == 0 else [])
                        ei += 1
                    kb_done[pi] = [self._tok(act), self._tok(dve)]

                def emit_s(it):
                    nonlocal si
                    pi, q0, nq, kt, first, lastk, newhead = it
                    li, h = pairs[pi]
                    if newhead:
                        run_q(pi, 99)
                        emit_k(pi)
                        if pi + 1 < len(pairs):
                            qpend[pi + 1] = q_steps(pi + 1)
                    sbk = si % 2
                    si += 1
                    for tk_ in kb_done[pi]:
                        pe.wait(tk_)
                    self.op(pe, lambda: nc.tensor.matmul(
                        PS[sbk][:, 0:nq], lhsT=Kb[0:96, pi % 2, kt * 128:(kt + 1) * 128], rhs=Qn[0:96, h, q0:q0 + nq], start=True, stop=True),
                        reads=[kb_bufs[pi % 2], qn_bufs[h]], writes=[PB[sbk]])
                    return sbk

                qpend[0] = q_steps(0)
                nstep = 1 if hf == 1 else 2
                sb_next = emit_s(iters[0])
                for idx, it in enumerate(iters):
                    pi, q0, nq, kt, first, lastk, newhead = it
                    li, h = pairs[pi]
                    sbk = sb_next
                    self.op(act, lambda: nc.scalar.activation(
                        out=PT[:, sbk, 0:nq], in_=PS[sbk][:, 0:nq], func=AF.Exp, scale=rk[:, kt, h:h + 1]),
                        reads=[PB[sbk], rk_buf], writes=[pt_bufs[sbk]])
                    if idx + 1 < len(iters):
                        sb_next = emit_s(iters[idx + 1])
                    ob, dbk = 2, 3
                    self.op(pe, [lambda: nc.tensor.matmul(
                        PS[ob][0:64, 0:nq], lhsT=Vt[:, kt, h * 64:(h + 1) * 64], rhs=PT[:, sbk, 0:nq], start=first, stop=lastk),
                        lambda: nc.tensor.matmul(
                        PS[dbk][0:64, 0:nq], lhsT=self.ones_bf[:, 0:64], rhs=PT[:, sbk, 0:nq], start=first, stop=lastk)],
                        reads=[vt_buf, pt_bufs[sbk], self.const_buf], writes=[PB[ob], PB[dbk]] if first else [])
                    run_q(pi + 1, nstep)
                    if lastk:
                        tk = self._tok(pe)
                        PB[ob].w = tk
                        PB[dbk].w = tk
                        self.op(act, lambda: nc.scalar.activation(out=rden[:, 0, 0:nq], in_=PS[dbk][0:64, 0:nq], func=AF.Ln),
                                reads=[PB[dbk]], writes=[rden_bufs[0]])
                        self.op(act, lambda: nc.scalar.activation(out=rden[:, 0, 0:nq], in_=rden[:, 0, 0:nq], func=AF.Exp, scale=-1.0),
                                reads=[], writes=[rden_bufs[0]])
                        self.op(dve, lambda: nc.vector.tensor_tensor(
                            out=mlaT[0:64, h, li * TN + q0: li * TN + q0 + nq], in0=PS[ob][0:64, 0:nq], in1=rden[:, 0, 0:nq], op=ALU.mult),
                            reads=[PB[ob], rden_bufs[0]], writes=[mla_buf])
                self.barrier()
            wo = self.even_w_out[j]
            sa_, sab = self.ring_load([(0, [4, D], wout[:, 0:4, :])])
            sb1, sbb1 = self.ring_load([(0, [4, D], wo[512:768, :].rearrange("(h p) d -> p h d", p=64), 64)])
            sb2, sbb2 = self.ring_load([(0, [4, D], wo[768:1024, :].rearrange("(h p) d -> p h d", p=64), 64)])
            di = 0
            for li in range(2):
                t = 2 * hf + li
                jc = hf
                sl = slice(li * TN, (li + 1) * TN)
                for dc in range(KC):
                    bank = 4 + di % 3
                    di += 1
                    fns = [lambda c=c, dc=dc, bank=bank: nc.tensor.matmul(
                        PS[bank][:], lhsT=self.ring[:, sa_, c * 1024 + dc * 128: c * 1024 + (dc + 1) * 128], rhs=ogT[:, c, sl],
                        start=(c == 0), stop=False) for c in range(4)]
                    for hh_ in range(8):
                        sx = sb1 if hh_ < 4 else sb2
                        fns.append(lambda hh_=hh_, sx=sx, dc=dc, bank=bank: nc.tensor.matmul(
                            PS[bank][:], lhsT=self.ring[0:64, sx, (hh_ % 4) * 1024 + dc * 128: (hh_ % 4) * 1024 + (dc + 1) * 128],
                            rhs=mlaT[0:64, hh_, sl], start=False, stop=(hh_ == 7)))
                    self.op(pe, fns, reads=[sab, sbb1, sbb2, og_buf, mla_buf], writes=[PB[bank]])
                    xs = self.xT[:, dc, t * TN:(t + 1) * TN]
                    self.op(dve, lambda xs=xs, dc=dc, jc=jc, bank=bank: nc.vector.scalar_tensor_tensor(
                        out=xs, in0=PS[bank][:], scalar=self.modG[:, k, dc, jc:jc + 1], in1=xs,
                        op0=ALU.mult, op1=ALU.add), reads=[PB[bank], self.mod_buf], writes=[self.x_bufs[t]])
            self.barrier()
            for e_ in (pe, act, dve):
                e_.wait((self.st3[0], self.st3[1]))

    def rms_rstd_w(self, sq, sq_buf, rows, nfeat, out, out_buf, w):
        nc = self.nc
        self.op(self.pe, lambda: nc.tensor.matmul(self.psum[7][0:rows, 0:w], lhsT=self.ones_bf[0:rows, 0:rows], rhs=sq[0:rows, 0, 0:w],
                                                  start=True, stop=True), reads=[sq_buf, self.const_buf], writes=[self.pbuf[7]])
        self.op(self.act, lambda: nc.scalar.activation(out=out[0:rows, 0:w], in_=self.psum[7][0:rows, 0:w], func=AF.Ln, scale=1.0 / nfeat, bias=EPS),
                reads=[self.pbuf[7]], writes=[out_buf])
        self.op(self.act, lambda: nc.scalar.activation(out=out[0:rows, 0:w], in_=out[0:rows, 0:w], func=AF.Exp, scale=-0.5),
                reads=[], writes=[out_buf])

    def rms_rstd(self, sq, sq_buf, nch, rows, nfeat, out, out_buf):
        nc = self.nc
        fns = [lambda c=c: nc.tensor.matmul(self.psum[7][0:rows, :], lhsT=self.ones_bf[0:rows, 0:rows], rhs=sq[0:rows, c, :],
                                            start=(c == 0), stop=(c == nch - 1)) for c in range(nch)]
        self.op(self.pe, fns, reads=[sq_buf, self.const_buf], writes=[self.pbuf[7]])
        self.op(self.act, lambda: nc.scalar.activation(out=out[0:rows, :], in_=self.psum[7][0:rows, :], func=AF.Ln, scale=1.0 / nfeat, bias=EPS),
                reads=[self.pbuf[7]], writes=[out_buf])
        self.op(self.act, lambda: nc.scalar.activation(out=out[0:rows, :], in_=out[0:rows, :], func=AF.Exp, scale=-0.5),
                reads=[], writes=[out_buf])

    def odd_mixer(self, l):
        nc = self.nc
        j = l // 2
        k = 1
        win = self.odd_w_in[j].rearrange("(kc p) f -> p kc f", p=128)
        wout = self.odd_w_out[j].rearrange("(c p) d -> p c d", p=128)
        with contextlib.ExitStack() as st:
            wv = self.sb("odd_wv", [128, KC, 2048], BF16, st)
            wv_buf = self.buf("wv")
            hT = self.sb("odd_hT", [128, KC, TN], BF16, st)
            h_bufs = [self.buf("h")]
            mix = self.sb("odd_mix", [128, 16, TN], BF16, st)
            mix_buf = self.buf("mix")
            gv = self.sb("odd_gv", [128, 2048], BF16, st)
            gv_buf = self.buf("gv")
            sqf = self.sb("odd_sqf", [128, 2, 512], F32, st)
            sq_bufs = [self.buf("sqf"), self.buf("sqf")]
            ss = self.sb("odd_ss", [128, 16], F32, st)
            ss_bufs = [self.buf("ss"), self.buf("ss")]
            wp = self.sb("odd_wp", [128, 4, 128], BF16, st)
            wp_buf = self.buf("wp")
            ut = self.sb("odd_u", [128, 2, TN], F32, st)
            u_bufs = [self.buf("u"), self.buf("u")]
            tmp, rstd = self.norm_tmp(st)
            for kc in range(KC):
                self.dma(self.pool, self.wv_sem, wv[:, kc, :], win[:, kc, 2048:4096], writes=[wv_buf] if kc == 0 else [])
            wv_buf.w = (self.wv_sem[0], self.wv_sem[1])
            bi = 0
            mi = 0
            for t in range(NT):
                jc = 0 if t < NT // 2 else 1
                self.norm_modulate(k, [t], hT, h_bufs, tmp, rstd)
                gvs = [gv, tmp["sq"][:].rearrange("p a b -> p (a b)")]
                gvb = [gv_buf, tmp["buf"]]

                def vfront(q4):
                    nonlocal bi
                    gq, gqb = gvs[q4 % 2], gvb[q4 % 2]
                    so = (q4 % 2) * 8
                    for g in range(4):
                        bank = bi % 4
                        bi += 1
                        fns = [lambda kc=kc, g=g, bank=bank: nc.tensor.matmul(
                            self.psum[bank][:], lhsT=hT[:, kc, q4 * 128:(q4 + 1) * 128],
                            rhs=wv[:, kc, g * 512:(g + 1) * 512], start=(kc == 0), stop=(kc == KC - 1)) for kc in range(KC)]
                        self.op(self.pe, fns, reads=[h_bufs[0], wv_buf], writes=[self.pbuf[bank]])
                        self.op(self.act, [
                            lambda g=g, bank=bank: nc.scalar.activation(out=gq[:, g * 512:(g + 1) * 512], in_=self.psum[bank][:],
                                                                        func=AF.Gelu_apprx_tanh),
                            lambda g=g, bank=bank: nc.scalar.activation(out=sqf[:, g % 2, :], in_=gq[:, g * 512:(g + 1) * 512],
                                                                        func=AF.Square)],
                            reads=[self.pbuf[bank]], writes=([gqb] if g == 0 else []) + [sq_bufs[g % 2]])
                        self.op(self.dve, lambda g=g: nc.vector.tensor_reduce(
                            out=ss[:, so + g:so + g + 1], in_=sqf[:, g % 2, :], axis=mybir.AxisListType.X, op=ALU.add),
                            reads=[sq_bufs[g % 2]], writes=[ss_bufs[q4 % 2]])
                    gqb.w = (self.act.sems[self.act.si], self.act.cnt)

                def vback(q4):
                    nonlocal mi
                    gq, gqb = gvs[q4 % 2], gvb[q4 % 2]
                    so = (q4 % 2) * 8
                    sb_ = ss_bufs[q4 % 2]
                    self.op(self.dve, lambda: nc.vector.tensor_reduce(out=ss[:, so + 4:so + 5], in_=ss[:, so:so + 4], axis=mybir.AxisListType.X, op=ALU.add),
                            reads=[], writes=[sb_], same=True)
                    self.op(self.act, lambda: nc.scalar.activation(out=ss[:, so + 5:so + 6], in_=ss[:, so + 4:so + 5], func=AF.Ln, scale=1.0 / 2048, bias=EPS),
                            reads=[], writes=[sb_], same=True)
                    self.op(self.act, lambda: nc.scalar.activation(out=ss[:, so + 6:so + 7], in_=ss[:, so + 5:so + 6], func=AF.Exp, scale=-0.5),
                            reads=[], writes=[sb_], same=True)
                    self.op(self.dve, lambda: nc.vector.tensor_scalar(
                        out=wp[:].rearrange("p a b -> p (a b)"), in0=self.wsT[:, j].rearrange("p a b -> p (a b)"),
                        scalar1=ss[:, so + 6:so + 7], scalar2=None, op0=ALU.mult), reads=[sb_], writes=[wp_buf])
                    for g in range(4):
                        bank = 4 + mi % 2
                        mi += 1
                        fns = [lambda g=g, cc=cc, bank=bank: nc.tensor.matmul(
                            self.psum[bank][:, cc * 128:(cc + 1) * 128], lhsT=gq[:, (g * 4 + cc) * 128:(g * 4 + cc + 1) * 128],
                            rhs=wp[:, g, :], start=True, stop=True) for cc in range(4)]
                        self.op(self.pe, fns, reads=[gqb, wp_buf], writes=[self.pbuf[bank]])
                        for cc in range(4):
                            c16 = g * 4 + cc
                            self.op(self.dve, lambda g=g, cc=cc, c16=c16, bank=bank: nc.vector.scalar_tensor_tensor(
                                out=mix[:, c16, q4 * 128:(q4 + 1) * 128], in0=self.psum[bank][:, cc * 128:(cc + 1) * 128],
                                scalar=self.oddvg[:, j, c16:c16 + 1], in1=self.bsb[:, j, g, :], op0=ALU.mult, op1=ALU.add),
                                reads=[self.pbuf[bank]], writes=[mix_buf] if (q4 == 0 and c16 == 0) else [])

                vfront(0)
                for q4 in range(4):
                    if q4 + 1 < 4:
                        vfront(q4 + 1)
                    vback(q4)
                mix_buf.w = (self.dve.sems[self.dve.si], self.dve.cnt)
                for sl in range(4):
                    s, sbuf_ = self.ring_load([(0, [KC, 512], win[:, :, sl * 512:(sl + 1) * 512])])
                    for cc in range(4):
                        c16 = sl * 4 + cc
                        bank = bi % 4
                        bi += 1
                        fns = [lambda kc=kc, cc=cc, s=s, bank=bank: nc.tensor.matmul(
                            self.psum[bank][:], lhsT=self.ring[:, s, kc * 512 + cc * 128: kc * 512 + (cc + 1) * 128],
                            rhs=hT[:, kc, :], start=(kc == 0), stop=(kc == KC - 1)) for kc in range(KC)]
                        self.op(self.pe, fns, reads=[sbuf_, h_bufs[0]], writes=[self.pbuf[bank]])
                        ub = c16 % 2
                        self.op(self.act, lambda ub=ub, bank=bank: nc.scalar.activation(
                            out=ut[:, ub, :], in_=self.psum[bank][:], func=AF.Gelu_apprx_tanh),
                            reads=[self.pbuf[bank]], writes=[u_bufs[ub]])
                        self.op(self.dve, lambda ub=ub, c16=c16: nc.vector.tensor_tensor(
                            out=mix[:, c16, :], in0=ut[:, ub, :], in1=mix[:, c16, :], op=ALU.mult),
                            reads=[u_bufs[ub]], writes=[mix_buf] if c16 == 0 else [])
                mix_buf.w = (self.dve.sems[self.dve.si], self.dve.cnt)
                mix_buf.r = {}
                for dc in range(KC):
                    if dc % 2 == 0:
                        so, sob = self.ring_load([(0, [16, 256], wout[:, :, dc * 128:(dc + 2) * 128])])
                    bank = 4 + mi % 3
                    mi += 1
                    fns = [lambda c16=c16, dc=dc, bank=bank, so=so: nc.tensor.matmul(
                        self.psum[bank][:], lhsT=self.ring[:, so, c16 * 256 + (dc % 2) * 128: c16 * 256 + (dc % 2 + 1) * 128],
                        rhs=mix[:, c16, :], start=(c16 == 0), stop=(c16 == 15)) for c16 in range(16)]
                    self.op(self.pe, fns, reads=[sob, mix_buf], writes=[self.pbuf[bank]])
                    xs = self.xT[:, dc, t * TN:(t + 1) * TN]
                    self.op(self.dve, lambda xs=xs, dc=dc, jc=jc, bank=bank: nc.vector.scalar_tensor_tensor(
                        out=xs, in0=self.psum[bank][:], scalar=self.modG[:, k, dc, jc:jc + 1], in1=xs,
                        op0=ALU.mult, op1=ALU.add), reads=[self.pbuf[bank], self.mod_buf], writes=[self.x_bufs[t]])

    def mod_blocks(self, l, par, cbs, first=False):
        nc = self.nc
        aw = self.ada_w[l].rearrange("(kc p) f -> p kc f", p=128)
        pb = self.pbuf[7]
        psv = self.psum[7][:, 0:144].rearrange("p (c j) -> p c j", j=2)
        for cb in cbs:
            s, sbuf_ = self.ring_load([(0, [KC, 512], aw[:, :, cb * 512:(cb + 1) * 512])])
            fns = []
            for cc in range(4):
                c = cb * 4 + cc
                for kc in range(KC):
                    fns.append(lambda c=c, cc=cc, kc=kc, s=s: nc.tensor.matmul(
                        psv[:, c, :], lhsT=self.ring[:, s, kc * 512 + cc * 128: kc * 512 + (cc + 1) * 128],
                        rhs=self.scond[:, kc, :], start=(kc == 0), stop=(kc == KC - 1)))
            self.op(self.pe, fns, reads=[sbuf_, self.scond_buf], writes=[pb] if first else [])
            first = False
            pb.w = (self.pe.sems[self.pe.si], self.pe.cnt)

    def mod_raw(self, l, par, ranges):
        nc = self.nc
        pb = self.pbuf[7]
        psv = self.psum[7][:, 0:144].rearrange("p (c j) -> p c j", j=2)
        mod, mod_buf = self.mod2[par], self.mod_bufs[par]
        for (c0, c1) in ranges:
            for j in range(2):
                self.op(self.dve, lambda j=j, c0=c0, c1=c1: nc.vector.tensor_tensor(
                    out=mod[:, c0:c1, j], in0=psv[:, c0:c1, j], in1=self.adab[:, l, c0:c1], op=ALU.add),
                    reads=[pb], writes=[mod_buf])

    def mod_derive(self, l, par, ks):
        nc = self.nc
        mod, modA, modB, modG, mod_buf = self.mod2[par], self.modA2[par], self.modB2[par], self.modG2[par], self.mod_bufs[par]
        for k in ks:
            sh = mod[:, (3 * k) * 8:(3 * k) * 8 + 8, :]
            sc = mod[:, (3 * k + 1) * 8:(3 * k + 1) * 8 + 8, :]
            gt = mod[:, (3 * k + 2) * 8:(3 * k + 2) * 8 + 8, :]
            for j in range(2):
                self.op(self.dve, lambda k=k, j=j, sc=sc: nc.vector.scalar_tensor_tensor(
                    out=modA[:, k, :, j], in0=sc[:, :, j], scalar=1.0, in1=self.normg[:, l, k, :],
                    op0=ALU.add, op1=ALU.mult), reads=[mod_buf], writes=[mod_buf], same=True)
            self.op(self.dve, lambda k=k, sh=sh: nc.vector.tensor_copy(out=modB[:, k, :, :], in_=sh), reads=[mod_buf], writes=[mod_buf], same=True)
            self.op(self.dve, lambda k=k, gt=gt: nc.vector.tensor_scalar(
                out=modG[:, k, :, :], in0=gt, scalar1=(1.0 if k == 1 else 0.5), scalar2=None, op0=ALU.mult),
                reads=[mod_buf], writes=[mod_buf], same=True)

    def norm_modulate(self, k, tiles, hT, h_bufs, tmp, rstd):
        for li, t in enumerate(tiles):
            self.norm_modulate_tile(k, t, li, hT, h_bufs, tmp, rstd)

    def norm_modulate_tile(self, k, t, li, hT, h_bufs, tmp, rstd):
        nc = self.nc
        if True:
            j = 0 if t < NT // 2 else 1
            xs = self.xT[:, :, t * TN:(t + 1) * TN]
            xb = self.x_bufs[t]
            tb = tmp["buf"]
            self.op(self.act, lambda xs=xs: nc.scalar.activation(out=tmp["sq"][:], in_=xs, func=AF.Square),
                    reads=[xb], writes=[tb])
            fns = [lambda kc=kc: nc.tensor.matmul(self.psum[7][:], lhsT=self.ones_bf[:], rhs=tmp["sq"][:, kc, :],
                                                  start=(kc == 0), stop=(kc == KC - 1)) for kc in range(KC)]
            self.op(self.pe, fns, reads=[tb, self.const_buf], writes=[self.pbuf[7]])
            rb = rstd["buf"]
            self.op(self.act, [lambda: nc.scalar.activation(out=rstd["t"][:], in_=self.psum[7][:], func=AF.Ln,
                                                            scale=1.0 / D, bias=EPS),
                               lambda: nc.scalar.activation(out=rstd["t"][:], in_=rstd["t"][:], func=AF.Exp, scale=-0.5)],
                    reads=[self.pbuf[7]], writes=[rb])
            for kc in range(KC):
                t2 = tmp["t2buf"][kc % 2]
                self.op(self.dve, lambda kc=kc, t=t, j=j: nc.vector.scalar_tensor_tensor(
                    out=tmp["t2"][:, kc % 2, :], in0=self.xT[:, kc, t * TN:(t + 1) * TN],
                    scalar=self.modA[:, k, kc, j:j + 1], in1=rstd["t"][:], op0=ALU.mult, op1=ALU.mult),
                    reads=[xb, rb, self.mod_buf], writes=[t2])
                self.op(self.act, lambda kc=kc, li=li, j=j: nc.scalar.activation(
                    out=hT[:, kc, li * TN:(li + 1) * TN], in_=tmp["t2"][:, kc % 2, :], func=AF.Identity,
                    bias=self.modB[:, k, kc, j:j + 1], scale=1.0),
                    reads=[t2, self.mod_buf], writes=[h_bufs[li]] if kc == 0 else [])
            h_bufs[li].w = (self.act.sems[self.act.si], self.act.cnt)

    def norm_tmp(self, stack):
        tmp = {"sq": self.sb("n_sq", [128, KC, TN], BF16, stack), "buf": self.buf("nsq"),
               "t2": self.sb("n_t2", [128, 2, TN], F32, stack), "t2buf": [self.buf("nt2"), self.buf("nt2")]}
        rstd = {"t": self.sb("n_rstd", [128, TN], F32, stack), "buf": self.buf("rstd")}
        return tmp, rstd

    def ffn(self, l, k, wg, wu, wd, hook=None):
        nc = self.nc
        with contextlib.ExitStack() as st:
            hT = self.sb("ffn_hT", [128, KC, NTOK], BF16, st)
            h_bufs = [self.buf("h") for _ in range(NT)]
            aT = self.sb("ffn_aT", [128, 2, GCH, TN], BF16, st)
            a_bufs = [self.buf("a"), self.buf("a")]
            sg = self.sb("ffn_sg", [128, 2, TN], F32, st)
            sg_bufs = [self.buf("sg"), self.buf("sg")]
            tmp, rstd = self.norm_tmp(st)
            self.norm_modulate(k, list(range(NT)), hT, h_bufs, tmp, rstd)
            wgl = wg[l].rearrange("(kc p) f -> p kc f", p=128)
            wul = wu[l].rearrange("(kc p) f -> p kc f", p=128)
            wdl = wd[l].rearrange("(c p) d -> p c d", p=128)
            pend = None
            u = 0
            ci = 0
            dn = 0

            def down_step(pd, dc):
                nonlocal dn
                s, sbuf_, t, ab, au = pd
                j = 0 if t < NT // 2 else 1
                if True:
                    bank = 4 + dn % 3
                    dn += 1
                    fns = [lambda c=c, dc=dc, s=s, au=au, bank=bank: nc.tensor.matmul(
                        self.psum[bank][:], lhsT=self.ring[:, s, 4096 + c * 1024 + dc * 128: 4096 + c * 1024 + (dc + 1) * 128],
                        rhs=aT[:, au, c, :], start=(c == 0), stop=(c == GCH - 1)) for c in range(GCH)]
                    self.op(self.pe, fns, reads=[sbuf_, ab], writes=[self.pbuf[bank]])
                    xs = self.xT[:, dc, t * TN:(t + 1) * TN]
                    self.op(self.dve, lambda xs=xs, dc=dc, j=j, bank=bank: nc.vector.scalar_tensor_tensor(
                        out=xs, in0=self.psum[bank][:], scalar=self.modG[:, k, dc, j:j + 1], in1=xs,
                        op0=ALU.mult, op1=ALU.add), reads=[self.pbuf[bank], self.mod_buf], writes=[self.x_bufs[t]])

            psteps = []

            def run_down(n):
                for _ in range(n):
                    if psteps:
                        pd_, dc_ = psteps.pop(0)
                        down_step(pd_, dc_)

            for g in range(NG):
                s, sbuf_ = self.ring_load([
                    (0, [KC, GCH * 128], wgl[:, :, g * GCH * 128:(g + 1) * GCH * 128]),
                    (2048, [KC, GCH * 128], wul[:, :, g * GCH * 128:(g + 1) * GCH * 128]),
                    (4096, [GCH, D], wdl[:, g * GCH:(g + 1) * GCH, :]),
                ])
                for t in range(NT):
                    au = u % 2
                    ab = a_bufs[au]
                    for c in range(GCH):
                        gb = ci % 2
                        ub = 2 + ci % 2
                        ci += 1
                        fns = [lambda kc=kc, c=c, s=s, t=t, gb=gb: nc.tensor.matmul(
                            self.psum[gb][:], lhsT=self.ring[:, s, kc * 256 + c * 128: kc * 256 + (c + 1) * 128],
                            rhs=hT[:, kc, t * TN:(t + 1) * TN], start=(kc == 0), stop=(kc == KC - 1)) for kc in range(KC)]
                        self.op(self.pe, fns, reads=[sbuf_, h_bufs[t]], writes=[self.pbuf[gb]])
                        run_down(KC // (2 * GCH))
                        fns = [lambda kc=kc, c=c, s=s, t=t, ub=ub: nc.tensor.matmul(
                            self.psum[ub][:], lhsT=self.ring[:, s, 2048 + kc * 256 + c * 128: 2048 + kc * 256 + (c + 1) * 128],
                            rhs=hT[:, kc, t * TN:(t + 1) * TN], start=(kc == 0), stop=(kc == KC - 1)) for kc in range(KC)]
                        self.op(self.pe, fns, reads=[sbuf_, h_bufs[t]], writes=[self.pbuf[ub]])
                        run_down(KC // (2 * GCH))
                        self.op(self.act, lambda gb=gb: nc.scalar.activation(out=sg[:, gb, :], in_=self.psum[gb][:], func=AF.Silu),
                                reads=[self.pbuf[gb]], writes=[sg_bufs[gb]])
                        self.op(self.dve, lambda gb=gb, ub=ub, au=au, c=c: nc.vector.tensor_tensor(
                            out=aT[:, au, c, :], in0=self.psum[ub][:], in1=sg[:, gb, :], op=ALU.mult),
                            reads=[self.pbuf[ub], sg_bufs[gb]], writes=[ab] if c == 0 else [])
                    ab.w = (self.dve.sems[self.dve.si], self.dve.cnt)
                    run_down(KC)
                    pend = (s, sbuf_, t, ab, au)
                    psteps.extend((pend, dc) for dc in range(KC))
                    u += 1
                if hook is not None:
                    hook(g)
            run_down(KC)
            self.barrier()


def _host_layout(inputs, core):
    i = core
    xp = np.asarray(inputs["x_prompt"])[4 * i:4 * i + 4].reshape(1024, D)
    xs = np.asarray(inputs["x_sample"])[i]
    xT = np.ascontiguousarray(np.concatenate([xp, xs], axis=0).T)
    cond = np.stack([np.asarray(inputs["c_ctx"]), np.asarray(inputs["c"])[i]], axis=-1)
    condT = np.ascontiguousarray(cond.reshape(KC, 128, 2).transpose(1, 0, 2))
    return {"xT": xT, "condT": condT,
            "ckvcT": np.ascontiguousarray(np.asarray(inputs["cache_ckv"])[i].transpose(0, 2, 1)),
            "kropecT": np.ascontiguousarray(np.asarray(inputs["cache_krope"])[i].transpose(0, 2, 1)),
            "sgla": np.ascontiguousarray(np.asarray(inputs["state_gla"])[i])}


def _const_tables():
    idx = np.arange(128)
    same = (idx[:, None] // 64) == (idx[None, :] // 64)
    le = idx[:, None] <= idx[None, :]
    ge = idx[:, None] >= idx[None, :]
    gt = idx[:, None] > idx[None, :]
    lt = idx[:, None] < idx[None, :]
    c = np.float32(-1.0 / 16.0)
    gm = np.zeros((128, 6, 128), np.float32)
    gm[:, 0] = np.where(same & le, c, 0)
    gm[:, 1] = np.where(same & ge, c, 0)
    gm[:, 2] = np.where(same & gt, c, 0)
    gm[:, 3] = np.where(same & lt, c, 0)
    gm[:, 4] = np.where(same & le, 1, 0)
    gm[:, 5] = np.where(same & ge, 1, 0)
    m4 = np.zeros((128, 2, 4, 128), np.float32)
    m4[:, 0] = gm[:, 4][:, None, :]
    m4[:, 1] = gm[:, 5][:, None, :]
    pos = np.arange(1024)
    row = (pos // 64).astype(np.float32)
    col = (pos % 64).astype(np.float32)
    inv = (np.float32(10000.0) ** (-np.arange(8, dtype=np.float32) / np.float32(8))).astype(np.float32)
    ang = np.concatenate([row[:, None] * inv, col[:, None] * inv], axis=-1).astype(np.float32)
    cosv, sinv = np.cos(ang).astype(np.float32), np.sin(ang).astype(np.float32)
    rope = np.zeros((96, 2, 1024), np.float32)
    for f in range(32):
        rope[64 + f, 0] = cosv[:, f // 2]
        rope[64 + f, 1] = sinv[:, f // 2] * (-1.0 if f % 2 == 0 else 1.0)
    return gm, m4, rope


def _shared_layout(inputs):
    sh = {}
    A = lambda n: np.asarray(inputs[n], dtype=np.float32)
    ada_b = A("ada_b")
    sh["adab"] = np.ascontiguousarray(ada_b.reshape(DEPTH, 72, 128).transpose(2, 0, 1))
    ng = A("norm_g")
    sh["normg"] = np.ascontiguousarray(ng.reshape(DEPTH, 3, KC, 128).transpose(3, 0, 1, 2))
    vg = A("odd_v_g")
    sh["oddvg"] = np.ascontiguousarray(vg.reshape(2, 16, 128).transpose(2, 0, 1))
    sh["wsT"] = np.ascontiguousarray(A("odd_ws").transpose(3, 0, 1, 2))
    sh["bsb"] = np.ascontiguousarray(np.broadcast_to(A("odd_bs")[None], (128, 2, 4, 128)))
    for n in ("odd_w_in", "odd_w_out", "ada_w", "ffn1_wg", "ffn1_wu", "ffn1_wd", "ffn2_wg", "ffn2_wu", "ffn2_wd",
              "even_w_in", "even_w_out"):
        sh[n] = np.ascontiguousarray(A(n))
    swap = np.arange(32) ^ 1
    win = A("even_w_in")
    wsw = np.zeros((2, D, 96), np.float32)
    wsw[:, :, 64:96] = win[:, :, 2208:2240][:, :, swap]
    sh["win_sw"] = wsw
    qb = A("mla_qb_w")
    sh["qbw"] = np.ascontiguousarray(qb.reshape(2, 384, 768))
    qs = qb.copy()
    qs[..., 64:96] = qb[..., 64:96][..., swap]
    sh["qbw_sw"] = np.ascontiguousarray(qs.reshape(2, 384, 768))
    kvb = A("mla_kvb_w")
    sh["kvbK"] = np.ascontiguousarray(kvb[..., :64].reshape(2, 256, 512))
    sh["kvbV"] = np.ascontiguousarray(kvb[..., 64:].reshape(2, 256, 512))
    gm, m4, rope = _const_tables()
    sh["gmask"], sh["mask4"], sh["rope"] = np.ascontiguousarray(gm[:, 0:4]), m4, rope
    qn, kn = A("mla_qn_g"), A("mla_kn_g")
    sw96 = np.arange(96)
    sw96[64:96] = 64 + swap
    sh["qkng"] = np.ascontiguousarray(np.stack([qn, qn[:, sw96], kn, kn[:, sw96]], axis=-1).transpose(1, 0, 2))
    sh["glag"] = np.ascontiguousarray(A("gla_norm_g").T)
    sh["qag"] = np.ascontiguousarray(A("mla_qa_g").reshape(2, 3, 128).transpose(2, 0, 1))
    sh["kvag"] = np.ascontiguousarray(A("mla_kva_g").reshape(2, 2, 128).transpose(2, 0, 1))
    w2 = A("gla_gate_w2")
    gb = A("gla_gate_b")
    w2aug = np.zeros((33, 2, 2, 256), np.float32)
    w2aug[0:16, :, 0, :] = w2[:, 0].transpose(1, 0, 2)
    w2aug[16:32, :, 1, :] = w2[:, 1].transpose(1, 0, 2)
    w2aug[32] = gb
    sh["w2aug"] = w2aug
    return sh


def run(inputs, cfg, cores=8, trace=False):
    b = Builder(cfg)
    nc = b.build()
    sh = _shared_layout(inputs)
    in_maps = []
    for i in range(cores):
        m = dict(sh)
        m.update(_host_layout(inputs, i))
        in_maps.append(m)
    res = run_bass_kernel_spmd(nc, in_maps, core_ids=list(range(cores)), trace=trace)
    return res


def kernel(**inputs):
    res = run(inputs, {})
    B, S, DB, DS = 32, 256, 8, 1024
    y_prompt = np.zeros((B, S, D), np.float32)
    y_sample = np.zeros((DB, DS, D), np.float32)
    new_ckv = np.zeros((B, 2, S, 256), np.float32)
    new_krope = np.zeros((B, 2, S, 32), np.float32)
    new_gla = np.zeros((B, 2, 2, 4, 64, 128), np.float32)
    for i, r in enumerate(res.results):
        yT = np.asarray(r["yT"])
        y_prompt[4 * i:4 * i + 4] = yT[:, :1024].T.reshape(4, S, D)
        y_sample[i] = yT[:, 1024:].T
        ck = np.asarray(r["ckvT_o"])
        new_ckv[4 * i:4 * i + 4] = ck.reshape(2, 256, 4, S).transpose(2, 0, 3, 1)
        kr = np.asarray(r["kropeT_o"])
        new_krope[4 * i:4 * i + 4] = kr.reshape(2, 32, 4, S).transpose(2, 0, 3, 1)
        new_gla[4 * i:4 * i + 4] = np.asarray(r["gla_o"])
    return (y_prompt, y_sample, new_ckv, new_krope, new_gla)


def check_states(r, states, inp):
    for (l, st) in states:
        j = l // 2
        ck = np.asarray(r["ckvT_o"])[j].reshape(256, 4, 256).transpose(1, 2, 0)
        kr = np.asarray(r["kropeT_o"])[j].reshape(32, 4, 256).transpose(1, 2, 0)
        gl = np.asarray(r["gla_o"])[:, j]
        for nm, a, b in (("ckv", ck, np.asarray(st[0])), ("krope", kr, np.asarray(st[1])), ("gla", gl, np.asarray(st[2]))):
            print("state", nm, "layer", l, "relvar", ((a - b) ** 2).mean() / (b ** 2).mean())
```

```python
import contextlib
import numpy as np
import concourse.bass as bass
import concourse.mybir as mybir
from concourse.bass_utils import run_bass_kernel_spmd

F32 = mybir.dt.float32
BF16 = mybir.dt.bfloat16
AF = mybir.ActivationFunctionType
ALU = mybir.AluOpType

D = 1024
KC = 8
NTOK = 2048
TN = 512
NT = NTOK // TN
DEPTH = 4
FH = 2816
FC = FH // 128
GCH = 2
NG = FC // GCH
EPS = 1e-6
SLOT = 6144
NSLOT = 3
SEM_LIMIT = 30000


class Buf:
    __slots__ = ("name", "w", "r")

    def __init__(self, name):
        self.name = name
        self.w = None
        self.r = {}


class Eng:
    def __init__(self, nc, h, name, nsem):
        self.nc = nc
        self.h = h
        self.name = name
        self.sems = [nc.alloc_semaphore(name=f"e_{name}_{i}") for i in range(nsem)]
        self.si = 0
        self.cnt = 0
        self.seen = {}
        self.own = set(id(s) for s in self.sems)

    def wait(self, tok, same=False):
        if tok is None:
            return
        sem, val = tok
        if id(sem) in self.own and not same:
            return
        k = id(sem)
        if self.seen.get(k, 0) >= val:
            return
        self.h.wait_ge(sem, val)
        self.seen[k] = val

    def signal(self, ins):
        sem = self.sems[self.si]
        ins.then_inc(sem, 1)
        self.cnt += 1
        tok = (sem, self.cnt)
        if self.cnt >= SEM_LIMIT:
            self.si += 1
            self.cnt = 0
        return tok


class Builder:
    def __init__(self, cfg):
        self.cfg = cfg
        nc = bass.Bass("TRN2", target_bir_lowering=False)
        self.nc = nc
        self.pe = Eng(nc, nc.tensor, "pe", 1)
        self.act = Eng(nc, nc.scalar, "act", 2)
        self.dve = Eng(nc, nc.vector, "dve", 2)
        self.pool = Eng(nc, nc.gpsimd, "pool", 0)
        self.sp = Eng(nc, nc.sync, "sp", 0)
        self.es = contextlib.ExitStack()
        self.dram = {}
        self.nbuf = 0

    def din(self, name, shape, dt=F32):
        t = self.nc.dram_tensor(name, list(shape), dt, kind="ExternalInput").ap()
        self.dram[name] = t
        return t

    def dout(self, name, shape, dt=F32):
        t = self.nc.dram_tensor(name, list(shape), dt, kind="ExternalOutput").ap()
        self.dram[name] = t
        return t

    def sb(self, name, shape, dt, stack=None):
        self.nbuf += 1
        return (stack or self.es).enter_context(self.nc.sbuf_tensor(f"{name}_{self.nbuf}", list(shape), dt))

    def buf(self, name="b"):
        self.nbuf += 1
        return Buf(f"{name}{self.nbuf}")

    def op(self, eng, fns, reads=(), writes=(), same=False):
        for b in reads:
            eng.wait(b.w, same)
        for b in writes:
            eng.wait(b.w, same)
            for t in b.r.values():
                eng.wait(t, same)
        if not isinstance(fns, (list, tuple)):
            fns = [fns]
        ins = None
        for f in fns:
            ins = f()
        tok = eng.signal(ins)
        for b in reads:
            b.r[eng.name] = tok
        for b in writes:
            b.w = tok
            b.r = {}
        return tok

    def dma(self, q, sem_state, out, in_, reads=(), writes=(), **kw):
        for t in getattr(self, "last_bar", []):
            q.wait(t)
        for b in reads:
            q.wait(b.w)
        for b in writes:
            q.wait(b.w)
            for t in b.r.values():
                q.wait(t)
        q.h.dma_start(out=out, in_=in_, **kw).then_inc(sem_state[0], 16)
        sem_state[1] += 16
        tok = (sem_state[0], sem_state[1])
        for b in reads:
            b.r["dma_" + q.name] = tok
        for b in writes:
            b.w = tok
            b.r = {}
        return tok

    def newsem(self, name):
        return [self.nc.alloc_semaphore(name=name), 0]

    def barrier(self, engs=None):
        engs = engs or [self.pe, self.act, self.dve]
        toks = []
        for e in engs:
            if e.cnt > 0:
                toks.append((e.sems[e.si], e.cnt))
        for e in engs:
            for t in toks:
                e.wait(t)
        self.last_bar = toks

    def ring_init(self):
        self.ring = self.sb("ring", [128, NSLOT, SLOT], BF16)
        self.ring_bufs = [self.buf("slot") for _ in range(NSLOT)]
        self.ring_sems = [self.newsem(f"ring{i}") for i in range(NSLOT)]
        self.ring_i = 0

    def ring_load(self, pieces):
        s = self.ring_i % NSLOT
        self.ring_i += 1
        b = self.ring_bufs[s]
        for piece in pieces:
            off, dims, src = piece[0], piece[1], piece[2]
            npart = piece[3] if len(piece) > 3 else 128
            n = int(np.prod(dims))
            dst = self.ring[0:npart, s, off:off + n]
            if len(dims) == 2:
                dst = dst.rearrange("p (a b) -> p a b", a=dims[0])
            for t in b.r.values():
                self.pool.wait(t)
            self.pool.h.dma_start(out=dst, in_=src).then_inc(self.ring_sems[s][0], 16)
            self.ring_sems[s][1] += 16
        b.w = (self.ring_sems[s][0], self.ring_sems[s][1])
        b.r = {}
        return s, b

    def build(self):
        cfg = self.cfg
        nc = self.nc
        depth = cfg.get("depth", DEPTH)
        xT_d = self.din("xT", [D, NTOK])
        condT_d = self.din("condT", [128, KC, 2])
        adab_d = self.din("adab", [128, DEPTH, 72])
        normg_d = self.din("normg", [128, DEPTH, 3, KC])
        ada_w = self.din("ada_w", [DEPTH, D, 9 * D])
        wg = [self.din("ffn1_wg", [DEPTH, D, FH]), self.din("ffn2_wg", [DEPTH, D, FH])]
        wu = [self.din("ffn1_wu", [DEPTH, D, FH]), self.din("ffn2_wu", [DEPTH, D, FH])]
        wd = [self.din("ffn1_wd", [DEPTH, FH, D]), self.din("ffn2_wd", [DEPTH, FH, D])]
        yT_d = self.dout("yT", [D, NTOK])

        self.xT = self.sb("xT_sb", [128, KC, NTOK], F32)
        self.x_bufs = [self.buf("x") for _ in range(NT)]
        self.ones_bf = self.sb("ones_bf", [128, 128], BF16)
        self.condT = self.sb("condT_sb", [128, KC, 2], F32)
        self.scond = self.sb("scond", [128, KC, 2], BF16)
        self.adab = self.sb("adab_sb", [128, DEPTH, 72], F32)
        self.normg = self.sb("normg_sb", [128, DEPTH, 3, KC], F32)
        self.mod2 = [self.sb("mod_sb", [128, 72, 2], F32)] * 2
        self.modA2 = [self.sb("modA", [128, 3, KC, 2], F32)] * 2
        self.modB2 = [self.sb("modB", [128, 3, KC, 2], F32)] * 2
        self.modG2 = [self.sb("modG", [128, 3, KC, 2], F32)] * 2
        self.mod_bufs = [self.buf("mod")] * 2
        self.mod_first = [True, True]
        self.ring_init()
        self.psum = [self.es.enter_context(nc.psum_tensor(f"ps{i}", [128, TN], F32)) for i in range(8)]
        self.pbuf = [self.buf("ps") for _ in range(8)]
        self.setup_sem = self.newsem("setup")
        self.setup2_sem = self.newsem("setup2")
        self.const_buf = self.buf("const")

        for kc in range(KC):
            nc.sync.dma_start(out=self.xT[:, kc, :], in_=xT_d[kc * 128:(kc + 1) * 128, :]).then_inc(self.setup_sem[0], 16)
            self.setup_sem[1] += 16
        for dst, src in ((self.condT, condT_d), (self.adab, adab_d), (self.normg, normg_d)):
            nc.sync.dma_start(out=dst[:], in_=src).then_inc(self.setup_sem[0], 16)
            self.setup_sem[1] += 16
        self.extra_setup()
        setup_tok = (self.setup_sem[0], self.setup_sem[1])
        setup2_tok = (self.setup2_sem[0], self.setup2_sem[1])
        for e in (self.pe, self.act, self.dve):
            e.wait(setup_tok)
            e.wait(setup2_tok)
        self.op(self.dve, lambda: nc.vector.memset(self.ones_bf[:], 1.0), writes=[self.const_buf])
        self.scond_buf = self.buf("scond")
        self.op(self.act, lambda: nc.scalar.activation(out=self.scond[:], in_=self.condT[:], func=AF.Silu),
                writes=[self.scond_buf])
        self.op(self.dve, lambda: nc.vector.tensor_copy(out=self.w2aug_bf[:], in_=self.cs["w2aug"][:]), writes=[self.const_buf])

        layers = list(cfg.get("layers", range(depth)))
        self.ada_w = ada_w
        PSET = [0, 1, 2, 3, 4, 5, 17]
        pre_done = False
        for i, l in enumerate(layers):
            par = i % 2
            self.mod, self.modA, self.modB, self.modG = self.mod2[par], self.modA2[par], self.modB2[par], self.modG2[par]
            self.mod_buf = self.mod_bufs[par]
            ride = cfg.get("mod_overlap", True) and cfg.get("ffn", True) and cfg.get("ffn2", True)
            if not ride:
                self.mod_blocks(l, par, range(18), first=True)
                self.mod_raw(l, par, [(0, 72)])
                self.mod_derive(l, par, [0, 1, 2])
            elif not pre_done:
                self.mod_blocks(l, par, PSET, first=True)
                self.mod_raw(l, par, [(0, 24), (68, 72)])
                self.mod_derive(l, par, [0])
            pre_done = False
            if cfg.get("ffn", True):
                hook = None
                if ride:
                    hook = lambda g, l=l, par=par: self.mod_blocks(l, par, [6 + g], first=(g == 0))
                self.ffn(l, 0, wg[0], wu[0], wd[0], hook=hook)
                if ride:
                    self.mod_raw(l, par, [(24, 68)])
                    self.mod_derive(l, par, [1, 2])
            if cfg.get("mixer", True):
                self.mixer(l)
            if cfg.get("ffn", True) and cfg.get("ffn2", True):
                hook = None
                nxt = ride and (i + 1 < len(layers))
                if nxt:
                    nl, npar = layers[i + 1], (i + 1) % 2
                    hook = lambda g, nl=nl, npar=npar: self.mod_blocks(nl, npar, [PSET[g]] if g < len(PSET) else [], first=(g == 0))
                self.ffn(l, 2, wg[1], wu[1], wd[1], hook=hook)
                if nxt:
                    self.mod_raw(nl, npar, [(0, 24), (68, 72)])
                    self.mod_derive(nl, npar, [0])
                    pre_done = True

        self.finish_outputs()
        osem = self.newsem("out")
        for b in self.x_bufs:
            self.sp.wait(b.w)
        for kc in range(KC):
            nc.sync.dma_start(out=yT_d[kc * 128:(kc + 1) * 128, :], in_=self.xT[:, kc, :]).then_inc(osem[0], 16)
            osem[1] += 16
        for s in self.out_sems:
            nc.sync.wait_ge(s[0], s[1])
        nc.sync.wait_ge(osem[0], osem[1])
        self.es.close()
        return nc

    def extra_setup(self):
        nc = self.nc
        self.out_sems = []
        self.odd_w_in = self.din("odd_w_in", [2, D, 4096])
        self.odd_w_out = self.din("odd_w_out", [2, 2048, D])
        oddvg_d = self.din("oddvg", [128, 2, 16])
        wsT_d = self.din("wsT", [128, 2, 4, 128])
        bsb_d = self.din("bsb", [128, 2, 4, 128])
        self.oddvg = self.sb("oddvg_sb", [128, 2, 16], F32)
        self.wsT = self.sb("wsT_sb", [128, 2, 4, 128], F32)
        self.bsb = self.sb("bsb_sb", [128, 2, 4, 128], F32)
        for dst, src in ((self.oddvg, oddvg_d), (self.wsT, wsT_d), (self.bsb, bsb_d)):
            nc.sync.dma_start(out=dst[:], in_=src).then_inc(self.setup_sem[0], 16)
            self.setup_sem[1] += 16
        self.wv_sem = self.newsem("wv")
        self.even_w_in = self.din("even_w_in", [2, D, 2240])
        self.win_sw = self.din("win_sw", [2, D, 96])
        self.even_w_out = self.din("even_w_out", [2, D, D])
        self.qbw = self.din("qbw", [2, 384, 768])
        self.qbw_sw = self.din("qbw_sw", [2, 384, 768])
        self.kvbK = self.din("kvbK", [2, 256, 512])
        self.kvbV = self.din("kvbV", [2, 256, 512])
        self.ckvcT_d = self.din("ckvcT", [2, 256, 256])
        self.kropecT_d = self.din("kropecT", [2, 32, 256])
        self.sgla_d = self.din("sgla", [2, 2, 4, 64, 128])
        self.ckvT_o = self.dout("ckvT_o", [2, 256, 1024])
        self.kropeT_o = self.dout("kropeT_o", [2, 32, 1024])
        self.gla_o = self.dout("gla_o", [4, 2, 2, 4, 64, 128])
        cs = {}
        self.cs = cs
        for nm, shp in (("gmask", [128, 4, 128]), ("mask4", [128, 2, 4, 128]), ("rope", [96, 2, 1024])):
            dd = self.din(nm, shp)
            t = self.sb(nm + "_sb", shp, BF16)
            nc.gpsimd.dma_start(out=t[:], in_=dd).then_inc(self.setup2_sem[0], 16)
            self.setup2_sem[1] += 16
            cs[nm] = t
        for nm, shp in (("qkng", [96, 2, 4]), ("glag", [128, 2]), ("qag", [128, 2, 3]), ("kvag", [128, 2, 2]),
                        ("w2aug", [33, 2, 2, 256])):
            dd = self.din(nm, shp)
            t = self.sb(nm + "_sb", shp, F32)
            nc.sync.dma_start(out=t[:], in_=dd).then_inc(self.setup_sem[0], 16)
            self.setup_sem[1] += 16
            cs[nm] = t
        self.cs = cs
        self.gmask_bf = cs["gmask"]
        self.w2aug_bf = self.sb("w2aug_bf", [33, 2, 2, 256], BF16)
        self.ld1 = self.newsem("ld1")
        self.ld2 = self.newsem("ld2")
        self.st1 = self.newsem("st1")
        self.st2 = self.newsem("st2")
        self.st3 = self.newsem("st3")
        self.out_sems += [self.st1, self.st2, self.st3]

    def finish_outputs(self):
        pass

    def mixer(self, l):
        if l % 2 == 1:
            self.odd_mixer(l)
        else:
            self.even_mixer(l)
        self.barrier()

    def even_mixer(self, l):
        for hf in range(2):
            self.even_half(l, hf)

    def _tok(self, eng):
        return (eng.sems[eng.si], eng.cnt)

    def even_half(self, l, hf):
        nc = self.nc
        j = l // 2
        k = 1
        L = 1024
        T0 = hf * L
        tiles = [2 * hf, 2 * hf + 1]
        win = self.even_w_in[j].rearrange("(kc p) f -> p kc f", p=128)
        winsw = self.win_sw[j].rearrange("(kc p) f -> p kc f", p=128)
        wout = self.even_w_out[j].rearrange("(c p) d -> p c d", p=128)
        cs = self.cs
        X = mybir.AxisListType.X
        pe, act, dve = self.pe, self.act, self.dve
        PS = self.psum
        PB = self.pbuf
        with contextlib.ExitStack() as hs:
            ogT = self.sb("e_ogT", [128, 4, L], BF16, hs); og_buf = self.buf("og")
            cqn = self.sb("e_cqn", [128, 3, L], BF16, hs); cqn_buf = self.buf("cqn")
            ckvn = self.sb("e_ckvn", [128, 2, L], BF16, hs); ckvn_buf = self.buf("ckvn")
            krT = self.sb("e_krT", [96, L], F32, hs); kr_buf = self.buf("kr")
            krsw = self.sb("e_krsw", [96, L], F32, hs); krsw_buf = self.buf("krsw")
            with contextlib.ExitStack() as gs:
                qT = self.sb("e_qT", [128, 2, L], BF16, gs); q_buf = self.buf("q")
                kT = self.sb("e_kT", [128, 2, L], BF16, gs); k_buf = self.buf("k")
                gl = self.sb("e_gl", [33, L], BF16, gs); gl_buf = self.buf("gl")
                ktok = self.sb("e_ktok", [128, 8, 256], BF16, gs); ktok_buf = self.buf("ktok")
                vtok = self.sb("e_vtok", [128, 8, 512], BF16, gs); vtok_buf = self.buf("vtok")
                with contextlib.ExitStack() as ps_:
                    hT = self.sb("e_hT", [128, KC, L], BF16, ps_)
                    h_bufs = [self.buf("h"), self.buf("h")]
                    tmp, rstd = self.norm_tmp(ps_)
                    self.norm_modulate(k, tiles, hT, h_bufs, tmp, rstd)
                    self.barrier()
                    stg = tmp["t2"]; stg_buf = self.buf("stg")
                    sq3 = tmp["sq"]; sq3_buf = self.buf("sq3")
                    rs2 = rstd["t"]; rs2_buf = self.buf("rs2")
                    self.op(dve, lambda: nc.vector.memset(gl[:], 1.0), writes=[gl_buf])
                    bi = [0]

                    def fm_group(s, sbuf_, ncols, c0, m, li, bank):
                        fns = [lambda kc=kc: nc.tensor.matmul(
                            PS[bank][0:m, :], lhsT=self.ring[:, s, kc * ncols + c0: kc * ncols + c0 + m],
                            rhs=hT[:, kc, li * TN:(li + 1) * TN], start=(kc == 0), stop=(kc == KC - 1)) for kc in range(KC)]
                        self.op(pe, fns, reads=[sbuf_, h_bufs[li]], writes=[PB[bank]])

                    def nb():
                        b = bi[0] % 4
                        bi[0] += 1
                        return b

                    s, sb_ = self.ring_load([(0, [KC, 512], win[:, :, 0:512])])
                    for li in range(2):
                        for c in range(2):
                            b = nb()
                            fm_group(s, sb_, 512, c * 128, 128, li, b)
                            self.op(act, lambda c=c, li=li, b=b: nc.scalar.activation(
                                out=qT[:, c, li * TN:(li + 1) * TN], in_=PS[b][:], func=AF.Copy, scale=0.125),
                                reads=[PB[b]], writes=[q_buf])
                        for c in range(2):
                            b = nb()
                            fm_group(s, sb_, 512, 256 + c * 128, 128, li, b)
                            self.op(dve, lambda c=c, li=li, b=b: nc.vector.tensor_copy(
                                out=kT[:, c, li * TN:(li + 1) * TN], in_=PS[b][:]), reads=[PB[b]], writes=[k_buf])
                    for blk in range(8):
                        b = nb()
                        fns = [lambda kc=kc, blk=blk, b=b: nc.tensor.matmul(
                            PS[b][:, 0:256], lhsT=hT[:, kc, blk * 128:(blk + 1) * 128],
                            rhs=self.ring[:, s, kc * 512 + 256: kc * 512 + 512], start=(kc == 0), stop=(kc == KC - 1)) for kc in range(KC)]
                        self.op(pe, fns, reads=[sb_, h_bufs[blk // 4]], writes=[PB[b]])
                        self.op(act, lambda blk=blk, b=b: nc.scalar.activation(out=ktok[:, blk, :], in_=PS[b][:, 0:256], func=AF.Copy),
                                reads=[PB[b]], writes=[ktok_buf])
                    s, sb_ = self.ring_load([(0, [KC, 512], win[:, :, 512:1024])])
                    for blk in range(8):
                        b = nb()
                        fns = [lambda kc=kc, blk=blk, b=b: nc.tensor.matmul(
                            PS[b][:], lhsT=hT[:, kc, blk * 128:(blk + 1) * 128],
                            rhs=self.ring[:, s, kc * 512: kc * 512 + 512], start=(kc == 0), stop=(kc == KC - 1)) for kc in range(KC)]
                        self.op(pe, fns, reads=[sb_, h_bufs[blk // 4]], writes=[PB[b]])
                        self.op(dve, lambda blk=blk, b=b: nc.vector.tensor_copy(out=vtok[:, blk, :], in_=PS[b][:]),
                                reads=[PB[b]], writes=[vtok_buf])
                    s, sb_ = self.ring_load([(0, [KC, 512], win[:, :, 1024:1536])])
                    for li in range(2):
                        for c in range(4):
                            b = nb()
                            fm_group(s, sb_, 512, c * 128, 128, li, b)
                            self.op(act, lambda c=c, li=li, b=b: nc.scalar.activation(
                                out=ogT[:, c, li * TN:(li + 1) * TN], in_=PS[b][:], func=AF.Silu),
                                reads=[PB[b]], writes=[og_buf])
                    s, sb_ = self.ring_load([(0, [KC, 416], win[:, :, 1536:1952])])
                    for li in range(2):
                        b = nb()
                        fm_group(s, sb_, 416, 0, 32, li, b)
                        self.op(act, lambda li=li, b=b: nc.scalar.activation(
                            out=gl[0:32, li * TN:(li + 1) * TN], in_=PS[b][0:32, :], func=AF.Copy),
                            reads=[PB[b]], writes=[gl_buf])
                        bs = [nb() for _ in range(3)]
                        for c in range(3):
                            fm_group(s, sb_, 416, 32 + c * 128, 128, li, bs[c])
                            self.op(act, lambda c=c, b=bs[c]: nc.scalar.activation(out=sq3[:, c, :], in_=PS[b][:], func=AF.Square),
                                    reads=[PB[bs[c]]], writes=[sq3_buf])
                        self.rms_rstd(sq3, sq3_buf, 3, 128, 384, rs2, rs2_buf)
                        for c in range(3):
                            self.op(dve, lambda c=c, li=li, b=bs[c]: nc.vector.scalar_tensor_tensor(
                                out=cqn[:, c, li * TN:(li + 1) * TN], in0=PS[b][:], scalar=cs["qag"][:, j, c:c + 1], in1=rs2[:],
                                op0=ALU.mult, op1=ALU.mult), reads=[PB[bs[c]], rs2_buf], writes=[cqn_buf])
                    s, sb_ = self.ring_load([(0, [KC, 288], win[:, :, 1952:2240]), (2304, [KC, 96], winsw[:, :, :])])
                    for li in range(2):
                        bs = [nb() for _ in range(2)]
                        for c in range(2):
                            fm_group(s, sb_, 288, c * 128, 128, li, bs[c])
                            self.op(act, lambda c=c, b=bs[c]: nc.scalar.activation(out=sq3[:, c, :], in_=PS[b][:], func=AF.Square),
                                    reads=[PB[bs[c]]], writes=[sq3_buf])
                        self.rms_rstd(sq3, sq3_buf, 2, 128, 256, rs2, rs2_buf)
                        for c in range(2):
                            self.op(dve, lambda c=c, li=li, b=bs[c]: nc.vector.scalar_tensor_tensor(
                                out=stg[:, c, :], in0=PS[b][:], scalar=cs["kvag"][:, j, c:c + 1], in1=rs2[:],
                                op0=ALU.mult, op1=ALU.mult), reads=[PB[bs[c]], rs2_buf], writes=[stg_buf])
                        self.op(act, lambda li=li: nc.scalar.activation(out=ckvn[:, :, li * TN:(li + 1) * TN], in_=stg[:], func=AF.Copy),
                                reads=[stg_buf], writes=[ckvn_buf])
                        if hf == 0:
                            self.dma(self.sp, self.st1, self.ckvT_o[j].rearrange("(c p) t -> p c t", p=128)[:, :, li * TN:(li + 1) * TN],
                                     stg[:], reads=[stg_buf])
                        b = nb()
                        fm_group(s, sb_, 288, 192, 96, li, b)
                        self.op(act, lambda li=li, b=b: nc.scalar.activation(
                            out=krT[64:96, li * TN:(li + 1) * TN], in_=PS[b][64:96, :], func=AF.Copy),
                            reads=[PB[b]], writes=[kr_buf])
                        if hf == 1:
                            b = nb()
                            fns = [lambda kc=kc, li=li, b=b: nc.tensor.matmul(
                                PS[b][0:96, :], lhsT=self.ring[:, s, 2304 + kc * 96: 2304 + (kc + 1) * 96],
                                rhs=hT[:, kc, li * TN:(li + 1) * TN], start=(kc == 0), stop=(kc == KC - 1)) for kc in range(KC)]
                            self.op(pe, fns, reads=[sb_, h_bufs[li]], writes=[PB[b]])
                            self.op(act, lambda li=li, b=b: nc.scalar.activation(
                                out=krsw[64:96, li * TN:(li + 1) * TN], in_=PS[b][64:96, :], func=AF.Copy),
                                reads=[PB[b]], writes=[krsw_buf])
                    if hf == 0:
                        self.dma(self.sp, self.st3, self.kropeT_o[j], krT[64:96, :], reads=[kr_buf])
                    self.barrier()
                    for e_ in (pe, act, dve):
                        e_.wait((self.st1[0], self.st1[1]))
                if self.cfg.get("even_stop", 9) <= 1:
                    return
                with contextlib.ExitStack() as ws:
                    oT = self.sb("g_oT", [128, 4, L], F32, ws); o_buf = self.buf("o")
                    ez = self.sb("g_ez", [128, 256], F32, ws)
                    lg = self.sb("g_l", [128, 256], BF16, ws); l_buf = self.buf("l")
                    Eq = self.sb("g_Eq", [128, 2, 2, 128], F32, ws); Eq_bufs = [self.buf("Eq"), self.buf("Eq")]
                    Ek = self.sb("g_Ek", [128, 2, 128], F32, ws)
                    Er = self.sb("g_Er", [128, 256], F32, ws); E_buf = self.buf("E")
                    qe = self.sb("g_qe", [128, 2, 2, 128], BF16, ws); qe_bufs = [self.buf("qe"), self.buf("qe")]
                    ke = self.sb("g_ke", [128, 2, 128], BF16, ws)
                    kl = self.sb("g_kl", [128, 2, 2, 128], BF16, ws); qk_buf = self.buf("qk")
                    self.op(dve, lambda: nc.vector.memset(kl[:], 0.0), writes=[qk_buf])
                    attm = self.sb("g_attm", [128, 2, 4, 128], BF16, ws); attm_bufs = [self.buf("attm"), self.buf("attm")]
                    S = self.sb("g_S", [128, 2, 2, 128], F32, ws); S_bufs = [self.buf("S"), self.buf("S")]
                    Sbf = self.sb("g_Sbf", [128, 3, 2, 2, 128], BF16, ws); Sbf_bufs = [self.buf("Sbf") for _ in range(3)]
                    self.op(dve, lambda: nc.vector.memset(Sbf[:], 0.0), writes=Sbf_bufs)
                    nseq = 4 if hf == 0 else 1
                    bps = 8 // nseq
                    sgl = self.sgla_d[j]
                    items = []
                    for dr in range(2):
                        for sq in range(nseq):
                            blks = list(range(sq * bps, (sq + 1) * bps))
                            if dr == 1:
                                blks = blks[::-1]
                            for bi_, blk in enumerate(blks):
                                items.append(dict(dr=dr, sq=sq, blk=blk, first=(bi_ == 0), last=(bi_ == len(blks) - 1), par=len(items) % 2))
                    st_ = {"sv": 0, "p": 0}

                    def P1(it):
                        dr, tsl = it["dr"], slice(it["blk"] * 128, (it["blk"] + 1) * 128)
                        self.op(pe, lambda: nc.tensor.matmul(PS[0][:, 0:256], lhsT=gl[0:33, tsl], rhs=self.w2aug_bf[0:33, j, dr, :],
                                                             start=True, stop=True), reads=[gl_buf, self.const_buf], writes=[PB[0]])
                        self.op(act, lambda: nc.scalar.activation(out=ez[:], in_=PS[0][:, 0:256], func=AF.Exp, scale=-1.0),
                                reads=[PB[0]], writes=[l_buf])
                        self.op(act, lambda: nc.scalar.activation(out=lg[:], in_=ez[:], func=AF.Ln, bias=1.0),
                                reads=[], writes=[l_buf])

                    def P2(it):
                        dr, par, blk = it["dr"], it["par"], it["blk"]
                        tsl = slice(blk * 128, (blk + 1) * 128)
                        fns = [lambda hp=hp: nc.tensor.matmul(PS[1][:, hp * 128:(hp + 1) * 128], lhsT=lg[:, hp * 128:(hp + 1) * 128],
                                                              rhs=self.gmask_bf[:, dr, :], start=True, stop=True) for hp in range(2)]
                        self.op(pe, fns, reads=[l_buf], writes=[PB[1]])
                        self.op(pe, lambda: nc.tensor.matmul(PS[2][:, 0:256], lhsT=self.gmask_bf[:, 2 + dr, :], rhs=lg[:],
                                                             start=True, stop=True), reads=[l_buf], writes=[PB[2]])
                        self.op(act, lambda: nc.scalar.activation(out=Eq[:, par].rearrange("p a b -> p (a b)"), in_=PS[1][:, 0:256], func=AF.Exp),
                                reads=[PB[1]], writes=[Eq_bufs[par]])
                        self.op(act, lambda: nc.scalar.activation(out=Ek[:].rearrange("p a b -> p (a b)"), in_=PS[1][:, 0:256], func=AF.Exp, scale=-1.0),
                                reads=[PB[1]], writes=[E_buf])
                        self.op(act, lambda: nc.scalar.activation(out=Er[:], in_=PS[2][:, 0:256], func=AF.Exp),
                                reads=[PB[2]], writes=[])
                        E_buf.w = self._tok(act)
                        self.op(dve, lambda: nc.vector.tensor_tensor(out=qe[:, par], in0=qT[:, :, tsl], in1=Eq[:, par], op=ALU.mult),
                                reads=[Eq_bufs[par], q_buf], writes=[qe_bufs[par]])
                        self.op(dve, lambda: nc.vector.tensor_tensor(out=ke[:], in0=kT[:, :, tsl], in1=Ek[:], op=ALU.mult),
                                reads=[k_buf, E_buf], writes=[qk_buf])
                        for hh in range(2):
                            self.op(dve, lambda hh=hh: nc.vector.tensor_tensor(
                                out=kl[:, :, hh, hh * 64:(hh + 1) * 64],
                                in0=ktok[:, blk, :].rearrange("p (a b) -> p a b", a=2)[:, :, hh * 64:(hh + 1) * 64],
                                in1=Er[:].rearrange("p (a b) -> p a b", a=2)[:, :, hh * 64:(hh + 1) * 64], op=ALU.mult),
                                reads=[ktok_buf], writes=[])
                        qk_buf.w = self._tok(dve)

                    def P3a(it):
                        dr, par = it["dr"], it["par"]
                        for hh in range(2):
                            bank = 3 if hh == 0 else 0
                            fns = [lambda hp=hp, hh=hh, bank=bank: nc.tensor.matmul(
                                PS[bank][:, hp * 128:(hp + 1) * 128], lhsT=ke[hh * 64:hh * 64 + 64, hp, :],
                                rhs=qe[hh * 64:hh * 64 + 64, par, hp, :], start=True, stop=True) for hp in range(2)]
                            self.op(pe, fns, reads=[qk_buf, qe_bufs[par]], writes=[PB[bank]])
                            self.op(dve, lambda hh=hh, bank=bank: nc.vector.tensor_tensor(
                                out=attm[:, par, hh::2, :], in0=PS[bank][:, 0:256].rearrange("p (a b) -> p a b", a=2),
                                in1=cs["mask4"][:, dr, 0:2, :], op=ALU.mult), reads=[PB[bank]], writes=[attm_bufs[par]] if hh == 0 else [])
                        attm_bufs[par].w = self._tok(dve)

                    def P3b(it):
                        blk = it["blk"]
                        for c2 in range(2):
                            fns = [lambda c2=c2, hp=hp, hh=hh: nc.tensor.matmul(
                                PS[4 + c2][:, hp * 128:(hp + 1) * 128], lhsT=kl[c2 * 64:(c2 + 1) * 64, hp, hh, :],
                                rhs=vtok[c2 * 64:(c2 + 1) * 64, blk, (hp * 2 + hh) * 128:(hp * 2 + hh + 1) * 128],
                                start=(hh == 0), stop=(hh == 1)) for hp in range(2) for hh in range(2)]
                            self.op(pe, fns, reads=[qk_buf, vtok_buf], writes=[PB[4 + c2]])

                    def copy_S(ver):
                        p = st_["p"]
                        self.op(act, [lambda hh=hh, p=p: nc.scalar.activation(out=Sbf[hh * 64:hh * 64 + 64, ver, :, hh, :],
                                                                              in_=S[hh * 64:hh * 64 + 64, p, :, :], func=AF.Copy) for hh in range(2)],
                                reads=[S_bufs[p]], writes=[Sbf_bufs[ver]])

                    def P4(it):
                        dr, par, sq = it["dr"], it["par"], it["sq"]
                        if it["first"]:
                            p = st_["p"]
                            if hf == 0:
                                self.op(dve, lambda p=p: nc.vector.memset(S[:, p], 0.0), writes=[S_bufs[p]])
                            else:
                                self.dma(self.sp, self.ld2, S[:, p], sgl[dr].rearrange("(hp hh) d e -> (hh d) hp e", hh=2), writes=[S_bufs[p]])
                            copy_S(st_["sv"])
                        corder = [0, 1] if dr == 0 else [1, 0]
                        svs = [st_["sv"]]
                        for c2 in corder:
                            last = (c2 * 64 + 63) if dr == 0 else (c2 * 64)
                            p = st_["p"]
                            q_ = 1 - p
                            for hp in range(2):
                                self.op(dve, lambda hp=hp, c2=c2, last=last, p=p, q_=q_: nc.vector.scalar_tensor_tensor(
                                    out=S[:, q_, hp, :], in0=S[:, p, hp, :], scalar=Eq[:, par, hp, last:last + 1],
                                    in1=PS[4 + c2][:, hp * 128:(hp + 1) * 128],
                                    op0=ALU.mult, op1=ALU.add), reads=[PB[4 + c2], Eq_bufs[par], S_bufs[p]],
                                    writes=[S_bufs[q_]] if hp == 0 else [])
                            S_bufs[q_].w = self._tok(dve)
                            st_["p"] = q_
                            nsv = (svs[-1] + 1) % 3
                            copy_S(nsv)
                            svs.append(nsv)
                        it["svs"] = svs
                        it["corder"] = corder
                        st_["sv"] = svs[2]
                        if it["last"] and hf == 0:
                            p = st_["p"]
                            self.dma(self.sp, self.st2, self.gla_o[sq, j, dr].rearrange("(hp hh) d e -> (hh d) hp e", hh=2), S[:, p],
                                     reads=[S_bufs[p]])

                    def P5(it):
                        dr, par, blk, svs, corder = it["dr"], it["par"], it["blk"], it["svs"], it["corder"]
                        tsl = slice(blk * 128, (blk + 1) * 128)
                        bank = 6 + par
                        fns = []
                        for h in range(4):
                            hp, r0 = h // 2, (h % 2) * 64
                            fns.append(lambda h=h: nc.tensor.matmul(PS[bank][:, h * 128:(h + 1) * 128], lhsT=vtok[:, blk, h * 128:(h + 1) * 128],
                                                                    rhs=attm[:, par, h, :], start=True, stop=False))
                            for ci_, c2 in enumerate(corder):
                                fns.append(lambda h=h, c2=c2, ci_=ci_, hp=hp, r0=r0: nc.tensor.matmul(
                                    PS[bank][:, h * 128 + c2 * 64: h * 128 + (c2 + 1) * 64], lhsT=Sbf[:, svs[ci_], hp, r0 // 64, :],
                                    rhs=qe[:, par, hp, c2 * 64:(c2 + 1) * 64], start=False, stop=(ci_ == 1)))
                        self.op(pe, fns, reads=[vtok_buf, attm_bufs[par], qe_bufs[par], Sbf_bufs[svs[0]], Sbf_bufs[svs[1]]], writes=[PB[bank]])
                        pv = PS[bank][:].rearrange("p (h t) -> p h t", h=4)
                        if dr == 0:
                            self.op(act, lambda: nc.scalar.activation(out=oT[:, :, tsl], in_=pv, func=AF.Copy),
                                    reads=[PB[bank]], writes=[o_buf])
                        else:
                            self.op(dve, lambda: nc.vector.tensor_tensor(out=oT[:, :, tsl], in0=pv, in1=oT[:, :, tsl], op=ALU.add),
                                    reads=[PB[bank]], writes=[o_buf])

                    P1(items[0]); P2(items[0]); P3a(items[0])
                    for i_, it in enumerate(items):
                        nxt = items[i_ + 1] if i_ + 1 < len(items) else None
                        if nxt is not None:
                            P1(nxt)
                        P3b(it)
                        if nxt is not None:
                            P2(nxt)
                        P4(it)
                        if nxt is not None:
                            P3a(nxt)
                        P5(it)
                    with contextlib.ExitStack() as ns:
                        sqh = self.sb("g_sq", [128, 1, TN], BF16, ns); sqh_buf = self.buf("sqh")
                        rs = self.sb("g_rs", [128, TN], F32, ns); rs_buf = self.buf("rs")
                        t3 = Eq[:].rearrange("p a b c -> p (a b c)"); t3_buf = self.buf("t3")
                        for li in range(2):
                            for h in range(4):
                                sl = slice(li * TN, (li + 1) * TN)
                                self.op(act, lambda h=h, sl=sl: nc.scalar.activation(out=sqh[:, 0, :], in_=oT[:, h, sl], func=AF.Square),
                                        reads=[o_buf], writes=[sqh_buf])
                                self.rms_rstd(sqh, sqh_buf, 1, 128, 128, rs, rs_buf)
                                self.op(dve, lambda h=h, sl=sl: nc.vector.scalar_tensor_tensor(
                                    out=t3, in0=oT[:, h, sl], scalar=cs["glag"][:, j:j + 1], in1=rs[:], op0=ALU.mult, op1=ALU.mult),
                                    reads=[o_buf, rs_buf], writes=[t3_buf])
                                self.op(dve, lambda h=h, sl=sl: nc.vector.tensor_tensor(out=ogT[:, h, sl], in0=t3, in1=ogT[:, h, sl], op=ALU.mult),
                                        reads=[t3_buf], writes=[og_buf])
                    self.barrier()
                    for e_ in (pe, act, dve):
                        e_.wait((self.st2[0], self.st2[1]))
                    if self.cfg.get("dump_o3") and hf == 1:
                        dsem = self.newsem("dbg")
                        self.out_sems.append(dsem)
                        for nm_, t_, shp_, dt_ in (("og3", ogT, [128, 4, L], BF16), ("oT3", oT, [128, 4, L], F32)):
                            self.sp.wait(self._tok(pe)); self.sp.wait(self._tok(act)); self.sp.wait(self._tok(dve))
                            nc.sync.dma_start(out=self.dout("dbg_" + nm_, shp_, dt_), in_=t_[:]).then_inc(dsem[0], 16)
                            dsem[1] += 16
                        for e_ in (pe, act, dve):
                            e_.wait((dsem[0], dsem[1]))
            if self.cfg.get("even_stop", 9) <= 2:
                return
            self.mla_half(l, hf, ogT, og_buf, cqn, cqn_buf, ckvn, ckvn_buf, krT, kr_buf, krsw, krsw_buf, wout)

    def mla_half(self, l, hf, ogT, og_buf, cqn, cqn_buf, ckvn, ckvn_buf, krT, kr_buf, krsw, krsw_buf, wout):
        nc = self.nc
        j = l // 2
        k = 1
        L = 1024
        cs = self.cs
        pe, act, dve = self.pe, self.act, self.dve
        PS, PB = self.psum, self.pbuf
        X = mybir.AxisListType.X
        koff = 256 if hf == 1 else 0
        nk = L + koff
        nkt = nk // 128
        SC = 96 ** -0.5
        g_q, g_qs, g_k, g_ks = (cs["qkng"][:, j, i:i + 1] for i in range(4))
        with contextlib.ExitStack() as ms:
            mlaT = self.sb("m_mlaT", [64, 8, L], BF16, ms); mla_buf = self.buf("mla")
            Kb = self.sb("m_Kb", [96, 2, nk], BF16, ms); kb_bufs = [self.buf("kb"), self.buf("kb")]
            Vt = self.sb("m_Vt", [128, nkt, 512], BF16, ms); vt_buf = self.buf("vt")
            ckvc = self.sb("m_ckvc", [128, 2, 256], BF16, ms); ckvc_buf = self.buf("ckvc")
            ssn = self.sb("m_ssn", [128, nkt, 8], F32, ms); ssn_buf = self.buf("ssn")
            ssr = self.sb("m_ssr", [128, nkt], F32, ms); ssr_buf = self.buf("ssr")
            rk = self.sb("m_rk", [128, nkt, 8], F32, ms); rk_buf = self.buf("rk")
            gq2 = self.sb("m_gq2", [96, 1], F32, ms); gq2_buf = self.buf("gq2")
            sq_, sqb = self.ring_load([(0, [3, 768], self.qbw[j].rearrange("(c p) f -> p c f", p=128)),
                                       (2304, [3, 768], self.qbw_sw[j].rearrange("(c p) f -> p c f", p=128))])
            sk_, skb = self.ring_load([(0, [2, 512], self.kvbK[j].rearrange("(c p) f -> p c f", p=128)),
                                       (1024, [2, 512], self.kvbV[j].rearrange("(c p) f -> p c f", p=128))])
            self.op(dve, lambda: nc.vector.tensor_tensor(out=gq2[:], in0=g_q, in1=g_k, op=ALU.mult), writes=[gq2_buf])
            with contextlib.ExitStack() as sa:
                krg = self.sb("m_krg", [96, nk], F32, sa); krg_buf = self.buf("krg")
                krr = self.sb("m_krr", [96, nk], BF16, sa); krr_buf = self.buf("krr")
                krc = self.sb("m_krc", [96, 256], F32, sa); krc_buf = self.buf("krc")
                ckvf = self.sb("m_ckvf", [128, 2, 256], F32, sa); ckvf_buf = self.buf("ckvf")
                sqt = self.sb("m_sqt", [128, 2, 512], F32, sa); sqt_bufs = [self.buf("sqt"), self.buf("sqt")]
                t1 = self.sb("m_t1a", [96, TN], F32, sa)
                t2 = self.sb("m_t2a", [96, TN], F32, sa); t_buf = self.buf("t12")
                if hf == 1:
                    for c in range(2):
                        self.dma(self.sp, self.ld1, ckvf[:, c, :], self.ckvcT_d[j, c * 128:(c + 1) * 128, :], writes=[ckvf_buf] if c == 0 else [])
                    self.dma(self.sp, self.ld1, krc[64:96, :], self.kropecT_d[j], writes=[krc_buf])
                    tok = (self.ld1[0], self.ld1[1])
                    ckvf_buf.w = tok
                    krc_buf.w = tok
                    self.op(dve, lambda: nc.vector.tensor_copy(out=ckvc[:], in_=ckvf[:]), reads=[ckvf_buf], writes=[ckvc_buf])
                    self.op(dve, lambda: nc.vector.tensor_scalar(out=krg[64:96, 0:256], in0=krc[64:96, :], scalar1=g_k[64:96, :],
                                                                 scalar2=None, op0=ALU.mult), reads=[krc_buf], writes=[krg_buf])
                    self.op(act, lambda: nc.scalar.activation(out=krr[64:96, 0:256], in_=krc[64:96, :], func=AF.Square),
                            reads=[krc_buf], writes=[krr_buf])
                    for li in range(2):
                        sl = slice(li * TN, (li + 1) * TN)
                        self.op(dve, lambda sl=sl: nc.vector.scalar_tensor_tensor(
                            out=t1[64:96, :], in0=krT[64:96, sl], scalar=g_k[64:96, :], in1=cs["rope"][64:96, 0, sl],
                            op0=ALU.mult, op1=ALU.mult), reads=[kr_buf], writes=[t_buf])
                        self.op(dve, lambda sl=sl: nc.vector.scalar_tensor_tensor(
                            out=t2[64:96, :], in0=krsw[64:96, sl], scalar=g_ks[64:96, :], in1=cs["rope"][64:96, 1, sl],
                            op0=ALU.mult, op1=ALU.mult), reads=[krsw_buf], writes=[])
                        self.op(dve, lambda li=li: nc.vector.tensor_tensor(
                            out=krg[64:96, koff + li * TN: koff + (li + 1) * TN], in0=t1[64:96, :], in1=t2[64:96, :], op=ALU.add),
                            reads=[], writes=[krg_buf])
                else:
                    self.op(dve, lambda: nc.vector.tensor_scalar(out=krg[64:96, :], in0=krT[64:96, :], scalar1=g_k[64:96, :],
                                                                 scalar2=None, op0=ALU.mult), reads=[kr_buf], writes=[krg_buf])
                self.op(act, lambda: nc.scalar.activation(out=krr[64:96, koff:nk], in_=krT[64:96, :], func=AF.Square),
                        reads=[kr_buf], writes=[krr_buf])
                for bb in range(2):
                    self.op(act if bb == 0 else dve,
                            (lambda bb=bb: nc.scalar.activation(out=Kb[64:96, bb, :], in_=krg[64:96, :], func=AF.Copy)) if bb == 0 else
                            (lambda bb=bb: nc.vector.tensor_copy(out=Kb[64:96, bb, :], in_=krg[64:96, :])),
                            reads=[krg_buf], writes=[kb_bufs[bb]])
                fns = [lambda kt=kt: nc.tensor.matmul(PS[6][:, kt:kt + 1], lhsT=krr[64:96, kt * 128:(kt + 1) * 128],
                                                      rhs=self.ones_bf[64:96, 0:1], start=True, stop=True) for kt in range(nkt)]
                self.op(pe, fns, reads=[krr_buf, self.const_buf], writes=[PB[6]])
                self.op(act, lambda: nc.scalar.activation(out=ssr[:], in_=PS[6][:, 0:nkt], func=AF.Copy), reads=[PB[6]], writes=[ssr_buf])
                for kt in range(nkt):
                    if kt * 128 < koff:
                        src, srcb, s0 = ckvc, ckvc_buf, kt * 128
                    else:
                        src, srcb, s0 = ckvn, ckvn_buf, kt * 128 - koff
                    b0 = kt % 2
                    fns = [lambda rc=rc, src=src, s0=s0, b0=b0: nc.tensor.matmul(
                        PS[b0][:], lhsT=src[:, rc, s0:s0 + 128], rhs=self.ring[:, sk_, rc * 512:(rc + 1) * 512],
                        start=(rc == 0), stop=(rc == 1)) for rc in range(2)]
                    self.op(pe, fns, reads=[skb, srcb], writes=[PB[b0]])
                    self.op(act, lambda b0=b0: nc.scalar.activation(out=sqt[:, b0, :], in_=PS[b0][:], func=AF.Square),
                            reads=[PB[b0]], writes=[sqt_bufs[b0]])
                    self.op(dve, lambda kt=kt, b0=b0: nc.vector.tensor_reduce(
                        out=ssn[:, kt, :], in_=sqt[:, b0, :].rearrange("p (h e) -> p h e", e=64), axis=X, op=ALU.add),
                        reads=[sqt_bufs[b0]], writes=[ssn_buf])
                    b1 = 2 + kt % 2
                    fns = [lambda rc=rc, src=src, s0=s0, b1=b1: nc.tensor.matmul(
                        PS[b1][:], lhsT=src[:, rc, s0:s0 + 128], rhs=self.ring[:, sk_, 1024 + rc * 512: 1024 + (rc + 1) * 512],
                        start=(rc == 0), stop=(rc == 1)) for rc in range(2)]
                    self.op(pe, fns, reads=[skb, srcb], writes=[PB[b1]])
                    self.op(dve, lambda kt=kt, b1=b1: nc.vector.tensor_copy(out=Vt[:, kt, :], in_=PS[b1][:]), reads=[PB[b1]], writes=[vt_buf])
                for kt in range(nkt):
                    self.op(dve, lambda kt=kt: nc.vector.tensor_scalar(out=rk[:, kt, :], in0=ssn[:, kt, :], scalar1=ssr[:, kt:kt + 1],
                                                                       scalar2=None, op0=ALU.add), reads=[ssn_buf, ssr_buf], writes=[rk_buf], same=True)
                self.op(act, lambda: nc.scalar.activation(out=rk[:].rearrange("p a b -> p (a b)"), in_=rk[:].rearrange("p a b -> p (a b)"),
                                                          func=AF.Ln, scale=1.0 / 96, bias=EPS), reads=[], writes=[rk_buf], same=True)
                self.op(act, lambda: nc.scalar.activation(out=rk[:].rearrange("p a b -> p (a b)"), in_=rk[:].rearrange("p a b -> p (a b)"),
                                                          func=AF.Exp, scale=-0.5, bias=float(np.log(SC))), reads=[], writes=[rk_buf], same=True)
                self.barrier()
            with contextlib.ExitStack() as sbk_:
                Qn = self.sb("m_Qn", [96, 8, TN], BF16, sbk_); qn_bufs = [self.buf("qn") for _ in range(8)]
                sq96 = self.sb("m_sq", [96, 2, TN], BF16, sbk_); sq_bufs = [self.buf("sq96"), self.buf("sq96")]
                rs96 = self.sb("m_rs", [96, 2, TN], F32, sbk_); rs_bufs = [self.buf("rs96"), self.buf("rs96")]
                t1 = self.sb("m_t1", [96, TN], F32, sbk_)
                t2 = self.sb("m_t2", [96, TN], F32, sbk_); t_buf = self.buf("t12")
                PT = self.sb("m_PT", [128, 2, TN], BF16, sbk_); pt_bufs = [self.buf("pt"), self.buf("pt")]
                rden = self.sb("m_rden", [64, 1, TN], F32, sbk_); rden_bufs = [self.buf("rden")]
                si = 0
                ei = 0
                pairs = [(li, h) for li in range(2) for h in range(8)]

                def q_steps(pi):
                    li, h = pairs[pi]
                    sl = slice(li * TN, (li + 1) * TN)
                    qb_ = 4 + pi % 2
                    db_ = pi % 2
                    steps = []

                    def s1():
                        fns = [lambda c3=c3: nc.tensor.matmul(
                            PS[qb_][0:96, :], lhsT=self.ring[:, sq_, c3 * 768 + h * 96: c3 * 768 + (h + 1) * 96],
                            rhs=cqn[:, c3, sl], start=(c3 == 0), stop=(c3 == 2)) for c3 in range(3)]
                        self.op(pe, fns, reads=[sqb, cqn_buf], writes=[PB[qb_]])
                        if hf == 1:
                            fns = [lambda c3=c3: nc.tensor.matmul(
                                PS[6][0:96, :], lhsT=self.ring[:, sq_, 2304 + c3 * 768 + h * 96: 2304 + c3 * 768 + (h + 1) * 96],
                                rhs=cqn[:, c3, sl], start=(c3 == 0), stop=(c3 == 2)) for c3 in range(3)]
                            self.op(pe, fns, reads=[sqb, cqn_buf], writes=[PB[6]])
                    steps.append(s1)
                    steps.append(lambda: self.op(act, lambda: nc.scalar.activation(out=sq96[:, db_, :], in_=PS[qb_][0:96, :], func=AF.Square),
                                                 reads=[PB[qb_]], writes=[sq_bufs[db_]]))
                    steps.append(lambda: self.op(pe, lambda: nc.tensor.matmul(PS[7][0:96, :], lhsT=self.ones_bf[0:96, 0:96], rhs=sq96[0:96, db_, :],
                                                                              start=True, stop=True), reads=[sq_bufs[db_], self.const_buf], writes=[PB[7]]))
                    steps.append(lambda: self.op(act, lambda: nc.scalar.activation(out=rs96[:, db_, :], in_=PS[7][0:96, :], func=AF.Ln, scale=1.0 / 96, bias=EPS),
                                                 reads=[PB[7]], writes=[rs_bufs[db_]]))
                    steps.append(lambda: self.op(act, lambda: nc.scalar.activation(out=rs96[:, db_, :], in_=rs96[:, db_, :], func=AF.Exp, scale=-0.5),
                                                 reads=[], writes=[rs_bufs[db_]]))
                    steps.append(lambda: self.op(dve, lambda: nc.vector.scalar_tensor_tensor(
                        out=Qn[0:64, h, :], in0=PS[qb_][0:64, :], scalar=gq2[0:64, :], in1=rs96[0:64, db_, :], op0=ALU.mult, op1=ALU.mult),
                        reads=[PB[qb_], rs_bufs[db_], gq2_buf], writes=[qn_bufs[h]]))

                    def s7():
                        if hf == 0:
                            self.op(dve, lambda: nc.vector.scalar_tensor_tensor(
                                out=Qn[64:96, h, :], in0=PS[qb_][64:96, :], scalar=g_q[64:96, :], in1=rs96[64:96, db_, :], op0=ALU.mult, op1=ALU.mult),
                                reads=[PB[qb_], rs_bufs[db_]], writes=[])
                        else:
                            self.op(dve, lambda: nc.vector.scalar_tensor_tensor(
                                out=t1[64:96, :], in0=PS[qb_][64:96, :], scalar=g_q[64:96, :], in1=cs["rope"][64:96, 0, sl],
                                op0=ALU.mult, op1=ALU.mult), reads=[PB[qb_]], writes=[t_buf])
                            self.op(dve, lambda: nc.vector.scalar_tensor_tensor(
                                out=t2[64:96, :], in0=PS[6][64:96, :], scalar=g_qs[64:96, :], in1=cs["rope"][64:96, 1, sl],
                                op0=ALU.mult, op1=ALU.mult), reads=[PB[6]], writes=[])
                            self.op(dve, lambda: nc.vector.tensor_tensor(out=t1[64:96, :], in0=t1[64:96, :], in1=t2[64:96, :], op=ALU.add),
                                    reads=[], writes=[])
                            self.op(dve, lambda: nc.vector.tensor_tensor(out=Qn[64:96, h, :], in0=t1[64:96, :], in1=rs96[64:96, db_, :], op=ALU.mult),
                                    reads=[rs_bufs[db_]], writes=[])
                        qn_bufs[h].w = self._tok(dve)
                    steps.append(s7)
                    return steps

                iters = []
                for pi, (li, h) in enumerate(pairs):
                    if hf == 1:
                        groups = [(0, TN, list(range(nkt)))]
                    else:
                        groups = [(s2 * 256, 256, [li * 4 + s2 * 2, li * 4 + s2 * 2 + 1]) for s2 in range(2)]
                    for gi, (q0, nq, kts) in enumerate(groups):
                        for ki, kt in enumerate(kts):
                            iters.append((pi, q0, nq, kt, ki == 0, ki == len(kts) - 1, gi == 0 and ki == 0))
                kb_done = {}
                qpend = {}

                def run_q(pi, n):
                    st = qpend.get(pi)
                    while st and n > 0:
                        st.pop(0)()
                        n -= 1

                def emit_k(pi):
                    nonlocal ei
                    li, h = pairs[pi]
                    kbi = pi % 2
                    if hf == 1:
                        kcols = [(0, 256, ckvc, ckvc_buf, 0), (256, 512, ckvn, ckvn_buf, 0), (768, 512, ckvn, ckvn_buf, 512)]
                    else:
                        kcols = [(li * TN, TN, ckvn, ckvn_buf, li * TN)]
                    for ci_, (c0, w, src, srcb, s0) in enumerate(kcols):
                        fns = [lambda rc=rc, src=src, s0=s0, w=w, h=h: nc.tensor.matmul(
                            PS[6][0:64, 0:w], lhsT=self.ring[:, sk_, rc * 512 + h * 64: rc * 512 + (h + 1) * 64],
                            rhs=src[:, rc, s0:s0 + w], start=(rc == 0), stop=(rc == 1)) for rc in range(2)]
                        self.op(pe, fns, reads=[skb, srcb], writes=[PB[6]])
                        if ei % 2 == 0:
                            self.op(act, lambda c0=c0, w=w, kbi=kbi: nc.scalar.activation(out=Kb[0:64, kbi, c0:c0 + w], in_=PS[6][0:64, 0:w], func=AF.Copy),
                                    reads=[PB[6]], writes=[kb_bufs[kbi]] if ci_ == 0 else [])
                        else:
                            self.op(dve, lambda c0=c0, w=w, kbi=kbi: nc.vector.tensor_copy(out=Kb[0:64, kbi, c0:c0 + w], in_=PS[6][0:64, 0:w]),
                                    reads=[PB[6]], writes=[kb_bufs[kbi]] if ci_ == 0 else [])
                        ei += 1
                    kb_done[pi] = [self._tok(act), self._tok(dve)]

                def emit_s(it):
                    nonlocal si
                    pi, q0, nq, kt, first, lastk, newhead = it
                    li, h = pairs[pi]
                    if newhead:
                        run_q(pi, 99)
                        emit_k(pi)
                        if pi + 1 < len(pairs):
                            qpend[pi + 1] = q_steps(pi + 1)
                    sbk = si % 2
                    si += 1
                    for tk_ in kb_done[pi]:
                        pe.wait(tk_)
                    self.op(pe, lambda: nc.tensor.matmul(
                        PS[sbk][:, 0:nq], lhsT=Kb[0:96, pi % 2, kt * 128:(kt + 1) * 128], rhs=Qn[0:96, h, q0:q0 + nq], start=True, stop=True),
                        reads=[kb_bufs[pi % 2], qn_bufs[h]], writes=[PB[sbk]])
                    return sbk

                qpend[0] = q_steps(0)
                nstep = 1 if hf == 1 else 2
                sb_next = emit_s(iters[0])
                for idx, it in enumerate(iters):
                    pi, q0, nq, kt, first, lastk, newhead = it
                    li, h = pairs[pi]
                    sbk = sb_next
                    self.op(act, lambda: nc.scalar.activation(
                        out=PT[:, sbk, 0:nq], in_=PS[sbk][:, 0:nq], func=AF.Exp, scale=rk[:, kt, h:h + 1]),
                        reads=[PB[sbk], rk_buf], writes=[pt_bufs[sbk]])
                    if idx + 1 < len(iters):
                        sb_next = emit_s(iters[idx + 1])
                    ob, dbk = 2, 3
                    self.op(pe, [lambda: nc.tensor.matmul(
                        PS[ob][0:64, 0:nq], lhsT=Vt[:, kt, h * 64:(h + 1) * 64], rhs=PT[:, sbk, 0:nq], start=first, stop=lastk),
                        lambda: nc.tensor.matmul(
                        PS[dbk][0:64, 0:nq], lhsT=self.ones_bf[:, 0:64], rhs=PT[:, sbk, 0:nq], start=first, stop=lastk)],
                        reads=[vt_buf, pt_bufs[sbk], self.const_buf], writes=[PB[ob], PB[dbk]] if first else [])
                    run_q(pi + 1, nstep)
                    if lastk:
                        tk = self._tok(pe)
                        PB[ob].w = tk
                        PB[dbk].w = tk
                        self.op(act, lambda: nc.scalar.activation(out=rden[:, 0, 0:nq], in_=PS[dbk][0:64, 0:nq], func=AF.Ln),
                                reads=[PB[dbk]], writes=[rden_bufs[0]])
                        self.op(act, lambda: nc.scalar.activation(out=rden[:, 0, 0:nq], in_=rden[:, 0, 0:nq], func=AF.Exp, scale=-1.0),
                                reads=[], writes=[rden_bufs[0]])
                        self.op(dve, lambda: nc.vector.tensor_tensor(
                            out=mlaT[0:64, h, li * TN + q0: li * TN + q0 + nq], in0=PS[ob][0:64, 0:nq], in1=rden[:, 0, 0:nq], op=ALU.mult),
                            reads=[PB[ob], rden_bufs[0]], writes=[mla_buf])
                self.barrier()
            wo = self.even_w_out[j]
            sa_, sab = self.ring_load([(0, [4, D], wout[:, 0:4, :])])
            sb1, sbb1 = self.ring_load([(0, [4, D], wo[512:768, :].rearrange("(h p) d -> p h d", p=64), 64)])
            sb2, sbb2 = self.ring_load([(0, [4, D], wo[768:1024, :].rearrange("(h p) d -> p h d", p=64), 64)])
            di = 0
            for li in range(2):
                t = 2 * hf + li
                jc = hf
                sl = slice(li * TN, (li + 1) * TN)
                for dc in range(KC):
                    bank = 4 + di % 3
                    di += 1
                    fns = [lambda c=c, dc=dc, bank=bank: nc.tensor.matmul(
                        PS[bank][:], lhsT=self.ring[:, sa_, c * 1024 + dc * 128: c * 1024 + (dc + 1) * 128], rhs=ogT[:, c, sl],
                        start=(c == 0), stop=False) for c in range(4)]
                    for hh_ in range(8):
                        sx = sb1 if hh_ < 4 else sb2
                        fns.append(lambda hh_=hh_, sx=sx, dc=dc, bank=bank: nc.tensor.matmul(
                            PS[bank][:], lhsT=self.ring[0:64, sx, (hh_ % 4) * 1024 + dc * 128: (hh_ % 4) * 1024 + (dc + 1) * 128],
                            rhs=mlaT[0:64, hh_, sl], start=False, stop=(hh_ == 7)))
                    self.op(pe, fns, reads=[sab, sbb1, sbb2, og_buf, mla_buf], writes=[PB[bank]])
                    xs = self.xT[:, dc, t * TN:(t + 1) * TN]
                    self.op(dve, lambda xs=xs, dc=dc, jc=jc, bank=bank: nc.vector.scalar_tensor_tensor(
                        out=xs, in0=PS[bank][:], scalar=self.modG[:, k, dc, jc:jc + 1], in1=xs,
                        op0=ALU.mult, op1=ALU.add), reads=[PB[bank], self.mod_buf], writes=[self.x_bufs[t]])
            self.barrier()
            for e_ in (pe, act, dve):
                e_.wait((self.st3[0], self.st3[1]))

    def rms_rstd_w(self, sq, sq_buf, rows, nfeat, out, out_buf, w):
        nc = self.nc
        self.op(self.pe, lambda: nc.tensor.matmul(self.psum[7][0:rows, 0:w], lhsT=self.ones_bf[0:rows, 0:rows], rhs=sq[0:rows, 0, 0:w],
                                                  start=True, stop=True), reads=[sq_buf, self.const_buf], writes=[self.pbuf[7]])
        self.op(self.act, lambda: nc.scalar.activation(out=out[0:rows, 0:w], in_=self.psum[7][0:rows, 0:w], func=AF.Ln, scale=1.0 / nfeat, bias=EPS),
                reads=[self.pbuf[7]], writes=[out_buf])
        self.op(self.act, lambda: nc.scalar.activation(out=out[0:rows, 0:w], in_=out[0:rows, 0:w], func=AF.Exp, scale=-0.5),
                reads=[], writes=[out_buf])

    def rms_rstd(self, sq, sq_buf, nch, rows, nfeat, out, out_buf):
        nc = self.nc
        fns = [lambda c=c: nc.tensor.matmul(self.psum[7][0:rows, :], lhsT=self.ones_bf[0:rows, 0:rows], rhs=sq[0:rows, c, :],
                                            start=(c == 0), stop=(c == nch - 1)) for c in range(nch)]
        self.op(self.pe, fns, reads=[sq_buf, self.const_buf], writes=[self.pbuf[7]])
        self.op(self.act, lambda: nc.scalar.activation(out=out[0:rows, :], in_=self.psum[7][0:rows, :], func=AF.Ln, scale=1.0 / nfeat, bias=EPS),
                reads=[self.pbuf[7]], writes=[out_buf])
        self.op(self.act, lambda: nc.scalar.activation(out=out[0:rows, :], in_=out[0:rows, :], func=AF.Exp, scale=-0.5),
                reads=[], writes=[out_buf])

    def odd_mixer(self, l):
        nc = self.nc
        j = l // 2
        k = 1
        win = self.odd_w_in[j].rearrange("(kc p) f -> p kc f", p=128)
        wout = self.odd_w_out[j].rearrange("(c p) d -> p c d", p=128)
        with contextlib.ExitStack() as st:
            wv = self.sb("odd_wv", [128, KC, 2048], BF16, st)
            wv_buf = self.buf("wv")
            hT = self.sb("odd_hT", [128, KC, TN], BF16, st)
            h_bufs = [self.buf("h")]
            mix = self.sb("odd_mix", [128, 16, TN], BF16, st)
            mix_buf = self.buf("mix")
            gv = self.sb("odd_gv", [128, 2048], BF16, st)
            gv_buf = self.buf("gv")
            sqf = self.sb("odd_sqf", [128, 2, 512], F32, st)
            sq_bufs = [self.buf("sqf"), self.buf("sqf")]
            ss = self.sb("odd_ss", [128, 16], F32, st)
            ss_bufs = [self.buf("ss"), self.buf("ss")]
            wp = self.sb("odd_wp", [128, 4, 128], BF16, st)
            wp_buf = self.buf("wp")
            ut = self.sb("odd_u", [128, 2, TN], F32, st)
            u_bufs = [self.buf("u"), self.buf("u")]
            tmp, rstd = self.norm_tmp(st)
            for kc in range(KC):
                self.dma(self.pool, self.wv_sem, wv[:, kc, :], win[:, kc, 2048:4096], writes=[wv_buf] if kc == 0 else [])
            wv_buf.w = (self.wv_sem[0], self.wv_sem[1])
            bi = 0
            mi = 0
            for t in range(NT):
                jc = 0 if t < NT // 2 else 1
                self.norm_modulate(k, [t], hT, h_bufs, tmp, rstd)
                gvs = [gv, tmp["sq"][:].rearrange("p a b -> p (a b)")]
                gvb = [gv_buf, tmp["buf"]]

                def vfront(q4):
                    nonlocal bi
                    gq, gqb = gvs[q4 % 2], gvb[q4 % 2]
                    so = (q4 % 2) * 8
                    for g in range(4):
                        bank = bi % 4
                        bi += 1
                        fns = [lambda kc=kc, g=g, bank=bank: nc.tensor.matmul(
                            self.psum[bank][:], lhsT=hT[:, kc, q4 * 128:(q4 + 1) * 128],
                            rhs=wv[:, kc, g * 512:(g + 1) * 512], start=(kc == 0), stop=(kc == KC - 1)) for kc in range(KC)]
                        self.op(self.pe, fns, reads=[h_bufs[0], wv_buf], writes=[self.pbuf[bank]])
                        self.op(self.act, [
                            lambda g=g, bank=bank: nc.scalar.activation(out=gq[:, g * 512:(g + 1) * 512], in_=self.psum[bank][:],
                                                                        func=AF.Gelu_apprx_tanh),
                            lambda g=g, bank=bank: nc.scalar.activation(out=sqf[:, g % 2, :], in_=gq[:, g * 512:(g + 1) * 512],
                                                                        func=AF.Square)],
                            reads=[self.pbuf[bank]], writes=([gqb] if g == 0 else []) + [sq_bufs[g % 2]])
                        self.op(self.dve, lambda g=g: nc.vector.tensor_reduce(
                            out=ss[:, so + g:so + g + 1], in_=sqf[:, g % 2, :], axis=mybir.AxisListType.X, op=ALU.add),
                            reads=[sq_bufs[g % 2]], writes=[ss_bufs[q4 % 2]])
                    gqb.w = (self.act.sems[self.act.si], self.act.cnt)

                def vback(q4):
                    nonlocal mi
                    gq, gqb = gvs[q4 % 2], gvb[q4 % 2]
                    so = (q4 % 2) * 8
                    sb_ = ss_bufs[q4 % 2]
                    self.op(self.dve, lambda: nc.vector.tensor_reduce(out=ss[:, so + 4:so + 5], in_=ss[:, so:so + 4], axis=mybir.AxisListType.X, op=ALU.add),
                            reads=[], writes=[sb_], same=True)
                    self.op(self.act, lambda: nc.scalar.activation(out=ss[:, so + 5:so + 6], in_=ss[:, so + 4:so + 5], func=AF.Ln, scale=1.0 / 2048, bias=EPS),
                            reads=[], writes=[sb_], same=True)
                    self.op(self.act, lambda: nc.scalar.activation(out=ss[:, so + 6:so + 7], in_=ss[:, so + 5:so + 6], func=AF.Exp, scale=-0.5),
                            reads=[], writes=[sb_], same=True)
                    self.op(self.dve, lambda: nc.vector.tensor_scalar(
                        out=wp[:].rearrange("p a b -> p (a b)"), in0=self.wsT[:, j].rearrange("p a b -> p (a b)"),
                        scalar1=ss[:, so + 6:so + 7], scalar2=None, op0=ALU.mult), reads=[sb_], writes=[wp_buf])
                    for g in range(4):
                        bank = 4 + mi % 2
                        mi += 1
                        fns = [lambda g=g, cc=cc, bank=bank: nc.tensor.matmul(
                            self.psum[bank][:, cc * 128:(cc + 1) * 128], lhsT=gq[:, (g * 4 + cc) * 128:(g * 4 + cc + 1) * 128],
                            rhs=wp[:, g, :], start=True, stop=True) for cc in range(4)]
                        self.op(self.pe, fns, reads=[gqb, wp_buf], writes=[self.pbuf[bank]])
                        for cc in range(4):
                            c16 = g * 4 + cc
                            self.op(self.dve, lambda g=g, cc=cc, c16=c16, bank=bank: nc.vector.scalar_tensor_tensor(
                                out=mix[:, c16, q4 * 128:(q4 + 1) * 128], in0=self.psum[bank][:, cc * 128:(cc + 1) * 128],
                                scalar=self.oddvg[:, j, c16:c16 + 1], in1=self.bsb[:, j, g, :], op0=ALU.mult, op1=ALU.add),
                                reads=[self.pbuf[bank]], writes=[mix_buf] if (q4 == 0 and c16 == 0) else [])

                for q0_ in (0, 2):
                    vfront(q0_)
                    vfront(q0_ + 1)
                    vback(q0_)
                    vback(q0_ + 1)
                mix_buf.w = (self.dve.sems[self.dve.si], self.dve.cnt)
                for sl in range(4):
                    s, sbuf_ = self.ring_load([(0, [KC, 512], win[:, :, sl * 512:(sl + 1) * 512])])
                    for cc in range(4):
                        c16 = sl * 4 + cc
                        bank = bi % 4
                        bi += 1
                        fns = [lambda kc=kc, cc=cc, s=s, bank=bank: nc.tensor.matmul(
                            self.psum[bank][:], lhsT=self.ring[:, s, kc * 512 + cc * 128: kc * 512 + (cc + 1) * 128],
                            rhs=hT[:, kc, :], start=(kc == 0), stop=(kc == KC - 1)) for kc in range(KC)]
                        self.op(self.pe, fns, reads=[sbuf_, h_bufs[0]], writes=[self.pbuf[bank]])
                        ub = c16 % 2
                        self.op(self.act, lambda ub=ub, bank=bank: nc.scalar.activation(
                            out=ut[:, ub, :], in_=self.psum[bank][:], func=AF.Gelu_apprx_tanh),
                            reads=[self.pbuf[bank]], writes=[u_bufs[ub]])
                        self.op(self.dve, lambda ub=ub, c16=c16: nc.vector.tensor_tensor(
                            out=mix[:, c16, :], in0=ut[:, ub, :], in1=mix[:, c16, :], op=ALU.mult),
                            reads=[u_bufs[ub]], writes=[mix_buf] if c16 == 0 else [])
                mix_buf.w = (self.dve.sems[self.dve.si], self.dve.cnt)
                mix_buf.r = {}
                for dc in range(KC):
                    if dc % 2 == 0:
                        so, sob = self.ring_load([(0, [16, 256], wout[:, :, dc * 128:(dc + 2) * 128])])
                    bank = 4 + mi % 3
                    mi += 1
                    fns = [lambda c16=c16, dc=dc, bank=bank, so=so: nc.tensor.matmul(
                        self.psum[bank][:], lhsT=self.ring[:, so, c16 * 256 + (dc % 2) * 128: c16 * 256 + (dc % 2 + 1) * 128],
                        rhs=mix[:, c16, :], start=(c16 == 0), stop=(c16 == 15)) for c16 in range(16)]
                    self.op(self.pe, fns, reads=[sob, mix_buf], writes=[self.pbuf[bank]])
                    xs = self.xT[:, dc, t * TN:(t + 1) * TN]
                    self.op(self.dve, lambda xs=xs, dc=dc, jc=jc, bank=bank: nc.vector.scalar_tensor_tensor(
                        out=xs, in0=self.psum[bank][:], scalar=self.modG[:, k, dc, jc:jc + 1], in1=xs,
                        op0=ALU.mult, op1=ALU.add), reads=[self.pbuf[bank], self.mod_buf], writes=[self.x_bufs[t]])

    def mod_blocks(self, l, par, cbs, first=False):
        nc = self.nc
        aw = self.ada_w[l].rearrange("(kc p) f -> p kc f", p=128)
        pb = self.pbuf[7]
        psv = self.psum[7][:, 0:144].rearrange("p (c j) -> p c j", j=2)
        for cb in cbs:
            s, sbuf_ = self.ring_load([(0, [KC, 512], aw[:, :, cb * 512:(cb + 1) * 512])])
            fns = []
            for cc in range(4):
                c = cb * 4 + cc
                for kc in range(KC):
                    fns.append(lambda c=c, cc=cc, kc=kc, s=s: nc.tensor.matmul(
                        psv[:, c, :], lhsT=self.ring[:, s, kc * 512 + cc * 128: kc * 512 + (cc + 1) * 128],
                        rhs=self.scond[:, kc, :], start=(kc == 0), stop=(kc == KC - 1)))
            self.op(self.pe, fns, reads=[sbuf_, self.scond_buf], writes=[pb] if first else [])
            first = False
            pb.w = (self.pe.sems[self.pe.si], self.pe.cnt)

    def mod_raw(self, l, par, ranges):
        nc = self.nc
        pb = self.pbuf[7]
        psv = self.psum[7][:, 0:144].rearrange("p (c j) -> p c j", j=2)
        mod, mod_buf = self.mod2[par], self.mod_bufs[par]
        for (c0, c1) in ranges:
            for j in range(2):
                self.op(self.dve, lambda j=j, c0=c0, c1=c1: nc.vector.tensor_tensor(
                    out=mod[:, c0:c1, j], in0=psv[:, c0:c1, j], in1=self.adab[:, l, c0:c1], op=ALU.add),
                    reads=[pb], writes=[mod_buf])

    def mod_derive(self, l, par, ks):
        nc = self.nc
        mod, modA, modB, modG, mod_buf = self.mod2[par], self.modA2[par], self.modB2[par], self.modG2[par], self.mod_bufs[par]
        for k in ks:
            sh = mod[:, (3 * k) * 8:(3 * k) * 8 + 8, :]
            sc = mod[:, (3 * k + 1) * 8:(3 * k + 1) * 8 + 8, :]
            gt = mod[:, (3 * k + 2) * 8:(3 * k + 2) * 8 + 8, :]
            for j in range(2):
                self.op(self.dve, lambda k=k, j=j, sc=sc: nc.vector.scalar_tensor_tensor(
                    out=modA[:, k, :, j], in0=sc[:, :, j], scalar=1.0, in1=self.normg[:, l, k, :],
                    op0=ALU.add, op1=ALU.mult), reads=[mod_buf], writes=[mod_buf], same=True)
            self.op(self.dve, lambda k=k, sh=sh: nc.vector.tensor_copy(out=modB[:, k, :, :], in_=sh), reads=[mod_buf], writes=[mod_buf], same=True)
            self.op(self.dve, lambda k=k, gt=gt: nc.vector.tensor_scalar(
                out=modG[:, k, :, :], in0=gt, scalar1=(1.0 if k == 1 else 0.5), scalar2=None, op0=ALU.mult),
                reads=[mod_buf], writes=[mod_buf], same=True)

    def norm_modulate(self, k, tiles, hT, h_bufs, tmp, rstd):
        for li, t in enumerate(tiles):
            self.norm_modulate_tile(k, t, li, hT, h_bufs, tmp, rstd)

    def norm_modulate_tile(self, k, t, li, hT, h_bufs, tmp, rstd):
        nc = self.nc
        if True:
            j = 0 if t < NT // 2 else 1
            xs = self.xT[:, :, t * TN:(t + 1) * TN]
            xb = self.x_bufs[t]
            tb = tmp["buf"]
            self.op(self.act, lambda xs=xs: nc.scalar.activation(out=tmp["sq"][:], in_=xs, func=AF.Square),
                    reads=[xb], writes=[tb])
            fns = [lambda kc=kc: nc.tensor.matmul(self.psum[7][:], lhsT=self.ones_bf[:], rhs=tmp["sq"][:, kc, :],
                                                  start=(kc == 0), stop=(kc == KC - 1)) for kc in range(KC)]
            self.op(self.pe, fns, reads=[tb, self.const_buf], writes=[self.pbuf[7]])
            rb = rstd["buf"]
            self.op(self.act, [lambda: nc.scalar.activation(out=rstd["t"][:], in_=self.psum[7][:], func=AF.Ln,
                                                            scale=1.0 / D, bias=EPS),
                               lambda: nc.scalar.activation(out=rstd["t"][:], in_=rstd["t"][:], func=AF.Exp, scale=-0.5)],
                    reads=[self.pbuf[7]], writes=[rb])
            for kc in range(KC):
                t2 = tmp["t2buf"][kc % 2]
                self.op(self.dve, lambda kc=kc, t=t, j=j: nc.vector.scalar_tensor_tensor(
                    out=tmp["t2"][:, kc % 2, :], in0=self.xT[:, kc, t * TN:(t + 1) * TN],
                    scalar=self.modA[:, k, kc, j:j + 1], in1=rstd["t"][:], op0=ALU.mult, op1=ALU.mult),
                    reads=[xb, rb, self.mod_buf], writes=[t2])
                self.op(self.act, lambda kc=kc, li=li, j=j: nc.scalar.activation(
                    out=hT[:, kc, li * TN:(li + 1) * TN], in_=tmp["t2"][:, kc % 2, :], func=AF.Identity,
                    bias=self.modB[:, k, kc, j:j + 1], scale=1.0),
                    reads=[t2, self.mod_buf], writes=[h_bufs[li]] if kc == 0 else [])
            h_bufs[li].w = (self.act.sems[self.act.si], self.act.cnt)

    def norm_tmp(self, stack):
        tmp = {"sq": self.sb("n_sq", [128, KC, TN], BF16, stack), "buf": self.buf("nsq"),
               "t2": self.sb("n_t2", [128, 2, TN], F32, stack), "t2buf": [self.buf("nt2"), self.buf("nt2")]}
        rstd = {"t": self.sb("n_rstd", [128, TN], F32, stack), "buf": self.buf("rstd")}
        return tmp, rstd

    def ffn(self, l, k, wg, wu, wd, hook=None):
        nc = self.nc
        with contextlib.ExitStack() as st:
            hT = self.sb("ffn_hT", [128, KC, NTOK], BF16, st)
            h_bufs = [self.buf("h") for _ in range(NT)]
            aT = self.sb("ffn_aT", [128, 2, GCH, TN], BF16, st)
            a_bufs = [self.buf("a"), self.buf("a")]
            sg = self.sb("ffn_sg", [128, 2, TN], F32, st)
            sg_bufs = [self.buf("sg"), self.buf("sg")]
            tmp, rstd = self.norm_tmp(st)
            self.norm_modulate(k, list(range(NT)), hT, h_bufs, tmp, rstd)
            wgl = wg[l].rearrange("(kc p) f -> p kc f", p=128)
            wul = wu[l].rearrange("(kc p) f -> p kc f", p=128)
            wdl = wd[l].rearrange("(c p) d -> p c d", p=128)
            pend = None
            u = 0
            ci = 0
            dn = 0

            def down_step(pd, dc):
                nonlocal dn
                s, sbuf_, t, ab, au = pd
                j = 0 if t < NT // 2 else 1
                if True:
                    bank = 4 + dn % 3
                    dn += 1
                    fns = [lambda c=c, dc=dc, s=s, au=au, bank=bank: nc.tensor.matmul(
                        self.psum[bank][:], lhsT=self.ring[:, s, 4096 + c * 1024 + dc * 128: 4096 + c * 1024 + (dc + 1) * 128],
                        rhs=aT[:, au, c, :], start=(c == 0), stop=(c == GCH - 1)) for c in range(GCH)]
                    self.op(self.pe, fns, reads=[sbuf_, ab], writes=[self.pbuf[bank]])
                    xs = self.xT[:, dc, t * TN:(t + 1) * TN]
                    self.op(self.dve, lambda xs=xs, dc=dc, j=j, bank=bank: nc.vector.scalar_tensor_tensor(
                        out=xs, in0=self.psum[bank][:], scalar=self.modG[:, k, dc, j:j + 1], in1=xs,
                        op0=ALU.mult, op1=ALU.add), reads=[self.pbuf[bank], self.mod_buf], writes=[self.x_bufs[t]])

            psteps = []

            def run_down(n):
                for _ in range(n):
                    if psteps:
                        pd_, dc_ = psteps.pop(0)
                        down_step(pd_, dc_)

            for g in range(NG):
                s, sbuf_ = self.ring_load([
                    (0, [KC, GCH * 128], wgl[:, :, g * GCH * 128:(g + 1) * GCH * 128]),
                    (2048, [KC, GCH * 128], wul[:, :, g * GCH * 128:(g + 1) * GCH * 128]),
                    (4096, [GCH, D], wdl[:, g * GCH:(g + 1) * GCH, :]),
                ])
                for t in range(NT):
                    au = u % 2
                    ab = a_bufs[au]
                    for c in range(GCH):
                        gb = ci % 2
                        ub = 2 + ci % 2
                        ci += 1
                        fns = [lambda kc=kc, c=c, s=s, t=t, gb=gb: nc.tensor.matmul(
                            self.psum[gb][:], lhsT=self.ring[:, s, kc * 256 + c * 128: kc * 256 + (c + 1) * 128],
                            rhs=hT[:, kc, t * TN:(t + 1) * TN], start=(kc == 0), stop=(kc == KC - 1)) for kc in range(KC)]
                        self.op(self.pe, fns, reads=[sbuf_, h_bufs[t]], writes=[self.pbuf[gb]])
                        run_down(KC // (2 * GCH))
                        fns = [lambda kc=kc, c=c, s=s, t=t, ub=ub: nc.tensor.matmul(
                            self.psum[ub][:], lhsT=self.ring[:, s, 2048 + kc * 256 + c * 128: 2048 + kc * 256 + (c + 1) * 128],
                            rhs=hT[:, kc, t * TN:(t + 1) * TN], start=(kc == 0), stop=(kc == KC - 1)) for kc in range(KC)]
                        self.op(self.pe, fns, reads=[sbuf_, h_bufs[t]], writes=[self.pbuf[ub]])
                        run_down(KC // (2 * GCH))
                        self.op(self.act, lambda gb=gb: nc.scalar.activation(out=sg[:, gb, :], in_=self.psum[gb][:], func=AF.Silu),
                                reads=[self.pbuf[gb]], writes=[sg_bufs[gb]])
                        self.op(self.dve, lambda gb=gb, ub=ub, au=au, c=c: nc.vector.tensor_tensor(
                            out=aT[:, au, c, :], in0=self.psum[ub][:], in1=sg[:, gb, :], op=ALU.mult),
                            reads=[self.pbuf[ub], sg_bufs[gb]], writes=[ab] if c == 0 else [])
                    ab.w = (self.dve.sems[self.dve.si], self.dve.cnt)
                    run_down(KC)
                    pend = (s, sbuf_, t, ab, au)
                    psteps.extend((pend, dc) for dc in range(KC))
                    u += 1
                if hook is not None:
                    hook(g)
            run_down(KC)
            self.barrier()


def _host_layout(inputs, core):
    i = core
    xp = np.asarray(inputs["x_prompt"])[4 * i:4 * i + 4].reshape(1024, D)
    xs = np.asarray(inputs["x_sample"])[i]
    xT = np.ascontiguousarray(np.concatenate([xp, xs], axis=0).T)
    cond = np.stack([np.asarray(inputs["c_ctx"]), np.asarray(inputs["c"])[i]], axis=-1)
    condT = np.ascontiguousarray(cond.reshape(KC, 128, 2).transpose(1, 0, 2))
    return {"xT": xT, "condT": condT,
            "ckvcT": np.ascontiguousarray(np.asarray(inputs["cache_ckv"])[i].transpose(0, 2, 1)),
            "kropecT": np.ascontiguousarray(np.asarray(inputs["cache_krope"])[i].transpose(0, 2, 1)),
            "sgla": np.ascontiguousarray(np.asarray(inputs["state_gla"])[i])}


def _const_tables():
    idx = np.arange(128)
    same = (idx[:, None] // 64) == (idx[None, :] // 64)
    le = idx[:, None] <= idx[None, :]
    ge = idx[:, None] >= idx[None, :]
    gt = idx[:, None] > idx[None, :]
    lt = idx[:, None] < idx[None, :]
    c = np.float32(-1.0 / 16.0)
    gm = np.zeros((128, 6, 128), np.float32)
    gm[:, 0] = np.where(same & le, c, 0)
    gm[:, 1] = np.where(same & ge, c, 0)
    gm[:, 2] = np.where(same & gt, c, 0)
    gm[:, 3] = np.where(same & lt, c, 0)
    gm[:, 4] = np.where(same & le, 1, 0)
    gm[:, 5] = np.where(same & ge, 1, 0)
    m4 = np.zeros((128, 2, 4, 128), np.float32)
    m4[:, 0] = gm[:, 4][:, None, :]
    m4[:, 1] = gm[:, 5][:, None, :]
    pos = np.arange(1024)
    row = (pos // 64).astype(np.float32)
    col = (pos % 64).astype(np.float32)
    inv = (np.float32(10000.0) ** (-np.arange(8, dtype=np.float32) / np.float32(8))).astype(np.float32)
    ang = np.concatenate([row[:, None] * inv, col[:, None] * inv], axis=-1).astype(np.float32)
    cosv, sinv = np.cos(ang).astype(np.float32), np.sin(ang).astype(np.float32)
    rope = np.zeros((96, 2, 1024), np.float32)
    for f in range(32):
        rope[64 + f, 0] = cosv[:, f // 2]
        rope[64 + f, 1] = sinv[:, f // 2] * (-1.0 if f % 2 == 0 else 1.0)
    return gm, m4, rope


def _shared_layout(inputs):
    sh = {}
    A = lambda n: np.asarray(inputs[n], dtype=np.float32)
    ada_b = A("ada_b")
    sh["adab"] = np.ascontiguousarray(ada_b.reshape(DEPTH, 72, 128).transpose(2, 0, 1))
    ng = A("norm_g")
    sh["normg"] = np.ascontiguousarray(ng.reshape(DEPTH, 3, KC, 128).transpose(3, 0, 1, 2))
    vg = A("odd_v_g")
    sh["oddvg"] = np.ascontiguousarray(vg.reshape(2, 16, 128).transpose(2, 0, 1))
    sh["wsT"] = np.ascontiguousarray(A("odd_ws").transpose(3, 0, 1, 2))
    sh["bsb"] = np.ascontiguousarray(np.broadcast_to(A("odd_bs")[None], (128, 2, 4, 128)))
    for n in ("odd_w_in", "odd_w_out", "ada_w", "ffn1_wg", "ffn1_wu", "ffn1_wd", "ffn2_wg", "ffn2_wu", "ffn2_wd",
              "even_w_in", "even_w_out"):
        sh[n] = np.ascontiguousarray(A(n))
    swap = np.arange(32) ^ 1
    win = A("even_w_in")
    wsw = np.zeros((2, D, 96), np.float32)
    wsw[:, :, 64:96] = win[:, :, 2208:2240][:, :, swap]
    sh["win_sw"] = wsw
    qb = A("mla_qb_w")
    sh["qbw"] = np.ascontiguousarray(qb.reshape(2, 384, 768))
    qs = qb.copy()
    qs[..., 64:96] = qb[..., 64:96][..., swap]
    sh["qbw_sw"] = np.ascontiguousarray(qs.reshape(2, 384, 768))
    kvb = A("mla_kvb_w")
    sh["kvbK"] = np.ascontiguousarray(kvb[..., :64].reshape(2, 256, 512))
    sh["kvbV"] = np.ascontiguousarray(kvb[..., 64:].reshape(2, 256, 512))
    gm, m4, rope = _const_tables()
    sh["gmask"], sh["mask4"], sh["rope"] = np.ascontiguousarray(gm[:, 0:4]), m4, rope
    qn, kn = A("mla_qn_g"), A("mla_kn_g")
    sw96 = np.arange(96)
    sw96[64:96] = 64 + swap
    sh["qkng"] = np.ascontiguousarray(np.stack([qn, qn[:, sw96], kn, kn[:, sw96]], axis=-1).transpose(1, 0, 2))
    sh["glag"] = np.ascontiguousarray(A("gla_norm_g").T)
    sh["qag"] = np.ascontiguousarray(A("mla_qa_g").reshape(2, 3, 128).transpose(2, 0, 1))
    sh["kvag"] = np.ascontiguousarray(A("mla_kva_g").reshape(2, 2, 128).transpose(2, 0, 1))
    w2 = A("gla_gate_w2")
    gb = A("gla_gate_b")
    w2aug = np.zeros((33, 2, 2, 256), np.float32)
    w2aug[0:16, :, 0, :] = w2[:, 0].transpose(1, 0, 2)
    w2aug[16:32, :, 1, :] = w2[:, 1].transpose(1, 0, 2)
    w2aug[32] = gb
    sh["w2aug"] = w2aug
    return sh


def run(inputs, cfg, cores=8, trace=False):
    b = Builder(cfg)
    nc = b.build()
    sh = _shared_layout(inputs)
    in_maps = []
    for i in range(cores):
        m = dict(sh)
        m.update(_host_layout(inputs, i))
        in_maps.append(m)
    res = run_bass_kernel_spmd(nc, in_maps, core_ids=list(range(cores)), trace=trace)
    return res


def kernel(**inputs):
    res = run(inputs, {})
    B, S, DB, DS = 32, 256, 8, 1024
    y_prompt = np.zeros((B, S, D), np.float32)
    y_sample = np.zeros((DB, DS, D), np.float32)
    new_ckv = np.zeros((B, 2, S, 256), np.float32)
    new_krope = np.zeros((B, 2, S, 32), np.float32)
    new_gla = np.zeros((B, 2, 2, 4, 64, 128), np.float32)
    for i, r in enumerate(res.results):
        yT = np.asarray(r["yT"])
        y_prompt[4 * i:4 * i + 4] = yT[:, :1024].T.reshape(4, S, D)
        y_sample[i] = yT[:, 1024:].T
        ck = np.asarray(r["ckvT_o"])
        new_ckv[4 * i:4 * i + 4] = ck.reshape(2, 256, 4, S).transpose(2, 0, 3, 1)
        kr = np.asarray(r["kropeT_o"])
        new_krope[4 * i:4 * i + 4] = kr.reshape(2, 32, 4, S).transpose(2, 0, 3, 1)
        new_gla[4 * i:4 * i + 4] = np.asarray(r["gla_o"])
    return (y_prompt, y_sample, new_ckv, new_krope, new_gla)


def check_states(r, states, inp):
    for (l, st) in states:
        j = l // 2
        ck = np.asarray(r["ckvT_o"])[j].reshape(256, 4, 256).transpose(1, 2, 0)
        kr = np.asarray(r["kropeT_o"])[j].reshape(32, 4, 256).transpose(1, 2, 0)
        gl = np.asarray(r["gla_o"])[:, j]
        for nm, a, b in (("ckv", ck, np.asarray(st[0])), ("krope", kr, np.asarray(st[1])), ("gla", gl, np.asarray(st[2]))):
            print("state", nm, "layer", l, "relvar", ((a - b) ** 2).mean() / (b ** 2).mean())
```
